# Optimizing a Trainium2 kernel written in Bass

```python
import math
import jax, jax.numpy as jnp
from jax import lax
import numpy as np

D_MODEL = 1024
BATCH = 2
SEQ = 8192
DEPTH = 1

GRID_W = 64
CTX_LEN = 256
EPS = 1e-6
N_MOD = 6
SSD_EXPAND = 2
D_SSD = SSD_EXPAND * D_MODEL
SSD_HEADDIM = 64
SSD_HEADS = D_SSD // SSD_HEADDIM
SSD_GROUPS = 4
SSD_HPG = SSD_HEADS // SSD_GROUPS
SSD_STATE = 128
SSD_CONV = 3
SSD_CHUNK = 128
D_BC = SSD_GROUPS * SSD_STATE
D_XBC = D_SSD + 2 * D_BC
DT_MIN = 1e-3
DT_MAX = 1e-1
ATT_HEADS = 16
ATT_KV_HEADS = 4
ATT_GROUP = ATT_HEADS // ATT_KV_HEADS
ATT_HEADDIM = 64
WINDOW = 128
ATT_BLOCK = 128
D_Q = ATT_HEADS * ATT_HEADDIM
D_KV = ATT_KV_HEADS * ATT_HEADDIM
ROPE_BASE = 10000.0
D_FF = 2816
FFN_CONV = 3
N_BRANCH = 2
Z0 = 0
XBC0 = Z0 + D_SSD
DT0 = XBC0 + D_XBC
Q0 = DT0 + 2 * SSD_HEADS
K0 = Q0 + D_Q
V0 = K0 + D_KV
G0 = V0 + D_KV
D_IN = G0 + N_BRANCH * D_MODEL

kernel_name = 'hybrid_ssd_swa_convffn_dit_block'


def rmsnorm(x, g):
    x32 = x.astype(jnp.float32)
    y = x32 * lax.rsqrt(jnp.mean(x32 * x32, axis=-1, keepdims=True) + EPS)
    return (y * g.astype(jnp.float32)).astype(x.dtype)


def modulate(x, g, shift, scale):
    return rmsnorm(x, g) * (1 + scale) + shift


def dwconv_centred(x, w, b):
    K, C = w.shape
    y = lax.conv_general_dilated(x, w[:, None, :].astype(x.dtype), window_strides=(1,),
                                 padding=[(K // 2, K // 2)],
                                 dimension_numbers=('NWC', 'WIO', 'NWC'),
                                 feature_group_count=C)
    return y + b.astype(x.dtype)


def segsum(a):
    T = a.shape[-1]
    cs = jnp.cumsum(a, axis=-1)
    diff = cs[..., :, None] - cs[..., None, :]
    return jnp.where(jnp.tril(jnp.ones((T, T), dtype=bool)), diff, -jnp.inf)


def ssd_scan(xs, la, bm, cm, h0, with_output):
    f32 = jnp.float32
    Bsz, L, G, R, P = xs.shape
    N = bm.shape[-1]
    T = SSD_CHUNK
    nc = L // T
    xc = xs.astype(f32).reshape(Bsz, nc, T, G, R, P)
    bc = bm.astype(f32).reshape(Bsz, nc, T, G, N)
    cc = cm.astype(f32).reshape(Bsz, nc, T, G, N)
    ac = la.astype(f32).reshape(Bsz, nc, T, G, R).transpose(0, 3, 4, 1, 2)
    cs = jnp.cumsum(ac, axis=-1)
    decay_to_end = jnp.exp(cs[..., -1:] - cs)
    states = jnp.einsum('bcsgn,bgrcs,bcsgrp->bcgrpn', bc, decay_to_end, xc)
    if h0 is None:
        h0 = jnp.zeros((Bsz, G, R, P, N), f32)
    states = jnp.concatenate([h0[:, None].astype(f32), states], axis=1)
    chunk_decay = jnp.exp(segsum(jnp.pad(cs[..., -1], ((0, 0), (0, 0), (0, 0), (1, 0)))))
    states = jnp.einsum('bgrzc,bcgrpn->bzgrpn', chunk_decay, states)
    final = states[:, -1]
    if not with_output:
        return None, final
    cb = jnp.einsum('bclgn,bcsgn->bgcls', cc, bc)
    scores = cb[:, :, None] * jnp.exp(segsum(ac))
    y_diag = jnp.einsum('bgrcls,bcsgrp->bclgrp', scores, xc)
    y_off = jnp.einsum('bclgn,bcgrpn,bgrcl->bclgrp', cc, states[:, :-1], jnp.exp(cs))
    return (y_diag + y_off).reshape(Bsz, L, G, R, P).astype(xs.dtype), final


def ssd_prepare(xbc_raw, dt_raw, conv_w, conv_b, dt_bias, a_log):
    xbc = jax.nn.silu(dwconv_centred(xbc_raw, conv_w, conv_b))
    Bsz, L, _ = xbc.shape
    xs = xbc[..., :D_SSD].reshape(Bsz, L, SSD_GROUPS, SSD_HPG, SSD_HEADDIM)
    bm = xbc[..., D_SSD:D_SSD + D_BC].reshape(Bsz, L, SSD_GROUPS, SSD_STATE)
    cm = xbc[..., D_SSD + D_BC:].reshape(Bsz, L, SSD_GROUPS, SSD_STATE)
    dt = jax.nn.softplus(dt_raw.astype(jnp.float32).reshape(Bsz, L, 2, SSD_HEADS)
                         + dt_bias.astype(jnp.float32))
    la = dt * (-jnp.exp(a_log.astype(jnp.float32)))
    dt = dt.reshape(Bsz, L, 2, SSD_GROUPS, SSD_HPG)
    la = la.reshape(Bsz, L, 2, SSD_GROUPS, SSD_HPG)
    return xs, bm, cm, dt, la


def ssd_bidir(xs, bm, cm, dt, la, h0f, h0b, with_output):
    xf = xs * dt[:, :, 0, :, :, None].astype(xs.dtype)
    xb = xs * dt[:, :, 1, :, :, None].astype(xs.dtype)
    yf, hf = ssd_scan(xf, la[:, :, 0], bm, cm, h0f, with_output)
    yb, hb = ssd_scan(jnp.flip(xb, axis=1), jnp.flip(la[:, :, 1], axis=1), jnp.flip(bm, axis=1),
                      jnp.flip(cm, axis=1), h0b, with_output)
    y = yf + jnp.flip(yb, axis=1) if with_output else None
    return y, hf, hb


def ssd_output(y, xs, z, d_skip, g):
    Bsz, L = y.shape[:2]
    y = y + d_skip.reshape(SSD_GROUPS, SSD_HPG, 1).astype(y.dtype) * xs
    return rmsnorm(y.reshape(Bsz, L, D_SSD) * jax.nn.silu(z), g)


def axial_rope(t, row, col):
    half = ATT_HEADDIM // 2
    inv = ROPE_BASE ** (-jnp.arange(0, half, 2, dtype=jnp.float32) / half)

    def rot(u, pos):
        ang = pos.astype(jnp.float32)[:, None] * inv[None]
        cos = jnp.cos(ang)[None, :, None, :].astype(u.dtype)
        sin = jnp.sin(ang)[None, :, None, :].astype(u.dtype)
        u1, u2 = u[..., :half // 2], u[..., half // 2:]
        return jnp.concatenate([u1 * cos - u2 * sin, u2 * cos + u1 * sin], axis=-1)

    return jnp.concatenate([rot(t[..., :half], row), rot(t[..., half:], col)], axis=-1)


def window_ctx_attention(q, k, v, kc, vc, sink):
    Bsz, L, H, Dh = q.shape
    nb = L // ATT_BLOCK
    scale = Dh ** -0.5
    qb = q.reshape(Bsz, nb, ATT_BLOCK, ATT_KV_HEADS, ATT_GROUP, Dh)

    def band(t):
        tb = t.reshape(Bsz, nb, ATT_BLOCK, ATT_KV_HEADS, Dh)
        tp = jnp.pad(tb, ((0, 0), (1, 1), (0, 0), (0, 0), (0, 0)))
        return jnp.concatenate([tp[:, :-2], tp[:, 1:-1], tp[:, 2:]], axis=2)

    kb, vb = band(k), band(v)
    s_win = jnp.einsum('bnqkgd,bnjkd->bnkgqj', qb, kb) * scale
    s_ctx = jnp.einsum('bnqkgd,bckd->bnkgqc', qb, kc) * scale
    qpos = jnp.arange(nb)[:, None] * ATT_BLOCK + jnp.arange(ATT_BLOCK)[None]
    kpos = (jnp.arange(nb)[:, None] - 1) * ATT_BLOCK + jnp.arange(3 * ATT_BLOCK)[None]
    valid = ((jnp.abs(qpos[:, :, None] - kpos[:, None, :]) <= WINDOW)
             & (kpos >= 0)[:, None, :] & (kpos < L)[:, None, :])
    s_win = jnp.where(valid[None, :, None, None], s_win, -jnp.inf)
    s_sink = jnp.broadcast_to(sink.reshape(1, 1, ATT_KV_HEADS, ATT_GROUP, 1, 1).astype(s_win.dtype),
                              s_win.shape[:-1] + (1,))
    logits = jnp.concatenate([s_win, s_ctx, s_sink], axis=-1).astype(jnp.float32)
    p = jax.nn.softmax(logits, axis=-1).astype(v.dtype)
    nw = 3 * ATT_BLOCK
    Lc = kc.shape[1]
    o = (jnp.einsum('bnkgqj,bnjkd->bnqkgd', p[..., :nw], vb)
         + jnp.einsum('bnkgqc,bckd->bnqkgd', p[..., nw:nw + Lc], vc))
    return o.reshape(Bsz, L, H * Dh)


def ctx_self_attention(q, k, v, sink):
    Bsz, Lc, H, Dh = q.shape
    qg = q.reshape(Bsz, Lc, ATT_KV_HEADS, ATT_GROUP, Dh)
    s = jnp.einsum('bqkgd,bckd->bkgqc', qg, k) * (Dh ** -0.5)
    s_sink = jnp.broadcast_to(sink.reshape(1, ATT_KV_HEADS, ATT_GROUP, 1, 1).astype(s.dtype),
                              s.shape[:-1] + (1,))
    p = jax.nn.softmax(jnp.concatenate([s, s_sink], axis=-1).astype(jnp.float32), axis=-1)
    o = jnp.einsum('bkgqc,bckd->bqkgd', p[..., :Lc].astype(v.dtype), v)
    return o.reshape(Bsz, Lc, H * Dh)


def merge_branches(p, y_ssd, y_att, w_o_ssd, w_o_att, w_out):
    gates = jax.nn.sigmoid(p[..., G0:D_IN])
    g_ssd, g_att = gates[..., :D_MODEL], gates[..., D_MODEL:]
    return (g_ssd * (y_ssd @ w_o_ssd) + g_att * (y_att @ w_o_att)) @ w_out


def token_mixer(h_lat, h_ctx, row, col, w_in, conv_w, conv_b, dt_bias, a_log, d_skip, ssd_g,
                w_o_ssd, w_o_att, sink, w_out, with_ctx_out):
    Bsz, L, _ = h_lat.shape
    Lc = h_ctx.shape[1]
    p_lat = h_lat @ w_in
    p_ctx = h_ctx @ w_in
    xs_c, bm_c, cm_c, dt_c, la_c = ssd_prepare(p_ctx[..., XBC0:DT0], p_ctx[..., DT0:Q0],
                                               conv_w, conv_b, dt_bias, a_log)
    y_c, hf_c, hb_c = ssd_bidir(xs_c, bm_c, cm_c, dt_c, la_c, None, None, with_ctx_out)
    xs_l, bm_l, cm_l, dt_l, la_l = ssd_prepare(p_lat[..., XBC0:DT0], p_lat[..., DT0:Q0],
                                               conv_w, conv_b, dt_bias, a_log)
    y_l, _, _ = ssd_bidir(xs_l, bm_l, cm_l, dt_l, la_l, hf_c, hb_c, True)
    ssd_lat = ssd_output(y_l, xs_l, p_lat[..., Z0:XBC0], d_skip, ssd_g)
    q_l = axial_rope(p_lat[..., Q0:K0].reshape(Bsz, L, ATT_HEADS, ATT_HEADDIM), row, col)
    k_l = axial_rope(p_lat[..., K0:V0].reshape(Bsz, L, ATT_KV_HEADS, ATT_HEADDIM), row, col)
    v_l = p_lat[..., V0:G0].reshape(Bsz, L, ATT_KV_HEADS, ATT_HEADDIM)
    k_c = p_ctx[..., K0:V0].reshape(Bsz, Lc, ATT_KV_HEADS, ATT_HEADDIM)
    v_c = p_ctx[..., V0:G0].reshape(Bsz, Lc, ATT_KV_HEADS, ATT_HEADDIM)
    att_lat = window_ctx_attention(q_l, k_l, v_l, k_c, v_c, sink)
    out_lat = merge_branches(p_lat, ssd_lat, att_lat, w_o_ssd, w_o_att, w_out)
    if not with_ctx_out:
        return out_lat, None
    ssd_ctx = ssd_output(y_c, xs_c, p_ctx[..., Z0:XBC0], d_skip, ssd_g)
    q_c = p_ctx[..., Q0:K0].reshape(Bsz, Lc, ATT_HEADS, ATT_HEADDIM)
    att_ctx = ctx_self_attention(q_c, k_c, v_c, sink)
    out_ctx = merge_branches(p_ctx, ssd_ctx, att_ctx, w_o_ssd, w_o_att, w_out)
    return out_lat, out_ctx


def conv_ffn(h, w_up, conv_w, conv_b, w_down):
    u = dwconv_centred(h @ w_up, conv_w, conv_b)
    a, b = u[..., :D_FF], u[..., D_FF:]
    return (jax.nn.silu(a) * b) @ w_down


def setup_inputs(seed: int = 0) -> dict:
    key = jax.random.key(seed)
    ks = jax.random.split(key, 24)
    f32 = jnp.float32

    def nrm(k, shape, scale):
        return jax.random.normal(k, shape, f32) * scale

    dt0 = jnp.exp(jax.random.uniform(ks[10], (DEPTH, 2, SSD_HEADS), f32,
                                     math.log(DT_MIN), math.log(DT_MAX)))
    return {
        'x': nrm(ks[0], (BATCH, SEQ, D_MODEL), 1.0),
        'c': nrm(ks[1], (BATCH, D_MODEL), 1.0),
        'ctx': nrm(ks[2], (BATCH, CTX_LEN, D_MODEL), 1.0),
        'c_ctx': nrm(ks[3], (D_MODEL,), 1.0),
        'w_mod': nrm(ks[4], (DEPTH, D_MODEL, N_MOD * D_MODEL), 0.5 * D_MODEL ** -0.5),
        'b_mod': nrm(ks[5], (DEPTH, N_MOD * D_MODEL), 0.01),
        'norm1_g': 1.0 + nrm(ks[6], (DEPTH, D_MODEL), 0.05),
        'norm2_g': 1.0 + nrm(ks[7], (DEPTH, D_MODEL), 0.05),
        'w_in': nrm(ks[8], (DEPTH, D_MODEL, D_IN), D_MODEL ** -0.5),
        'ssd_conv_w': nrm(ks[9], (DEPTH, SSD_CONV, D_XBC), SSD_CONV ** -0.5),
        'ssd_conv_b': nrm(ks[11], (DEPTH, D_XBC), 0.01),
        'ssd_dt_bias': dt0 + jnp.log(-jnp.expm1(-dt0)),
        'ssd_a_log': jnp.log(jax.random.uniform(ks[12], (DEPTH, 2, SSD_HEADS), f32, 1.0, 16.0)),
        'ssd_d': 1.0 + nrm(ks[13], (DEPTH, SSD_HEADS), 0.1),
        'ssd_norm_g': 1.0 + nrm(ks[14], (DEPTH, D_SSD), 0.05),
        'w_o_ssd': nrm(ks[15], (DEPTH, D_SSD, D_MODEL), D_SSD ** -0.5),
        'w_o_att': nrm(ks[16], (DEPTH, D_Q, D_MODEL), D_Q ** -0.5),
        'att_sink': nrm(ks[17], (DEPTH, ATT_HEADS), 0.5),
        'w_out': nrm(ks[18], (DEPTH, D_MODEL, D_MODEL), D_MODEL ** -0.5),
        'w_up': nrm(ks[19], (DEPTH, D_MODEL, 2 * D_FF), D_MODEL ** -0.5),
        'ffn_conv_w': nrm(ks[20], (DEPTH, FFN_CONV, 2 * D_FF), FFN_CONV ** -0.5),
        'ffn_conv_b': nrm(ks[21], (DEPTH, 2 * D_FF), 0.01),
        'w_down': nrm(ks[22], (DEPTH, D_FF, D_MODEL), D_FF ** -0.5),
        'final_g': 1.0 + nrm(ks[23], (D_MODEL,), 0.05),
    }


def reference(x, c, ctx, c_ctx, w_mod, b_mod, norm1_g, norm2_g, w_in, ssd_conv_w, ssd_conv_b,
              ssd_dt_bias, ssd_a_log, ssd_d, ssd_norm_g, w_o_ssd, w_o_att, att_sink, w_out,
              w_up, ffn_conv_w, ffn_conv_b, w_down, final_g):
    L = x.shape[1]
    ROWS = L // GRID_W
    row = jnp.broadcast_to(jnp.arange(ROWS)[:, None], (ROWS, GRID_W)).reshape(-1)
    col = jnp.broadcast_to(jnp.arange(GRID_W)[None, :], (ROWS, GRID_W)).reshape(-1)
    for l in range(DEPTH):
        last = l == DEPTH - 1
        mod = jax.nn.silu(c) @ w_mod[l] + b_mod[l]
        sh1, sc1, gt1, sh2, sc2, gt2 = jnp.split(mod[:, None, :], N_MOD, axis=-1)
        mod_c = jax.nn.silu(c_ctx) @ w_mod[l] + b_mod[l]
        csh1, csc1, cgt1, csh2, csc2, cgt2 = jnp.split(mod_c, N_MOD, axis=-1)
        h_lat = modulate(x, norm1_g[l], sh1, sc1)
        h_ctx = modulate(ctx, norm1_g[l], csh1, csc1)
        o_lat, o_ctx = token_mixer(h_lat, h_ctx, row, col, w_in[l], ssd_conv_w[l], ssd_conv_b[l],
                                   ssd_dt_bias[l], ssd_a_log[l], ssd_d[l], ssd_norm_g[l],
                                   w_o_ssd[l], w_o_att[l], att_sink[l], w_out[l], not last)
        x = x + gt1 * o_lat
        x = x + gt2 * conv_ffn(modulate(x, norm2_g[l], sh2, sc2), w_up[l], ffn_conv_w[l],
                               ffn_conv_b[l], w_down[l])
        if not last:
            ctx = ctx + cgt1 * o_ctx
            ctx = ctx + cgt2 * conv_ffn(modulate(ctx, norm2_g[l], csh2, csc2), w_up[l],
                                        ffn_conv_w[l], ffn_conv_b[l], w_down[l])
    return rmsnorm(x, final_g)
```

```python
import os
from contextlib import ExitStack
import numpy as np
import concourse.bass as bass
import concourse.mybir as mybir
from concourse.bass_utils import run_bass_kernel_spmd

F32 = mybir.dt.float32
BF16 = mybir.dt.bfloat16
AF = mybir.ActivationFunctionType
ALU = mybir.AluOpType

D = 1024
L = 8192
NQ = 4
TOK = 2048
EPS = 1e-6
XBC0 = 2048
DT0 = 5120
Q0 = 5184
K0 = 6208
V0 = 6464
G0 = 6720
D_FF = 2816
NOWN = 20
NSLOT = 48
NSB = 16
NFLAG = 20 + 2 * NSLOT + 2 * NSLOT

DEBUG = os.environ.get("KDEBUG", "")


class Reg:
    __slots__ = ("name", "lw", "rd", "excl")

    def __init__(self, name, excl=False):
        self.name = name
        self.lw = None
        self.rd = []
        self.excl = excl


class Op:
    __slots__ = ("eng", "fn", "deps", "dma", "sem", "val", "needed", "prev")


class Sched:
    ENG = ("pe", "act", "dve", "pool", "sp")

    def __init__(self, nc, n_dma_sems=12):
        self.nc = nc
        self.esem = {e: nc.alloc_semaphore("s_" + e) for e in self.ENG}
        self.ecnt = {e: 0 for e in self.ENG}
        self.dq = ("sp", "pool", "act")
        self.dsem = {q: [nc.alloc_semaphore("d_%s%d" % (q, i)) for i in range(n_dma_sems)] for q in self.dq}
        self.dcnt = {q: [0] * n_dma_sems for q in self.dq}
        self.drr = {q: 0 for q in self.dq}
        self.dlast = {q: [None] * n_dma_sems for q in self.dq}
        self.ops = []
        self.known = {e: {} for e in self.ENG}
        self.nops = 0

    def op(self, eng, fn, r=(), w=(), dma=False):
        o = Op()
        o.eng = eng; o.fn = fn; o.dma = dma; o.needed = False; o.sem = None; o.val = 0; o.prev = None
        deps = []
        for t in r:
            if t.lw is not None:
                deps.append(t.lw)
            if t.excl:
                deps.extend(x for x in t.rd if x.eng != eng)
        for t in w:
            if t.lw is not None:
                deps.append(t.lw)
            deps.extend(t.rd)
        o.deps = []
        seen = set()
        for d in deps:
            if d is o or id(d) in seen:
                continue
            seen.add(id(d))
            if d.eng == "pe" and eng == "pe" and not d.dma and not dma:
                continue
            o.deps.append(d)
        for t in w:
            t.lw = o
            t.rd = []
        for t in r:
            if t.lw is not o:
                t.rd.append(o)
        self.ops.append(o)
        return o

    def flush(self):
        ops = self.ops
        for o in ops:
            for d in o.deps:
                d.needed = True
        last = {}
        for o in ops:
            if o.dma:
                o.needed = True
            else:
                last[o.eng] = o
        for o in last.values():
            o.needed = True
        for o in ops:
            if o.dma:
                q = o.eng
                i = self.drr[q]
                self.drr[q] = (i + 1) % len(self.dsem[q])
                o.prev = self.dlast[q][i]
                self.dcnt[q][i] += 16
                o.sem = ("d", q, i)
                o.val = self.dcnt[q][i]
                self.dlast[q][i] = o
            elif o.needed:
                self.ecnt[o.eng] += 1
                o.sem = ("e", o.eng)
                o.val = self.ecnt[o.eng]
        per = {e: [] for e in self.ENG}
        for o in ops:
            per[o.eng].append(o)
        fin = [o for o in ops if (o.dma or o is last.get(o.eng))]

        def semh(key):
            return self.esem[key[1]] if key[0] == "e" else self.dsem[key[1]][key[2]]

        def emit(e, eh):
            kn = self.known[e]

            def wait(key, val):
                if kn.get(key, 0) >= val:
                    return
                eh.wait_ge(semh(key), val)
                kn[key] = val

            for o in per[e]:
                for d in o.deps:
                    if d.sem is not None:
                        wait(d.sem, d.val)
                if o.dma and o.prev is not None:
                    wait(o.prev.sem, o.prev.val)
                ins = o.fn(eh)
                if o.dma:
                    ins.then_inc(semh(o.sem), 16)
                elif o.needed:
                    ins.then_inc(semh(o.sem), 1)
            for d in fin:
                if d.eng == e and not d.dma:
                    continue
                wait(d.sem, d.val)

        with self.nc.Block() as block:
            @block.tensor
            def _(t):
                emit("pe", t)

            @block.scalar
            def _(t):
                emit("act", t)

            @block.vector
            def _(t):
                emit("dve", t)

            @block.gpsimd
            def _(t):
                emit("pool", t)

            @block.sync
            def _(t):
                emit("sp", t)
        self.nops += len(ops)
        self.ops = []


class Arena:
    def __init__(self, lo, hi):
        self.lo = lo; self.hi = hi; self.cur = lo

    def take(self, n, name=""):
        o = self.cur
        assert o + n <= self.hi, "arena overflow for %s: need %d have %d" % (name, n, self.hi - o)
        self.cur = o + n
        return o

    def close(self):
        self.cur = self.lo


def A(name, *args, **kw):
    return lambda e: getattr(e, name)(*args, **kw)


class K:
    def __init__(self):
        self.nc = bass.Bass("TRN2", target_bir_lowering=False)
        self.S = Sched(self.nc)
        self.dbg = {}
        self.pbank = 0
        self.es = ExitStack()
        self.uid = 0
        self.arC = Arena(17408, 26624)
        self.arACC = Arena(26624, 43008)
        self.arHT = Arena(43008, 83968)
        self.arYZ = Arena(83968, 157696)
        self.arW = Arena(157696, 229376)
        self.arBIG = Arena(43008, 229376 - 18 * 1024)

    def T(self, name, shape, dt=F32, es=None):
        ar = es or self.arC
        nbytes = int(np.prod(shape[1:])) * (2 if dt == BF16 else 4)
        nbytes = (nbytes + 63) // 64 * 64
        off = ar.take(nbytes, name)
        self.offs = getattr(self, "offs", {})
        self.offs[name] = off
        self.uid += 1
        t = self.nc.alloc_sbuf_tensor_at("sb%d_%s" % (self.uid, name), list(shape), dt, offset=off)
        return t, Reg(name)

    def din(self, name, shape, dt=F32):
        return self.nc.dram_tensor(name, list(shape), dt, kind="ExternalInput").ap()

    def pe(self, fn, r, w): return self.S.op("pe", fn, r, w)
    def act(self, fn, r, w): return self.S.op("act", fn, r, w)
    def dve(self, fn, r, w): return self.S.op("dve", fn, r, w)
    def pool(self, fn, r, w): return self.S.op("pool", fn, r, w)
    def dma(self, out, in_, r, w, q="sp"): return self.S.op(q, A("dma_start", out=out, in_=in_), r, w, dma=True)

    def bank(self):
        i = self.pbank
        self.pbank = (i + 1) % 8
        return self.PS[i], self.PR[i]

    def dump(self, name, ap, reg, shape, dt=F32):
        if name not in DEBUG.split(","):
            return
        o = self.nc.dram_tensor("dbg_" + name, list(shape), dt, kind="ExternalOutput").ap()
        self.dma(o, ap, list(reg) if isinstance(reg, (list, tuple)) else [reg], [], q="sp")
        self.dbg[name] = "dbg_" + name

    def build(self):
        nc = self.nc
        k = self
        I = {}
        I["x_own"] = k.din("x_own", [NOWN * 128, D])
        I["x_oth"] = k.din("x_oth", [NSLOT * 128, D])
        I["x_oth_h"] = k.din("x_oth_h", [2 * NSLOT, D])
        I["ctx"] = k.din("ctx", [256, D])
        I["cvec"] = k.din("cvec", [128, 16])
        I["flags"] = k.din("flags", [128, NFLAG])
        I["w_mod"] = k.din("w_mod", [D, 6 * D])
        I["bmodT"] = k.din("bmodT", [128, 48])
        I["n1g"] = k.din("n1g", [128, 8])
        I["n2g"] = k.din("n2g", [128, 8])
        I["fg_row"] = k.din("fg_row", [128, D])
        I["w_in"] = k.din("w_in", [D, 8768])
        I["w_qkp"] = k.din("w_qkp", [D, 1280])
        I["ssd_cw"] = k.din("ssd_cw", [128, 24 * 3])
        I["ssd_cb"] = k.din("ssd_cb", [128, 24])
        I["dtb"] = k.din("dtb", [128, 64])
        I["alog"] = k.din("alog", [128, 64])
        I["dsk"] = k.din("dsk", [128, 32])
        I["sng"] = k.din("sng", [128, 16])
        I["w_o_ssd"] = k.din("w_o_ssd", [2048, D])
        I["w_o_att"] = k.din("w_o_att", [D, D])
        I["sink"] = k.din("sink", [128, 16])
        I["w_out"] = k.din("w_out", [D, D])
        I["w_up"] = k.din("w_up", [D, 2 * D_FF])
        I["ffn_cw"] = k.din("ffn_cw", [128, 44 * 3])
        I["ffn_cb"] = k.din("ffn_cb", [128, 44])
        I["w_down"] = k.din("w_down", [D_FF, D])
        I["cosT"] = k.din("cosT", [64, NOWN * 128])
        I["sinT"] = k.din("sinT", [64, NOWN * 128])
        I["consts"] = k.din("consts", [128, 3 * 128])
        self.I = I
        self.out = nc.dram_tensor("out", [TOK, D], F32, kind="ExternalOutput").ap()
        self.x1s = nc.dram_tensor("x1s", [TOK, D], F32, kind="Internal").ap()

        self.PS = []
        self.PR = []
        for i in range(8):
            self.PS.append(self.es.enter_context(nc.psum_tensor("ps%d" % i, [128, 512], F32)))
            self.PR.append(Reg("ps%d" % i, excl=True))

        stop = os.environ.get("KSTOP", "")
        for ph in ("setup", "others", "own_h", "ssd", "ssd_proj", "att", "wout", "ffn"):
            getattr(self, "phase_" + ph)()
            if stop == ph:
                break
        return nc

    def phase_setup(self):
        k = self; nc = self.nc; I = self.I
        es = Arena(229376 - 18 * 1024, 229376)
        self.cst, self.rcst = k.T("cst", [128, 3 * 128])
        self.cstb, self.rcstb = k.T("cstb", [128, 3 * 128], BF16)
        self.ones, self.rones = k.T("ones", [128, 128])
        self.flags, self.rflags = k.T("flags", [128, NFLAG])
        self.modT, self.rmodT = k.T("modT", [128, 48 * 2])
        self.gs, self.rgs = k.T("gs", [128, 8 * 4])
        self.negA, self.rnegA = k.T("negA", [128, 64])
        self.esink, self.resink = k.T("esink", [128, 16])
        self.dtb, self.rdtb = k.T("dtb", [128, 64])
        self.dsk, self.rdsk = k.T("dsk", [128, 32])
        self.sng, self.rsng = k.T("sng", [128, 16])
        self.scw, self.rscw = k.T("scw", [128, 72])
        self.scb, self.rscb = k.T("scb", [128, 24])
        self.n12, self.rn12 = k.T("n12", [128, 16])
        cvec, rcvec = k.T("cvec", [128, 16], es=es)
        sc, rsc = k.T("sc", [128, 16], es=es)
        bmodT, rbmodT = k.T("bmodT", [128, 48], es=es)
        wm = [k.T("wm%d" % i, [128, 8 * 256], es=es) for i in range(2)]

        k.dma(self.cst[:], I["consts"], [], [self.rcst])
        k.dma(self.flags[:], I["flags"], [], [self.rflags])
        k.dma(cvec[:], I["cvec"], [], [rcvec])
        k.dma(bmodT[:], I["bmodT"], [], [rbmodT])
        k.dma(self.n12[:, 0:8], I["n1g"], [], [self.rn12])
        k.dma(self.n12[:, 8:16], I["n2g"], [], [self.rn12])
        k.dma(self.negA[:], I["alog"], [], [self.rnegA])
        k.dma(self.esink[:], I["sink"], [], [self.resink])
        k.dma(self.dtb[:], I["dtb"], [], [self.rdtb])
        k.dma(self.dsk[:], I["dsk"], [], [self.rdsk])
        k.dma(self.sng[:], I["sng"], [], [self.rsng])
        k.dma(self.scw[:], I["ssd_cw"], [], [self.rscw])
        k.dma(self.scb[:], I["ssd_cb"], [], [self.rscb])
        k.dve(A("tensor_copy", out=self.cstb[:], in_=self.cst[:]), [self.rcst], [self.rcstb])
        k.pool(A("memset", self.ones[:], 1.0), [], [self.rones])
        k.act(A("activation", out=sc[:], in_=cvec[:], func=AF.Silu), [rcvec], [rsc])
        k.act(A("activation", out=self.negA[:], in_=self.negA[:], func=AF.Exp), [self.rnegA], [self.rnegA])
        k.dve(A("tensor_scalar", out=self.negA[:], in0=self.negA[:], scalar1=-1.0, scalar2=None, op0=ALU.mult), [self.rnegA], [self.rnegA])
        k.act(A("activation", out=self.esink[:], in_=self.esink[:], func=AF.Exp), [self.resink], [self.resink])

        wmv = I["w_mod"].rearrange("(kc p) c -> p kc c", p=128)

        def mod_ct(ct):
            wt, rw = wm[ct % 2]
            wt3 = wt[:].rearrange("p (kc c) -> p kc c", kc=8)
            k.dma(wt3, wmv[:, :, ct * 256:(ct + 1) * 256], [], [rw])
            for j in range(2):
                blk = ct * 2 + j
                ps, rps = self.PS[7], self.PR[7]
                for kc in range(8):
                    k.pe(A("matmul", ps[:, 400 + 2 * j:402 + 2 * j], wt3[:, kc, j * 128:(j + 1) * 128], sc[:, 2 * kc:2 * kc + 2],
                           start=(kc == 0), stop=(kc == 7)), [rsc, rw], [rps])
                rm_ = self.rmodT if blk < 16 else self.rmodT2
                k.dve(A("tensor_scalar", out=self.modT[:, 2 * blk:2 * blk + 2], in0=ps[:, 400 + 2 * j:402 + 2 * j], scalar1=bmodT[:, blk:blk + 1],
                        scalar2=None, op0=ALU.add), [rps, rbmodT], [rm_])
        self.mod_ct = mod_ct
        self.rmodT2 = Reg("modT2")
        for ct in range(8):
            mod_ct(ct)
        m3 = self.modT[:].rearrange("p (b v) -> p b v", v=2)
        g3 = self.gs[:].rearrange("p (kc v) -> p kc v", v=4)
        for v in range(2):
            k.dve(A("scalar_tensor_tensor", out=g3[:, :, v], in0=m3[:, 8:16, v], scalar=1.0, in1=self.n12[:, 0:8],
                    op0=ALU.add, op1=ALU.mult), [self.rmodT, self.rn12], [self.rgs])
        k.dump("modT", self.modT[:], self.rmodT, [128, 96])
        k.dump("gs", self.gs[:], self.rgs, [128, 32])

    def norm_a(self, xt, rxt, rows, tmp):
        self.norm_a_multi([(xt, rxt, rows, tmp)])

    def norm_a_multi(self, items):
        k = self
        for (xt, rxt, rows, ((junk, rjunk), (ss, rss), (xn, rxn))) in items:
            k.dve(A("memset", ss[:, 0:1], 0.0), [], [rss])
        for (xt, rxt, rows, ((junk, rjunk), (ss, rss), (xn, rxn))) in items:
            k.act(A("activation", out=junk[0:rows, :], in_=xt[0:rows, :], func=AF.Square, accum_out=ss[0:rows, 0:1]), [rxt], [rjunk, rss])
        for (xt, rxt, rows, ((junk, rjunk), (ss, rss), (xn, rxn))) in items:
            k.dve(A("tensor_scalar", out=ss[0:rows, 1:2], in0=ss[0:rows, 0:1], scalar1=1.0 / D, scalar2=EPS, op0=ALU.mult, op1=ALU.add), [rss], [rss])
        for (xt, rxt, rows, ((junk, rjunk), (ss, rss), (xn, rxn))) in items:
            k.act(A("activation", out=ss[0:rows, 1:2], in_=ss[0:rows, 1:2], func=AF.Sqrt), [rss], [rss])
        for (xt, rxt, rows, ((junk, rjunk), (ss, rss), (xn, rxn))) in items:
            k.dve(A("reciprocal", out=ss[0:rows, 2:3], in_=ss[0:rows, 1:2]), [rss], [rss])
        for (xt, rxt, rows, ((junk, rjunk), (ss, rss), (xn, rxn))) in items:
            k.act(A("activation", out=xn[0:rows, :], in_=xt[0:rows, :], func=AF.Copy, scale=ss[0:rows, 2:3]), [rxt, rss], [rxn])

    def norm_b(self, rows, dst_fn, rdst, gcol, shblk, v, tmp, src_view=None, bankfn=None, dst3=None):
        k = self
        (junk, rjunk), (ss, rss), (xn, rxn) = tmp
        sv = src_view or (lambda a: a)
        ps, rps = (bankfn or k.bank)()
        psb = ps[:].bitcast(BF16)
        for kc in range(8):
            k.pe(A("transpose", psb[:, kc * 128:kc * 128 + rows], xn[0:rows, kc * 128:(kc + 1) * 128], self.cstb[0:rows, 0:rows]),
                 [rxn, self.rcstb], [rps])
        g3 = self.gs[:].rearrange("p (kc v) -> p kc v", v=4)
        m3 = self.modT[:].rearrange("p (b v) -> p b v", v=2)
        if dst3 is not None and rows == 128:
            p3 = psb[:, 0:1024].rearrange("p (kc t) -> p kc t", kc=8)
            k.dve(A("tensor_tensor", out=dst3, in0=p3, in1=g3[:, :, gcol:gcol + 1].to_broadcast([128, 8, 128]), op=ALU.mult),
                  [rps, self.rgs], [rdst])
            k.dve(A("tensor_tensor", out=dst3, in0=dst3, in1=m3[:, shblk:shblk + 8, v:v + 1].to_broadcast([128, 8, 128]), op=ALU.add),
                  [rdst, self.rmodT], [rdst])
            return
        for kc in range(8):
            k.act(A("activation", out=dst_fn(kc), in_=sv(psb[:, kc * 128:kc * 128 + rows]), func=AF.Identity,
                    scale=g3[:, kc, gcol:gcol + 1], bias=m3[:, shblk + kc, v:v + 1]), [rps, self.rgs, self.rmodT], [rdst])

    def norm_T(self, xt, rxt, rows, dst_fn, rdst, gcol, shblk, v, tmp, src_view=None):
        self.norm_a(xt, rxt, rows, tmp)
        self.norm_b(rows, dst_fn, rdst, gcol, shblk, v, tmp, src_view)

    def load_w(self, stage, dst3, src, c0, ncols, scale_ap=None, eng="pool"):
        k = self
        st, rst = stage
        st3 = st[:, 0:8 * ncols].rearrange("p (kc c) -> p kc c", kc=8)
        k.dma(st3, src[:, c0:c0 + ncols].rearrange("(kc p) c -> p kc c", p=128), [], [rst])
        dst, rdst = dst3
        k.S.op(eng, A("tensor_copy", out=dst, in_=st3), [rst], [rdst])

    def phase_others(self):
        k = self; nc = self.nc; I = self.I
        es = self.arBIG
        self.acc, self.racc = k.T("acc", [128, 2 * 2048], es=self.arACC)
        self.kcT, self.rkcT = k.T("kcT", [64, 4 * 256], BF16)
        self.vca, self.rvca = k.T("vca", [128, 2 * 4 * 65], BF16)
        logP, rlogP = k.T("logP", [128, 64], es=es)
        wres, rwres = k.T("wres", [128, 8 * 2624], BF16, es=es)
        wres3 = wres[:].rearrange("p (kc c) -> p kc c", kc=8)
        wkv, rwkv = k.T("wkv", [128, 8 * 512], BF16, es=es)
        wkv3 = wkv[:].rearrange("p (kc c) -> p kc c", kc=8)
        stg = [k.T("stg%d" % i, [128, 8 * 128], es=es) for i in range(2)]
        xts = [k.T("xto%d" % i, [128, D], es=es) for i in range(3)]
        xh, rxh = k.T("xh", [6, D], es=es)
        junk_ = k.T("junk", [128, D], BF16, es=es)
        tmps = [(junk_, k.T("ss%d" % i, [128, 4], es=es), k.T("xn%d" % i, [128, D], BF16, es=es)) for i in range(4)]
        tmp = tmps[0]
        hTo = [k.T("hTo%d" % i, [128, 8 * 390], BF16, es=es) for i in range(2)]
        raw = [k.T("raw%d" % i, [128, 388], es=es) for i in range(3)]
        xc = [k.T("xc%d" % i, [128, 388], BF16, es=es) for i in range(4)]
        xtok = [[k.T("xtok%d_%d" % (i, s), [128, 2048], BF16, es=es) for s in range(3)] for i in range(2)]
        btok = [[k.T("btok%d_%d" % (i, s), [128, 512], BF16, es=es) for s in range(3)] for i in range(2)]
        xp = [[k.T("xp%d_%d" % (d_, s), [128, 2048], BF16, es=es) for s in range(3)] for d_ in range(2)]
        dtr, rdtr = k.T("dtr", [128, 192], es=es)
        dte, rdte = k.T("dte", [128, 192], es=es)
        la, rla = k.T("la", [128, 192], es=es)
        arg, rarg = k.T("arg", [128, 192], es=es)
        wts = [k.T("wt%d" % i, [128, 192], es=es) for i in range(2)]
        wt_, rwt = wts[0]
        L1, rL1 = k.T("L1", [128, 192], es=es)
        hTc = self.nc.alloc_sbuf_tensor_at("sb_hTc_alias", [128, 8 * 258], BF16, offset=self.offs["hTo0"])
        rhTc = hTo[0][1]
        short = [0]

        def sbank():
            i = short[0]; short[0] = (i + 1) % 3
            return self.PS[i], self.PR[i]

        k.dve(A("memset", self.acc[:], 0.0), [], [self.racc])
        k.dve(A("memset", logP[:], 0.0), [], [rlogP])
        k.dve(A("memset", self.vca[:], 1.0), [], [self.rvca])
        for i in range(20):
            k.load_w(stg[i % 2], (wres3[:, :, i * 128:(i + 1) * 128], rwres), I["w_in"], XBC0 + i * 128, 128)
        k.load_w(stg[0], (wres3[:, :, 2560:2624], rwres), I["w_in"], DT0, 64)
        for i in range(4):
            k.load_w(stg[(i + 1) % 2], (wkv3[:, :, i * 128:(i + 1) * 128], rwkv), I["w_in"], K0 + i * 128, 128)
        cw3 = self.scw[:].rearrange("p (b t) -> p b t", t=3)
        idb = self.cstb[:, 0:128]
        triU = self.cst[:, 128:256]; triL = self.cst[:, 256:384]

        def conv_block(ps, n, cb, rw, xcw):
            return conv3(k, ps, self.PRmap[id(ps)], n, cw3[:, cb, :], self.scb[:, cb:cb + 1], rw, xcw[0][:, 0:n], xcw[1], AF.Silu,
                         [self.rscw, self.rscb], defer=True)
        self.conv_block = conv_block

        def dt_chain(nsl, hT3, col0, stride, flag0, split=False):
            psd, rpsd = self.PS[6], self.PR[6]
            for s in range(nsl):
                for kc in range(8):
                    k.pe(A("matmul", psd[:, s * 64:(s + 1) * 64], hT3[:, kc, col0 + s * stride:col0 + s * stride + 128], wres3[:, kc, 2560:2624],
                           start=(kc == 0), stop=(kc == 7)), [rwres, self.rhcur], [rpsd])
            n = nsl * 64
            k.dve(A("tensor_tensor", out=dtr[:, 0:n].rearrange("p (s c) -> p s c", c=64), in0=psd[:, 0:n].rearrange("p (s c) -> p s c", c=64),
                    in1=self.dtb[:].unsqueeze(1).to_broadcast([128, nsl, 64]), op=ALU.add), [rpsd, self.rdtb], [rdtr])
            k.act(A("activation", out=dtr[:, 0:n], in_=dtr[:, 0:n], func=AF.Exp), [rdtr], [rdtr])
            k.act(A("activation", out=dte[:, 0:n], in_=dtr[:, 0:n], func=AF.Ln, bias=1.0), [rdtr], [rdte])
            if flag0 is not None:
                for s in range(nsl):
                    for d in range(2):
                        f = flag0 + s * 2 + d
                        k.dve(A("tensor_scalar", out=dte[:, s * 64 + d * 32:s * 64 + d * 32 + 32], in0=dte[:, s * 64 + d * 32:s * 64 + d * 32 + 32],
                                scalar1=self.flags[:, f:f + 1], scalar2=None, op0=ALU.mult), [rdte, self.rflags], [rdte])
            k.dve(A("tensor_tensor", out=la[:, 0:n].rearrange("p (s c) -> p s c", c=64), in0=dte[:, 0:n].rearrange("p (s c) -> p s c", c=64),
                    in1=self.negA[:].unsqueeze(1).to_broadcast([128, nsl, 64]), op=ALU.mult), [rdte, self.rnegA], [rla])
            if split:
                return None, None
            return dt_cs(nsl)

        def dt_cs(nsl):
            psc, rpsc = self.PS[7], self.PR[7]
            for s in range(nsl):
                k.pe(A("matmul", psc[:, s * 64:s * 64 + 32], triU, la[:, s * 64:s * 64 + 32], start=True, stop=True), [rla, self.rcst], [rpsc])
                k.pe(A("matmul", psc[:, s * 64 + 32:s * 64 + 64], triL, la[:, s * 64 + 32:s * 64 + 64], start=True, stop=True), [rla, self.rcst], [rpsc])
                k.pe(A("matmul", psc[:, 192 + s * 64:192 + (s + 1) * 64], self.ones[:], la[:, s * 64:(s + 1) * 64], start=True, stop=True),
                     [rla, self.rones], [rpsc])
            return psc, rpsc
        self.dt_chain = dt_chain
        self.PRmap = {id(self.PS[i]): self.PR[i] for i in range(8)}

        def states(nsl, xt_set, bt_set, wtp=None):
            states_x(nsl, xt_set, bt_set, wtp)
            states_mm(nsl, xt_set, bt_set, wtp)

        def states_x(nsl, xt_set, bt_set, wtp=None):
            wt_, rwt = wtp or wts[0]
            for d in range(2):
                for s in range(nsl):
                    (k.pool if s == 2 else k.dve)(A("tensor_tensor", out=xp[d][s][0][:].rearrange("p (h c) -> p h c", c=64),
                             in0=xt_set[s][0][:].rearrange("p (h c) -> p h c", c=64),
                             in1=wt_[:, s * 64 + d * 32:s * 64 + d * 32 + 32].unsqueeze(2).to_broadcast([128, 32, 64]), op=ALU.mult),
                           [xt_set[s][1], rwt], [xp[d][s][1]])

        def states_mm(nsl, xt_set, bt_set, wtp=None):
            for d in range(2):
                for g in range(4):
                    ps, rps = sbank()
                    for s in range(nsl):
                        k.pe(A("matmul", ps[:, 0:512], bt_set[s][0][:, g * 128:(g + 1) * 128], xp[d][s][0][:, g * 512:(g + 1) * 512],
                               start=(s == 0), stop=(s == nsl - 1)), [bt_set[s][1], xp[d][s][1]], [rps])
                    a = self.acc[:, d * 2048 + g * 512:d * 2048 + (g + 1) * 512]
                    k.dve(A("tensor_tensor", out=a, in0=ps[:, 0:512], in1=a, op=ALU.add), [rps, self.racc], [self.racc])

        def states_old(nsl, xt_set, bt_set, wtp=None):
            wt_, rwt = wtp or wts[0]
            for d in range(2):
                for s in range(nsl):
                    (k.pool if s == 2 else k.dve)(A("tensor_tensor", out=xp[d][s][0][:].rearrange("p (h c) -> p h c", c=64),
                             in0=xt_set[s][0][:].rearrange("p (h c) -> p h c", c=64),
                             in1=wt_[:, s * 64 + d * 32:s * 64 + d * 32 + 32].unsqueeze(2).to_broadcast([128, 32, 64]), op=ALU.mult),
                           [xt_set[s][1], rwt], [xp[d][s][1]])
                for g in range(4):
                    ps, rps = sbank()
                    for s in range(nsl):
                        k.pe(A("matmul", ps[:, 0:512], bt_set[s][0][:, g * 128:(g + 1) * 128], xp[d][s][0][:, g * 512:(g + 1) * 512],
                               start=(s == 0), stop=(s == nsl - 1)), [bt_set[s][1], xp[d][s][1]], [rps])
                    a = self.acc[:, d * 2048 + g * 512:d * 2048 + (g + 1) * 512]
                    k.dve(A("tensor_tensor", out=a, in0=ps[:, 0:512], in1=a, op=ALU.add), [rps, self.racc], [self.racc])

        def proj_blocks(nsl, hT3, ncol, slot_off, xt_set, bt_set, hook=None):
            tb = [(self.PS[3 + s], self.PR[3 + s]) for s in range(nsl)]

            def trans(cb, xcw):
                for s in range(nsl):
                    tbb = tb[s][0][:].bitcast(BF16)
                    j = cb % 8
                    k.pe(A("transpose", tbb[:, j * 128:(j + 1) * 128], xcw[0][:, slot_off(s):slot_off(s) + 128], idb), [xcw[1], self.rcstb], [tb[s][1]])
                    if cb == 7:
                        k.dve(A("tensor_copy", out=xt_set[s][0][:, 0:1024], in_=tbb[:, 0:1024]), [tb[s][1]], [xt_set[s][1]])
                    elif cb == 15:
                        k.act(A("activation", out=xt_set[s][0][:, 1024:2048], in_=tbb[:, 0:1024], func=AF.Copy), [tb[s][1]], [xt_set[s][1]])
                    elif cb == 19:
                        k.dve(A("tensor_copy", out=bt_set[s][0][:, 0:512], in_=tbb[:, 0:512]), [tb[s][1]], [bt_set[s][1]])

            pend = []
            for cb in range(20):
                ps, rps = sbank()
                for kc in range(8):
                    k.pe(A("matmul", ps[:, 0:ncol], wres3[:, kc, cb * 128:(cb + 1) * 128], hT3[:, kc, 0:ncol], start=(kc == 0), stop=(kc == 7)),
                         [rwres, self.rhcur], [rps])
                xcw = xc[cb % 4]
                fin = conv_block(ps, ncol - 2, cb, raw[cb % 3], xcw)
                if pend:
                    pend[-1][2]()
                if len(pend) == 2:
                    trans(pend[0][0], pend[0][1])
                    pend.pop(0)
                pend.append((cb, xcw, fin))
                if hook is not None and cb in hook:
                    hook[cb]()
            pend[-1][2]()
            for p_ in pend:
                trans(p_[0], p_[1])

        xo = I["x_oth"]

        def prep_dma(sb):
            for s in range(3):
                xt_, rxt = xts[s]
                k.dma(xt_[:], xo[(sb * 3 + s) * 128:(sb * 3 + s + 1) * 128, :], [], [rxt])
            k.dma(xh[:], I["x_oth_h"][sb * 6:sb * 6 + 6, :], [], [rxh])

        def prep_a(sb, dma=True):
            if dma:
                prep_dma(sb)
            k.norm_a_multi([(xts[s][0], xts[s][1], 128, tmps[s]) for s in range(3)] + [(xh, rxh, 6, tmps[3])])

        def prep_b(sb):
            hT_, rhT = hTo[sb % 2]
            hT3 = hT_[:].rearrange("p (kc c) -> p kc c", kc=8)
            for s in range(3):
                k.norm_b(128, (lambda kc, s=s: hT3[:, kc, s * 130 + 1:s * 130 + 129]), rhT, 0, 0, 0, tmps[s], bankfn=sbank)
            k.norm_b(6, (lambda kc: hT3[:, kc, :].rearrange("p (s t) -> p s t", t=130)[:, :, 0::129]), rhT, 0, 0, 0, tmps[3],
                     src_view=lambda a: a.rearrange("p (s t) -> p s t", t=2), bankfn=sbank)
            hv = self.flags[:, 116 + sb * 6:116 + sb * 6 + 6].rearrange("p (s t) -> p s t", t=2)
            for kc in range(8):
                hh = hT3[:, kc, :].rearrange("p (s t) -> p s t", t=130)[:, :, 0::129]
                k.dve(A("tensor_tensor", out=hh, in0=hh, in1=hv, op=ALU.mult), [rhT, self.rflags], [rhT])

        def nb_tile(sb, s):
            hT_, rhT = hTo[sb % 2]
            hT3 = hT_[:].rearrange("p (kc c) -> p kc c", kc=8)
            if s < 3:
                k.norm_b(128, (lambda kc, s=s: hT3[:, kc, s * 130 + 1:s * 130 + 129]), rhT, 0, 0, 0, tmps[s], bankfn=sbank,
                         dst3=(hT3[:, :, s * 130 + 1:s * 130 + 129] if s != 1 else None))
            else:
                k.norm_b(6, (lambda kc: hT3[:, kc, :].rearrange("p (s t) -> p s t", t=130)[:, :, 0::129]), rhT, 0, 0, 0, tmps[3],
                         src_view=lambda a: a.rearrange("p (s t) -> p s t", t=2), bankfn=sbank)
                hv = self.flags[:, 116 + sb * 6:116 + sb * 6 + 6].rearrange("p (s t) -> p s t", t=2)
                for kc in range(8):
                    hh = hT3[:, kc, :].rearrange("p (s t) -> p s t", t=130)[:, :, 0::129]
                    k.dve(A("tensor_tensor", out=hh, in0=hh, in1=hv, op=ALU.mult), [rhT, self.rflags], [rhT])

        def dt1(sb):
            hT_, rhT = hTo[sb % 2]
            hT3 = hT_[:].rearrange("p (kc c) -> p kc c", kc=8)
            old_ = getattr(self, "rhcur", None)
            self.rhcur = rhT
            dt_chain(3, hT3, 1, 130, 20 + sb * 6, split=True)
            self.rhcur = old_

        def dt2(sb):
            wtp = wts[sb % 2]
            psc, rpsc = dt_cs(3)
            for s in range(3):
                k.dve(A("tensor_tensor", out=logP[:], in0=psc[:, 192 + s * 64:192 + (s + 1) * 64], in1=logP[:], op=ALU.add), [rpsc, rlogP], [rlogP])
                k.dve(A("tensor_tensor", out=arg[:, s * 64:(s + 1) * 64], in0=logP[:], in1=psc[:, s * 64:(s + 1) * 64], op=ALU.subtract),
                      [rpsc, rlogP], [rarg])
            k.act(A("activation", out=arg[:], in_=arg[:], func=AF.Exp), [rarg], [rarg])
            k.dve(A("tensor_tensor", out=wtp[0][:], in0=arg[:], in1=dte[:], op=ALU.mult), [rarg, rdte], [wtp[1]])

        prep_a(0)
        for s in range(4):
            nb_tile(0, s)
        dt1(0)
        prep_a(1)
        prev = None
        for sb in range(NSB):
            hT_, rhT = hTo[sb % 2]
            hT3 = hT_[:].rearrange("p (kc c) -> p kc c", kc=8)
            self.mod_ct(8 + sb)
            xt_set = xtok[sb % 2]; bt_set = btok[sb % 2]
            def h1(sb=sb):
                dt2(sb)
                if sb + 2 < NSB:
                    prep_dma(sb + 2)
            hooks = {1: h1}
            if prev is not None:
                hooks[2] = (lambda p=prev: states_x(3, *p))
                hooks[6] = (lambda p=prev: states_mm(3, *p))
            if sb + 1 < NSB:
                for s in range(4):
                    hooks[8 + 2 * s] = (lambda sb=sb, s=s: nb_tile(sb + 1, s))
                hooks[15] = (lambda sb=sb: dt1(sb + 1))
            if sb + 2 < NSB:
                hooks[17] = (lambda sb=sb: prep_a(sb + 2, dma=False))
            self.rhcur = rhT
            proj_blocks(3, hT3, 390, lambda s: s * 130, xt_set, bt_set, hook=hooks)
            prev = (xt_set, bt_set, wts[sb % 2])
            if sb == 0:
                k.dump("hTo0", hT_[:], rhT, [128, 8 * 390], BF16)
                k.dump("xtok0", xt_set[0][0][:], xt_set[0][1], [128, 2048], BF16)
        states(3, *prev)

        k.dve(A("memset", hTc[:], 0.0), [], [rhTc])
        hc3 = hTc[:].rearrange("p (kc c) -> p kc c", kc=8)
        self.rhcur = rhTc
        for s in range(2):
            xt_, rxt = xts[s]
            k.dma(xt_[:], I["ctx"][s * 128:(s + 1) * 128, :], [], [rxt])
            k.norm_T(xt_, rxt, 128, (lambda kc, s=s: hc3[:, kc, 1 + s * 128:1 + (s + 1) * 128]), rhTc, 1, 0, 1, tmps[s])
        xt_set = xtok[0]; bt_set = btok[0]
        proj_blocks(2, hc3, 258, lambda s: s * 128, xt_set, bt_set)
        psc, rpsc = dt_chain(2, hc3, 1, 128, None)
        tot0 = psc[:, 192:256]; tot1 = psc[:, 256:320]
        k.dve(A("tensor_tensor", out=L1[:, 0:64], in0=tot1, in1=logP[:], op=ALU.add), [rpsc, rlogP], [rL1])
        k.dve(A("tensor_tensor", out=L1[:, 64:128], in0=tot0, in1=logP[:], op=ALU.add), [rpsc, rlogP], [rL1])
        k.dve(A("tensor_tensor", out=L1[:, 128:192], in0=tot0, in1=L1[:, 0:64], op=ALU.add), [rpsc, rL1], [rL1])
        for (s, d, src) in ((1, 0, 0), (0, 0, 128), (0, 1, 64), (1, 1, 128)):
            k.dve(A("tensor_tensor", out=arg[:, s * 64 + d * 32:s * 64 + d * 32 + 32], in0=L1[:, src + d * 32:src + d * 32 + 32],
                    in1=psc[:, s * 64 + d * 32:s * 64 + d * 32 + 32], op=ALU.subtract), [rpsc, rL1], [rarg])
        k.act(A("activation", out=arg[:, 0:128], in_=arg[:, 0:128], func=AF.Exp), [rarg], [rarg])
        k.dve(A("tensor_tensor", out=wt_[:, 0:128], in0=arg[:, 0:128], in1=dte[:, 0:128], op=ALU.mult), [rarg, rdte], [rwt])
        states(2, xt_set, bt_set)
        for g in range(4):
            ps, rps = sbank()
            for kc in range(8):
                k.pe(A("matmul", ps[0:64, 0:256], wkv3[:, kc, g * 64:(g + 1) * 64], hc3[:, kc, 1:257], start=(kc == 0), stop=(kc == 7)),
                     [rwkv, rhTc], [rps])
            k.act(A("activation", out=self.kcT[:, g * 256:(g + 1) * 256], in_=ps[0:64, 0:256], func=AF.Copy), [rps], [self.rkcT])
        va = self.vca[:].rearrange("p (s g c) -> p s g c", s=2, g=4)
        for s in range(2):
            ps, rps = sbank()
            for kc in range(8):
                k.pe(A("matmul", ps[:, 0:256], hc3[:, kc, 1 + s * 128:1 + (s + 1) * 128], wkv3[:, kc, 256:512], start=(kc == 0), stop=(kc == 7)),
                     [rwkv, rhTc], [rps])
            k.act(A("activation", out=va[:, s, :, 0:64], in_=ps[:, 0:256].rearrange("p (g c) -> p g c", c=64), func=AF.Copy), [rps], [self.rvca])
        m3 = self.modT[:].rearrange("p (b v) -> p b v", v=2)
        g3 = self.gs[:].rearrange("p (kc v) -> p kc v", v=4)
        k.dve(A("scalar_tensor_tensor", out=g3[:, :, 2], in0=m3[:, 32:40, 0], scalar=1.0, in1=self.n12[:, 8:16],
                op0=ALU.add, op1=ALU.mult), [self.rmodT2, self.rn12], [self.rgs])
        k.dump("acc", self.acc[:], self.racc, [128, 4096])
        self.S.flush()
        es.close()


def _rep(v, n=128):
    v = np.asarray(v, np.float32).reshape(1, -1)
    return np.ascontiguousarray(np.broadcast_to(v, (n, v.shape[1])))


def _fm(v, nblk):
    return np.ascontiguousarray(np.asarray(v, np.float32).reshape(nblk, 128).T)


def prep_inputs(inp):
    x = np.asarray(inp["x"], np.float32)
    ctx = np.asarray(inp["ctx"], np.float32)
    w_in = np.ascontiguousarray(np.asarray(inp["w_in"], np.float32)[0])
    d = np.arange(64)
    within = d % 32
    partner = np.where(within < 16, d + 16, d - 16)
    qk = np.concatenate([Q0 + h * 64 + partner for h in range(16)] + [K0 + h * 64 + partner for h in range(4)])
    w_qkp = np.ascontiguousarray(w_in[:, qk])
    consts = np.zeros((128, 3, 128), np.float32)
    ii = np.arange(128)
    consts[:, 0] = (ii[:, None] == ii[None, :])
    consts[:, 1] = (ii[:, None] <= ii[None, :])
    consts[:, 2] = (ii[:, None] >= ii[None, :])
    inv = (10000.0 ** (-np.arange(0, 32, 2, dtype=np.float32) / 32)).astype(np.float32)
    j = within % 16
    sgn = np.where(within < 16, -1.0, 1.0).astype(np.float32)
    shared = dict(
        w_mod=np.ascontiguousarray(np.asarray(inp["w_mod"], np.float32)[0]),
        bmodT=_fm(inp["b_mod"][0], 48),
        n1g=_fm(inp["norm1_g"][0], 8), n2g=_fm(inp["norm2_g"][0], 8), fg_row=_rep(inp["final_g"]),
        w_in=w_in, w_qkp=w_qkp,
        ssd_cw=np.ascontiguousarray(np.asarray(inp["ssd_conv_w"], np.float32)[0].reshape(3, 24, 128).transpose(2, 1, 0).reshape(128, 72)),
        ssd_cb=_fm(inp["ssd_conv_b"][0], 24),
        dtb=_rep(np.asarray(inp["ssd_dt_bias"])[0].reshape(-1)), alog=_rep(np.asarray(inp["ssd_a_log"])[0].reshape(-1)),
        dsk=_rep(inp["ssd_d"][0]), sng=_fm(inp["ssd_norm_g"][0], 16),
        w_o_ssd=np.ascontiguousarray(np.asarray(inp["w_o_ssd"], np.float32)[0]),
        w_o_att=np.ascontiguousarray(np.asarray(inp["w_o_att"], np.float32)[0]),
        sink=_rep(inp["att_sink"][0]),
        w_out=np.ascontiguousarray(np.asarray(inp["w_out"], np.float32)[0]),
        w_up=np.ascontiguousarray(np.asarray(inp["w_up"], np.float32)[0]),
        ffn_cw=np.ascontiguousarray(np.asarray(inp["ffn_conv_w"], np.float32)[0].reshape(3, 44, 128).transpose(2, 1, 0).reshape(128, 132)),
        ffn_cb=_fm(inp["ffn_conv_b"][0], 44),
        w_down=np.ascontiguousarray(np.asarray(inp["w_down"], np.float32)[0]),
        consts=consts.reshape(128, 384),
    )
    maps = []
    zero_row = np.zeros((1, D), np.float32)
    for core in range(8):
        b, q = core // 4, core % 4
        xb = x[b]
        t0 = q * TOK - 256
        x_own = np.zeros((NOWN * 128, D), np.float32)
        lo, hi = max(t0, 0), min(t0 + NOWN * 128, L)
        x_own[lo - t0:hi - t0] = xb[lo:hi]
        flags = np.zeros((NFLAG,), np.float32)
        for t in range(NOWN):
            flags[t] = 1.0 if 0 <= t0 + t * 128 < L else 0.0
        before = list(range(16 * q - 2, -1, -1))
        after = list(range(16 * q + 17, 64))
        slots = [(c, 0) for c in before] + [(c, 1) for c in after]
        assert len(slots) <= NSLOT
        x_oth = np.zeros((NSLOT * 128, D), np.float32)
        x_oth_h = np.zeros((2 * NSLOT, D), np.float32)
        for s, (c, side) in enumerate(slots):
            x_oth[s * 128:(s + 1) * 128] = xb[c * 128:(c + 1) * 128]
            flags[20 + 2 * s + side] = 1.0
            if c * 128 - 1 >= 0:
                x_oth_h[2 * s] = xb[c * 128 - 1]; flags[116 + 2 * s] = 1.0
            if c * 128 + 128 < L:
                x_oth_h[2 * s + 1] = xb[c * 128 + 128]; flags[116 + 2 * s + 1] = 1.0
        cv = np.zeros((128, 16), np.float32)
        cv[:, 0::2] = np.asarray(inp["c"], np.float32)[b].reshape(8, 128).T
        cv[:, 1::2] = np.asarray(inp["c_ctx"], np.float32).reshape(8, 128).T
        tpos = t0 + np.arange(NOWN * 128)
        pos = np.where((d < 32)[:, None], (tpos // 64)[None, :], (tpos % 64)[None, :]).astype(np.float32)
        ang = (pos * inv[j][:, None]).astype(np.float32)
        m = dict(shared)
        m.update(x_own=x_own, x_oth=x_oth, x_oth_h=x_oth_h, ctx=np.ascontiguousarray(ctx[b]), cvec=cv, flags=_rep(flags),
                 cosT=np.cos(ang).astype(np.float32), sinT=(np.sin(ang) * sgn[:, None]).astype(np.float32))
        maps.append(m)
    return maps


_NC_CACHE = {}


def kernel(**inputs):
    kb = K()
    nc = kb.build()
    maps = prep_inputs(inputs)
    cores = [int(c) for c in os.environ.get("KCORES", "0,1,2,3,4,5,6,7").split(",")]
    res = run_bass_kernel_spmd(nc, [maps[c] for c in cores], core_ids=list(range(len(cores))))
    out = np.zeros((2, L, D), np.float32)
    for i, core in enumerate(cores):
        b, q = core // 4, core % 4
        out[b, q * TOK:(q + 1) * TOK] = res.results[i]["out"]
    if DEBUG:
        kernel.dbg = {core: {n: res.results[i][v] for n, v in kb.dbg.items()} for i, core in enumerate(cores)}
    return out


def _p2(self):
    pass


def conv3(k, ps, rps, n, w3, bcol, rw, out_ap, rout, func, wregs, defer=False):
    (r_, rr_) = rw
    k.act(A("activation", out=r_[:, 0:n], in_=ps[:, 1:n + 1], func=AF.Identity, scale=w3[:, 1:2], bias=bcol), [rps] + wregs, [rr_])
    k.dve(A("scalar_tensor_tensor", out=r_[:, 0:n], in0=ps[:, 0:n], scalar=w3[:, 0:1], in1=r_[:, 0:n], op0=ALU.mult, op1=ALU.add),
          [rps, rr_] + wregs, [rr_])
    k.dve(A("scalar_tensor_tensor", out=r_[:, 0:n], in0=ps[:, 2:n + 2], scalar=w3[:, 2:3], in1=r_[:, 0:n], op0=ALU.mult, op1=ALU.add),
          [rps, rr_] + wregs, [rr_])
    fin = lambda: k.act(A("activation", out=out_ap, in_=r_[:, 0:n], func=func), [rr_], [rout])
    if defer:
        return fin
    fin()


def phase_own_h(self):
    k = self; I = self.I; es = self.arW
    self.hT, self.rhT = k.T("hT", [128, 8 * 2560], BF16, es=self.arHT)
    hT3 = self.hT[:].rearrange("p (kc c) -> p kc c", kc=8)
    self.hT3 = hT3
    xts = [k.T("xo%d" % i, [128, D], es=es) for i in range(8)]
    junk_ = k.T("junk", [128, D], BF16, es=es)
    tmps = [(junk_, k.T("ss%d" % i, [128, 4], es=es), k.T("xn%d" % i, [128, D], BF16, es=es)) for i in range(8)]

    def ld(gp):
        for t in range(gp * 4, gp * 4 + 4):
            k.dma(xts[t % 8][0][:], I["x_own"][t * 128:(t + 1) * 128, :], [], [xts[t % 8][1]])

    def st_a(gp):
        k.norm_a_multi([(xts[t % 8][0], xts[t % 8][1], 128, tmps[t % 8]) for t in range(gp * 4, gp * 4 + 4)])

    def st_b(t):
        k.norm_b(128, (lambda kc, t=t: hT3[:, kc, t * 128:(t + 1) * 128]), self.rhT, 0, 0, 0, tmps[t % 8])
        if t in (0, 1, 18, 19):
            for kc in range(8):
                k.dve(A("tensor_scalar", out=hT3[:, kc, t * 128:(t + 1) * 128], in0=hT3[:, kc, t * 128:(t + 1) * 128],
                        scalar1=self.flags[:, t:t + 1], scalar2=None, op0=ALU.mult), [self.rhT, self.rflags], [self.rhT])

    ld(0); ld(1)
    st_a(0)
    for gp in range(5):
        if gp + 1 < 5:
            st_a(gp + 1)
        if gp + 2 < 5:
            ld(gp + 2)
        for t in range(gp * 4, gp * 4 + 4):
            st_b(t)
    self.S.flush()
    es.close()


def phase_ssd(self):
    k = self; I = self.I; es = self.arW
    hT3 = self.hT3
    self.yz, _ = k.T("yz", [128, 18 * 2048], BF16, es=self.arYZ)
    self.ry = [Reg("y%d" % c) for c in range(18)]
    yz3 = self.yz[:].rearrange("p (c f) -> p c f", c=18)
    stage = k.T("stg", [128, 8 * 128], es=es)
    wbf = [k.T("wbf%d" % i, [128, 8 * 128], BF16, es=es) for i in range(2)]
    wdt, rwdt = k.T("wdt", [128, 8 * 64], BF16, es=es)
    wdt3 = wdt[:].rearrange("p (kc c) -> p kc c", kc=8)
    wdg, rwdg = k.T("wdg", [128, 8 * 16], BF16, es=es)
    wdg3 = wdg[:].rearrange("p (kc c) -> p kc c", kc=8)
    dbg_, rdbg = k.T("dbg_", [128, 16], es=es)
    nag_, rnag = k.T("nag_", [128, 16], es=es)
    xcg, rxcg = k.T("xcg", [128, 2304], BF16, es=es)
    raw = [k.T("raw%d" % i, [128, 512], es=es) for i in range(2)]
    xtok, _ = k.T("xtok", [128, 18 * 512], BF16, es=es)
    xtok3 = xtok[:].rearrange("p (c f) -> p c f", c=18)
    rxt = Reg("xtok")
    btok, rbt = k.T("btok", [128, 18 * 128], BF16, es=es)
    btok3 = btok[:].rearrange("p (c f) -> p c f", c=18)
    BT, rBT = k.T("BT", [128, 2304], BF16, es=es)
    CT, rCT = k.T("CT", [128, 2304], BF16, es=es)
    dte, rdte = k.T("dte", [128, 288], es=es)
    la, rla = k.T("la", [128, 288], es=es)
    dtr, rdtr = la, rla
    csx, rcsx = k.T("csx", [128, 288], es=es)
    wgt, rwgt = k.T("wgt", [128, 288], es=es)
    ecs, recs = k.T("ecs", [128, 288], es=es)
    dtot, rdtot = k.T("dtot", [128, 288], es=es)
    St, rSt = k.T("St", [128, 512], es=es)
    Sbf, rSbf = k.T("Sbf", [128, 512], BF16, es=es)
    xpd = [k.T("xpd%d" % i, [128, 512], BF16, es=es) for i in range(2)]
    CB, rCB = k.T("CB", [128, 128], es=es)
    Em = [k.T("Em%d" % i, [128, 512], BF16, es=es) for i in range(2)]
    ncs, rncs = k.T("ncs", [128, 288], es=es)
    Mt, rMt = k.T("Mt", [128, 8 * 128], BF16, es=es)
    tt, rtt = k.T("tt", [128, 512], es=es)
    uu, ruu = k.T("uu", [128, 512], BF16, es=es)
    idf = self.cst[:, 0:128]; triU = self.cst[:, 128:256]; triL = self.cst[:, 256:384]
    idb = self.cstb[:, 0:128]
    cw3 = self.scw[:].rearrange("p (b t) -> p b t", t=3)
    rot = [0]

    def rbank():
        i = rot[0]; rot[0] = (i + 1) % 2
        return self.PS[i], self.PR[i]

    k.load_w(stage, (wdt3, rwdt), I["w_in"], DT0, 64)
    tiles = [(128 + 510 * t, min(510, 2432 - (128 + 510 * t))) for t in range(5)]
    flg = self.flags[:, 1:19].unsqueeze(2).to_broadcast([128, 18, 16])

    def v3(ap):
        return ap.rearrange("p (c j) -> p c j", j=16)

    for g in range(4):
        blocks = [(XBC0 + (g * 4 + i) * 128, g * 4 + i, "x", i) for i in range(4)]
        blocks += [(XBC0 + 2048 + g * 128, 16 + g, "B", 0), (XBC0 + 2560 + g * 128, 20 + g, "C", 0)]
        for bi, (c0, cb, kind, i) in enumerate(blocks):
            wb, rwb = wbf[bi % 2]
            wb3 = wb[:].rearrange("p (kc c) -> p kc c", kc=8)
            k.load_w(stage, (wb3, rwb), I["w_in"], c0, 128)
            dst, rdst = {"x": (xcg, rxcg), "B": (BT, rBT), "C": (CT, rCT)}[kind]
            pendf = None
            for ti, (o0, n) in enumerate(tiles):
                ps, rps = rbank()
                for kc in range(8):
                    k.pe(A("matmul", ps[:, 0:n + 2], wb3[:, kc, :], hT3[:, kc, o0 - 1:o0 + n + 1], start=(kc == 0), stop=(kc == 7)),
                         [rwb, self.rhT], [rps])
                f_ = conv3(k, ps, rps, n, cw3[:, cb, :], self.scb[:, cb:cb + 1], raw[ti % 2], dst[:, o0 - 128:o0 - 128 + n], rdst, AF.Silu,
                           [self.rscw, self.rscb], defer=True)
                if pendf is not None:
                    pendf()
                pendf = f_
            pendf()
            if kind in ("x", "B"):
                for c8 in range(0, 18, 8):
                    nb = min(8, 18 - c8)
                    tb, rtb = self.PS[2 + (c8 // 8) % 2], self.PR[2 + (c8 // 8) % 2]
                    tbb = tb[:].bitcast(BF16)
                    for j in range(nb):
                        c = c8 + j
                        k.pe(A("transpose", tbb[:, j * 128:(j + 1) * 128], dst[:, c * 128:(c + 1) * 128], idb), [rdst, self.rcstb], [rtb])
                    if kind == "x":
                        k.act(A("activation", out=xtok3[:, c8:c8 + nb, i * 128:(i + 1) * 128],
                                in_=tbb[:, 0:nb * 128].rearrange("p (c f) -> p c f", f=128), func=AF.Copy), [rtb], [rxt])
                    else:
                        k.act(A("activation", out=btok3[:, c8:c8 + nb, :],
                                in_=tbb[:, 0:nb * 128].rearrange("p (c f) -> p c f", f=128), func=AF.Copy), [rtb], [rbt])
        if os.environ.get("KSSD") == "1":
            break
        for d in range(2):
            k.pool(A("tensor_copy", out=wdg3[:, :, d * 8:(d + 1) * 8], in_=wdt3[:, :, d * 32 + g * 8:d * 32 + g * 8 + 8]), [rwdt], [rwdg])
            k.dve(A("tensor_copy", out=dbg_[:, d * 8:(d + 1) * 8], in_=self.dtb[:, d * 32 + g * 8:d * 32 + g * 8 + 8]), [self.rdtb], [rdbg])
            k.dve(A("tensor_copy", out=nag_[:, d * 8:(d + 1) * 8], in_=self.negA[:, d * 32 + g * 8:d * 32 + g * 8 + 8]), [self.rnegA], [rnag])
        psd, rpsd = rbank()
        for c in range(18):
            for kc in range(8):
                k.pe(A("matmul", psd[:, c * 16:(c + 1) * 16], hT3[:, kc, (c + 1) * 128:(c + 2) * 128], wdg3[:, kc, :],
                       start=(kc == 0), stop=(kc == 7)), [rwdg, self.rhT], [rpsd])
        k.dve(A("tensor_tensor", out=v3(dtr[:]), in0=v3(psd[:, 0:288]), in1=dbg_[:].unsqueeze(1).to_broadcast([128, 18, 16]), op=ALU.add),
              [rpsd, rdbg], [rdtr])
        k.act(A("activation", out=dtr[:], in_=dtr[:], func=AF.Exp), [rdtr], [rdtr])
        k.act(A("activation", out=dte[:], in_=dtr[:], func=AF.Ln, bias=1.0), [rdtr], [rdte])
        for c in (0, 17):
            k.dve(A("tensor_scalar", out=dte[:, c * 16:(c + 1) * 16], in0=dte[:, c * 16:(c + 1) * 16], scalar1=self.flags[:, 1 + c:2 + c],
                    scalar2=None, op0=ALU.mult), [rdte, self.rflags], [rdte])
        if os.environ.get("KSSD") == "2a":
            break
        k.dve(A("tensor_tensor", out=v3(la[:]), in0=v3(dte[:]), in1=nag_[:].unsqueeze(1).to_broadcast([128, 18, 16]), op=ALU.mult),
              [rdte, rnag], [rla])
        psU, rpsU = rbank()
        k.pe(A("matmul", psU[:, 0:288], triU, la[:], start=True, stop=True), [rla, self.rcst], [rpsU])
        psL, rpsL = self.PS[2], self.PR[2]
        k.pe(A("matmul", psL[:, 0:288], triL, la[:], start=True, stop=True), [rla, self.rcst], [rpsL])
        k.act(A("activation", out=v3(csx[:])[:, :, 0:8], in_=v3(psU[:, 0:288])[:, :, 0:8], func=AF.Copy), [rpsU], [rcsx])
        k.act(A("activation", out=v3(csx[:])[:, :, 8:16], in_=v3(psL[:, 0:288])[:, :, 8:16], func=AF.Copy), [rpsL], [rcsx])
        if os.environ.get("KSSD") == "2b":
            break
        pst, rpst = rbank()
        k.pe(A("matmul", pst[:, 0:288], self.ones[:], la[:], start=True, stop=True), [rla, self.rones], [rpst])
        if os.environ.get("KSSD") == "2c":
            break
        k.act(A("activation", out=ecs[:], in_=csx[:], func=AF.Exp), [rcsx], [recs])
        if os.environ.get("KSSD") == "2d":
            break
        k.act(A("activation", out=dtot[:], in_=pst[:, 0:288], func=AF.Exp), [rpst], [rdtot])
        if os.environ.get("KSSD") == "2e":
            break
        k.dve(A("tensor_tensor", out=wgt[:], in0=pst[:, 0:288], in1=csx[:], op=ALU.subtract), [rpst, rcsx], [rwgt])
        k.act(A("activation", out=wgt[:], in_=wgt[:], func=AF.Exp), [rwgt], [rwgt])
        k.dve(A("tensor_tensor", out=wgt[:], in0=wgt[:], in1=dte[:], op=ALU.mult), [rwgt, rdte], [rwgt])
        if g == 0:
            k.dump("dte", dte[:], rdte, [128, 288]); k.dump("csx", csx[:], rcsx, [128, 288])
            k.dump("xtokg", xtok[:], rxt, [128, 18 * 512], BF16); k.dump("BTg", BT[:], rBT, [128, 2304], BF16)
        if os.environ.get("KSSD") == "2":
            break
        k.dve(A("tensor_scalar", out=ncs[:], in0=dte[:], scalar1=1e-30, scalar2=None, op0=ALU.max), [rdte], [rncs])
        k.act(A("activation", out=ncs[:], in_=ncs[:], func=AF.Ln), [rncs], [rncs])
        k.dve(A("tensor_tensor", out=ncs[:], in0=ncs[:], in1=csx[:], op=ALU.subtract), [rncs, rcsx], [rncs])
        h3 = lambda ap: ap.rearrange("p (h c) -> p h c", c=64)
        for d in (1, 0):
            a0 = d * 2048 + g * 512
            k.dve(A("tensor_copy", out=St[:], in_=self.acc[:, a0:a0 + 512]), [self.racc], [rSt])
            mask = triL if d == 1 else triU
            order = range(17, -1, -1) if d == 1 else range(18)
            order = list(order)

            def stA(c):
                cs_ = slice(c * 128, (c + 1) * 128)
                q8 = slice(c * 16 + d * 8, c * 16 + d * 8 + 8)
                xp_, rxp = xpd[c % 2]
                k.pool(A("tensor_tensor", out=h3(xp_[:]), in0=h3(xtok3[:, c, :]), in1=wgt[:, q8].unsqueeze(2).to_broadcast([128, 8, 64]), op=ALU.mult),
                       [rxt, rwgt], [rxp])
                for r4 in range(0, 8, 4):
                    pb, rpb = self.PS[2 + (r4 // 4)], self.PR[2 + (r4 // 4)]
                    for j in range(4):
                        col = c * 16 + d * 8 + r4 + j
                        k.pe(A("matmul", pb[:, j * 128:(j + 1) * 128], csx[:, col:col + 1].to_broadcast([128, 128]), idf, start=True, stop=False),
                             [rcsx, self.rcst], [rpb])
                        k.pe(A("matmul", pb[:, j * 128:(j + 1) * 128], idf, ncs[:, col:col + 1].to_broadcast([128, 128]), start=False, stop=True),
                             [rncs, self.rcst], [rpb])
                pss, rpss = self.PS[6 + (c % 2)], self.PR[6 + (c % 2)]
                k.pe(A("matmul", pss[:, 0:512], btok3[:, c, :], xp_[:], start=True, stop=True), [rbt, rxp], [rpss])
                pcb, rpcb = rbank()
                k.pe(A("matmul", pcb[:, 0:128], BT[:, cs_], CT[:, cs_], start=True, stop=True), [rBT, rCT], [rpcb])
                return pcb, rpcb

            def stB(c, pcb, rpcb):
                k.dve(A("tensor_tensor", out=CB[:], in0=pcb[:, 0:128], in1=mask, op=ALU.mult), [rpcb, self.rcst], [rCB])
                for ri, r4 in enumerate(range(0, 8, 4)):
                    pb, rpb = self.PS[2 + ri], self.PR[2 + ri]
                    em, rem = Em[ri]
                    k.act(A("activation", out=em[:], in_=pb[:, 0:512], func=AF.Exp), [rpb], [rem])
                    k.dve(A("scalar_tensor_tensor", out=Mt[:, r4 * 128:(r4 + 4) * 128].rearrange("p (h c) -> p h c", c=128),
                            in0=em[:].rearrange("p (h c) -> p h c", c=128), scalar=100.0, in1=CB[:].unsqueeze(1).to_broadcast([128, 4, 128]),
                            op0=ALU.min, op1=ALU.mult), [rem, rCB], [rMt])

            def stC(c):
                cs_ = slice(c * 128, (c + 1) * 128)
                q8 = slice(c * 16 + d * 8, c * 16 + d * 8 + 8)
                pss, rpss = self.PS[6 + (c % 2)], self.PR[6 + (c % 2)]
                py, rpy = self.PS[4], self.PR[4]
                for r in range(8):
                    k.pe(A("matmul", py[:, r * 64:(r + 1) * 64], Mt[:, r * 128:(r + 1) * 128], xtok3[:, c, r * 64:(r + 1) * 64],
                           start=True, stop=True), [rMt, rxt], [rpy])
                k.act(A("activation", out=Sbf[:], in_=St[:], func=AF.Copy), [rSt], [rSbf])
                po, rpo = self.PS[5], self.PR[5]
                k.pe(A("matmul", po[:, 0:512], CT[:, cs_], Sbf[:], start=True, stop=True), [rCT, rSbf], [rpo])
                t8 = dtot[:, q8].unsqueeze(2).to_broadcast([128, 8, 64])
                k.pool(A("tensor_tensor", out=h3(St[:]), in0=h3(St[:]), in1=t8, op=ALU.mult), [rSt, rdtot, rSbf], [rSt])
                k.dve(A("tensor_tensor", out=St[:], in0=St[:], in1=pss[:, 0:512], op=ALU.add), [rSt, rpss], [rSt])
                for r in range(8):
                    col = c * 16 + d * 8 + r
                    k.act(A("activation", out=tt[:, r * 64:(r + 1) * 64], in_=po[:, r * 64:(r + 1) * 64], func=AF.Copy, scale=ecs[:, col:col + 1]),
                          [rpo, recs], [rtt])
                ydst = yz3[:, c, g * 512:(g + 1) * 512]
                if d == 1:
                    k.dve(A("tensor_tensor", out=ydst, in0=tt[:], in1=py[:, 0:512], op=ALU.add), [rtt, rpy], [self.ry[c]])
                else:
                    d8 = self.dsk[:, g * 8:(g + 1) * 8].unsqueeze(2).to_broadcast([128, 8, 64])
                    k.pool(A("tensor_tensor", out=h3(uu[:]), in0=h3(xtok3[:, c, :]), in1=d8, op=ALU.mult), [rxt, self.rdsk], [ruu])
                    k.pool(A("tensor_tensor", out=uu[:], in0=uu[:], in1=ydst, op=ALU.add), [ruu, self.ry[c]], [ruu])
                    k.dve(A("tensor_tensor", out=tt[:], in0=tt[:], in1=py[:, 0:512], op=ALU.add), [rtt, rpy], [rtt])
                    k.dve(A("tensor_tensor", out=ydst, in0=tt[:], in1=uu[:], op=ALU.add), [rtt, ruu], [self.ry[c]])

            pc = stA(order[0])
            stB(order[0], *pc)
            for ci, c in enumerate(order):
                nxt = order[ci + 1] if ci + 1 < len(order) else None
                if nxt is not None:
                    pc = stA(nxt)
                stC(c)
                if nxt is not None:
                    stB(nxt, *pc)
    k.dump("yz", self.yz[:], self.ry, [128, 18 * 2048], BF16)
    self.S.flush()
    es.close()


K.phase_own_h = phase_own_h
K.phase_ssd = phase_ssd


def row_bcast(k, dst, rdst, colsT, rcols, nblk):
    for b0 in range(0, nblk, 4):
        ps, rps = k.bank()
        for b in range(b0, min(b0 + 4, nblk)):
            k.pe(A("transpose", ps[:, (b - b0) * 128:(b - b0 + 1) * 128], colsT(b).to_broadcast([128, 128]), k.cst[:, 0:128]),
                 [rcols, k.rcst], [rps])
        n = min(4, nblk - b0)
        k.act(A("activation", out=dst[:, b0 * 128:(b0 + n) * 128], in_=ps[:, 0:n * 128], func=AF.Copy), [rps], [rdst])


def phase_ssd_proj(self):
    k = self; I = self.I
    hT3 = self.hT3
    arW = self.arW; arA = self.arACC
    arA.close()
    self.mg, self.rmg = k.T("merged", [128, 8 * 2304], BF16, es=arW)
    mg3 = self.mg[:].rearrange("p (kc t) -> p kc t", kc=8)
    self.mg3 = mg3
    SUP = [(0, 4), (4, 4), (8, 4), (12, 4), (16, 2)]
    ryT = [Reg("yT%d" % i) for i in range(5)]
    tmpT, rtmpT = k.T("tmpT", [128, 16 * 512], BF16, es=arW)
    tmpT3 = tmpT[:].rearrange("p (j t) -> p j t", j=16)
    stage = k.T("stg", [128, 16 * 128], es=arW)
    wbf = [k.T("wbf%d" % i, [128, 8 * 128], BF16, es=arW) for i in range(2)]
    wpb, rwpb = k.T("wpb", [128, 16 * 128], BF16, es=arW)
    rstd = [k.T("rstd%d" % i, [128, 512], es=arA) for i in range(5)]
    szb = [k.T("szb%d" % i, [128, 512], BF16, es=arA) for i in range(2)]
    sqb = [k.T("sqb%d" % i, [128, 512], BF16, es=arA) for i in range(2)]
    sg, rsg = k.T("sg", [128, 512], es=arA)
    idb = self.cstb[:, 0:128]
    onesb = self.cstb[:, 128:256]
    ob, rob = k.T("onesb", [128, 128], BF16)
    k.dve(A("tensor_copy", out=ob[:], in_=self.ones[:]), [self.rones], [rob])
    yzf = self.yz

    def yT(st):
        n = SUP[st][1] * 128
        return yzf[:, st * 8192:st * 8192 + 16 * n].rearrange("p (j t) -> p j t", j=16), n

    yz3 = self.yz[:].rearrange("p (c f) -> p c f", c=18)
    for st, (c0, nc_) in enumerate(SUP):
        n = nc_ * 128
        for ci in range(nc_):
            for j8 in range(2):
                tb, rtb = k.bank()
                tbb = tb[:].bitcast(BF16)
                for j in range(8):
                    k.pe(A("transpose", tbb[:, j * 128:(j + 1) * 128], yz3[:, c0 + ci, (j8 * 8 + j) * 128:(j8 * 8 + j + 1) * 128], idb),
                         [self.ry[c0 + ci], self.rcstb], [rtb])
                k.act(A("activation", out=tmpT3[:, j8 * 8:j8 * 8 + 8, ci * 128:(ci + 1) * 128],
                        in_=tbb[:, 0:1024].rearrange("p (j t) -> p j t", j=8), func=AF.Copy), [rtb], [rtmpT])
        v, _ = yT(st)
        k.dve(A("tensor_copy", out=v[:, 0:8, :], in_=tmpT3[:, 0:8, 0:n]), [rtmpT] + [self.ry[c0 + ci] for ci in range(nc_)], [ryT[st]] + [self.ry[c0 + ci] for ci in range(nc_)])
        k.act(A("activation", out=v[:, 8:16, :], in_=tmpT3[:, 8:16, 0:n], func=AF.Copy), [rtmpT] + [self.ry[c0 + ci] for ci in range(nc_)], [ryT[st]] + [self.ry[c0 + ci] for ci in range(nc_)])
    accb = [(self.PS[3 + st], self.PR[3 + st]) for st in range(5)]
    rot = [0]

    def rbank():
        i = rot[0]; rot[0] = (i + 1) % 3
        return self.PS[i], self.PR[i]

    pend = None
    u = 0
    for j in range(16):
        wb, rwb = wbf[j % 2]
        wb3 = wb[:].rearrange("p (kc c) -> p kc c", kc=8)
        k.load_w((stage[0], stage[1]), (wb3, rwb), I["w_in"], j * 128, 128)
        for st, (c0, nc_) in enumerate(SUP):
            n = nc_ * 128
            t0 = (c0 + 1) * 128
            ps, rps = rbank()
            for kc in range(8):
                k.pe(A("matmul", ps[:, 0:n], wb3[:, kc, :], hT3[:, kc, t0:t0 + n], start=(kc == 0), stop=(kc == 7)), [rwb, self.rhT], [rps])
            if pend is not None:
                pend()
            sz_, rsz = szb[u % 2]; sq_, rsq = sqb[u % 2]
            u += 1
            k.act(A("activation", out=sz_[:, 0:n], in_=ps[:, 0:n], func=AF.Silu), [rps], [rsz])
            v, _ = yT(st)
            k.dve(A("tensor_tensor", out=v[:, j, :], in0=v[:, j, :], in1=sz_[:, 0:n], op=ALU.mult), [ryT[st], rsz], [ryT[st]])
            k.act(A("activation", out=sq_[:, 0:n], in_=v[:, j, :], func=AF.Square), [ryT[st]], [rsq])
            pend = (lambda st=st, n=n, sq_=sq_, rsq=rsq, j=j: k.pe(A("matmul", accb[st][0][:, 0:n], ob[:], sq_[:, 0:n], start=(j == 0), stop=(j == 15)),
                                                                   [rob, rsq], [accb[st][1]]))
    pend()
    for st, (c0, nc_) in enumerate(SUP):
        n = nc_ * 128
        r_, rr_ = rstd[st]
        k.dve(A("tensor_scalar", out=r_[:, 0:n], in0=accb[st][0][:, 0:n], scalar1=1.0 / 2048, scalar2=EPS, op0=ALU.mult, op1=ALU.add),
              [accb[st][1]], [rr_])
        k.act(A("activation", out=r_[:, 0:n], in_=r_[:, 0:n], func=AF.Sqrt), [rr_], [rr_])
        k.dve(A("reciprocal", out=r_[:, 0:n], in_=r_[:, 0:n]), [rr_], [rr_])
    ar2 = Arena(self.arW.lo + 8 * 2304 * 2, self.arW.lo + 8 * 2304 * 2 + 16 * 512 * 2)
    stgA = [stage, k.T("stgA1", [128, 16 * 128], es=ar2)]
    stgB = [k.T("stgB0", [128, 8 * 128], es=ar2)]
    wpbs = [(wpb, rwpb), k.T("wpb1", [128, 16 * 128], BF16, es=ar2)]
    first = [True]

    def load_cb(cbk):
        sA, rsA = stgA[cbk % 2]
        sA3 = sA[:].rearrange("p (j c) -> p j c", j=16)
        wp_, rwp_ = wpbs[cbk % 2]
        wp3_ = wp_[:].rearrange("p (j c) -> p j c", j=16)
        extra = [rtmpT] if cbk <= 1 else []
        k.dma(sA3, I["w_o_ssd"][:, cbk * 128:(cbk + 1) * 128].rearrange("(j p) c -> p j c", p=128), [], [rsA] + extra)
        k.dve(A("tensor_tensor", out=wp3_, in0=sA3, in1=self.sng[:].unsqueeze(2).to_broadcast([128, 16, 128]), op=ALU.mult),
              [rsA, self.rsng], [rwp_] + extra)
        wb, rwb = wbf[cbk % 2]
        wb3 = wb[:].rearrange("p (kc c) -> p kc c", kc=8)
        sB, rsB = stgB[0]
        sB3 = sB[:].rearrange("p (kc c) -> p kc c", kc=8)
        k.dma(sB3, I["w_in"][:, G0 + cbk * 128:G0 + (cbk + 1) * 128].rearrange("(kc p) c -> p kc c", p=128), [], [rsB] + ([rtmpT] if first[0] else []))
        k.pool(A("tensor_copy", out=wb3, in_=sB3), [rsB], [rwb])
        first[0] = False

    load_cb(0)
    for cbk in range(8):
        if cbk + 1 < 8:
            load_cb(cbk + 1)
        wp3 = wpbs[cbk % 2][0][:].rearrange("p (j c) -> p j c", j=16)
        rwpb = wpbs[cbk % 2][1]
        wb, rwb = wbf[cbk % 2]
        wb3 = wb[:].rearrange("p (kc c) -> p kc c", kc=8)
        for st, (c0, nc_) in enumerate(SUP):
            n = nc_ * 128
            t0 = (c0 + 1) * 128
            v, _ = yT(st)
            po, rpo = rbank()
            for j in range(16):
                k.pe(A("matmul", po[:, 0:n], wp3[:, j, :], v[:, j, :], start=(j == 0), stop=(j == 15)), [rwpb, ryT[st]], [rpo])
            pg, rpg = rbank()
            for kc in range(8):
                k.pe(A("matmul", pg[:, 0:n], wb3[:, kc, :], hT3[:, kc, t0:t0 + n], start=(kc == 0), stop=(kc == 7)), [rwb, self.rhT], [rpg])
            k.act(A("activation", out=sg[:, 0:n], in_=pg[:, 0:n], func=AF.Sigmoid), [rpg], [rsg])
            k.dve(A("tensor_tensor", out=sg[:, 0:n], in0=sg[:, 0:n], in1=rstd[st][0][:, 0:n], op=ALU.mult), [rsg, rstd[st][1]], [rsg])
            k.dve(A("tensor_tensor", out=mg3[:, cbk, c0 * 128:c0 * 128 + n], in0=po[:, 0:n], in1=sg[:, 0:n], op=ALU.mult), [rpo, rsg], [self.rmg])
    k.dump("mg_ssd", self.mg[:], self.rmg, [128, 8 * 2304], BF16)
    self.S.flush()
    arA.close()
    arW.cur = arW.lo + 8 * 2304 * 2


K.phase_ssd_proj = phase_ssd_proj


def phase_att(self):
    k = self; I = self.I
    hT3 = self.hT3
    arY = self.arYZ; arA = self.arACC; arW = self.arW
    arY.close(); arA.close()
    yatt, ryatt = k.T("yatt", [128, 18 * 1024], BF16, es=arY)
    yatt3 = yatt[:].rearrange("p (i f) -> p i f", i=18)
    cosT, rcos = k.T("cosT", [64, 2560], es=arY)
    sinT, rsin = k.T("sinT", [64, 2560], es=arY)
    kT, rkT = k.T("kT", [64, 2560], BF16, es=arY)
    Pbs = [k.T("Pb%d" % i, [128, 5 * 512], BF16, es=arY) for i in range(2)]
    Va, rVa = k.T("Va", [128, 20 * 4 * 65], BF16, es=arA)
    Va4 = Va[:].rearrange("p (t g c) -> p t g c", t=20, g=4)
    t_, rt_ = k.T("rt", [64, 512], es=arA)
    u_, ru_ = k.T("ru", [64, 512], es=arA)
    wmark = arW.cur
    qg, rqg = k.T("qg", [64, 18 * 512], BF16, es=arW)
    qg4 = qg[:].rearrange("p (i h c) -> p i h c", i=18, h=4)
    stage = k.T("stg", [128, 8 * 128], es=arW)
    wv, rwv = k.T("wv", [128, 8 * 256], BF16, es=arW)
    wv3 = wv[:].rearrange("p (kc c) -> p kc c", kc=8)
    wm = [k.T("wqm%d" % i, [128, 8 * 64], BF16, es=arW) for i in range(2)]
    wp = [k.T("wqp%d" % i, [128, 8 * 64], BF16, es=arW) for i in range(2)]
    den, rden = k.T("den", [128, 8], es=arW)
    vca4 = self.vca[:].rearrange("p (s g c) -> p s g c", s=2, g=4)
    triLb = self.cstb[:, 256:384]; triUb = self.cstb[:, 128:256]

    k.dma(cosT[:], I["cosT"], [], [rcos])
    k.dma(sinT[:], I["sinT"], [], [rsin])
    k.dve(A("memset", Va[:], 1.0), [], [rVa])
    for i in range(2):
        k.load_w(stage, (wv3[:, :, i * 128:(i + 1) * 128], rwv), I["w_in"], V0 + i * 128, 128)
    for t in range(NOWN):
        ps, rps = k.bank()
        for kc in range(8):
            k.pe(A("matmul", ps[:, 0:256], hT3[:, kc, t * 128:(t + 1) * 128], wv3[:, kc, :], start=(kc == 0), stop=(kc == 7)), [rwv, self.rhT], [rps])
        k.act(A("activation", out=Va4[:, t, :, 0:64], in_=ps[:, 0:256].rearrange("p (g c) -> p g c", c=64), func=AF.Copy), [rps], [rVa])

    def proj_rope(c_main, c_perm, wi, tok0, ntok, dst_fn, rdst):
        wm_, rwm = wm[wi % 2]; wp_, rwp = wp[wi % 2]
        wm3 = wm_[:].rearrange("p (kc c) -> p kc c", kc=8); wp3 = wp_[:].rearrange("p (kc c) -> p kc c", kc=8)
        k.load_w(stage, (wm3, rwm), I["w_in"], c_main, 64)
        k.load_w(stage, (wp3, rwp), I["w_qkp"], c_perm, 64)
        for e0 in range(0, ntok, 512):
            n = min(512, ntok - e0)
            pa, rpa = k.bank(); pb, rpb = k.bank()
            for kc in range(8):
                k.pe(A("matmul", pa[0:64, 0:n], wm3[:, kc, :], hT3[:, kc, tok0 + e0:tok0 + e0 + n], start=(kc == 0), stop=(kc == 7)), [rwm, self.rhT], [rpa])
            for kc in range(8):
                k.pe(A("matmul", pb[0:64, 0:n], wp3[:, kc, :], hT3[:, kc, tok0 + e0:tok0 + e0 + n], start=(kc == 0), stop=(kc == 7)), [rwp, self.rhT], [rpb])
            k.dve(A("tensor_tensor", out=t_[:, 0:n], in0=pa[0:64, 0:n], in1=cosT[:, tok0 + e0:tok0 + e0 + n], op=ALU.mult), [rpa, rcos], [rt_])
            k.dve(A("tensor_tensor", out=u_[:, 0:n], in0=pb[0:64, 0:n], in1=sinT[:, tok0 + e0:tok0 + e0 + n], op=ALU.mult), [rpb, rsin], [ru_])
            o_, view = dst_fn(e0, n)
            k.dve(A("tensor_tensor", out=o_, in0=view(t_[:, 0:n]), in1=view(u_[:, 0:n]), op=ALU.add), [rt_, ru_], [rdst])

    for g in range(4):
        proj_rope(K0 + g * 64, 1024 + g * 64, 0, 0, 2560, lambda e0, n: (kT[:, e0:e0 + n], (lambda a: a)), rkT)
        for h in range(4):
            hh = 4 * g + h
            proj_rope(Q0 + hh * 64, hh * 64, 1 + h, 128, 2304,
                      (lambda e0, n, h=h: (qg4[:, e0 // 128:(e0 + n) // 128, h, :], (lambda a: a.rearrange("p (i c) -> p i c", c=128)))), rqg)
        if g == 0:
            k.dump("kT0", kT[:], rkT, [64, 2560], BF16)
            k.dump("qg0", qg[:], rqg, [64, 18 * 512], BF16)
        def scores(i):
            Pb, rPb = Pbs[i % 2]
            kbs = [(kT[:, i * 128:(i + 1) * 128], rkT, Va4[:, i, g, :], rVa, triLb, i),
                   (kT[:, (i + 1) * 128:(i + 2) * 128], rkT, Va4[:, i + 1, g, :], rVa, None, None),
                   (kT[:, (i + 2) * 128:(i + 3) * 128], rkT, Va4[:, i + 2, g, :], rVa, triUb, i + 2),
                   (self.kcT[:, g * 256:g * 256 + 128], self.rkcT, vca4[:, 0, g, :], self.rvca, None, None),
                   (self.kcT[:, g * 256 + 128:g * 256 + 256], self.rkcT, vca4[:, 1, g, :], self.rvca, None, None)]
            for kb, (kap, rk_, vap, rv_, mask, fl) in enumerate(kbs):
                ps, rps = k.bank()
                k.pe(A("matmul", ps[:, 0:512], kap, qg[:, i * 512:(i + 1) * 512], start=True, stop=True), [rk_, rqg], [rps])
                pk = Pb[:, kb * 512:(kb + 1) * 512]
                k.act(A("activation", out=pk, in_=ps[:, 0:512], func=AF.Exp, scale=0.125), [rps], [rPb])
                if mask is not None:
                    p3 = pk.rearrange("p (h c) -> p h c", c=128)
                    k.dve(A("scalar_tensor_tensor", out=p3, in0=p3, scalar=self.flags[:, fl:fl + 1], in1=mask.unsqueeze(1).to_broadcast([128, 4, 128]),
                            op0=ALU.mult, op1=ALU.mult), [rPb, self.rflags, self.rcstb], [rPb])
            return kbs

        def pv(i, kbs):
            Pb, rPb = Pbs[i % 2]
            po, rpo = k.bank()
            for h in range(4):
                for kb, (kap, rk_, vap, rv_, mask, fl) in enumerate(kbs):
                    k.pe(A("matmul", po[:, h * 65:(h + 1) * 65], Pb[:, kb * 512 + h * 128:kb * 512 + (h + 1) * 128], vap, start=(kb == 0), stop=(kb == 4)),
                         [rPb, rv_], [rpo])
            po3 = po[:, 0:260].rearrange("p (h c) -> p h c", c=65)
            k.dve(A("tensor_tensor", out=den[:, 0:4], in0=po3[:, :, 64], in1=self.esink[:, 4 * g:4 * g + 4], op=ALU.add), [rpo, self.resink], [rden])
            k.dve(A("reciprocal", out=den[:, 4:8], in_=den[:, 0:4]), [rden], [rden])
            k.dve(A("tensor_tensor", out=yatt3[:, i, g * 256:(g + 1) * 256].rearrange("p (h c) -> p h c", c=64), in0=po3[:, :, 0:64],
                    in1=den[:, 4:8].unsqueeze(2).to_broadcast([128, 4, 64]), op=ALU.mult), [rpo, rden], [ryatt])

        prevk = None
        for i in range(18):
            cur = scores(i)
            if prevk is not None:
                pv(i - 1, prevk)
            prevk = cur
        pv(17, prevk)
    k.dump("yatt", yatt[:], ryatt, [128, 18 * 1024], BF16)
    self.S.flush()
    arW.cur = wmark
    arA.close()
    arY.cur = arY.lo + 18 * 1024 * 2
    woa, rwoa = k.T("woa", [128, 8 * 1024], BF16, es=arY)
    wga, rwga = k.T("wga", [128, 8 * 1024], BF16, es=arY)
    woa3 = woa[:].rearrange("p (kc c) -> p kc c", kc=8); wga3 = wga[:].rearrange("p (kc c) -> p kc c", kc=8)
    stage = k.T("stg", [128, 8 * 128], es=arW)
    yT_, ryT = k.T("yattT", [128, 8 * 512], BF16, es=arA)
    yT3 = yT_[:].rearrange("p (kc t) -> p kc t", kc=8)
    sgs = [k.T("sg%d" % i, [128, 512], es=arA) for i in range(2)]
    tts = [k.T("tt%d" % i, [128, 512], es=arA) for i in range(2)]
    idb = self.cstb[:, 0:128]
    mg3 = self.mg3
    for i in range(8):
        k.load_w(stage, (woa3[:, :, i * 128:(i + 1) * 128], rwoa), I["w_o_att"], i * 128, 128)
        k.load_w(stage, (wga3[:, :, i * 128:(i + 1) * 128], rwga), I["w_in"], G0 + 1024 + i * 128, 128)
    SUP = [(0, 4), (4, 4), (8, 4), (12, 4), (16, 2)]
    for st, (c0, nc_) in enumerate(SUP):
        n = nc_ * 128
        t0 = (c0 + 1) * 128
        for ci in range(nc_):
            tb, rtb = k.bank()
            tbb = tb[:].bitcast(BF16)
            for kc in range(8):
                k.pe(A("transpose", tbb[:, kc * 128:(kc + 1) * 128], yatt3[:, c0 + ci, kc * 128:(kc + 1) * 128], idb), [ryatt, self.rcstb], [rtb])
            k.act(A("activation", out=yT3[:, :, ci * 128:(ci + 1) * 128], in_=tbb[:, 0:1024].rearrange("p (kc t) -> p kc t", kc=8), func=AF.Copy), [rtb], [ryT])
        for cbk in range(8):
            sg, rsg = sgs[cbk % 2]; tt, rtt = tts[cbk % 2]
            po, rpo = k.bank(); pg, rpg = k.bank()
            for kc in range(8):
                k.pe(A("matmul", po[:, 0:n], woa3[:, kc, cbk * 128:(cbk + 1) * 128], yT3[:, kc, 0:n], start=(kc == 0), stop=(kc == 7)), [rwoa, ryT], [rpo])
            for kc in range(8):
                k.pe(A("matmul", pg[:, 0:n], wga3[:, kc, cbk * 128:(cbk + 1) * 128], hT3[:, kc, t0:t0 + n], start=(kc == 0), stop=(kc == 7)), [rwga, self.rhT], [rpg])
            k.act(A("activation", out=sg[:, 0:n], in_=pg[:, 0:n], func=AF.Sigmoid), [rpg], [rsg])
            k.dve(A("tensor_tensor", out=tt[:, 0:n], in0=po[:, 0:n], in1=sg[:, 0:n], op=ALU.mult), [rpo, rsg], [rtt])
            m_ = mg3[:, cbk, c0 * 128:c0 * 128 + n]
            k.dve(A("tensor_tensor", out=m_, in0=m_, in1=tt[:, 0:n], op=ALU.add), [rtt, self.rmg], [self.rmg])
    k.dump("mg", self.mg[:], self.rmg, [128, 8 * 2304], BF16)
    self.S.flush()
    arA.close(); arY.close()
    arW.cur = wmark


K.phase_att = phase_att


def phase_wout(self):
    k = self; I = self.I
    arY = self.arYZ; arA = self.arACC; arH = self.arHT
    arY.close(); arA.close(); arH.close()
    mg3 = self.mg3
    self.h2T, self.rh2T = k.T("h2T", [128, 8 * 2050], BF16, es=arH)
    h2T3 = self.h2T[:].rearrange("p (kc t) -> p kc t", kc=8)
    self.h2T3 = h2T3
    g1, rg1 = k.T("g1row", [128, D], es=arY)
    wo, rwo = k.T("wo", [128, 8 * D], BF16, es=arY)
    wo3 = wo[:].rearrange("p (kc c) -> p kc c", kc=8)
    stage = k.T("stg", [128, 8 * 128], es=arY)
    stg3 = stage[0][:].rearrange("p (kc c) -> p kc c", kc=8)
    xts = [k.T("xw%d" % i, [128, D], es=arY) for i in range(3)]
    x1t = [k.T("x1t%d" % i, [128, D], es=arY) for i in range(3)]
    tmps = [(k.T("junk%d" % i, [128, D], BF16, es=arY), k.T("ss%d" % i, [128, 4], es=arY), k.T("xn%d" % i, [128, D], BF16, es=arY)) for i in range(3)]
    m3 = self.modT[:].rearrange("p (b v) -> p b v", v=2)
    row_bcast(k, g1, rg1, (lambda b: m3[:, 16 + b, 0:1]), self.rmodT2, 8)
    for i in range(8):
        k.dma(stg3, I["w_out"][:, i * 128:(i + 1) * 128].rearrange("(kc p) c -> p kc c", p=128), [], [stage[1]])
        k.dve(A("tensor_tensor", out=wo3[:, :, i * 128:(i + 1) * 128], in0=stg3, in1=g1[:, i * 128:(i + 1) * 128].unsqueeze(1).to_broadcast([128, 8, 128]),
                op=ALU.mult), [stage[1], rg1], [rwo])
    self.rx1s = [Reg("x1s%d" % i) for i in range(16)]

    def st_a(i):
        xt_, rxt = xts[i % 3]; x1_, rx1 = x1t[i % 3]
        k.dma(xt_[:], I["x_own"][(i + 1) * 128:(i + 2) * 128, :], [], [rxt])
        for half in range(2):
            ps, rps = k.bank()
            for kc in range(8):
                k.pe(A("matmul", ps[:, 0:512], mg3[:, kc, i * 128:(i + 1) * 128], wo3[:, kc, half * 512:(half + 1) * 512], start=(kc == 0), stop=(kc == 7)),
                     [self.rmg, rwo], [rps])
            k.dve(A("tensor_tensor", out=x1_[:, half * 512:(half + 1) * 512], in0=ps[:, 0:512], in1=xt_[:, half * 512:(half + 1) * 512], op=ALU.add),
                  [rps, rxt], [rx1])
        if 1 <= i <= 16:
            k.dma(self.x1s[(i - 1) * 128:i * 128, :], x1_[:], [rx1], [self.rx1s[i - 1]], q="pool")
        k.norm_a(x1_, rx1, 128, tmps[i % 3])
        if i == 5:
            k.dump("x1_5", x1_[:], rx1, [128, D])

    def st_b(i):
        if 1 <= i <= 16:
            k.norm_b(128, (lambda kc, i=i: h2T3[:, kc, 1 + (i - 1) * 128:1 + i * 128]), self.rh2T, 2, 24, 0, tmps[i % 3])
        elif i == 0:
            k.norm_b(128, (lambda kc: h2T3[:, kc, 0:1]), self.rh2T, 2, 24, 0, tmps[i % 3], src_view=lambda a: a[:, 127:128])
        else:
            k.norm_b(128, (lambda kc: h2T3[:, kc, 2049:2050]), self.rh2T, 2, 24, 0, tmps[i % 3], src_view=lambda a: a[:, 0:1])

    st_a(0)
    for i in range(18):
        if i + 1 < 18:
            st_a(i + 1)
        st_b(i)
    for (col, fl) in ((0, 1), (2049, 18)):
        for kc in range(8):
            k.dve(A("tensor_scalar", out=h2T3[:, kc, col:col + 1], in0=h2T3[:, kc, col:col + 1], scalar1=self.flags[:, fl:fl + 1], scalar2=None, op0=ALU.mult),
                  [self.rh2T, self.rflags], [self.rh2T])
    k.dump("h2T", self.h2T[:], self.rh2T, [128, 8 * 2050], BF16)
    self.S.flush()
    arY.close()


def phase_ffn(self):
    k = self; I = self.I
    h2T3 = self.h2T3
    arA = self.arACC
    arB = Arena(self.arYZ.lo, self.arW.hi)
    arH = self.arHT
    arA.close()
    Gb, rGb = k.T("Gb", [128, 22 * 2048], BF16, es=arB)
    Gb3 = Gb[:].rearrange("p (j t) -> p j t", j=22)
    wd, rwd = k.T("wd", [128, 22 * D], BF16, es=arB)
    wd3 = wd[:].rearrange("p (j c) -> p j c", j=22)
    wab = [k.T("wab%d" % i, [128, 8 * 128], BF16, es=arB) for i in range(2)]
    stage = k.T("stg", [128, 8 * 128], es=arH)
    raw = [k.T("raw%d" % i, [128, 412], es=arA) for i in range(2)]
    ac, rac = k.T("ac", [128, 412], es=arA)
    bc, rbc = k.T("bc", [128, 412], es=arA)
    fcw, rfcw = k.T("fcw", [128, 132], es=arA)
    fcb, rfcb = k.T("fcb", [128, 44], es=arA)
    k.dma(fcw[:], I["ffn_cw"], [], [rfcw])
    k.dma(fcb[:], I["ffn_cb"], [], [rfcb])
    fw3 = fcw[:].rearrange("p (b t) -> p b t", t=3)
    g2, rg2 = k.T("g2row", [128, D], es=arA)
    stgd = k.T("stgd", [128, D], es=arA)
    m3 = self.modT[:].rearrange("p (b v) -> p b v", v=2)
    row_bcast(k, g2, rg2, (lambda b: m3[:, 40 + b, 0:1]), self.rmodT2, 8)
    tiles = [(1, 410), (411, 410), (821, 410), (1231, 410), (1641, 408)]
    for j in range(22):
        k.dma(stgd[0][:], I["w_down"][j * 128:(j + 1) * 128, :], [], [stgd[1]])
        k.pool(A("tensor_tensor", out=wd3[:, j, :], in0=stgd[0][:], in1=g2[:], op=ALU.mult), [stgd[1], rg2], [rwd])
        for ab in range(2):
            wb_, rwb = wab[ab]
            wb3 = wb_[:].rearrange("p (kc c) -> p kc c", kc=8)
            k.load_w(stage, (wb3, rwb), I["w_up"], ab * D_FF + j * 128, 128)
        wa3 = wab[0][0][:].rearrange("p (kc c) -> p kc c", kc=8); rwa = wab[0][1]
        wbb3 = wab[1][0][:].rearrange("p (kc c) -> p kc c", kc=8); rwb = wab[1][1]
        for (o0, n) in tiles:
            pa, rpa = k.bank(); pb, rpb = k.bank()
            for kc in range(8):
                k.pe(A("matmul", pa[:, 0:n + 2], wa3[:, kc, :], h2T3[:, kc, o0 - 1:o0 + n + 1], start=(kc == 0), stop=(kc == 7)), [rwa, self.rh2T], [rpa])
            for kc in range(8):
                k.pe(A("matmul", pb[:, 0:n + 2], wbb3[:, kc, :], h2T3[:, kc, o0 - 1:o0 + n + 1], start=(kc == 0), stop=(kc == 7)), [rwb, self.rh2T], [rpb])
            fa = conv3(k, pa, rpa, n, fw3[:, j, :], fcb[:, j:j + 1], raw[0], ac[:, 0:n], rac, AF.Silu, [rfcw, rfcb], defer=True)
            (rb_, rrb_) = raw[1]
            k.act(A("activation", out=rb_[:, 0:n], in_=pb[:, 1:n + 1], func=AF.Identity, scale=fw3[:, 22 + j, 1:2], bias=fcb[:, 22 + j:23 + j]),
                  [rpb, rfcw, rfcb], [rrb_])
            fa()
            k.dve(A("scalar_tensor_tensor", out=rb_[:, 0:n], in0=pb[:, 0:n], scalar=fw3[:, 22 + j, 0:1], in1=rb_[:, 0:n], op0=ALU.mult, op1=ALU.add),
                  [rpb, rrb_, rfcw], [rrb_])
            k.dve(A("scalar_tensor_tensor", out=rb_[:, 0:n], in0=pb[:, 2:n + 2], scalar=fw3[:, 22 + j, 2:3], in1=rb_[:, 0:n], op0=ALU.mult, op1=ALU.add),
                  [rpb, rrb_, rfcw], [rrb_])
            k.dve(A("tensor_tensor", out=Gb3[:, j, o0 - 1:o0 - 1 + n], in0=ac[:, 0:n], in1=rb_[:, 0:n], op=ALU.mult), [rac, rrb_], [rGb])
    k.dump("Gb", Gb[:], rGb, [128, 22 * 2048], BF16)
    self.S.flush()
    arA.close()
    fg, rfg = k.T("fgrow", [128, D], es=arA)
    x1r = [k.T("x1r%d" % i, [128, D], es=arA) for i in range(2)]
    k.dma(fg[:], I["fg_row"], [], [rfg])
    arB.cur = arB.lo + 22 * 2048 * 2 + 22 * D * 2
    arH.cur = arH.lo + 8 * 2050 * 2 + 64
    stg2 = k.T("stgd", [128, D], es=arB)
    xos = [k.T("xo0", [128, D], es=arB), stg2]
    junk, rjunk = k.T("junk", [128, D], BF16, es=arH)
    sss = [k.T("ssa", [128, 4], es=arH), k.T("ssb", [128, 4], es=arH)]
    for i in range(16):
        xr, rxr = x1r[i % 2]
        xo, rxo = xos[i % 2]
        ss, rss = sss[i % 2]
        k.dma(xr[:], self.x1s[i * 128:(i + 1) * 128, :], [self.rx1s[i]], [rxr])
        for half in range(2):
            ps, rps = k.bank()
            for j in range(22):
                k.pe(A("matmul", ps[:, 0:512], Gb3[:, j, i * 128:(i + 1) * 128], wd3[:, j, half * 512:(half + 1) * 512], start=(j == 0), stop=(j == 21)),
                     [rGb, rwd], [rps])
            k.dve(A("tensor_tensor", out=xo[:, half * 512:(half + 1) * 512], in0=ps[:, 0:512], in1=xr[:, half * 512:(half + 1) * 512], op=ALU.add),
                  [rps, rxr], [rxo])
        k.dve(A("memset", ss[:, 0:1], 0.0), [], [rss])
        k.act(A("activation", out=junk[:], in_=xo[:], func=AF.Square, accum_out=ss[:, 0:1]), [rxo], [rjunk, rss])
        k.dve(A("tensor_scalar", out=ss[:, 1:2], in0=ss[:, 0:1], scalar1=1.0 / D, scalar2=EPS, op0=ALU.mult, op1=ALU.add), [rss], [rss])
        k.act(A("activation", out=ss[:, 1:2], in_=ss[:, 1:2], func=AF.Sqrt), [rss], [rss])
        k.dve(A("reciprocal", out=ss[:, 2:3], in_=ss[:, 1:2]), [rss], [rss])
        k.dve(A("scalar_tensor_tensor", out=xr[:], in0=xo[:], scalar=ss[:, 2:3], in1=fg[:], op0=ALU.mult, op1=ALU.mult), [rxo, rss, rfg], [rxr])
        k.dma(self.out[i * 128:(i + 1) * 128, :], xr[:], [rxr], [], q="pool")
    self.S.flush()


K.phase_wout = phase_wout
K.phase_ffn = phase_ffn
```

```python
import os
from contextlib import ExitStack
import numpy as np
import concourse.bass as bass
import concourse.mybir as mybir
from concourse.bass_utils import run_bass_kernel_spmd

F32 = mybir.dt.float32
BF16 = mybir.dt.bfloat16
AF = mybir.ActivationFunctionType
ALU = mybir.AluOpType

D = 1024
L = 8192
NQ = 4
TOK = 2048
EPS = 1e-6
XBC0 = 2048
DT0 = 5120
Q0 = 5184
K0 = 6208
V0 = 6464
G0 = 6720
D_FF = 2816
NOWN = 20
NSLOT = 48
NSB = 16
NFLAG = 20 + 2 * NSLOT + 2 * NSLOT

DEBUG = os.environ.get("KDEBUG", "")


class Reg:
    __slots__ = ("name", "lw", "rd", "excl")

    def __init__(self, name, excl=False):
        self.name = name
        self.lw = None
        self.rd = []
        self.excl = excl


class Op:
    __slots__ = ("eng", "fn", "deps", "dma", "sem", "val", "needed", "prev")


class Sched:
    ENG = ("pe", "act", "dve", "pool", "sp")

    def __init__(self, nc, n_dma_sems=12):
        self.nc = nc
        self.esem = {e: nc.alloc_semaphore("s_" + e) for e in self.ENG}
        self.ecnt = {e: 0 for e in self.ENG}
        self.dq = ("sp", "pool", "act")
        self.dsem = {q: [nc.alloc_semaphore("d_%s%d" % (q, i)) for i in range(n_dma_sems)] for q in self.dq}
        self.dcnt = {q: [0] * n_dma_sems for q in self.dq}
        self.drr = {q: 0 for q in self.dq}
        self.dlast = {q: [None] * n_dma_sems for q in self.dq}
        self.ops = []
        self.known = {e: {} for e in self.ENG}
        self.nops = 0

    def op(self, eng, fn, r=(), w=(), dma=False):
        o = Op()
        o.eng = eng; o.fn = fn; o.dma = dma; o.needed = False; o.sem = None; o.val = 0; o.prev = None
        deps = []
        for t in r:
            if t.lw is not None:
                deps.append(t.lw)
            if t.excl:
                deps.extend(x for x in t.rd if x.eng != eng)
        for t in w:
            if t.lw is not None and (t.lw.eng != eng or t.lw.dma or dma):
                deps.append(t.lw)
            deps.extend(x for x in t.rd if (x.eng != eng or x.dma or dma))
        o.deps = []
        seen = set()
        for d in deps:
            if d is o or id(d) in seen:
                continue
            seen.add(id(d))
            if d.eng == "pe" and eng == "pe" and not d.dma and not dma:
                continue
            o.deps.append(d)
        for t in w:
            t.lw = o
            t.rd = []
        for t in r:
            if t.lw is not o:
                t.rd.append(o)
        self.ops.append(o)
        return o

    def flush(self):
        ops = self.ops
        for o in ops:
            for d in o.deps:
                d.needed = True
        last = {}
        for o in ops:
            if o.dma:
                o.needed = True
            else:
                last[o.eng] = o
        for o in last.values():
            o.needed = True
        for o in ops:
            if o.dma:
                q = o.eng
                i = self.drr[q]
                self.drr[q] = (i + 1) % len(self.dsem[q])
                o.prev = self.dlast[q][i]
                self.dcnt[q][i] += 16
                o.sem = ("d", q, i)
                o.val = self.dcnt[q][i]
                self.dlast[q][i] = o
            elif o.needed:
                self.ecnt[o.eng] += 1
                o.sem = ("e", o.eng)
                o.val = self.ecnt[o.eng]
        per = {e: [] for e in self.ENG}
        for o in ops:
            per[o.eng].append(o)
        fin = [o for o in ops if (o.dma or o is last.get(o.eng))]

        def semh(key):
            return self.esem[key[1]] if key[0] == "e" else self.dsem[key[1]][key[2]]

        def emit(e, eh):
            kn = self.known[e]

            def wait(key, val):
                if kn.get(key, 0) >= val:
                    return
                eh.wait_ge(semh(key), val)
                kn[key] = val

            for o in per[e]:
                for d in o.deps:
                    if d.sem is not None:
                        wait(d.sem, d.val)
                if o.dma and o.prev is not None:
                    wait(o.prev.sem, o.prev.val)
                ins = o.fn(eh)
                if o.dma:
                    ins.then_inc(semh(o.sem), 16)
                elif o.needed:
                    ins.then_inc(semh(o.sem), 1)
            for d in fin:
                if d.eng == e and not d.dma:
                    continue
                wait(d.sem, d.val)

        with self.nc.Block() as block:
            @block.tensor
            def _(t):
                emit("pe", t)

            @block.scalar
            def _(t):
                emit("act", t)

            @block.vector
            def _(t):
                emit("dve", t)

            @block.gpsimd
            def _(t):
                emit("pool", t)

            @block.sync
            def _(t):
                emit("sp", t)
        self.nops += len(ops)
        self.ops = []


class Arena:
    def __init__(self, lo, hi):
        self.lo = lo; self.hi = hi; self.cur = lo

    def take(self, n, name=""):
        o = self.cur
        assert o + n <= self.hi, "arena overflow for %s: need %d have %d" % (name, n, self.hi - o)
        self.cur = o + n
        return o

    def close(self):
        self.cur = self.lo


def A(name, *args, **kw):
    return lambda e: getattr(e, name)(*args, **kw)


class K:
    def __init__(self):
        self.nc = bass.Bass("TRN2", target_bir_lowering=False)
        self.S = Sched(self.nc)
        self.dbg = {}
        self.pbank = 0
        self.es = ExitStack()
        self.uid = 0
        self.arC = Arena(17408, 26624)
        self.arACC = Arena(26624, 43008)
        self.arHT = Arena(43008, 83968)
        self.arYZ = Arena(83968, 157696)
        self.arW = Arena(157696, 229376)
        self.arBIG = Arena(43008, 229376 - 18 * 1024)

    def T(self, name, shape, dt=F32, es=None):
        ar = es or self.arC
        nbytes = int(np.prod(shape[1:])) * (2 if dt == BF16 else 4)
        nbytes = (nbytes + 63) // 64 * 64
        off = ar.take(nbytes, name)
        self.offs = getattr(self, "offs", {})
        self.offs[name] = off
        self.uid += 1
        t = self.nc.alloc_sbuf_tensor_at("sb%d_%s" % (self.uid, name), list(shape), dt, offset=off)
        return t, Reg(name)

    def din(self, name, shape, dt=F32):
        return self.nc.dram_tensor(name, list(shape), dt, kind="ExternalInput").ap()

    def pe(self, fn, r, w): return self.S.op("pe", fn, r, w)
    def act(self, fn, r, w): return self.S.op("act", fn, r, w)
    def dve(self, fn, r, w): return self.S.op("dve", fn, r, w)
    def pool(self, fn, r, w): return self.S.op("pool", fn, r, w)
    def dma(self, out, in_, r, w, q="sp"): return self.S.op(q, A("dma_start", out=out, in_=in_), r, w, dma=True)

    def bank(self):
        i = self.pbank
        self.pbank = (i + 1) % 8
        return self.PS[i], self.PR[i]

    def dump(self, name, ap, reg, shape, dt=F32):
        if name not in DEBUG.split(","):
            return
        o = self.nc.dram_tensor("dbg_" + name, list(shape), dt, kind="ExternalOutput").ap()
        self.dma(o, ap, list(reg) if isinstance(reg, (list, tuple)) else [reg], [], q="sp")
        self.dbg[name] = "dbg_" + name

    def build(self):
        nc = self.nc
        k = self
        I = {}
        I["x_own"] = k.din("x_own", [NOWN * 128, D])
        I["x_oth"] = k.din("x_oth", [NSLOT * 128, D])
        I["x_oth_h"] = k.din("x_oth_h", [2 * NSLOT, D])
        I["ctx"] = k.din("ctx", [256, D])
        I["cvec"] = k.din("cvec", [128, 16])
        I["flags"] = k.din("flags", [128, NFLAG])
        I["w_mod"] = k.din("w_mod", [D, 6 * D])
        I["bmodT"] = k.din("bmodT", [128, 48])
        I["n1g"] = k.din("n1g", [128, 8])
        I["n2g"] = k.din("n2g", [128, 8])
        I["fg_row"] = k.din("fg_row", [128, D])
        I["w_in"] = k.din("w_in", [D, 8768])
        I["w_qkp"] = k.din("w_qkp", [D, 1280])
        I["ssd_cw"] = k.din("ssd_cw", [128, 24 * 3])
        I["ssd_cb"] = k.din("ssd_cb", [128, 24])
        I["dtb"] = k.din("dtb", [128, 64])
        I["alog"] = k.din("alog", [128, 64])
        I["dsk"] = k.din("dsk", [128, 32])
        I["sng"] = k.din("sng", [128, 16])
        I["w_o_ssd"] = k.din("w_o_ssd", [2048, D])
        I["w_o_att"] = k.din("w_o_att", [D, D])
        I["sink"] = k.din("sink", [128, 16])
        I["w_out"] = k.din("w_out", [D, D])
        I["w_up"] = k.din("w_up", [D, 2 * D_FF])
        I["ffn_cw"] = k.din("ffn_cw", [128, 44 * 3])
        I["ffn_cb"] = k.din("ffn_cb", [128, 44])
        I["w_down"] = k.din("w_down", [D_FF, D])
        I["cosT"] = k.din("cosT", [64, NOWN * 128])
        I["sinT"] = k.din("sinT", [64, NOWN * 128])
        I["consts"] = k.din("consts", [128, 3 * 128])
        self.I = I
        self.out = nc.dram_tensor("out", [TOK, D], F32, kind="ExternalOutput").ap()
        self.x1s = nc.dram_tensor("x1s", [TOK, D], F32, kind="Internal").ap()

        self.PS = []
        self.PR = []
        for i in range(8):
            self.PS.append(self.es.enter_context(nc.psum_tensor("ps%d" % i, [128, 512], F32)))
            self.PR.append(Reg("ps%d" % i, excl=True))

        stop = os.environ.get("KSTOP", "")
        for ph in ("setup", "others", "own_h", "ssd", "ssd_proj", "att", "wout", "ffn"):
            getattr(self, "phase_" + ph)()
            if stop == ph:
                break
        return nc

    def phase_setup(self):
        k = self; nc = self.nc; I = self.I
        es = Arena(229376 - 18 * 1024, 229376)
        self.cst, self.rcst = k.T("cst", [128, 3 * 128])
        self.cstb, self.rcstb = k.T("cstb", [128, 3 * 128], BF16)
        self.ones, self.rones = k.T("ones", [128, 128])
        self.flags, self.rflags = k.T("flags", [128, NFLAG])
        self.modT, self.rmodT = k.T("modT", [128, 48 * 2])
        self.gs, self.rgs = k.T("gs", [128, 8 * 4])
        self.negA, self.rnegA = k.T("negA", [128, 64])
        self.esink, self.resink = k.T("esink", [128, 16])
        self.dtb, self.rdtb = k.T("dtb", [128, 64])
        self.dsk, self.rdsk = k.T("dsk", [128, 32])
        self.sng, self.rsng = k.T("sng", [128, 16])
        self.scw, self.rscw = k.T("scw", [128, 72])
        self.scb, self.rscb = k.T("scb", [128, 24])
        self.n12, self.rn12 = k.T("n12", [128, 16])
        cvec, rcvec = k.T("cvec", [128, 16], es=es)
        sc, rsc = k.T("sc", [128, 16], es=es)
        bmodT, rbmodT = k.T("bmodT", [128, 48], es=es)
        wm = [k.T("wm%d" % i, [128, 8 * 256], es=es) for i in range(2)]

        k.dma(self.cst[:], I["consts"], [], [self.rcst])
        k.dma(self.flags[:], I["flags"], [], [self.rflags])
        k.dma(cvec[:], I["cvec"], [], [rcvec])
        k.dma(bmodT[:], I["bmodT"], [], [rbmodT])
        k.dma(self.n12[:, 0:8], I["n1g"], [], [self.rn12])
        k.dma(self.n12[:, 8:16], I["n2g"], [], [self.rn12])
        k.dma(self.negA[:], I["alog"], [], [self.rnegA])
        k.dma(self.esink[:], I["sink"], [], [self.resink])
        k.dma(self.dtb[:], I["dtb"], [], [self.rdtb])
        k.dma(self.dsk[:], I["dsk"], [], [self.rdsk])
        k.dma(self.sng[:], I["sng"], [], [self.rsng])
        k.dma(self.scw[:], I["ssd_cw"], [], [self.rscw])
        k.dma(self.scb[:], I["ssd_cb"], [], [self.rscb])
        k.dve(A("tensor_copy", out=self.cstb[:], in_=self.cst[:]), [self.rcst], [self.rcstb])
        k.pool(A("memset", self.ones[:], 1.0), [], [self.rones])
        k.act(A("activation", out=sc[:], in_=cvec[:], func=AF.Silu), [rcvec], [rsc])
        k.act(A("activation", out=self.negA[:], in_=self.negA[:], func=AF.Exp), [self.rnegA], [self.rnegA])
        k.dve(A("tensor_scalar", out=self.negA[:], in0=self.negA[:], scalar1=-1.0, scalar2=None, op0=ALU.mult), [self.rnegA], [self.rnegA])
        k.act(A("activation", out=self.esink[:], in_=self.esink[:], func=AF.Exp), [self.resink], [self.resink])

        wmv = I["w_mod"].rearrange("(kc p) c -> p kc c", p=128)

        def mod_ct(ct):
            wt, rw = wm[ct % 2]
            wt3 = wt[:].rearrange("p (kc c) -> p kc c", kc=8)
            k.dma(wt3, wmv[:, :, ct * 256:(ct + 1) * 256], [], [rw])
            for j in range(2):
                blk = ct * 2 + j
                ps, rps = self.PS[7], self.PR[7]
                for kc in range(8):
                    k.pe(A("matmul", ps[:, 400 + 2 * j:402 + 2 * j], wt3[:, kc, j * 128:(j + 1) * 128], sc[:, 2 * kc:2 * kc + 2],
                           start=(kc == 0), stop=(kc == 7)), [rsc, rw], [rps])
                rm_ = self.rmodT if blk < 16 else self.rmodT2
                k.dve(A("tensor_scalar", out=self.modT[:, 2 * blk:2 * blk + 2], in0=ps[:, 400 + 2 * j:402 + 2 * j], scalar1=bmodT[:, blk:blk + 1],
                        scalar2=None, op0=ALU.add), [rps, rbmodT], [rm_])
        self.mod_ct = mod_ct
        self.rmodT2 = Reg("modT2")
        for ct in range(8):
            mod_ct(ct)
        m3 = self.modT[:].rearrange("p (b v) -> p b v", v=2)
        g3 = self.gs[:].rearrange("p (kc v) -> p kc v", v=4)
        for v in range(2):
            k.dve(A("scalar_tensor_tensor", out=g3[:, :, v], in0=m3[:, 8:16, v], scalar=1.0, in1=self.n12[:, 0:8],
                    op0=ALU.add, op1=ALU.mult), [self.rmodT, self.rn12], [self.rgs])
        k.dump("modT", self.modT[:], self.rmodT, [128, 96])
        k.dump("gs", self.gs[:], self.rgs, [128, 32])

    def norm_a(self, xt, rxt, rows, tmp):
        self.norm_a_multi([(xt, rxt, rows, tmp)])

    def norm_a_multi(self, items):
        k = self
        for (xt, rxt, rows, ((junk, rjunk), (ss, rss), (xn, rxn))) in items:
            k.dve(A("memset", ss[:, 0:1], 0.0), [], [rss])
        for (xt, rxt, rows, ((junk, rjunk), (ss, rss), (xn, rxn))) in items:
            k.act(A("activation", out=junk[0:rows, :], in_=xt[0:rows, :], func=AF.Square, accum_out=ss[0:rows, 0:1]), [rxt], [rjunk, rss])
        for (xt, rxt, rows, ((junk, rjunk), (ss, rss), (xn, rxn))) in items:
            k.dve(A("tensor_scalar", out=ss[0:rows, 1:2], in0=ss[0:rows, 0:1], scalar1=1.0 / D, scalar2=EPS, op0=ALU.mult, op1=ALU.add), [rss], [rss])
        for (xt, rxt, rows, ((junk, rjunk), (ss, rss), (xn, rxn))) in items:
            k.act(A("activation", out=ss[0:rows, 1:2], in_=ss[0:rows, 1:2], func=AF.Sqrt), [rss], [rss])
        for (xt, rxt, rows, ((junk, rjunk), (ss, rss), (xn, rxn))) in items:
            k.dve(A("reciprocal", out=ss[0:rows, 2:3], in_=ss[0:rows, 1:2]), [rss], [rss])
        for (xt, rxt, rows, ((junk, rjunk), (ss, rss), (xn, rxn))) in items:
            k.act(A("activation", out=xn[0:rows, :], in_=xt[0:rows, :], func=AF.Copy, scale=ss[0:rows, 2:3]), [rxt, rss], [rxn])

    def norm_b(self, rows, dst_fn, rdst, gcol, shblk, v, tmp, src_view=None, bankfn=None, dst3=None):
        k = self
        (junk, rjunk), (ss, rss), (xn, rxn) = tmp
        sv = src_view or (lambda a: a)
        ps, rps = (bankfn or k.bank)()
        psb = ps[:].bitcast(BF16)
        for kc in range(8):
            k.pe(A("transpose", psb[:, kc * 128:kc * 128 + rows], xn[0:rows, kc * 128:(kc + 1) * 128], self.cstb[0:rows, 0:rows]),
                 [rxn, self.rcstb], [rps])
        g3 = self.gs[:].rearrange("p (kc v) -> p kc v", v=4)
        m3 = self.modT[:].rearrange("p (b v) -> p b v", v=2)
        if dst3 is not None and rows == 128:
            p3 = psb[:, 0:1024].rearrange("p (kc t) -> p kc t", kc=8)
            k.dve(A("tensor_tensor", out=dst3, in0=p3, in1=g3[:, :, gcol:gcol + 1].to_broadcast([128, 8, 128]), op=ALU.mult),
                  [rps, self.rgs], [rdst])
            k.dve(A("tensor_tensor", out=dst3, in0=dst3, in1=m3[:, shblk:shblk + 8, v:v + 1].to_broadcast([128, 8, 128]), op=ALU.add),
                  [rdst, self.rmodT], [rdst])
            return
        for kc in range(8):
            k.act(A("activation", out=dst_fn(kc), in_=sv(psb[:, kc * 128:kc * 128 + rows]), func=AF.Identity,
                    scale=g3[:, kc, gcol:gcol + 1], bias=m3[:, shblk + kc, v:v + 1]), [rps, self.rgs, self.rmodT], [rdst])

    def norm_T(self, xt, rxt, rows, dst_fn, rdst, gcol, shblk, v, tmp, src_view=None):
        self.norm_a(xt, rxt, rows, tmp)
        self.norm_b(rows, dst_fn, rdst, gcol, shblk, v, tmp, src_view)

    def load_w(self, stage, dst3, src, c0, ncols, scale_ap=None, eng="pool"):
        k = self
        st, rst = stage
        st3 = st[:, 0:8 * ncols].rearrange("p (kc c) -> p kc c", kc=8)
        k.dma(st3, src[:, c0:c0 + ncols].rearrange("(kc p) c -> p kc c", p=128), [], [rst])
        dst, rdst = dst3
        k.S.op(eng, A("tensor_copy", out=dst, in_=st3), [rst], [rdst])

    def phase_others(self):
        k = self; nc = self.nc; I = self.I
        es = self.arBIG
        self.acc, self.racc = k.T("acc", [128, 2 * 2048], es=self.arACC)
        self.kcT, self.rkcT = k.T("kcT", [64, 4 * 256], BF16)
        self.vca, self.rvca = k.T("vca", [128, 2 * 4 * 65], BF16)
        logP, rlogP = k.T("logP", [128, 64], es=es)
        wres, rwres = k.T("wres", [128, 8 * 2624], BF16, es=es)
        wres3 = wres[:].rearrange("p (kc c) -> p kc c", kc=8)
        wkv, rwkv = k.T("wkv", [128, 8 * 512], BF16, es=es)
        wkv3 = wkv[:].rearrange("p (kc c) -> p kc c", kc=8)
        stg = [k.T("stg%d" % i, [128, 8 * 128], es=es) for i in range(2)]
        xts = [k.T("xto%d" % i, [128, D], es=es) for i in range(3)]
        xh, rxh = k.T("xh", [6, D], es=es)
        junk_ = k.T("junk", [128, D], BF16, es=es)
        tmps = [(junk_, k.T("ss%d" % i, [128, 4], es=es), k.T("xn%d" % i, [128, D], BF16, es=es)) for i in range(4)]
        tmp = tmps[0]
        hTo = [k.T("hTo%d" % i, [128, 8 * 390], BF16, es=es) for i in range(2)]
        raw = [k.T("raw%d" % i, [128, 388], es=es) for i in range(3)]
        xc = [k.T("xc%d" % i, [128, 388], BF16, es=es) for i in range(4)]
        xtok = [[k.T("xtok%d_%d" % (i, s), [128, 2048], BF16, es=es) for s in range(3)] for i in range(2)]
        btok = [[k.T("btok%d_%d" % (i, s), [128, 512], BF16, es=es) for s in range(3)] for i in range(2)]
        xp = [[k.T("xp%d_%d" % (d_, s), [128, 2048], BF16, es=es) for s in range(3)] for d_ in range(2)]
        dtr, rdtr = k.T("dtr", [128, 192], es=es)
        dte, rdte = k.T("dte", [128, 192], es=es)
        la, rla = k.T("la", [128, 192], es=es)
        arg, rarg = k.T("arg", [128, 192], es=es)
        wts = [k.T("wt%d" % i, [128, 192], es=es) for i in range(2)]
        wt_, rwt = wts[0]
        L1, rL1 = k.T("L1", [128, 192], es=es)
        hTc = self.nc.alloc_sbuf_tensor_at("sb_hTc_alias", [128, 8 * 258], BF16, offset=self.offs["hTo0"])
        rhTc = hTo[0][1]
        short = [0]

        def sbank():
            i = short[0]; short[0] = (i + 1) % 3
            return self.PS[i], self.PR[i]

        k.dve(A("memset", self.acc[:], 0.0), [], [self.racc])
        k.dve(A("memset", logP[:], 0.0), [], [rlogP])
        k.dve(A("memset", self.vca[:], 1.0), [], [self.rvca])
        for i in range(20):
            k.load_w(stg[i % 2], (wres3[:, :, i * 128:(i + 1) * 128], rwres), I["w_in"], XBC0 + i * 128, 128)
        k.load_w(stg[0], (wres3[:, :, 2560:2624], rwres), I["w_in"], DT0, 64)
        for i in range(4):
            k.load_w(stg[(i + 1) % 2], (wkv3[:, :, i * 128:(i + 1) * 128], rwkv), I["w_in"], K0 + i * 128, 128)
        cw3 = self.scw[:].rearrange("p (b t) -> p b t", t=3)
        idb = self.cstb[:, 0:128]
        triU = self.cst[:, 128:256]; triL = self.cst[:, 256:384]

        def conv_block(ps, n, cb, rw, xcw):
            return conv3(k, ps, self.PRmap[id(ps)], n, cw3[:, cb, :], self.scb[:, cb:cb + 1], rw, xcw[0][:, 0:n], xcw[1], AF.Silu,
                         [self.rscw, self.rscb], defer=True)
        self.conv_block = conv_block

        def dt_chain(nsl, hT3, col0, stride, flag0, split=False):
            psd, rpsd = self.PS[6], self.PR[6]
            for s in range(nsl):
                for kc in range(8):
                    k.pe(A("matmul", psd[:, s * 64:(s + 1) * 64], hT3[:, kc, col0 + s * stride:col0 + s * stride + 128], wres3[:, kc, 2560:2624],
                           start=(kc == 0), stop=(kc == 7)), [rwres, self.rhcur], [rpsd])
            n = nsl * 64
            k.dve(A("tensor_tensor", out=dtr[:, 0:n].rearrange("p (s c) -> p s c", c=64), in0=psd[:, 0:n].rearrange("p (s c) -> p s c", c=64),
                    in1=self.dtb[:].unsqueeze(1).to_broadcast([128, nsl, 64]), op=ALU.add), [rpsd, self.rdtb], [rdtr])
            k.act(A("activation", out=dtr[:, 0:n], in_=dtr[:, 0:n], func=AF.Exp), [rdtr], [rdtr])
            k.act(A("activation", out=dte[:, 0:n], in_=dtr[:, 0:n], func=AF.Ln, bias=1.0), [rdtr], [rdte])
            if flag0 is not None:
                for s in range(nsl):
                    for d in range(2):
                        f = flag0 + s * 2 + d
                        k.dve(A("tensor_scalar", out=dte[:, s * 64 + d * 32:s * 64 + d * 32 + 32], in0=dte[:, s * 64 + d * 32:s * 64 + d * 32 + 32],
                                scalar1=self.flags[:, f:f + 1], scalar2=None, op0=ALU.mult), [rdte, self.rflags], [rdte])
            k.dve(A("tensor_tensor", out=la[:, 0:n].rearrange("p (s c) -> p s c", c=64), in0=dte[:, 0:n].rearrange("p (s c) -> p s c", c=64),
                    in1=self.negA[:].unsqueeze(1).to_broadcast([128, nsl, 64]), op=ALU.mult), [rdte, self.rnegA], [rla])
            if split:
                return None, None
            return dt_cs(nsl)

        def dt_cs(nsl):
            psc, rpsc = self.PS[7], self.PR[7]
            for s in range(nsl):
                k.pe(A("matmul", psc[:, s * 64:s * 64 + 32], triU, la[:, s * 64:s * 64 + 32], start=True, stop=True), [rla, self.rcst], [rpsc])
                k.pe(A("matmul", psc[:, s * 64 + 32:s * 64 + 64], triL, la[:, s * 64 + 32:s * 64 + 64], start=True, stop=True), [rla, self.rcst], [rpsc])
                k.pe(A("matmul", psc[:, 192 + s * 64:192 + (s + 1) * 64], self.ones[:], la[:, s * 64:(s + 1) * 64], start=True, stop=True),
                     [rla, self.rones], [rpsc])
            return psc, rpsc
        self.dt_chain = dt_chain
        self.PRmap = {id(self.PS[i]): self.PR[i] for i in range(8)}

        def states(nsl, xt_set, bt_set, wtp=None):
            states_x(nsl, xt_set, bt_set, wtp)
            states_mm(nsl, xt_set, bt_set, wtp)

        def states_x(nsl, xt_set, bt_set, wtp=None):
            wt_, rwt = wtp or wts[0]
            for d in range(2):
                for s in range(nsl):
                    (k.pool if s == 2 else k.dve)(A("tensor_tensor", out=xp[d][s][0][:].rearrange("p (h c) -> p h c", c=64),
                             in0=xt_set[s][0][:].rearrange("p (h c) -> p h c", c=64),
                             in1=wt_[:, s * 64 + d * 32:s * 64 + d * 32 + 32].unsqueeze(2).to_broadcast([128, 32, 64]), op=ALU.mult),
                           [xt_set[s][1], rwt], [xp[d][s][1]])

        def states_mm(nsl, xt_set, bt_set, wtp=None):
            for d in range(2):
                for g in range(4):
                    ps, rps = sbank()
                    for s in range(nsl):
                        k.pe(A("matmul", ps[:, 0:512], bt_set[s][0][:, g * 128:(g + 1) * 128], xp[d][s][0][:, g * 512:(g + 1) * 512],
                               start=(s == 0), stop=(s == nsl - 1)), [bt_set[s][1], xp[d][s][1]], [rps])
                    a = self.acc[:, d * 2048 + g * 512:d * 2048 + (g + 1) * 512]
                    k.dve(A("tensor_tensor", out=a, in0=ps[:, 0:512], in1=a, op=ALU.add), [rps, self.racc], [self.racc])

        def states_old(nsl, xt_set, bt_set, wtp=None):
            wt_, rwt = wtp or wts[0]
            for d in range(2):
                for s in range(nsl):
                    (k.pool if s == 2 else k.dve)(A("tensor_tensor", out=xp[d][s][0][:].rearrange("p (h c) -> p h c", c=64),
                             in0=xt_set[s][0][:].rearrange("p (h c) -> p h c", c=64),
                             in1=wt_[:, s * 64 + d * 32:s * 64 + d * 32 + 32].unsqueeze(2).to_broadcast([128, 32, 64]), op=ALU.mult),
                           [xt_set[s][1], rwt], [xp[d][s][1]])
                for g in range(4):
                    ps, rps = sbank()
                    for s in range(nsl):
                        k.pe(A("matmul", ps[:, 0:512], bt_set[s][0][:, g * 128:(g + 1) * 128], xp[d][s][0][:, g * 512:(g + 1) * 512],
                               start=(s == 0), stop=(s == nsl - 1)), [bt_set[s][1], xp[d][s][1]], [rps])
                    a = self.acc[:, d * 2048 + g * 512:d * 2048 + (g + 1) * 512]
                    k.dve(A("tensor_tensor", out=a, in0=ps[:, 0:512], in1=a, op=ALU.add), [rps, self.racc], [self.racc])

        def proj_blocks(nsl, hT3, ncol, slot_off, xt_set, bt_set, hook=None):
            tb = [(self.PS[3 + s], self.PR[3 + s]) for s in range(nsl)]

            def trans(cb, xcw):
                for s in range(nsl):
                    tbb = tb[s][0][:].bitcast(BF16)
                    j = cb % 8
                    k.pe(A("transpose", tbb[:, j * 128:(j + 1) * 128], xcw[0][:, slot_off(s):slot_off(s) + 128], idb), [xcw[1], self.rcstb], [tb[s][1]])
                    if cb == 7:
                        k.dve(A("tensor_copy", out=xt_set[s][0][:, 0:1024], in_=tbb[:, 0:1024]), [tb[s][1]], [xt_set[s][1]])
                    elif cb == 15:
                        k.act(A("activation", out=xt_set[s][0][:, 1024:2048], in_=tbb[:, 0:1024], func=AF.Copy), [tb[s][1]], [xt_set[s][1]])
                    elif cb == 19:
                        k.dve(A("tensor_copy", out=bt_set[s][0][:, 0:512], in_=tbb[:, 0:512]), [tb[s][1]], [bt_set[s][1]])

            pend = []
            for cb in range(20):
                ps, rps = sbank()
                for kc in range(8):
                    k.pe(A("matmul", ps[:, 0:ncol], wres3[:, kc, cb * 128:(cb + 1) * 128], hT3[:, kc, 0:ncol], start=(kc == 0), stop=(kc == 7)),
                         [rwres, self.rhcur], [rps])
                xcw = xc[cb % 4]
                fin = conv_block(ps, ncol - 2, cb, raw[cb % 3], xcw)
                if pend:
                    pend[-1][2]()
                if len(pend) == 2:
                    trans(pend[0][0], pend[0][1])
                    pend.pop(0)
                pend.append((cb, xcw, fin))
                if hook is not None and cb in hook:
                    hook[cb]()
            pend[-1][2]()
            for p_ in pend:
                trans(p_[0], p_[1])

        xo = I["x_oth"]

        def prep_dma(sb):
            for s in range(3):
                xt_, rxt = xts[s]
                k.dma(xt_[:], xo[(sb * 3 + s) * 128:(sb * 3 + s + 1) * 128, :], [], [rxt])
            k.dma(xh[:], I["x_oth_h"][sb * 6:sb * 6 + 6, :], [], [rxh])

        def prep_a(sb, dma=True):
            if dma:
                prep_dma(sb)
            k.norm_a_multi([(xts[s][0], xts[s][1], 128, tmps[s]) for s in range(3)] + [(xh, rxh, 6, tmps[3])])

        def prep_b(sb):
            hT_, rhT = hTo[sb % 2]
            hT3 = hT_[:].rearrange("p (kc c) -> p kc c", kc=8)
            for s in range(3):
                k.norm_b(128, (lambda kc, s=s: hT3[:, kc, s * 130 + 1:s * 130 + 129]), rhT, 0, 0, 0, tmps[s], bankfn=sbank)
            k.norm_b(6, (lambda kc: hT3[:, kc, :].rearrange("p (s t) -> p s t", t=130)[:, :, 0::129]), rhT, 0, 0, 0, tmps[3],
                     src_view=lambda a: a.rearrange("p (s t) -> p s t", t=2), bankfn=sbank)
            hv = self.flags[:, 116 + sb * 6:116 + sb * 6 + 6].rearrange("p (s t) -> p s t", t=2)
            for kc in range(8):
                hh = hT3[:, kc, :].rearrange("p (s t) -> p s t", t=130)[:, :, 0::129]
                k.dve(A("tensor_tensor", out=hh, in0=hh, in1=hv, op=ALU.mult), [rhT, self.rflags], [rhT])

        def nb_tile(sb, s):
            hT_, rhT = hTo[sb % 2]
            hT3 = hT_[:].rearrange("p (kc c) -> p kc c", kc=8)
            if s < 3:
                k.norm_b(128, (lambda kc, s=s: hT3[:, kc, s * 130 + 1:s * 130 + 129]), rhT, 0, 0, 0, tmps[s], bankfn=sbank,
                         dst3=(hT3[:, :, s * 130 + 1:s * 130 + 129] if s != 1 else None))
            else:
                k.norm_b(6, (lambda kc: hT3[:, kc, :].rearrange("p (s t) -> p s t", t=130)[:, :, 0::129]), rhT, 0, 0, 0, tmps[3],
                         src_view=lambda a: a.rearrange("p (s t) -> p s t", t=2), bankfn=sbank)
                hv = self.flags[:, 116 + sb * 6:116 + sb * 6 + 6].rearrange("p (s t) -> p s t", t=2)
                for kc in range(8):
                    hh = hT3[:, kc, :].rearrange("p (s t) -> p s t", t=130)[:, :, 0::129]
                    k.dve(A("tensor_tensor", out=hh, in0=hh, in1=hv, op=ALU.mult), [rhT, self.rflags], [rhT])

        def dt1(sb):
            hT_, rhT = hTo[sb % 2]
            hT3 = hT_[:].rearrange("p (kc c) -> p kc c", kc=8)
            old_ = getattr(self, "rhcur", None)
            self.rhcur = rhT
            dt_chain(3, hT3, 1, 130, 20 + sb * 6, split=True)
            self.rhcur = old_

        def dt2(sb):
            wtp = wts[sb % 2]
            psc, rpsc = dt_cs(3)
            for s in range(3):
                k.dve(A("tensor_tensor", out=logP[:], in0=psc[:, 192 + s * 64:192 + (s + 1) * 64], in1=logP[:], op=ALU.add), [rpsc, rlogP], [rlogP])
                k.dve(A("tensor_tensor", out=arg[:, s * 64:(s + 1) * 64], in0=logP[:], in1=psc[:, s * 64:(s + 1) * 64], op=ALU.subtract),
                      [rpsc, rlogP], [rarg])
            k.act(A("activation", out=arg[:], in_=arg[:], func=AF.Exp), [rarg], [rarg])
            k.dve(A("tensor_tensor", out=wtp[0][:], in0=arg[:], in1=dte[:], op=ALU.mult), [rarg, rdte], [wtp[1]])

        prep_a(0)
        for s in range(4):
            nb_tile(0, s)
        dt1(0)
        prep_a(1)
        prev = None
        for sb in range(NSB):
            hT_, rhT = hTo[sb % 2]
            hT3 = hT_[:].rearrange("p (kc c) -> p kc c", kc=8)
            self.mod_ct(8 + sb)
            xt_set = xtok[sb % 2]; bt_set = btok[sb % 2]
            def h1(sb=sb):
                dt2(sb)
                if sb + 2 < NSB:
                    prep_dma(sb + 2)
            hooks = {1: h1}
            if prev is not None:
                hooks[2] = (lambda p=prev: states_x(3, *p))
                hooks[6] = (lambda p=prev: states_mm(3, *p))
            if sb + 1 < NSB:
                for s in range(4):
                    hooks[8 + 2 * s] = (lambda sb=sb, s=s: nb_tile(sb + 1, s))
                hooks[15] = (lambda sb=sb: dt1(sb + 1))
            if sb + 2 < NSB:
                hooks[17] = (lambda sb=sb: prep_a(sb + 2, dma=False))
            self.rhcur = rhT
            proj_blocks(3, hT3, 390, lambda s: s * 130, xt_set, bt_set, hook=hooks)
            prev = (xt_set, bt_set, wts[sb % 2])
            if sb == 0:
                k.dump("hTo0", hT_[:], rhT, [128, 8 * 390], BF16)
                k.dump("xtok0", xt_set[0][0][:], xt_set[0][1], [128, 2048], BF16)
        states(3, *prev)

        k.dve(A("memset", hTc[:], 0.0), [], [rhTc])
        hc3 = hTc[:].rearrange("p (kc c) -> p kc c", kc=8)
        self.rhcur = rhTc
        for s in range(2):
            xt_, rxt = xts[s]
            k.dma(xt_[:], I["ctx"][s * 128:(s + 1) * 128, :], [], [rxt])
            k.norm_T(xt_, rxt, 128, (lambda kc, s=s: hc3[:, kc, 1 + s * 128:1 + (s + 1) * 128]), rhTc, 1, 0, 1, tmps[s])
        xt_set = xtok[0]; bt_set = btok[0]
        proj_blocks(2, hc3, 258, lambda s: s * 128, xt_set, bt_set)
        psc, rpsc = dt_chain(2, hc3, 1, 128, None)
        tot0 = psc[:, 192:256]; tot1 = psc[:, 256:320]
        k.dve(A("tensor_tensor", out=L1[:, 0:64], in0=tot1, in1=logP[:], op=ALU.add), [rpsc, rlogP], [rL1])
        k.dve(A("tensor_tensor", out=L1[:, 64:128], in0=tot0, in1=logP[:], op=ALU.add), [rpsc, rlogP], [rL1])
        k.dve(A("tensor_tensor", out=L1[:, 128:192], in0=tot0, in1=L1[:, 0:64], op=ALU.add), [rpsc, rL1], [rL1])
        for (s, d, src) in ((1, 0, 0), (0, 0, 128), (0, 1, 64), (1, 1, 128)):
            k.dve(A("tensor_tensor", out=arg[:, s * 64 + d * 32:s * 64 + d * 32 + 32], in0=L1[:, src + d * 32:src + d * 32 + 32],
                    in1=psc[:, s * 64 + d * 32:s * 64 + d * 32 + 32], op=ALU.subtract), [rpsc, rL1], [rarg])
        k.act(A("activation", out=arg[:, 0:128], in_=arg[:, 0:128], func=AF.Exp), [rarg], [rarg])
        k.dve(A("tensor_tensor", out=wt_[:, 0:128], in0=arg[:, 0:128], in1=dte[:, 0:128], op=ALU.mult), [rarg, rdte], [rwt])
        states(2, xt_set, bt_set)
        for g in range(4):
            ps, rps = sbank()
            for kc in range(8):
                k.pe(A("matmul", ps[0:64, 0:256], wkv3[:, kc, g * 64:(g + 1) * 64], hc3[:, kc, 1:257], start=(kc == 0), stop=(kc == 7)),
                     [rwkv, rhTc], [rps])
            k.act(A("activation", out=self.kcT[:, g * 256:(g + 1) * 256], in_=ps[0:64, 0:256], func=AF.Copy), [rps], [self.rkcT])
        va = self.vca[:].rearrange("p (s g c) -> p s g c", s=2, g=4)
        for s in range(2):
            ps, rps = sbank()
            for kc in range(8):
                k.pe(A("matmul", ps[:, 0:256], hc3[:, kc, 1 + s * 128:1 + (s + 1) * 128], wkv3[:, kc, 256:512], start=(kc == 0), stop=(kc == 7)),
                     [rwkv, rhTc], [rps])
            k.act(A("activation", out=va[:, s, :, 0:64], in_=ps[:, 0:256].rearrange("p (g c) -> p g c", c=64), func=AF.Copy), [rps], [self.rvca])
        m3 = self.modT[:].rearrange("p (b v) -> p b v", v=2)
        g3 = self.gs[:].rearrange("p (kc v) -> p kc v", v=4)
        k.dve(A("scalar_tensor_tensor", out=g3[:, :, 2], in0=m3[:, 32:40, 0], scalar=1.0, in1=self.n12[:, 8:16],
                op0=ALU.add, op1=ALU.mult), [self.rmodT2, self.rn12], [self.rgs])
        k.dump("acc", self.acc[:], self.racc, [128, 4096])
        self.S.flush()
        es.close()


def _rep(v, n=128):
    v = np.asarray(v, np.float32).reshape(1, -1)
    return np.ascontiguousarray(np.broadcast_to(v, (n, v.shape[1])))


def _fm(v, nblk):
    return np.ascontiguousarray(np.asarray(v, np.float32).reshape(nblk, 128).T)


def prep_inputs(inp):
    x = np.asarray(inp["x"], np.float32)
    ctx = np.asarray(inp["ctx"], np.float32)
    w_in = np.ascontiguousarray(np.asarray(inp["w_in"], np.float32)[0])
    d = np.arange(64)
    within = d % 32
    partner = np.where(within < 16, d + 16, d - 16)
    qk = np.concatenate([Q0 + h * 64 + partner for h in range(16)] + [K0 + h * 64 + partner for h in range(4)])
    w_qkp = np.ascontiguousarray(w_in[:, qk])
    consts = np.zeros((128, 3, 128), np.float32)
    ii = np.arange(128)
    consts[:, 0] = (ii[:, None] == ii[None, :])
    consts[:, 1] = (ii[:, None] <= ii[None, :])
    consts[:, 2] = (ii[:, None] >= ii[None, :])
    inv = (10000.0 ** (-np.arange(0, 32, 2, dtype=np.float32) / 32)).astype(np.float32)
    j = within % 16
    sgn = np.where(within < 16, -1.0, 1.0).astype(np.float32)
    shared = dict(
        w_mod=np.ascontiguousarray(np.asarray(inp["w_mod"], np.float32)[0]),
        bmodT=_fm(inp["b_mod"][0], 48),
        n1g=_fm(inp["norm1_g"][0], 8), n2g=_fm(inp["norm2_g"][0], 8), fg_row=_rep(inp["final_g"]),
        w_in=w_in, w_qkp=w_qkp,
        ssd_cw=np.ascontiguousarray(np.asarray(inp["ssd_conv_w"], np.float32)[0].reshape(3, 24, 128).transpose(2, 1, 0).reshape(128, 72)),
        ssd_cb=_fm(inp["ssd_conv_b"][0], 24),
        dtb=_rep(np.asarray(inp["ssd_dt_bias"])[0].reshape(-1)), alog=_rep(np.asarray(inp["ssd_a_log"])[0].reshape(-1)),
        dsk=_rep(inp["ssd_d"][0]), sng=_fm(inp["ssd_norm_g"][0], 16),
        w_o_ssd=np.ascontiguousarray(np.asarray(inp["w_o_ssd"], np.float32)[0]),
        w_o_att=np.ascontiguousarray(np.asarray(inp["w_o_att"], np.float32)[0]),
        sink=_rep(inp["att_sink"][0]),
        w_out=np.ascontiguousarray(np.asarray(inp["w_out"], np.float32)[0]),
        w_up=np.ascontiguousarray(np.asarray(inp["w_up"], np.float32)[0]),
        ffn_cw=np.ascontiguousarray(np.asarray(inp["ffn_conv_w"], np.float32)[0].reshape(3, 44, 128).transpose(2, 1, 0).reshape(128, 132)),
        ffn_cb=_fm(inp["ffn_conv_b"][0], 44),
        w_down=np.ascontiguousarray(np.asarray(inp["w_down"], np.float32)[0]),
        consts=consts.reshape(128, 384),
    )
    maps = []
    zero_row = np.zeros((1, D), np.float32)
    for core in range(8):
        b, q = core // 4, core % 4
        xb = x[b]
        t0 = q * TOK - 256
        x_own = np.zeros((NOWN * 128, D), np.float32)
        lo, hi = max(t0, 0), min(t0 + NOWN * 128, L)
        x_own[lo - t0:hi - t0] = xb[lo:hi]
        flags = np.zeros((NFLAG,), np.float32)
        for t in range(NOWN):
            flags[t] = 1.0 if 0 <= t0 + t * 128 < L else 0.0
        before = list(range(16 * q - 2, -1, -1))
        after = list(range(16 * q + 17, 64))
        slots = [(c, 0) for c in before] + [(c, 1) for c in after]
        assert len(slots) <= NSLOT
        x_oth = np.zeros((NSLOT * 128, D), np.float32)
        x_oth_h = np.zeros((2 * NSLOT, D), np.float32)
        for s, (c, side) in enumerate(slots):
            x_oth[s * 128:(s + 1) * 128] = xb[c * 128:(c + 1) * 128]
            flags[20 + 2 * s + side] = 1.0
            if c * 128 - 1 >= 0:
                x_oth_h[2 * s] = xb[c * 128 - 1]; flags[116 + 2 * s] = 1.0
            if c * 128 + 128 < L:
                x_oth_h[2 * s + 1] = xb[c * 128 + 128]; flags[116 + 2 * s + 1] = 1.0
        cv = np.zeros((128, 16), np.float32)
        cv[:, 0::2] = np.asarray(inp["c"], np.float32)[b].reshape(8, 128).T
        cv[:, 1::2] = np.asarray(inp["c_ctx"], np.float32).reshape(8, 128).T
        tpos = t0 + np.arange(NOWN * 128)
        pos = np.where((d < 32)[:, None], (tpos // 64)[None, :], (tpos % 64)[None, :]).astype(np.float32)
        ang = (pos * inv[j][:, None]).astype(np.float32)
        m = dict(shared)
        m.update(x_own=x_own, x_oth=x_oth, x_oth_h=x_oth_h, ctx=np.ascontiguousarray(ctx[b]), cvec=cv, flags=_rep(flags),
                 cosT=np.cos(ang).astype(np.float32), sinT=(np.sin(ang) * sgn[:, None]).astype(np.float32))
        maps.append(m)
    return maps


_NC_CACHE = {}


def kernel(**inputs):
    kb = K()
    nc = kb.build()
    maps = prep_inputs(inputs)
    cores = [int(c) for c in os.environ.get("KCORES", "0,1,2,3,4,5,6,7").split(",")]
    res = run_bass_kernel_spmd(nc, [maps[c] for c in cores], core_ids=list(range(len(cores))))
    out = np.zeros((2, L, D), np.float32)
    for i, core in enumerate(cores):
        b, q = core // 4, core % 4
        out[b, q * TOK:(q + 1) * TOK] = res.results[i]["out"]
    if DEBUG:
        kernel.dbg = {core: {n: res.results[i][v] for n, v in kb.dbg.items()} for i, core in enumerate(cores)}
    return out


def _p2(self):
    pass


def conv3(k, ps, rps, n, w3, bcol, rw, out_ap, rout, func, wregs, defer=False):
    (r_, rr_) = rw
    k.act(A("activation", out=r_[:, 0:n], in_=ps[:, 1:n + 1], func=AF.Identity, scale=w3[:, 1:2], bias=bcol), [rps] + wregs, [rr_])
    k.dve(A("scalar_tensor_tensor", out=r_[:, 0:n], in0=ps[:, 0:n], scalar=w3[:, 0:1], in1=r_[:, 0:n], op0=ALU.mult, op1=ALU.add),
          [rps, rr_] + wregs, [rr_])
    k.dve(A("scalar_tensor_tensor", out=r_[:, 0:n], in0=ps[:, 2:n + 2], scalar=w3[:, 2:3], in1=r_[:, 0:n], op0=ALU.mult, op1=ALU.add),
          [rps, rr_] + wregs, [rr_])
    fin = lambda: k.act(A("activation", out=out_ap, in_=r_[:, 0:n], func=func), [rr_], [rout])
    if defer:
        return fin
    fin()


def phase_own_h(self):
    k = self; I = self.I; es = self.arW
    self.hT, self.rhT = k.T("hT", [128, 8 * 2560], BF16, es=self.arHT)
    hT3 = self.hT[:].rearrange("p (kc c) -> p kc c", kc=8)
    self.hT3 = hT3
    xts = [k.T("xo%d" % i, [128, D], es=es) for i in range(8)]
    junk_ = k.T("junk", [128, D], BF16, es=es)
    tmps = [(junk_, k.T("ss%d" % i, [128, 4], es=es), k.T("xn%d" % i, [128, D], BF16, es=es)) for i in range(8)]

    def ld(gp):
        for t in range(gp * 4, gp * 4 + 4):
            k.dma(xts[t % 8][0][:], I["x_own"][t * 128:(t + 1) * 128, :], [], [xts[t % 8][1]])

    def st_a(gp):
        k.norm_a_multi([(xts[t % 8][0], xts[t % 8][1], 128, tmps[t % 8]) for t in range(gp * 4, gp * 4 + 4)])

    def st_b(t):
        k.norm_b(128, (lambda kc, t=t: hT3[:, kc, t * 128:(t + 1) * 128]), self.rhT, 0, 0, 0, tmps[t % 8])
        if t in (0, 1, 18, 19):
            for kc in range(8):
                k.dve(A("tensor_scalar", out=hT3[:, kc, t * 128:(t + 1) * 128], in0=hT3[:, kc, t * 128:(t + 1) * 128],
                        scalar1=self.flags[:, t:t + 1], scalar2=None, op0=ALU.mult), [self.rhT, self.rflags], [self.rhT])

    ld(0); ld(1)
    st_a(0)
    for gp in range(5):
        if gp + 1 < 5:
            st_a(gp + 1)
        if gp + 2 < 5:
            ld(gp + 2)
        for t in range(gp * 4, gp * 4 + 4):
            st_b(t)
    self.S.flush()
    es.close()


def phase_ssd(self):
    k = self; I = self.I; es = self.arW
    hT3 = self.hT3
    self.yz, _ = k.T("yz", [128, 18 * 2048], BF16, es=self.arYZ)
    self.ry = [Reg("y%d" % c) for c in range(18)]
    yz3 = self.yz[:].rearrange("p (c f) -> p c f", c=18)
    stage = k.T("stg", [128, 8 * 128], es=es)
    wbf = [k.T("wbf%d" % i, [128, 8 * 128], BF16, es=es) for i in range(2)]
    wdt, rwdt = k.T("wdt", [128, 8 * 64], BF16, es=es)
    wdt3 = wdt[:].rearrange("p (kc c) -> p kc c", kc=8)
    wdg, rwdg = k.T("wdg", [128, 8 * 16], BF16, es=es)
    wdg3 = wdg[:].rearrange("p (kc c) -> p kc c", kc=8)
    dbg_, rdbg = k.T("dbg_", [128, 16], es=es)
    nag_, rnag = k.T("nag_", [128, 16], es=es)
    xcg, rxcg = k.T("xcg", [128, 2304], BF16, es=es)
    raw = [k.T("raw%d" % i, [128, 512], es=es) for i in range(2)]
    xtok, _ = k.T("xtok", [128, 18 * 512], BF16, es=es)
    xtok3 = xtok[:].rearrange("p (c f) -> p c f", c=18)
    rxt = Reg("xtok")
    btok, rbt = k.T("btok", [128, 18 * 128], BF16, es=es)
    btok3 = btok[:].rearrange("p (c f) -> p c f", c=18)
    BT, rBT = k.T("BT", [128, 2304], BF16, es=es)
    CT, rCT = k.T("CT", [128, 2304], BF16, es=es)
    dte, rdte = k.T("dte", [128, 288], es=es)
    la, rla = k.T("la", [128, 288], es=es)
    dtr, rdtr = la, rla
    csx, rcsx = k.T("csx", [128, 288], es=es)
    wgt, rwgt = k.T("wgt", [128, 288], es=es)
    ecs, recs = k.T("ecs", [128, 288], es=es)
    dtot, rdtot = k.T("dtot", [128, 288], es=es)
    St, rSt = k.T("St", [128, 512], es=es)
    Sbf, rSbf = k.T("Sbf", [128, 512], BF16, es=es)
    xpd = [k.T("xpd%d" % i, [128, 512], BF16, es=es) for i in range(2)]
    CB, rCB = k.T("CB", [128, 128], es=es)
    Em = [k.T("Em%d" % i, [128, 512], BF16, es=es) for i in range(2)]
    ncs, rncs = k.T("ncs", [128, 288], es=es)
    Mt, rMt = k.T("Mt", [128, 8 * 128], BF16, es=es)
    tt, rtt = k.T("tt", [128, 512], es=es)
    uu, ruu = k.T("uu", [128, 512], BF16, es=es)
    idf = self.cst[:, 0:128]; triU = self.cst[:, 128:256]; triL = self.cst[:, 256:384]
    idb = self.cstb[:, 0:128]
    cw3 = self.scw[:].rearrange("p (b t) -> p b t", t=3)
    rot = [0]

    def rbank():
        i = rot[0]; rot[0] = (i + 1) % 2
        return self.PS[i], self.PR[i]

    k.load_w(stage, (wdt3, rwdt), I["w_in"], DT0, 64)
    tiles = [(128 + 510 * t, min(510, 2432 - (128 + 510 * t))) for t in range(5)]
    flg = self.flags[:, 1:19].unsqueeze(2).to_broadcast([128, 18, 16])

    def v3(ap):
        return ap.rearrange("p (c j) -> p c j", j=16)

    for g in range(4):
        blocks = [(XBC0 + (g * 4 + i) * 128, g * 4 + i, "x", i) for i in range(4)]
        blocks += [(XBC0 + 2048 + g * 128, 16 + g, "B", 0), (XBC0 + 2560 + g * 128, 20 + g, "C", 0)]
        for bi, (c0, cb, kind, i) in enumerate(blocks):
            wb, rwb = wbf[bi % 2]
            wb3 = wb[:].rearrange("p (kc c) -> p kc c", kc=8)
            k.load_w(stage, (wb3, rwb), I["w_in"], c0, 128)
            dst, rdst = {"x": (xcg, rxcg), "B": (BT, rBT), "C": (CT, rCT)}[kind]
            pendf = None
            for ti, (o0, n) in enumerate(tiles):
                ps, rps = rbank()
                for kc in range(8):
                    k.pe(A("matmul", ps[:, 0:n + 2], wb3[:, kc, :], hT3[:, kc, o0 - 1:o0 + n + 1], start=(kc == 0), stop=(kc == 7)),
                         [rwb, self.rhT], [rps])
                f_ = conv3(k, ps, rps, n, cw3[:, cb, :], self.scb[:, cb:cb + 1], raw[ti % 2], dst[:, o0 - 128:o0 - 128 + n], rdst, AF.Silu,
                           [self.rscw, self.rscb], defer=True)
                if pendf is not None:
                    pendf()
                pendf = f_
            pendf()
            if kind in ("x", "B"):
                for c8 in range(0, 18, 8):
                    nb = min(8, 18 - c8)
                    tb, rtb = self.PS[2 + (c8 // 8) % 2], self.PR[2 + (c8 // 8) % 2]
                    tbb = tb[:].bitcast(BF16)
                    for j in range(nb):
                        c = c8 + j
                        k.pe(A("transpose", tbb[:, j * 128:(j + 1) * 128], dst[:, c * 128:(c + 1) * 128], idb), [rdst, self.rcstb], [rtb])
                    if kind == "x":
                        k.act(A("activation", out=xtok3[:, c8:c8 + nb, i * 128:(i + 1) * 128],
                                in_=tbb[:, 0:nb * 128].rearrange("p (c f) -> p c f", f=128), func=AF.Copy), [rtb], [rxt])
                    else:
                        k.act(A("activation", out=btok3[:, c8:c8 + nb, :],
                                in_=tbb[:, 0:nb * 128].rearrange("p (c f) -> p c f", f=128), func=AF.Copy), [rtb], [rbt])
        if os.environ.get("KSSD") == "1":
            break
        for d in range(2):
            k.pool(A("tensor_copy", out=wdg3[:, :, d * 8:(d + 1) * 8], in_=wdt3[:, :, d * 32 + g * 8:d * 32 + g * 8 + 8]), [rwdt], [rwdg])
            k.dve(A("tensor_copy", out=dbg_[:, d * 8:(d + 1) * 8], in_=self.dtb[:, d * 32 + g * 8:d * 32 + g * 8 + 8]), [self.rdtb], [rdbg])
            k.dve(A("tensor_copy", out=nag_[:, d * 8:(d + 1) * 8], in_=self.negA[:, d * 32 + g * 8:d * 32 + g * 8 + 8]), [self.rnegA], [rnag])
        psd, rpsd = rbank()
        for c in range(18):
            for kc in range(8):
                k.pe(A("matmul", psd[:, c * 16:(c + 1) * 16], hT3[:, kc, (c + 1) * 128:(c + 2) * 128], wdg3[:, kc, :],
                       start=(kc == 0), stop=(kc == 7)), [rwdg, self.rhT], [rpsd])
        k.dve(A("tensor_tensor", out=v3(dtr[:]), in0=v3(psd[:, 0:288]), in1=dbg_[:].unsqueeze(1).to_broadcast([128, 18, 16]), op=ALU.add),
              [rpsd, rdbg], [rdtr])
        k.act(A("activation", out=dtr[:], in_=dtr[:], func=AF.Exp), [rdtr], [rdtr])
        k.act(A("activation", out=dte[:], in_=dtr[:], func=AF.Ln, bias=1.0), [rdtr], [rdte])
        for c in (0, 17):
            k.dve(A("tensor_scalar", out=dte[:, c * 16:(c + 1) * 16], in0=dte[:, c * 16:(c + 1) * 16], scalar1=self.flags[:, 1 + c:2 + c],
                    scalar2=None, op0=ALU.mult), [rdte, self.rflags], [rdte])
        if os.environ.get("KSSD") == "2a":
            break
        k.dve(A("tensor_tensor", out=v3(la[:]), in0=v3(dte[:]), in1=nag_[:].unsqueeze(1).to_broadcast([128, 18, 16]), op=ALU.mult),
              [rdte, rnag], [rla])
        psU, rpsU = rbank()
        k.pe(A("matmul", psU[:, 0:288], triU, la[:], start=True, stop=True), [rla, self.rcst], [rpsU])
        psL, rpsL = self.PS[2], self.PR[2]
        k.pe(A("matmul", psL[:, 0:288], triL, la[:], start=True, stop=True), [rla, self.rcst], [rpsL])
        k.act(A("activation", out=v3(csx[:])[:, :, 0:8], in_=v3(psU[:, 0:288])[:, :, 0:8], func=AF.Copy), [rpsU], [rcsx])
        k.act(A("activation", out=v3(csx[:])[:, :, 8:16], in_=v3(psL[:, 0:288])[:, :, 8:16], func=AF.Copy), [rpsL], [rcsx])
        if os.environ.get("KSSD") == "2b":
            break
        pst, rpst = rbank()
        k.pe(A("matmul", pst[:, 0:288], self.ones[:], la[:], start=True, stop=True), [rla, self.rones], [rpst])
        if os.environ.get("KSSD") == "2c":
            break
        k.act(A("activation", out=ecs[:], in_=csx[:], func=AF.Exp), [rcsx], [recs])
        if os.environ.get("KSSD") == "2d":
            break
        k.act(A("activation", out=dtot[:], in_=pst[:, 0:288], func=AF.Exp), [rpst], [rdtot])
        if os.environ.get("KSSD") == "2e":
            break
        k.dve(A("tensor_tensor", out=wgt[:], in0=pst[:, 0:288], in1=csx[:], op=ALU.subtract), [rpst, rcsx], [rwgt])
        k.act(A("activation", out=wgt[:], in_=wgt[:], func=AF.Exp), [rwgt], [rwgt])
        k.dve(A("tensor_tensor", out=wgt[:], in0=wgt[:], in1=dte[:], op=ALU.mult), [rwgt, rdte], [rwgt])
        if g == 0:
            k.dump("dte", dte[:], rdte, [128, 288]); k.dump("csx", csx[:], rcsx, [128, 288])
            k.dump("xtokg", xtok[:], rxt, [128, 18 * 512], BF16); k.dump("BTg", BT[:], rBT, [128, 2304], BF16)
        if os.environ.get("KSSD") == "2":
            break
        k.dve(A("tensor_scalar", out=ncs[:], in0=dte[:], scalar1=1e-30, scalar2=None, op0=ALU.max), [rdte], [rncs])
        k.act(A("activation", out=ncs[:], in_=ncs[:], func=AF.Ln), [rncs], [rncs])
        k.dve(A("tensor_tensor", out=ncs[:], in0=ncs[:], in1=csx[:], op=ALU.subtract), [rncs, rcsx], [rncs])
        h3 = lambda ap: ap.rearrange("p (h c) -> p h c", c=64)
        for d in (1, 0):
            a0 = d * 2048 + g * 512
            k.dve(A("tensor_copy", out=St[:], in_=self.acc[:, a0:a0 + 512]), [self.racc], [rSt])
            mask = triL if d == 1 else triU
            order = range(17, -1, -1) if d == 1 else range(18)
            order = list(order)

            def stA(c):
                cs_ = slice(c * 128, (c + 1) * 128)
                q8 = slice(c * 16 + d * 8, c * 16 + d * 8 + 8)
                xp_, rxp = xpd[c % 2]
                k.pool(A("tensor_tensor", out=h3(xp_[:]), in0=h3(xtok3[:, c, :]), in1=wgt[:, q8].unsqueeze(2).to_broadcast([128, 8, 64]), op=ALU.mult),
                       [rxt, rwgt], [rxp])
                for r4 in range(0, 8, 4):
                    pb, rpb = self.PS[2 + (r4 // 4)], self.PR[2 + (r4 // 4)]
                    for j in range(4):
                        col = c * 16 + d * 8 + r4 + j
                        k.pe(A("matmul", pb[:, j * 128:(j + 1) * 128], csx[:, col:col + 1].to_broadcast([128, 128]), idf, start=True, stop=False),
                             [rcsx, self.rcst], [rpb])
                        k.pe(A("matmul", pb[:, j * 128:(j + 1) * 128], idf, ncs[:, col:col + 1].to_broadcast([128, 128]), start=False, stop=True),
                             [rncs, self.rcst], [rpb])
                pss, rpss = self.PS[6 + (c % 2)], self.PR[6 + (c % 2)]
                k.pe(A("matmul", pss[:, 0:512], btok3[:, c, :], xp_[:], start=True, stop=True), [rbt, rxp], [rpss])
                pcb, rpcb = rbank()
                k.pe(A("matmul", pcb[:, 0:128], BT[:, cs_], CT[:, cs_], start=True, stop=True), [rBT, rCT], [rpcb])
                return pcb, rpcb

            def stB(c, pcb, rpcb):
                k.dve(A("tensor_tensor", out=CB[:], in0=pcb[:, 0:128], in1=mask, op=ALU.mult), [rpcb, self.rcst], [rCB])
                for ri, r4 in enumerate(range(0, 8, 4)):
                    pb, rpb = self.PS[2 + ri], self.PR[2 + ri]
                    em, rem = Em[ri]
                    k.act(A("activation", out=em[:], in_=pb[:, 0:512], func=AF.Exp), [rpb], [rem])
                    k.dve(A("scalar_tensor_tensor", out=Mt[:, r4 * 128:(r4 + 4) * 128].rearrange("p (h c) -> p h c", c=128),
                            in0=em[:].rearrange("p (h c) -> p h c", c=128), scalar=100.0, in1=CB[:].unsqueeze(1).to_broadcast([128, 4, 128]),
                            op0=ALU.min, op1=ALU.mult), [rem, rCB], [rMt])

            def stC(c):
                cs_ = slice(c * 128, (c + 1) * 128)
                q8 = slice(c * 16 + d * 8, c * 16 + d * 8 + 8)
                pss, rpss = self.PS[6 + (c % 2)], self.PR[6 + (c % 2)]
                py, rpy = self.PS[4], self.PR[4]
                for r in range(8):
                    k.pe(A("matmul", py[:, r * 64:(r + 1) * 64], Mt[:, r * 128:(r + 1) * 128], xtok3[:, c, r * 64:(r + 1) * 64],
                           start=True, stop=True), [rMt, rxt], [rpy])
                k.act(A("activation", out=Sbf[:], in_=St[:], func=AF.Copy), [rSt], [rSbf])
                po, rpo = self.PS[5], self.PR[5]
                k.pe(A("matmul", po[:, 0:512], CT[:, cs_], Sbf[:], start=True, stop=True), [rCT, rSbf], [rpo])
                t8 = dtot[:, q8].unsqueeze(2).to_broadcast([128, 8, 64])
                k.pool(A("tensor_tensor", out=h3(St[:]), in0=h3(St[:]), in1=t8, op=ALU.mult), [rSt, rdtot, rSbf], [rSt])
                k.dve(A("tensor_tensor", out=St[:], in0=St[:], in1=pss[:, 0:512], op=ALU.add), [rSt, rpss], [rSt])
                e8 = ecs[:, q8].unsqueeze(2).to_broadcast([128, 8, 64])
                k.dve(A("tensor_tensor", out=h3(tt[:]), in0=h3(po[:, 0:512]), in1=e8, op=ALU.mult), [rpo, recs], [rtt])
                ydst = yz3[:, c, g * 512:(g + 1) * 512]
                if d == 1:
                    k.dve(A("tensor_tensor", out=ydst, in0=tt[:], in1=py[:, 0:512], op=ALU.add), [rtt, rpy], [self.ry[c]])
                else:
                    d8 = self.dsk[:, g * 8:(g + 1) * 8].unsqueeze(2).to_broadcast([128, 8, 64])
                    k.pool(A("tensor_tensor", out=h3(uu[:]), in0=h3(xtok3[:, c, :]), in1=d8, op=ALU.mult), [rxt, self.rdsk], [ruu])
                    k.pool(A("tensor_tensor", out=uu[:], in0=uu[:], in1=ydst, op=ALU.add), [ruu, self.ry[c]], [ruu])
                    k.dve(A("tensor_tensor", out=tt[:], in0=tt[:], in1=py[:, 0:512], op=ALU.add), [rtt, rpy], [rtt])
                    k.dve(A("tensor_tensor", out=ydst, in0=tt[:], in1=uu[:], op=ALU.add), [rtt, ruu], [self.ry[c]])

            pc = stA(order[0])
            stB(order[0], *pc)
            for ci, c in enumerate(order):
                nxt = order[ci + 1] if ci + 1 < len(order) else None
                if nxt is not None:
                    pc = stA(nxt)
                stC(c)
                if nxt is not None:
                    stB(nxt, *pc)
    k.dump("yz", self.yz[:], self.ry, [128, 18 * 2048], BF16)
    self.S.flush()
    es.close()


K.phase_own_h = phase_own_h
K.phase_ssd = phase_ssd


def row_bcast(k, dst, rdst, colsT, rcols, nblk):
    for b0 in range(0, nblk, 4):
        ps, rps = k.bank()
        for b in range(b0, min(b0 + 4, nblk)):
            k.pe(A("transpose", ps[:, (b - b0) * 128:(b - b0 + 1) * 128], colsT(b).to_broadcast([128, 128]), k.cst[:, 0:128]),
                 [rcols, k.rcst], [rps])
        n = min(4, nblk - b0)
        k.act(A("activation", out=dst[:, b0 * 128:(b0 + n) * 128], in_=ps[:, 0:n * 128], func=AF.Copy), [rps], [rdst])


def phase_ssd_proj(self):
    k = self; I = self.I
    hT3 = self.hT3
    arW = self.arW; arA = self.arACC
    arA.close()
    self.mg, self.rmg = k.T("merged", [128, 8 * 2304], BF16, es=arW)
    mg3 = self.mg[:].rearrange("p (kc t) -> p kc t", kc=8)
    self.mg3 = mg3
    SUP = [(0, 4), (4, 4), (8, 4), (12, 4), (16, 2)]
    ryT = [Reg("yT%d" % i) for i in range(5)]
    tmpT, rtmpT = k.T("tmpT", [128, 16 * 512], BF16, es=arW)
    tmpT3 = tmpT[:].rearrange("p (j t) -> p j t", j=16)
    stage = k.T("stg", [128, 16 * 128], es=arW)
    wbf = [k.T("wbf%d" % i, [128, 8 * 128], BF16, es=arW) for i in range(2)]
    wpb, rwpb = k.T("wpb", [128, 16 * 128], BF16, es=arW)
    rstd = [k.T("rstd%d" % i, [128, 512], es=arA) for i in range(5)]
    szb = [k.T("szb%d" % i, [128, 512], BF16, es=arA) for i in range(2)]
    sqb = [k.T("sqb%d" % i, [128, 512], BF16, es=arA) for i in range(2)]
    sg, rsg = k.T("sg", [128, 512], es=arA)
    idb = self.cstb[:, 0:128]
    onesb = self.cstb[:, 128:256]
    ob, rob = k.T("onesb", [128, 128], BF16)
    k.dve(A("tensor_copy", out=ob[:], in_=self.ones[:]), [self.rones], [rob])
    yzf = self.yz

    def yT(st):
        n = SUP[st][1] * 128
        return yzf[:, st * 8192:st * 8192 + 16 * n].rearrange("p (j t) -> p j t", j=16), n

    yz3 = self.yz[:].rearrange("p (c f) -> p c f", c=18)
    for st, (c0, nc_) in enumerate(SUP):
        n = nc_ * 128
        for ci in range(nc_):
            for j8 in range(2):
                tb, rtb = k.bank()
                tbb = tb[:].bitcast(BF16)
                for j in range(8):
                    k.pe(A("transpose", tbb[:, j * 128:(j + 1) * 128], yz3[:, c0 + ci, (j8 * 8 + j) * 128:(j8 * 8 + j + 1) * 128], idb),
                         [self.ry[c0 + ci], self.rcstb], [rtb])
                k.act(A("activation", out=tmpT3[:, j8 * 8:j8 * 8 + 8, ci * 128:(ci + 1) * 128],
                        in_=tbb[:, 0:1024].rearrange("p (j t) -> p j t", j=8), func=AF.Copy), [rtb], [rtmpT])
        v, _ = yT(st)
        k.dve(A("tensor_copy", out=v[:, 0:8, :], in_=tmpT3[:, 0:8, 0:n]), [rtmpT] + [self.ry[c0 + ci] for ci in range(nc_)], [ryT[st]] + [self.ry[c0 + ci] for ci in range(nc_)])
        k.act(A("activation", out=v[:, 8:16, :], in_=tmpT3[:, 8:16, 0:n], func=AF.Copy), [rtmpT] + [self.ry[c0 + ci] for ci in range(nc_)], [ryT[st]] + [self.ry[c0 + ci] for ci in range(nc_)])
    accb = [(self.PS[3 + st], self.PR[3 + st]) for st in range(5)]
    rot = [0]

    def rbank():
        i = rot[0]; rot[0] = (i + 1) % 3
        return self.PS[i], self.PR[i]

    pend = None
    u = 0
    for j in range(16):
        wb, rwb = wbf[j % 2]
        wb3 = wb[:].rearrange("p (kc c) -> p kc c", kc=8)
        k.load_w((stage[0], stage[1]), (wb3, rwb), I["w_in"], j * 128, 128)
        for st, (c0, nc_) in enumerate(SUP):
            n = nc_ * 128
            t0 = (c0 + 1) * 128
            ps, rps = rbank()
            for kc in range(8):
                k.pe(A("matmul", ps[:, 0:n], wb3[:, kc, :], hT3[:, kc, t0:t0 + n], start=(kc == 0), stop=(kc == 7)), [rwb, self.rhT], [rps])
            if pend is not None:
                pend()
            sz_, rsz = szb[u % 2]; sq_, rsq = sqb[u % 2]
            u += 1
            k.act(A("activation", out=sz_[:, 0:n], in_=ps[:, 0:n], func=AF.Silu), [rps], [rsz])
            v, _ = yT(st)
            k.dve(A("tensor_tensor", out=v[:, j, :], in0=v[:, j, :], in1=sz_[:, 0:n], op=ALU.mult), [ryT[st], rsz], [ryT[st]])
            k.act(A("activation", out=sq_[:, 0:n], in_=v[:, j, :], func=AF.Square), [ryT[st]], [rsq])
            pend = (lambda st=st, n=n, sq_=sq_, rsq=rsq, j=j: k.pe(A("matmul", accb[st][0][:, 0:n], ob[:], sq_[:, 0:n], start=(j == 0), stop=(j == 15)),
                                                                   [rob, rsq], [accb[st][1]]))
    pend()
    for st, (c0, nc_) in enumerate(SUP):
        n = nc_ * 128
        r_, rr_ = rstd[st]
        k.dve(A("tensor_scalar", out=r_[:, 0:n], in0=accb[st][0][:, 0:n], scalar1=1.0 / 2048, scalar2=EPS, op0=ALU.mult, op1=ALU.add),
              [accb[st][1]], [rr_])
        k.act(A("activation", out=r_[:, 0:n], in_=r_[:, 0:n], func=AF.Sqrt), [rr_], [rr_])
        k.dve(A("reciprocal", out=r_[:, 0:n], in_=r_[:, 0:n]), [rr_], [rr_])
    ar2 = Arena(self.arW.lo + 8 * 2304 * 2, self.arW.lo + 8 * 2304 * 2 + 16 * 512 * 2)
    stgA = [stage, k.T("stgA1", [128, 16 * 128], es=ar2)]
    stgB = [k.T("stgB0", [128, 8 * 128], es=ar2)]
    wpbs = [(wpb, rwpb), k.T("wpb1", [128, 16 * 128], BF16, es=ar2)]
    first = [True]

    def load_cb(cbk):
        sA, rsA = stgA[cbk % 2]
        sA3 = sA[:].rearrange("p (j c) -> p j c", j=16)
        wp_, rwp_ = wpbs[cbk % 2]
        wp3_ = wp_[:].rearrange("p (j c) -> p j c", j=16)
        extra = [rtmpT] if cbk <= 1 else []
        k.dma(sA3, I["w_o_ssd"][:, cbk * 128:(cbk + 1) * 128].rearrange("(j p) c -> p j c", p=128), [], [rsA] + extra)
        k.dve(A("tensor_tensor", out=wp3_, in0=sA3, in1=self.sng[:].unsqueeze(2).to_broadcast([128, 16, 128]), op=ALU.mult),
              [rsA, self.rsng], [rwp_] + extra)
        wb, rwb = wbf[cbk % 2]
        wb3 = wb[:].rearrange("p (kc c) -> p kc c", kc=8)
        sB, rsB = stgB[0]
        sB3 = sB[:].rearrange("p (kc c) -> p kc c", kc=8)
        k.dma(sB3, I["w_in"][:, G0 + cbk * 128:G0 + (cbk + 1) * 128].rearrange("(kc p) c -> p kc c", p=128), [], [rsB] + ([rtmpT] if first[0] else []))
        k.pool(A("tensor_copy", out=wb3, in_=sB3), [rsB], [rwb])
        first[0] = False

    load_cb(0)
    for cbk in range(8):
        if cbk + 1 < 8:
            load_cb(cbk + 1)
        wp3 = wpbs[cbk % 2][0][:].rearrange("p (j c) -> p j c", j=16)
        rwpb = wpbs[cbk % 2][1]
        wb, rwb = wbf[cbk % 2]
        wb3 = wb[:].rearrange("p (kc c) -> p kc c", kc=8)
        for st, (c0, nc_) in enumerate(SUP):
            n = nc_ * 128
            t0 = (c0 + 1) * 128
            v, _ = yT(st)
            po, rpo = rbank()
            for j in range(16):
                k.pe(A("matmul", po[:, 0:n], wp3[:, j, :], v[:, j, :], start=(j == 0), stop=(j == 15)), [rwpb, ryT[st]], [rpo])
            pg, rpg = rbank()
            for kc in range(8):
                k.pe(A("matmul", pg[:, 0:n], wb3[:, kc, :], hT3[:, kc, t0:t0 + n], start=(kc == 0), stop=(kc == 7)), [rwb, self.rhT], [rpg])
            k.act(A("activation", out=sg[:, 0:n], in_=pg[:, 0:n], func=AF.Sigmoid), [rpg], [rsg])
            k.dve(A("tensor_tensor", out=sg[:, 0:n], in0=sg[:, 0:n], in1=rstd[st][0][:, 0:n], op=ALU.mult), [rsg, rstd[st][1]], [rsg])
            k.dve(A("tensor_tensor", out=mg3[:, cbk, c0 * 128:c0 * 128 + n], in0=po[:, 0:n], in1=sg[:, 0:n], op=ALU.mult), [rpo, rsg], [self.rmg])
    k.dump("mg_ssd", self.mg[:], self.rmg, [128, 8 * 2304], BF16)
    self.S.flush()
    arA.close()
    arW.cur = arW.lo + 8 * 2304 * 2


K.phase_ssd_proj = phase_ssd_proj


def phase_att(self):
    k = self; I = self.I
    hT3 = self.hT3
    arY = self.arYZ; arA = self.arACC; arW = self.arW
    arY.close(); arA.close()
    yatt, ryatt = k.T("yatt", [128, 18 * 1024], BF16, es=arY)
    yatt3 = yatt[:].rearrange("p (i f) -> p i f", i=18)
    cosT, rcos = k.T("cosT", [64, 2560], es=arY)
    sinT, rsin = k.T("sinT", [64, 2560], es=arY)
    kT, rkT = k.T("kT", [64, 2560], BF16, es=arY)
    Pbs = [k.T("Pb%d" % i, [128, 5 * 512], BF16, es=arY) for i in range(2)]
    Va, rVa = k.T("Va", [128, 20 * 4 * 65], BF16, es=arA)
    Va4 = Va[:].rearrange("p (t g c) -> p t g c", t=20, g=4)
    t_, rt_ = k.T("rt", [64, 512], es=arA)
    u_, ru_ = k.T("ru", [64, 512], es=arA)
    wmark = arW.cur
    qg, rqg = k.T("qg", [64, 18 * 512], BF16, es=arW)
    qg4 = qg[:].rearrange("p (i h c) -> p i h c", i=18, h=4)
    stage = k.T("stg", [128, 8 * 128], es=arW)
    wv, rwv = k.T("wv", [128, 8 * 256], BF16, es=arW)
    wv3 = wv[:].rearrange("p (kc c) -> p kc c", kc=8)
    wm = [k.T("wqm%d" % i, [128, 8 * 64], BF16, es=arW) for i in range(2)]
    wp = [k.T("wqp%d" % i, [128, 8 * 64], BF16, es=arW) for i in range(2)]
    den, rden = k.T("den", [128, 8], es=arW)
    vca4 = self.vca[:].rearrange("p (s g c) -> p s g c", s=2, g=4)
    triLb = self.cstb[:, 256:384]; triUb = self.cstb[:, 128:256]

    k.dma(cosT[:], I["cosT"], [], [rcos])
    k.dma(sinT[:], I["sinT"], [], [rsin])
    k.dve(A("memset", Va[:], 1.0), [], [rVa])
    for i in range(2):
        k.load_w(stage, (wv3[:, :, i * 128:(i + 1) * 128], rwv), I["w_in"], V0 + i * 128, 128)
    for t in range(NOWN):
        ps, rps = k.bank()
        for kc in range(8):
            k.pe(A("matmul", ps[:, 0:256], hT3[:, kc, t * 128:(t + 1) * 128], wv3[:, kc, :], start=(kc == 0), stop=(kc == 7)), [rwv, self.rhT], [rps])
        k.act(A("activation", out=Va4[:, t, :, 0:64], in_=ps[:, 0:256].rearrange("p (g c) -> p g c", c=64), func=AF.Copy), [rps], [rVa])

    def proj_rope(c_main, c_perm, wi, tok0, ntok, dst_fn, rdst):
        wm_, rwm = wm[wi % 2]; wp_, rwp = wp[wi % 2]
        wm3 = wm_[:].rearrange("p (kc c) -> p kc c", kc=8); wp3 = wp_[:].rearrange("p (kc c) -> p kc c", kc=8)
        k.load_w(stage, (wm3, rwm), I["w_in"], c_main, 64)
        k.load_w(stage, (wp3, rwp), I["w_qkp"], c_perm, 64)
        for e0 in range(0, ntok, 512):
            n = min(512, ntok - e0)
            pa, rpa = k.bank(); pb, rpb = k.bank()
            for kc in range(8):
                k.pe(A("matmul", pa[0:64, 0:n], wm3[:, kc, :], hT3[:, kc, tok0 + e0:tok0 + e0 + n], start=(kc == 0), stop=(kc == 7)), [rwm, self.rhT], [rpa])
            for kc in range(8):
                k.pe(A("matmul", pb[0:64, 0:n], wp3[:, kc, :], hT3[:, kc, tok0 + e0:tok0 + e0 + n], start=(kc == 0), stop=(kc == 7)), [rwp, self.rhT], [rpb])
            k.dve(A("tensor_tensor", out=t_[:, 0:n], in0=pa[0:64, 0:n], in1=cosT[:, tok0 + e0:tok0 + e0 + n], op=ALU.mult), [rpa, rcos], [rt_])
            k.dve(A("tensor_tensor", out=u_[:, 0:n], in0=pb[0:64, 0:n], in1=sinT[:, tok0 + e0:tok0 + e0 + n], op=ALU.mult), [rpb, rsin], [ru_])
            o_, view = dst_fn(e0, n)
            k.dve(A("tensor_tensor", out=o_, in0=view(t_[:, 0:n]), in1=view(u_[:, 0:n]), op=ALU.add), [rt_, ru_], [rdst])

    for g in range(4):
        proj_rope(K0 + g * 64, 1024 + g * 64, 0, 0, 2560, lambda e0, n: (kT[:, e0:e0 + n], (lambda a: a)), rkT)
        for h in range(4):
            hh = 4 * g + h
            proj_rope(Q0 + hh * 64, hh * 64, 1 + h, 128, 2304,
                      (lambda e0, n, h=h: (qg4[:, e0 // 128:(e0 + n) // 128, h, :], (lambda a: a.rearrange("p (i c) -> p i c", c=128)))), rqg)
        if g == 0:
            k.dump("kT0", kT[:], rkT, [64, 2560], BF16)
            k.dump("qg0", qg[:], rqg, [64, 18 * 512], BF16)
        def scores(i):
            Pb, rPb = Pbs[i % 2]
            kbs = [(kT[:, i * 128:(i + 1) * 128], rkT, Va4[:, i, g, :], rVa, triLb, i),
                   (kT[:, (i + 1) * 128:(i + 2) * 128], rkT, Va4[:, i + 1, g, :], rVa, None, None),
                   (kT[:, (i + 2) * 128:(i + 3) * 128], rkT, Va4[:, i + 2, g, :], rVa, triUb, i + 2),
                   (self.kcT[:, g * 256:g * 256 + 128], self.rkcT, vca4[:, 0, g, :], self.rvca, None, None),
                   (self.kcT[:, g * 256 + 128:g * 256 + 256], self.rkcT, vca4[:, 1, g, :], self.rvca, None, None)]
            for kb, (kap, rk_, vap, rv_, mask, fl) in enumerate(kbs):
                ps, rps = k.bank()
                k.pe(A("matmul", ps[:, 0:512], kap, qg[:, i * 512:(i + 1) * 512], start=True, stop=True), [rk_, rqg], [rps])
                pk = Pb[:, kb * 512:(kb + 1) * 512]
                k.act(A("activation", out=pk, in_=ps[:, 0:512], func=AF.Exp, scale=0.125), [rps], [rPb])
                if mask is not None:
                    p3 = pk.rearrange("p (h c) -> p h c", c=128)
                    k.dve(A("scalar_tensor_tensor", out=p3, in0=p3, scalar=self.flags[:, fl:fl + 1], in1=mask.unsqueeze(1).to_broadcast([128, 4, 128]),
                            op0=ALU.mult, op1=ALU.mult), [rPb, self.rflags, self.rcstb], [rPb])
            return kbs

        def pv(i, kbs):
            Pb, rPb = Pbs[i % 2]
            po, rpo = k.bank()
            for h in range(4):
                for kb, (kap, rk_, vap, rv_, mask, fl) in enumerate(kbs):
                    k.pe(A("matmul", po[:, h * 65:(h + 1) * 65], Pb[:, kb * 512 + h * 128:kb * 512 + (h + 1) * 128], vap, start=(kb == 0), stop=(kb == 4)),
                         [rPb, rv_], [rpo])
            po3 = po[:, 0:260].rearrange("p (h c) -> p h c", c=65)
            k.dve(A("tensor_tensor", out=den[:, 0:4], in0=po3[:, :, 64], in1=self.esink[:, 4 * g:4 * g + 4], op=ALU.add), [rpo, self.resink], [rden])
            k.dve(A("reciprocal", out=den[:, 4:8], in_=den[:, 0:4]), [rden], [rden])
            k.dve(A("tensor_tensor", out=yatt3[:, i, g * 256:(g + 1) * 256].rearrange("p (h c) -> p h c", c=64), in0=po3[:, :, 0:64],
                    in1=den[:, 4:8].unsqueeze(2).to_broadcast([128, 4, 64]), op=ALU.mult), [rpo, rden], [ryatt])

        prevk = None
        for i in range(18):
            cur = scores(i)
            if prevk is not None:
                pv(i - 1, prevk)
            prevk = cur
        pv(17, prevk)
    k.dump("yatt", yatt[:], ryatt, [128, 18 * 1024], BF16)
    self.S.flush()
    arW.cur = wmark
    arA.close()
    arY.cur = arY.lo + 18 * 1024 * 2
    woa, rwoa = k.T("woa", [128, 8 * 1024], BF16, es=arY)
    wga, rwga = k.T("wga", [128, 8 * 1024], BF16, es=arY)
    woa3 = woa[:].rearrange("p (kc c) -> p kc c", kc=8); wga3 = wga[:].rearrange("p (kc c) -> p kc c", kc=8)
    stage = k.T("stg", [128, 8 * 128], es=arW)
    yT_, ryT = k.T("yattT", [128, 8 * 512], BF16, es=arA)
    yT3 = yT_[:].rearrange("p (kc t) -> p kc t", kc=8)
    sgs = [k.T("sg%d" % i, [128, 512], es=arA) for i in range(2)]
    tts = [k.T("tt%d" % i, [128, 512], es=arA) for i in range(2)]
    idb = self.cstb[:, 0:128]
    mg3 = self.mg3
    for i in range(8):
        k.load_w(stage, (woa3[:, :, i * 128:(i + 1) * 128], rwoa), I["w_o_att"], i * 128, 128)
        k.load_w(stage, (wga3[:, :, i * 128:(i + 1) * 128], rwga), I["w_in"], G0 + 1024 + i * 128, 128)
    SUP = [(0, 4), (4, 4), (8, 4), (12, 4), (16, 2)]
    for st, (c0, nc_) in enumerate(SUP):
        n = nc_ * 128
        t0 = (c0 + 1) * 128
        for ci in range(nc_):
            tb, rtb = k.bank()
            tbb = tb[:].bitcast(BF16)
            for kc in range(8):
                k.pe(A("transpose", tbb[:, kc * 128:(kc + 1) * 128], yatt3[:, c0 + ci, kc * 128:(kc + 1) * 128], idb), [ryatt, self.rcstb], [rtb])
            k.act(A("activation", out=yT3[:, :, ci * 128:(ci + 1) * 128], in_=tbb[:, 0:1024].rearrange("p (kc t) -> p kc t", kc=8), func=AF.Copy), [rtb], [ryT])
        for cbk in range(8):
            sg, rsg = sgs[cbk % 2]; tt, rtt = tts[cbk % 2]
            po, rpo = k.bank(); pg, rpg = k.bank()
            for kc in range(8):
                k.pe(A("matmul", po[:, 0:n], woa3[:, kc, cbk * 128:(cbk + 1) * 128], yT3[:, kc, 0:n], start=(kc == 0), stop=(kc == 7)), [rwoa, ryT], [rpo])
            for kc in range(8):
                k.pe(A("matmul", pg[:, 0:n], wga3[:, kc, cbk * 128:(cbk + 1) * 128], hT3[:, kc, t0:t0 + n], start=(kc == 0), stop=(kc == 7)), [rwga, self.rhT], [rpg])
            k.act(A("activation", out=sg[:, 0:n], in_=pg[:, 0:n], func=AF.Sigmoid), [rpg], [rsg])
            k.dve(A("tensor_tensor", out=tt[:, 0:n], in0=po[:, 0:n], in1=sg[:, 0:n], op=ALU.mult), [rpo, rsg], [rtt])
            m_ = mg3[:, cbk, c0 * 128:c0 * 128 + n]
            k.dve(A("tensor_tensor", out=m_, in0=m_, in1=tt[:, 0:n], op=ALU.add), [rtt, self.rmg], [self.rmg])
    k.dump("mg", self.mg[:], self.rmg, [128, 8 * 2304], BF16)
    self.S.flush()
    arA.close(); arY.close()
    arW.cur = wmark


K.phase_att = phase_att


def phase_wout(self):
    k = self; I = self.I
    arY = self.arYZ; arA = self.arACC; arH = self.arHT
    arY.close(); arA.close(); arH.close()
    mg3 = self.mg3
    self.h2T, self.rh2T = k.T("h2T", [128, 8 * 2050], BF16, es=arH)
    h2T3 = self.h2T[:].rearrange("p (kc t) -> p kc t", kc=8)
    self.h2T3 = h2T3
    g1, rg1 = k.T("g1row", [128, D], es=arY)
    wo, rwo = k.T("wo", [128, 8 * D], BF16, es=arY)
    wo3 = wo[:].rearrange("p (kc c) -> p kc c", kc=8)
    stage = k.T("stg", [128, 8 * 128], es=arY)
    stg3 = stage[0][:].rearrange("p (kc c) -> p kc c", kc=8)
    xts = [k.T("xw%d" % i, [128, D], es=arY) for i in range(3)]
    x1t = [k.T("x1t%d" % i, [128, D], es=arY) for i in range(3)]
    tmps = [(k.T("junk%d" % i, [128, D], BF16, es=arY), k.T("ss%d" % i, [128, 4], es=arY), k.T("xn%d" % i, [128, D], BF16, es=arY)) for i in range(3)]
    m3 = self.modT[:].rearrange("p (b v) -> p b v", v=2)
    row_bcast(k, g1, rg1, (lambda b: m3[:, 16 + b, 0:1]), self.rmodT2, 8)
    for i in range(8):
        k.dma(stg3, I["w_out"][:, i * 128:(i + 1) * 128].rearrange("(kc p) c -> p kc c", p=128), [], [stage[1]])
        k.dve(A("tensor_tensor", out=wo3[:, :, i * 128:(i + 1) * 128], in0=stg3, in1=g1[:, i * 128:(i + 1) * 128].unsqueeze(1).to_broadcast([128, 8, 128]),
                op=ALU.mult), [stage[1], rg1], [rwo])
    self.rx1s = [Reg("x1s%d" % i) for i in range(16)]

    def st_a(i):
        xt_, rxt = xts[i % 3]; x1_, rx1 = x1t[i % 3]
        k.dma(xt_[:], I["x_own"][(i + 1) * 128:(i + 2) * 128, :], [], [rxt])
        for half in range(2):
            ps, rps = k.bank()
            for kc in range(8):
                k.pe(A("matmul", ps[:, 0:512], mg3[:, kc, i * 128:(i + 1) * 128], wo3[:, kc, half * 512:(half + 1) * 512], start=(kc == 0), stop=(kc == 7)),
                     [self.rmg, rwo], [rps])
            k.dve(A("tensor_tensor", out=x1_[:, half * 512:(half + 1) * 512], in0=ps[:, 0:512], in1=xt_[:, half * 512:(half + 1) * 512], op=ALU.add),
                  [rps, rxt], [rx1])
        if 1 <= i <= 16:
            k.dma(self.x1s[(i - 1) * 128:i * 128, :], x1_[:], [rx1], [self.rx1s[i - 1]], q="pool")
        k.norm_a(x1_, rx1, 128, tmps[i % 3])
        if i == 5:
            k.dump("x1_5", x1_[:], rx1, [128, D])

    def st_b(i):
        if 1 <= i <= 16:
            k.norm_b(128, (lambda kc, i=i: h2T3[:, kc, 1 + (i - 1) * 128:1 + i * 128]), self.rh2T, 2, 24, 0, tmps[i % 3])
        elif i == 0:
            k.norm_b(128, (lambda kc: h2T3[:, kc, 0:1]), self.rh2T, 2, 24, 0, tmps[i % 3], src_view=lambda a: a[:, 127:128])
        else:
            k.norm_b(128, (lambda kc: h2T3[:, kc, 2049:2050]), self.rh2T, 2, 24, 0, tmps[i % 3], src_view=lambda a: a[:, 0:1])

    st_a(0)
    for i in range(18):
        if i + 1 < 18:
            st_a(i + 1)
        st_b(i)
    for (col, fl) in ((0, 1), (2049, 18)):
        for kc in range(8):
            k.dve(A("tensor_scalar", out=h2T3[:, kc, col:col + 1], in0=h2T3[:, kc, col:col + 1], scalar1=self.flags[:, fl:fl + 1], scalar2=None, op0=ALU.mult),
                  [self.rh2T, self.rflags], [self.rh2T])
    k.dump("h2T", self.h2T[:], self.rh2T, [128, 8 * 2050], BF16)
    self.S.flush()
    arY.close()


def phase_ffn(self):
    k = self; I = self.I
    h2T3 = self.h2T3
    arA = self.arACC
    arB = Arena(self.arYZ.lo, self.arW.hi)
    arH = self.arHT
    arA.close()
    Gb, rGb = k.T("Gb", [128, 22 * 2048], BF16, es=arB)
    Gb3 = Gb[:].rearrange("p (j t) -> p j t", j=22)
    wd, rwd = k.T("wd", [128, 22 * D], BF16, es=arB)
    wd3 = wd[:].rearrange("p (j c) -> p j c", j=22)
    wab = [k.T("wab%d" % i, [128, 8 * 128], BF16, es=arB) for i in range(2)]
    stage = k.T("stg", [128, 8 * 128], es=arH)
    raw = [k.T("raw%d" % i, [128, 412], es=arA) for i in range(2)]
    ac, rac = k.T("ac", [128, 412], es=arA)
    bc, rbc = k.T("bc", [128, 412], es=arA)
    fcw, rfcw = k.T("fcw", [128, 132], es=arA)
    fcb, rfcb = k.T("fcb", [128, 44], es=arA)
    k.dma(fcw[:], I["ffn_cw"], [], [rfcw])
    k.dma(fcb[:], I["ffn_cb"], [], [rfcb])
    fw3 = fcw[:].rearrange("p (b t) -> p b t", t=3)
    g2, rg2 = k.T("g2row", [128, D], es=arA)
    stgd = k.T("stgd", [128, D], es=arA)
    m3 = self.modT[:].rearrange("p (b v) -> p b v", v=2)
    row_bcast(k, g2, rg2, (lambda b: m3[:, 40 + b, 0:1]), self.rmodT2, 8)
    tiles = [(1, 410), (411, 410), (821, 410), (1231, 410), (1641, 408)]
    for j in range(22):
        k.dma(stgd[0][:], I["w_down"][j * 128:(j + 1) * 128, :], [], [stgd[1]])
        k.pool(A("tensor_tensor", out=wd3[:, j, :], in0=stgd[0][:], in1=g2[:], op=ALU.mult), [stgd[1], rg2], [rwd])
        for ab in range(2):
            wb_, rwb = wab[ab]
            wb3 = wb_[:].rearrange("p (kc c) -> p kc c", kc=8)
            k.load_w(stage, (wb3, rwb), I["w_up"], ab * D_FF + j * 128, 128)
        wa3 = wab[0][0][:].rearrange("p (kc c) -> p kc c", kc=8); rwa = wab[0][1]
        wbb3 = wab[1][0][:].rearrange("p (kc c) -> p kc c", kc=8); rwb = wab[1][1]
        for (o0, n) in tiles:
            pa, rpa = k.bank(); pb, rpb = k.bank()
            for kc in range(8):
                k.pe(A("matmul", pa[:, 0:n + 2], wa3[:, kc, :], h2T3[:, kc, o0 - 1:o0 + n + 1], start=(kc == 0), stop=(kc == 7)), [rwa, self.rh2T], [rpa])
            for kc in range(8):
                k.pe(A("matmul", pb[:, 0:n + 2], wbb3[:, kc, :], h2T3[:, kc, o0 - 1:o0 + n + 1], start=(kc == 0), stop=(kc == 7)), [rwb, self.rh2T], [rpb])
            fa = conv3(k, pa, rpa, n, fw3[:, j, :], fcb[:, j:j + 1], raw[0], ac[:, 0:n], rac, AF.Silu, [rfcw, rfcb], defer=True)
            (rb_, rrb_) = raw[1]
            k.act(A("activation", out=rb_[:, 0:n], in_=pb[:, 1:n + 1], func=AF.Identity, scale=fw3[:, 22 + j, 1:2], bias=fcb[:, 22 + j:23 + j]),
                  [rpb, rfcw, rfcb], [rrb_])
            fa()
            k.dve(A("scalar_tensor_tensor", out=rb_[:, 0:n], in0=pb[:, 0:n], scalar=fw3[:, 22 + j, 0:1], in1=rb_[:, 0:n], op0=ALU.mult, op1=ALU.add),
                  [rpb, rrb_, rfcw], [rrb_])
            k.dve(A("scalar_tensor_tensor", out=rb_[:, 0:n], in0=pb[:, 2:n + 2], scalar=fw3[:, 22 + j, 2:3], in1=rb_[:, 0:n], op0=ALU.mult, op1=ALU.add),
                  [rpb, rrb_, rfcw], [rrb_])
            k.dve(A("tensor_tensor", out=Gb3[:, j, o0 - 1:o0 - 1 + n], in0=ac[:, 0:n], in1=rb_[:, 0:n], op=ALU.mult), [rac, rrb_], [rGb])
    k.dump("Gb", Gb[:], rGb, [128, 22 * 2048], BF16)
    self.S.flush()
    arA.close()
    fg, rfg = k.T("fgrow", [128, D], es=arA)
    x1r = [k.T("x1r%d" % i, [128, D], es=arA) for i in range(2)]
    k.dma(fg[:], I["fg_row"], [], [rfg])
    arB.cur = arB.lo + 22 * 2048 * 2 + 22 * D * 2
    arH.cur = arH.lo + 8 * 2050 * 2 + 64
    stg2 = k.T("stgd", [128, D], es=arB)
    xos = [k.T("xo0", [128, D], es=arB), stg2]
    junk, rjunk = k.T("junk", [128, D], BF16, es=arH)
    sss = [k.T("ssa", [128, 4], es=arH), k.T("ssb", [128, 4], es=arH)]
    for i in range(16):
        xr, rxr = x1r[i % 2]
        xo, rxo = xos[i % 2]
        ss, rss = sss[i % 2]
        k.dma(xr[:], self.x1s[i * 128:(i + 1) * 128, :], [self.rx1s[i]], [rxr])
        for half in range(2):
            ps, rps = k.bank()
            for j in range(22):
                k.pe(A("matmul", ps[:, 0:512], Gb3[:, j, i * 128:(i + 1) * 128], wd3[:, j, half * 512:(half + 1) * 512], start=(j == 0), stop=(j == 21)),
                     [rGb, rwd], [rps])
            k.dve(A("tensor_tensor", out=xo[:, half * 512:(half + 1) * 512], in0=ps[:, 0:512], in1=xr[:, half * 512:(half + 1) * 512], op=ALU.add),
                  [rps, rxr], [rxo])
        k.dve(A("memset", ss[:, 0:1], 0.0), [], [rss])
        k.act(A("activation", out=junk[:], in_=xo[:], func=AF.Square, accum_out=ss[:, 0:1]), [rxo], [rjunk, rss])
        k.dve(A("tensor_scalar", out=ss[:, 1:2], in0=ss[:, 0:1], scalar1=1.0 / D, scalar2=EPS, op0=ALU.mult, op1=ALU.add), [rss], [rss])
        k.act(A("activation", out=ss[:, 1:2], in_=ss[:, 1:2], func=AF.Sqrt), [rss], [rss])
        k.dve(A("reciprocal", out=ss[:, 2:3], in_=ss[:, 1:2]), [rss], [rss])
        k.dve(A("scalar_tensor_tensor", out=xr[:], in0=xo[:], scalar=ss[:, 2:3], in1=fg[:], op0=ALU.mult, op1=ALU.mult), [rxo, rss, rfg], [rxr])
        k.dma(self.out[i * 128:(i + 1) * 128, :], xr[:], [rxr], [], q="pool")
    self.S.flush()


K.phase_wout = phase_wout
K.phase_ffn = phase_ffn
```

```python
import os
from contextlib import ExitStack
import numpy as np
import concourse.bass as bass
import concourse.mybir as mybir
from concourse.bass_utils import run_bass_kernel_spmd

F32 = mybir.dt.float32
BF16 = mybir.dt.bfloat16
AF = mybir.ActivationFunctionType
ALU = mybir.AluOpType

D = 1024
L = 8192
NQ = 4
TOK = 2048
EPS = 1e-6
XBC0 = 2048
DT0 = 5120
Q0 = 5184
K0 = 6208
V0 = 6464
G0 = 6720
D_FF = 2816
NOWN = 20
NSLOT = 48
NSB = 16
NFLAG = 20 + 2 * NSLOT + 2 * NSLOT

DEBUG = os.environ.get("KDEBUG", "")


class Reg:
    __slots__ = ("name", "lw", "rd", "excl")

    def __init__(self, name, excl=False):
        self.name = name
        self.lw = None
        self.rd = []
        self.excl = excl


class Op:
    __slots__ = ("eng", "fn", "deps", "dma", "sem", "val", "needed", "prev")


class Sched:
    ENG = ("pe", "act", "dve", "pool", "sp")

    def __init__(self, nc, n_dma_sems=12):
        self.nc = nc
        self.esem = {e: nc.alloc_semaphore("s_" + e) for e in self.ENG}
        self.ecnt = {e: 0 for e in self.ENG}
        self.dq = ("sp", "pool", "act")
        self.dsem = {q: [nc.alloc_semaphore("d_%s%d" % (q, i)) for i in range(n_dma_sems)] for q in self.dq}
        self.dcnt = {q: [0] * n_dma_sems for q in self.dq}
        self.drr = {q: 0 for q in self.dq}
        self.dlast = {q: [None] * n_dma_sems for q in self.dq}
        self.ops = []
        self.known = {e: {} for e in self.ENG}
        self.nops = 0

    def op(self, eng, fn, r=(), w=(), dma=False):
        o = Op()
        o.eng = eng; o.fn = fn; o.dma = dma; o.needed = False; o.sem = None; o.val = 0; o.prev = None
        deps = []
        for t in r:
            if t.lw is not None:
                deps.append(t.lw)
            if t.excl:
                deps.extend(x for x in t.rd if x.eng != eng)
        for t in w:
            if t.lw is not None and (t.lw.eng != eng or t.lw.dma or dma):
                deps.append(t.lw)
            deps.extend(x for x in t.rd if (x.eng != eng or x.dma or dma))
        o.deps = []
        seen = set()
        for d in deps:
            if d is o or id(d) in seen:
                continue
            seen.add(id(d))
            if d.eng == "pe" and eng == "pe" and not d.dma and not dma:
                continue
            o.deps.append(d)
        for t in w:
            t.lw = o
            t.rd = []
        for t in r:
            if t.lw is not o:
                t.rd.append(o)
        self.ops.append(o)
        return o

    def flush(self):
        ops = self.ops
        for o in ops:
            for d in o.deps:
                d.needed = True
        last = {}
        for o in ops:
            if o.dma:
                o.needed = True
            else:
                last[o.eng] = o
        for o in last.values():
            o.needed = True
        for o in ops:
            if o.dma:
                q = o.eng
                i = self.drr[q]
                self.drr[q] = (i + 1) % len(self.dsem[q])
                o.prev = self.dlast[q][i]
                self.dcnt[q][i] += 16
                o.sem = ("d", q, i)
                o.val = self.dcnt[q][i]
                self.dlast[q][i] = o
            elif o.needed:
                self.ecnt[o.eng] += 1
                o.sem = ("e", o.eng)
                o.val = self.ecnt[o.eng]
        per = {e: [] for e in self.ENG}
        for o in ops:
            per[o.eng].append(o)
        fin = [o for o in ops if (o.dma or o is last.get(o.eng))]

        def semh(key):
            return self.esem[key[1]] if key[0] == "e" else self.dsem[key[1]][key[2]]

        snap = {}
        waits = {}
        for o in ops:
            kn = self.known[o.eng]
            wl = []
            dl = list(o.deps)
            if o.dma and o.prev is not None:
                dl.append(o.prev)
            for d in dl:
                if d.sem is None or kn.get(d.sem, 0) >= d.val:
                    continue
                wl.append((d.sem, d.val))
                kn[d.sem] = d.val
                sd = snap.get(id(d))
                if sd:
                    for k_, v_ in sd.items():
                        if kn.get(k_, 0) < v_:
                            kn[k_] = v_
            waits[id(o)] = wl
            if o.sem is not None:
                snap[id(o)] = dict(kn)

        def emit(e, eh):
            kn = self.known[e]

            def wait(key, val):
                if kn.get(key, 0) >= val:
                    return
                eh.wait_ge(semh(key), val)
                kn[key] = val

            for o in per[e]:
                for (key, val) in waits[id(o)]:
                    eh.wait_ge(semh(key), val)
                ins = o.fn(eh)
                if o.dma:
                    ins.then_inc(semh(o.sem), 16)
                elif o.needed:
                    ins.then_inc(semh(o.sem), 1)
            for d in fin:
                if d.eng == e and not d.dma:
                    continue
                wait(d.sem, d.val)

        with self.nc.Block() as block:
            @block.tensor
            def _(t):
                emit("pe", t)

            @block.scalar
            def _(t):
                emit("act", t)

            @block.vector
            def _(t):
                emit("dve", t)

            @block.gpsimd
            def _(t):
                emit("pool", t)

            @block.sync
            def _(t):
                emit("sp", t)
        self.nops += len(ops)
        self.ops = []


class Arena:
    def __init__(self, lo, hi):
        self.lo = lo; self.hi = hi; self.cur = lo

    def take(self, n, name=""):
        o = self.cur
        assert o + n <= self.hi, "arena overflow for %s: need %d have %d" % (name, n, self.hi - o)
        self.cur = o + n
        return o

    def close(self):
        self.cur = self.lo


def A(name, *args, **kw):
    return lambda e: getattr(e, name)(*args, **kw)


class K:
    def __init__(self):
        self.nc = bass.Bass("TRN2", target_bir_lowering=False)
        self.S = Sched(self.nc)
        self.dbg = {}
        self.pbank = 0
        self.es = ExitStack()
        self.uid = 0
        self.arC = Arena(17408, 26624)
        self.arACC = Arena(26624, 43008)
        self.arHT = Arena(43008, 83968)
        self.arYZ = Arena(83968, 157696)
        self.arW = Arena(157696, 229376)
        self.arBIG = Arena(43008, 229376 - 18 * 1024)

    def T(self, name, shape, dt=F32, es=None):
        ar = es or self.arC
        nbytes = int(np.prod(shape[1:])) * (2 if dt == BF16 else 4)
        nbytes = (nbytes + 63) // 64 * 64
        off = ar.take(nbytes, name)
        self.offs = getattr(self, "offs", {})
        self.offs[name] = off
        self.uid += 1
        t = self.nc.alloc_sbuf_tensor_at("sb%d_%s" % (self.uid, name), list(shape), dt, offset=off)
        return t, Reg(name)

    def din(self, name, shape, dt=F32):
        return self.nc.dram_tensor(name, list(shape), dt, kind="ExternalInput").ap()

    def pe(self, fn, r, w): return self.S.op("pe", fn, r, w)
    def act(self, fn, r, w): return self.S.op("act", fn, r, w)
    def dve(self, fn, r, w): return self.S.op("dve", fn, r, w)
    def pool(self, fn, r, w): return self.S.op("pool", fn, r, w)
    def dma(self, out, in_, r, w, q="sp"): return self.S.op(q, A("dma_start", out=out, in_=in_), r, w, dma=True)

    def bank(self):
        i = self.pbank
        self.pbank = (i + 1) % 8
        return self.PS[i], self.PR[i]

    def dump(self, name, ap, reg, shape, dt=F32):
        if name not in DEBUG.split(","):
            return
        o = self.nc.dram_tensor("dbg_" + name, list(shape), dt, kind="ExternalOutput").ap()
        self.dma(o, ap, list(reg) if isinstance(reg, (list, tuple)) else [reg], [], q="sp")
        self.dbg[name] = "dbg_" + name

    def build(self):
        nc = self.nc
        k = self
        I = {}
        I["x_own"] = k.din("x_own", [NOWN * 128, D])
        I["x_oth"] = k.din("x_oth", [NSLOT * 128, D])
        I["x_oth_h"] = k.din("x_oth_h", [2 * NSLOT, D])
        I["ctx"] = k.din("ctx", [256, D])
        I["cvec"] = k.din("cvec", [128, 16])
        I["flags"] = k.din("flags", [128, NFLAG])
        I["w_mod"] = k.din("w_mod", [D, 6 * D])
        I["bmodT"] = k.din("bmodT", [128, 48])
        I["n1g"] = k.din("n1g", [128, 8])
        I["n2g"] = k.din("n2g", [128, 8])
        I["fg_row"] = k.din("fg_row", [128, D])
        I["w_in"] = k.din("w_in", [D, 8768])
        I["w_qkp"] = k.din("w_qkp", [D, 1280])
        I["ssd_cw"] = k.din("ssd_cw", [128, 24 * 3])
        I["ssd_cb"] = k.din("ssd_cb", [128, 24])
        I["dtb"] = k.din("dtb", [128, 64])
        I["alog"] = k.din("alog", [128, 64])
        I["dsk"] = k.din("dsk", [128, 32])
        I["sng"] = k.din("sng", [128, 16])
        I["w_o_ssd"] = k.din("w_o_ssd", [2048, D])
        I["w_o_att"] = k.din("w_o_att", [D, D])
        I["sink"] = k.din("sink", [128, 16])
        I["w_out"] = k.din("w_out", [D, D])
        I["w_up"] = k.din("w_up", [D, 2 * D_FF])
        I["ffn_cw"] = k.din("ffn_cw", [128, 44 * 3])
        I["ffn_cb"] = k.din("ffn_cb", [128, 44])
        I["w_down"] = k.din("w_down", [D_FF, D])
        I["cosT"] = k.din("cosT", [64, NOWN * 128])
        I["sinT"] = k.din("sinT", [64, NOWN * 128])
        I["consts"] = k.din("consts", [128, 3 * 128])
        self.I = I
        self.out = nc.dram_tensor("out", [TOK, D], F32, kind="ExternalOutput").ap()
        self.x1s = nc.dram_tensor("x1s", [TOK, D], F32, kind="Internal").ap()

        self.PS = []
        self.PR = []
        for i in range(8):
            self.PS.append(self.es.enter_context(nc.psum_tensor("ps%d" % i, [128, 512], F32)))
            self.PR.append(Reg("ps%d" % i, excl=True))

        stop = os.environ.get("KSTOP", "")
        for ph in ("setup", "others", "own_h", "ssd", "ssd_proj", "att", "wout", "ffn"):
            getattr(self, "phase_" + ph)()
            if stop == ph:
                break
        return nc

    def phase_setup(self):
        k = self; nc = self.nc; I = self.I
        es = Arena(229376 - 18 * 1024, 229376)
        self.cst, self.rcst = k.T("cst", [128, 3 * 128])
        self.cstb, self.rcstb = k.T("cstb", [128, 3 * 128], BF16)
        self.ones, self.rones = k.T("ones", [128, 128])
        self.flags, self.rflags = k.T("flags", [128, NFLAG])
        self.modT, self.rmodT = k.T("modT", [128, 48 * 2])
        self.gs, self.rgs = k.T("gs", [128, 8 * 4])
        self.negA, self.rnegA = k.T("negA", [128, 64])
        self.esink, self.resink = k.T("esink", [128, 16])
        self.dtb, self.rdtb = k.T("dtb", [128, 64])
        self.dsk, self.rdsk = k.T("dsk", [128, 32])
        self.sng, self.rsng = k.T("sng", [128, 16])
        self.scw, self.rscw = k.T("scw", [128, 72])
        self.scb, self.rscb = k.T("scb", [128, 24])
        self.n12, self.rn12 = k.T("n12", [128, 16])
        cvec, rcvec = k.T("cvec", [128, 16], es=es)
        sc, rsc = k.T("sc", [128, 16], es=es)
        bmodT, rbmodT = k.T("bmodT", [128, 48], es=es)
        wm = [k.T("wm%d" % i, [128, 8 * 256], es=es) for i in range(2)]

        k.dma(self.cst[:], I["consts"], [], [self.rcst])
        k.dma(self.flags[:], I["flags"], [], [self.rflags])
        k.dma(cvec[:], I["cvec"], [], [rcvec])
        k.dma(bmodT[:], I["bmodT"], [], [rbmodT])
        k.dma(self.n12[:, 0:8], I["n1g"], [], [self.rn12])
        k.dma(self.n12[:, 8:16], I["n2g"], [], [self.rn12])
        k.dma(self.negA[:], I["alog"], [], [self.rnegA])
        k.dma(self.esink[:], I["sink"], [], [self.resink])
        k.dma(self.dtb[:], I["dtb"], [], [self.rdtb])
        k.dma(self.dsk[:], I["dsk"], [], [self.rdsk])
        k.dma(self.sng[:], I["sng"], [], [self.rsng])
        k.dma(self.scw[:], I["ssd_cw"], [], [self.rscw])
        k.dma(self.scb[:], I["ssd_cb"], [], [self.rscb])
        k.dve(A("tensor_copy", out=self.cstb[:], in_=self.cst[:]), [self.rcst], [self.rcstb])
        k.pool(A("memset", self.ones[:], 1.0), [], [self.rones])
        k.act(A("activation", out=sc[:], in_=cvec[:], func=AF.Silu), [rcvec], [rsc])
        k.act(A("activation", out=self.negA[:], in_=self.negA[:], func=AF.Exp), [self.rnegA], [self.rnegA])
        k.dve(A("tensor_scalar", out=self.negA[:], in0=self.negA[:], scalar1=-1.0, scalar2=None, op0=ALU.mult), [self.rnegA], [self.rnegA])
        k.act(A("activation", out=self.esink[:], in_=self.esink[:], func=AF.Exp), [self.resink], [self.resink])

        wmv = I["w_mod"].rearrange("(kc p) c -> p kc c", p=128)

        def mod_ct(ct):
            wt, rw = wm[ct % 2]
            wt3 = wt[:].rearrange("p (kc c) -> p kc c", kc=8)
            k.dma(wt3, wmv[:, :, ct * 256:(ct + 1) * 256], [], [rw])
            for j in range(2):
                blk = ct * 2 + j
                ps, rps = self.PS[7], self.PR[7]
                for kc in range(8):
                    k.pe(A("matmul", ps[:, 400 + 2 * j:402 + 2 * j], wt3[:, kc, j * 128:(j + 1) * 128], sc[:, 2 * kc:2 * kc + 2],
                           start=(kc == 0), stop=(kc == 7)), [rsc, rw], [rps])
                rm_ = self.rmodT if blk < 16 else self.rmodT2
                k.dve(A("tensor_scalar", out=self.modT[:, 2 * blk:2 * blk + 2], in0=ps[:, 400 + 2 * j:402 + 2 * j], scalar1=bmodT[:, blk:blk + 1],
                        scalar2=None, op0=ALU.add), [rps, rbmodT], [rm_])
        self.mod_ct = mod_ct
        self.rmodT2 = Reg("modT2")
        for ct in range(8):
            mod_ct(ct)
        m3 = self.modT[:].rearrange("p (b v) -> p b v", v=2)
        g3 = self.gs[:].rearrange("p (kc v) -> p kc v", v=4)
        for v in range(2):
            k.dve(A("scalar_tensor_tensor", out=g3[:, :, v], in0=m3[:, 8:16, v], scalar=1.0, in1=self.n12[:, 0:8],
                    op0=ALU.add, op1=ALU.mult), [self.rmodT, self.rn12], [self.rgs])
        k.dump("modT", self.modT[:], self.rmodT, [128, 96])
        k.dump("gs", self.gs[:], self.rgs, [128, 32])

    def norm_a(self, xt, rxt, rows, tmp):
        self.norm_a_multi([(xt, rxt, rows, tmp)])

    def norm_a_multi(self, items):
        k = self
        for (xt, rxt, rows, ((junk, rjunk), (ss, rss), (xn, rxn))) in items:
            k.dve(A("memset", ss[:, 0:1], 0.0), [], [rss])
        for (xt, rxt, rows, ((junk, rjunk), (ss, rss), (xn, rxn))) in items:
            k.act(A("activation", out=junk[0:rows, :], in_=xt[0:rows, :], func=AF.Square, accum_out=ss[0:rows, 0:1]), [rxt], [rjunk, rss])
        for (xt, rxt, rows, ((junk, rjunk), (ss, rss), (xn, rxn))) in items:
            k.dve(A("tensor_scalar", out=ss[0:rows, 1:2], in0=ss[0:rows, 0:1], scalar1=1.0 / D, scalar2=EPS, op0=ALU.mult, op1=ALU.add), [rss], [rss])
        for (xt, rxt, rows, ((junk, rjunk), (ss, rss), (xn, rxn))) in items:
            k.act(A("activation", out=ss[0:rows, 1:2], in_=ss[0:rows, 1:2], func=AF.Sqrt), [rss], [rss])
        for (xt, rxt, rows, ((junk, rjunk), (ss, rss), (xn, rxn))) in items:
            k.dve(A("reciprocal", out=ss[0:rows, 2:3], in_=ss[0:rows, 1:2]), [rss], [rss])
        for (xt, rxt, rows, ((junk, rjunk), (ss, rss), (xn, rxn))) in items:
            k.act(A("activation", out=xn[0:rows, :], in_=xt[0:rows, :], func=AF.Copy, scale=ss[0:rows, 2:3]), [rxt, rss], [rxn])

    def norm_b(self, rows, dst_fn, rdst, gcol, shblk, v, tmp, src_view=None, bankfn=None, dst3=None):
        k = self
        (junk, rjunk), (ss, rss), (xn, rxn) = tmp
        sv = src_view or (lambda a: a)
        ps, rps = (bankfn or k.bank)()
        psb = ps[:].bitcast(BF16)
        for kc in range(8):
            k.pe(A("transpose", psb[:, kc * 128:kc * 128 + rows], xn[0:rows, kc * 128:(kc + 1) * 128], self.cstb[0:rows, 0:rows]),
                 [rxn, self.rcstb], [rps])
        g3 = self.gs[:].rearrange("p (kc v) -> p kc v", v=4)
        m3 = self.modT[:].rearrange("p (b v) -> p b v", v=2)
        if dst3 is not None and rows == 128:
            p3 = psb[:, 0:1024].rearrange("p (kc t) -> p kc t", kc=8)
            k.dve(A("tensor_tensor", out=dst3, in0=p3, in1=g3[:, :, gcol:gcol + 1].to_broadcast([128, 8, 128]), op=ALU.mult),
                  [rps, self.rgs], [rdst])
            k.dve(A("tensor_tensor", out=dst3, in0=dst3, in1=m3[:, shblk:shblk + 8, v:v + 1].to_broadcast([128, 8, 128]), op=ALU.add),
                  [rdst, self.rmodT], [rdst])
            return
        for kc in range(8):
            k.act(A("activation", out=dst_fn(kc), in_=sv(psb[:, kc * 128:kc * 128 + rows]), func=AF.Identity,
                    scale=g3[:, kc, gcol:gcol + 1], bias=m3[:, shblk + kc, v:v + 1]), [rps, self.rgs, self.rmodT], [rdst])

    def norm_T(self, xt, rxt, rows, dst_fn, rdst, gcol, shblk, v, tmp, src_view=None):
        self.norm_a(xt, rxt, rows, tmp)
        self.norm_b(rows, dst_fn, rdst, gcol, shblk, v, tmp, src_view)

    def load_w(self, stage, dst3, src, c0, ncols, scale_ap=None, eng="pool"):
        k = self
        st, rst = stage
        st3 = st[:, 0:8 * ncols].rearrange("p (kc c) -> p kc c", kc=8)
        k.dma(st3, src[:, c0:c0 + ncols].rearrange("(kc p) c -> p kc c", p=128), [], [rst])
        dst, rdst = dst3
        k.S.op(eng, A("tensor_copy", out=dst, in_=st3), [rst], [rdst])

    def phase_others(self):
        k = self; nc = self.nc; I = self.I
        es = self.arBIG
        self.acc, self.racc = k.T("acc", [128, 2 * 2048], es=self.arACC)
        self.kcT, self.rkcT = k.T("kcT", [64, 4 * 256], BF16)
        self.vca, self.rvca = k.T("vca", [128, 2 * 4 * 65], BF16)
        logP, rlogP = k.T("logP", [128, 64], es=es)
        wres, rwres = k.T("wres", [128, 8 * 2624], BF16, es=es)
        wres3 = wres[:].rearrange("p (kc c) -> p kc c", kc=8)
        wkv, rwkv = k.T("wkv", [128, 8 * 512], BF16, es=es)
        wkv3 = wkv[:].rearrange("p (kc c) -> p kc c", kc=8)
        stg = [k.T("stg%d" % i, [128, 8 * 128], es=es) for i in range(2)]
        xts = [k.T("xto%d" % i, [128, D], es=es) for i in range(3)]
        xh, rxh = k.T("xh", [6, D], es=es)
        junk_ = k.T("junk", [128, D], BF16, es=es)
        tmps = [(junk_, k.T("ss%d" % i, [128, 4], es=es), k.T("xn%d" % i, [128, D], BF16, es=es)) for i in range(4)]
        tmp = tmps[0]
        hTo = [k.T("hTo%d" % i, [128, 8 * 390], BF16, es=es) for i in range(2)]
        raw = [k.T("raw%d" % i, [128, 388], es=es) for i in range(3)]
        xc = [k.T("xc%d" % i, [128, 388], BF16, es=es) for i in range(4)]
        xtok = [[k.T("xtok%d_%d" % (i, s), [128, 2048], BF16, es=es) for s in range(3)] for i in range(2)]
        btok = [[k.T("btok%d_%d" % (i, s), [128, 512], BF16, es=es) for s in range(3)] for i in range(2)]
        xp = [[k.T("xp%d_%d" % (d_, s), [128, 2048], BF16, es=es) for s in range(3)] for d_ in range(2)]
        dtr, rdtr = k.T("dtr", [128, 192], es=es)
        dte, rdte = k.T("dte", [128, 192], es=es)
        la, rla = k.T("la", [128, 192], es=es)
        arg, rarg = k.T("arg", [128, 192], es=es)
        wts = [k.T("wt%d" % i, [128, 192], es=es) for i in range(2)]
        wt_, rwt = wts[0]
        L1, rL1 = k.T("L1", [128, 192], es=es)
        hTc = self.nc.alloc_sbuf_tensor_at("sb_hTc_alias", [128, 8 * 258], BF16, offset=self.offs["hTo0"])
        rhTc = hTo[0][1]
        short = [0]

        def sbank():
            i = short[0]; short[0] = (i + 1) % 3
            return self.PS[i], self.PR[i]

        k.dve(A("memset", self.acc[:], 0.0), [], [self.racc])
        k.dve(A("memset", logP[:], 0.0), [], [rlogP])
        k.dve(A("memset", self.vca[:], 1.0), [], [self.rvca])
        for i in range(20):
            k.load_w(stg[i % 2], (wres3[:, :, i * 128:(i + 1) * 128], rwres), I["w_in"], XBC0 + i * 128, 128)
        k.load_w(stg[0], (wres3[:, :, 2560:2624], rwres), I["w_in"], DT0, 64)
        for i in range(4):
            k.load_w(stg[(i + 1) % 2], (wkv3[:, :, i * 128:(i + 1) * 128], rwkv), I["w_in"], K0 + i * 128, 128)
        cw3 = self.scw[:].rearrange("p (b t) -> p b t", t=3)
        idb = self.cstb[:, 0:128]
        triU = self.cst[:, 128:256]; triL = self.cst[:, 256:384]

        def conv_block(ps, n, cb, rw, xcw):
            return conv3(k, ps, self.PRmap[id(ps)], n, cw3[:, cb, :], self.scb[:, cb:cb + 1], rw, xcw[0][:, 0:n], xcw[1], AF.Silu,
                         [self.rscw, self.rscb], defer=True)
        self.conv_block = conv_block

        def dt_chain(nsl, hT3, col0, stride, flag0, split=False):
            psd, rpsd = self.PS[6], self.PR[6]
            for s in range(nsl):
                for kc in range(8):
                    k.pe(A("matmul", psd[:, s * 64:(s + 1) * 64], hT3[:, kc, col0 + s * stride:col0 + s * stride + 128], wres3[:, kc, 2560:2624],
                           start=(kc == 0), stop=(kc == 7)), [rwres, self.rhcur], [rpsd])
            n = nsl * 64
            k.dve(A("tensor_tensor", out=dtr[:, 0:n].rearrange("p (s c) -> p s c", c=64), in0=psd[:, 0:n].rearrange("p (s c) -> p s c", c=64),
                    in1=self.dtb[:].unsqueeze(1).to_broadcast([128, nsl, 64]), op=ALU.add), [rpsd, self.rdtb], [rdtr])
            k.act(A("activation", out=dtr[:, 0:n], in_=dtr[:, 0:n], func=AF.Exp), [rdtr], [rdtr])
            k.act(A("activation", out=dte[:, 0:n], in_=dtr[:, 0:n], func=AF.Ln, bias=1.0), [rdtr], [rdte])
            if flag0 is not None:
                for s in range(nsl):
                    for d in range(2):
                        f = flag0 + s * 2 + d
                        k.dve(A("tensor_scalar", out=dte[:, s * 64 + d * 32:s * 64 + d * 32 + 32], in0=dte[:, s * 64 + d * 32:s * 64 + d * 32 + 32],
                                scalar1=self.flags[:, f:f + 1], scalar2=None, op0=ALU.mult), [rdte, self.rflags], [rdte])
            k.dve(A("tensor_tensor", out=la[:, 0:n].rearrange("p (s c) -> p s c", c=64), in0=dte[:, 0:n].rearrange("p (s c) -> p s c", c=64),
                    in1=self.negA[:].unsqueeze(1).to_broadcast([128, nsl, 64]), op=ALU.mult), [rdte, self.rnegA], [rla])
            if split:
                return None, None
            return dt_cs(nsl)

        def dt_cs(nsl):
            psc, rpsc = self.PS[7], self.PR[7]
            for s in range(nsl):
                k.pe(A("matmul", psc[:, s * 64:s * 64 + 32], triU, la[:, s * 64:s * 64 + 32], start=True, stop=True), [rla, self.rcst], [rpsc])
                k.pe(A("matmul", psc[:, s * 64 + 32:s * 64 + 64], triL, la[:, s * 64 + 32:s * 64 + 64], start=True, stop=True), [rla, self.rcst], [rpsc])
                k.pe(A("matmul", psc[:, 192 + s * 64:192 + (s + 1) * 64], self.ones[:], la[:, s * 64:(s + 1) * 64], start=True, stop=True),
                     [rla, self.rones], [rpsc])
            return psc, rpsc
        self.dt_chain = dt_chain
        self.PRmap = {id(self.PS[i]): self.PR[i] for i in range(8)}

        def states(nsl, xt_set, bt_set, wtp=None):
            states_x(nsl, xt_set, bt_set, wtp)
            states_mm(nsl, xt_set, bt_set, wtp)

        def states_x(nsl, xt_set, bt_set, wtp=None):
            wt_, rwt = wtp or wts[0]
            for d in range(2):
                for s in range(nsl):
                    (k.pool if s == 2 else k.dve)(A("tensor_tensor", out=xp[d][s][0][:].rearrange("p (h c) -> p h c", c=64),
                             in0=xt_set[s][0][:].rearrange("p (h c) -> p h c", c=64),
                             in1=wt_[:, s * 64 + d * 32:s * 64 + d * 32 + 32].unsqueeze(2).to_broadcast([128, 32, 64]), op=ALU.mult),
                           [xt_set[s][1], rwt], [xp[d][s][1]])

        def states_mm(nsl, xt_set, bt_set, wtp=None):
            for d in range(2):
                for g in range(4):
                    ps, rps = sbank()
                    for s in range(nsl):
                        k.pe(A("matmul", ps[:, 0:512], bt_set[s][0][:, g * 128:(g + 1) * 128], xp[d][s][0][:, g * 512:(g + 1) * 512],
                               start=(s == 0), stop=(s == nsl - 1)), [bt_set[s][1], xp[d][s][1]], [rps])
                    a = self.acc[:, d * 2048 + g * 512:d * 2048 + (g + 1) * 512]
                    k.dve(A("tensor_tensor", out=a, in0=ps[:, 0:512], in1=a, op=ALU.add), [rps, self.racc], [self.racc])

        def states_old(nsl, xt_set, bt_set, wtp=None):
            wt_, rwt = wtp or wts[0]
            for d in range(2):
                for s in range(nsl):
                    (k.pool if s == 2 else k.dve)(A("tensor_tensor", out=xp[d][s][0][:].rearrange("p (h c) -> p h c", c=64),
                             in0=xt_set[s][0][:].rearrange("p (h c) -> p h c", c=64),
                             in1=wt_[:, s * 64 + d * 32:s * 64 + d * 32 + 32].unsqueeze(2).to_broadcast([128, 32, 64]), op=ALU.mult),
                           [xt_set[s][1], rwt], [xp[d][s][1]])
                for g in range(4):
                    ps, rps = sbank()
                    for s in range(nsl):
                        k.pe(A("matmul", ps[:, 0:512], bt_set[s][0][:, g * 128:(g + 1) * 128], xp[d][s][0][:, g * 512:(g + 1) * 512],
                               start=(s == 0), stop=(s == nsl - 1)), [bt_set[s][1], xp[d][s][1]], [rps])
                    a = self.acc[:, d * 2048 + g * 512:d * 2048 + (g + 1) * 512]
                    k.dve(A("tensor_tensor", out=a, in0=ps[:, 0:512], in1=a, op=ALU.add), [rps, self.racc], [self.racc])

        def proj_blocks(nsl, hT3, ncol, slot_off, xt_set, bt_set, hook=None):
            tb = [(self.PS[3 + s], self.PR[3 + s]) for s in range(nsl)]

            def trans(cb, xcw):
                for s in range(nsl):
                    tbb = tb[s][0][:].bitcast(BF16)
                    j = cb % 8
                    k.pe(A("transpose", tbb[:, j * 128:(j + 1) * 128], xcw[0][:, slot_off(s):slot_off(s) + 128], idb), [xcw[1], self.rcstb], [tb[s][1]])
                    if cb == 7:
                        k.dve(A("tensor_copy", out=xt_set[s][0][:, 0:1024], in_=tbb[:, 0:1024]), [tb[s][1]], [xt_set[s][1]])
                    elif cb == 15:
                        k.act(A("activation", out=xt_set[s][0][:, 1024:2048], in_=tbb[:, 0:1024], func=AF.Copy), [tb[s][1]], [xt_set[s][1]])
                    elif cb == 19:
                        k.dve(A("tensor_copy", out=bt_set[s][0][:, 0:512], in_=tbb[:, 0:512]), [tb[s][1]], [bt_set[s][1]])

            pend = []
            for cb in range(20):
                ps, rps = sbank()
                for kc in range(8):
                    k.pe(A("matmul", ps[:, 0:ncol], wres3[:, kc, cb * 128:(cb + 1) * 128], hT3[:, kc, 0:ncol], start=(kc == 0), stop=(kc == 7)),
                         [rwres, self.rhcur], [rps])
                xcw = xc[cb % 4]
                fin = conv_block(ps, ncol - 2, cb, raw[cb % 3], xcw)
                if pend:
                    pend[-1][2]()
                if len(pend) == 2:
                    trans(pend[0][0], pend[0][1])
                    pend.pop(0)
                pend.append((cb, xcw, fin))
                if hook is not None and cb in hook:
                    hook[cb]()
            pend[-1][2]()
            for p_ in pend:
                trans(p_[0], p_[1])

        xo = I["x_oth"]

        def prep_dma(sb):
            for s in range(3):
                xt_, rxt = xts[s]
                k.dma(xt_[:], xo[(sb * 3 + s) * 128:(sb * 3 + s + 1) * 128, :], [], [rxt])
            k.dma(xh[:], I["x_oth_h"][sb * 6:sb * 6 + 6, :], [], [rxh])

        def prep_a(sb, dma=True):
            if dma:
                prep_dma(sb)
            k.norm_a_multi([(xts[s][0], xts[s][1], 128, tmps[s]) for s in range(3)] + [(xh, rxh, 6, tmps[3])])

        def prep_b(sb):
            hT_, rhT = hTo[sb % 2]
            hT3 = hT_[:].rearrange("p (kc c) -> p kc c", kc=8)
            for s in range(3):
                k.norm_b(128, (lambda kc, s=s: hT3[:, kc, s * 130 + 1:s * 130 + 129]), rhT, 0, 0, 0, tmps[s], bankfn=sbank)
            k.norm_b(6, (lambda kc: hT3[:, kc, :].rearrange("p (s t) -> p s t", t=130)[:, :, 0::129]), rhT, 0, 0, 0, tmps[3],
                     src_view=lambda a: a.rearrange("p (s t) -> p s t", t=2), bankfn=sbank)
            hv = self.flags[:, 116 + sb * 6:116 + sb * 6 + 6].rearrange("p (s t) -> p s t", t=2)
            for kc in range(8):
                hh = hT3[:, kc, :].rearrange("p (s t) -> p s t", t=130)[:, :, 0::129]
                k.dve(A("tensor_tensor", out=hh, in0=hh, in1=hv, op=ALU.mult), [rhT, self.rflags], [rhT])

        def nb_tile(sb, s):
            hT_, rhT = hTo[sb % 2]
            hT3 = hT_[:].rearrange("p (kc c) -> p kc c", kc=8)
            if s < 3:
                k.norm_b(128, (lambda kc, s=s: hT3[:, kc, s * 130 + 1:s * 130 + 129]), rhT, 0, 0, 0, tmps[s], bankfn=sbank,
                         dst3=(hT3[:, :, s * 130 + 1:s * 130 + 129] if s != 1 else None))
            else:
                k.norm_b(6, (lambda kc: hT3[:, kc, :].rearrange("p (s t) -> p s t", t=130)[:, :, 0::129]), rhT, 0, 0, 0, tmps[3],
                         src_view=lambda a: a.rearrange("p (s t) -> p s t", t=2), bankfn=sbank)
                hv = self.flags[:, 116 + sb * 6:116 + sb * 6 + 6].rearrange("p (s t) -> p s t", t=2)
                for kc in range(8):
                    hh = hT3[:, kc, :].rearrange("p (s t) -> p s t", t=130)[:, :, 0::129]
                    k.dve(A("tensor_tensor", out=hh, in0=hh, in1=hv, op=ALU.mult), [rhT, self.rflags], [rhT])

        def dt1(sb):
            hT_, rhT = hTo[sb % 2]
            hT3 = hT_[:].rearrange("p (kc c) -> p kc c", kc=8)
            old_ = getattr(self, "rhcur", None)
            self.rhcur = rhT
            dt_chain(3, hT3, 1, 130, 20 + sb * 6, split=True)
            self.rhcur = old_

        def dt2(sb):
            wtp = wts[sb % 2]
            psc, rpsc = dt_cs(3)
            for s in range(3):
                k.dve(A("tensor_tensor", out=logP[:], in0=psc[:, 192 + s * 64:192 + (s + 1) * 64], in1=logP[:], op=ALU.add), [rpsc, rlogP], [rlogP])
                k.dve(A("tensor_tensor", out=arg[:, s * 64:(s + 1) * 64], in0=logP[:], in1=psc[:, s * 64:(s + 1) * 64], op=ALU.subtract),
                      [rpsc, rlogP], [rarg])
            k.act(A("activation", out=arg[:], in_=arg[:], func=AF.Exp), [rarg], [rarg])
            k.dve(A("tensor_tensor", out=wtp[0][:], in0=arg[:], in1=dte[:], op=ALU.mult), [rarg, rdte], [wtp[1]])

        prep_a(0)
        for s in range(4):
            nb_tile(0, s)
        dt1(0)
        prep_a(1)
        prev = None
        for sb in range(NSB):
            hT_, rhT = hTo[sb % 2]
            hT3 = hT_[:].rearrange("p (kc c) -> p kc c", kc=8)
            self.mod_ct(8 + sb)
            xt_set = xtok[sb % 2]; bt_set = btok[sb % 2]
            def h1(sb=sb):
                dt2(sb)
                if sb + 2 < NSB:
                    prep_dma(sb + 2)
            hooks = {1: h1}
            if prev is not None:
                hooks[2] = (lambda p=prev: states_x(3, *p))
                hooks[6] = (lambda p=prev: states_mm(3, *p))
            if sb + 1 < NSB:
                for s in range(4):
                    hooks[8 + 2 * s] = (lambda sb=sb, s=s: nb_tile(sb + 1, s))
                hooks[15] = (lambda sb=sb: dt1(sb + 1))
            if sb + 2 < NSB:
                hooks[17] = (lambda sb=sb: prep_a(sb + 2, dma=False))
            self.rhcur = rhT
            proj_blocks(3, hT3, 390, lambda s: s * 130, xt_set, bt_set, hook=hooks)
            prev = (xt_set, bt_set, wts[sb % 2])
            if sb == 0:
                k.dump("hTo0", hT_[:], rhT, [128, 8 * 390], BF16)
                k.dump("xtok0", xt_set[0][0][:], xt_set[0][1], [128, 2048], BF16)
        states(3, *prev)

        k.dve(A("memset", hTc[:], 0.0), [], [rhTc])
        hc3 = hTc[:].rearrange("p (kc c) -> p kc c", kc=8)
        self.rhcur = rhTc
        for s in range(2):
            xt_, rxt = xts[s]
            k.dma(xt_[:], I["ctx"][s * 128:(s + 1) * 128, :], [], [rxt])
            k.norm_T(xt_, rxt, 128, (lambda kc, s=s: hc3[:, kc, 1 + s * 128:1 + (s + 1) * 128]), rhTc, 1, 0, 1, tmps[s])
        xt_set = xtok[0]; bt_set = btok[0]
        proj_blocks(2, hc3, 258, lambda s: s * 128, xt_set, bt_set)
        psc, rpsc = dt_chain(2, hc3, 1, 128, None)
        tot0 = psc[:, 192:256]; tot1 = psc[:, 256:320]
        k.dve(A("tensor_tensor", out=L1[:, 0:64], in0=tot1, in1=logP[:], op=ALU.add), [rpsc, rlogP], [rL1])
        k.dve(A("tensor_tensor", out=L1[:, 64:128], in0=tot0, in1=logP[:], op=ALU.add), [rpsc, rlogP], [rL1])
        k.dve(A("tensor_tensor", out=L1[:, 128:192], in0=tot0, in1=L1[:, 0:64], op=ALU.add), [rpsc, rL1], [rL1])
        for (s, d, src) in ((1, 0, 0), (0, 0, 128), (0, 1, 64), (1, 1, 128)):
            k.dve(A("tensor_tensor", out=arg[:, s * 64 + d * 32:s * 64 + d * 32 + 32], in0=L1[:, src + d * 32:src + d * 32 + 32],
                    in1=psc[:, s * 64 + d * 32:s * 64 + d * 32 + 32], op=ALU.subtract), [rpsc, rL1], [rarg])
        k.act(A("activation", out=arg[:, 0:128], in_=arg[:, 0:128], func=AF.Exp), [rarg], [rarg])
        k.dve(A("tensor_tensor", out=wt_[:, 0:128], in0=arg[:, 0:128], in1=dte[:, 0:128], op=ALU.mult), [rarg, rdte], [rwt])
        states(2, xt_set, bt_set)
        for g in range(4):
            ps, rps = sbank()
            for kc in range(8):
                k.pe(A("matmul", ps[0:64, 0:256], wkv3[:, kc, g * 64:(g + 1) * 64], hc3[:, kc, 1:257], start=(kc == 0), stop=(kc == 7)),
                     [rwkv, rhTc], [rps])
            k.act(A("activation", out=self.kcT[:, g * 256:(g + 1) * 256], in_=ps[0:64, 0:256], func=AF.Copy), [rps], [self.rkcT])
        va = self.vca[:].rearrange("p (s g c) -> p s g c", s=2, g=4)
        for s in range(2):
            ps, rps = sbank()
            for kc in range(8):
                k.pe(A("matmul", ps[:, 0:256], hc3[:, kc, 1 + s * 128:1 + (s + 1) * 128], wkv3[:, kc, 256:512], start=(kc == 0), stop=(kc == 7)),
                     [rwkv, rhTc], [rps])
            k.act(A("activation", out=va[:, s, :, 0:64], in_=ps[:, 0:256].rearrange("p (g c) -> p g c", c=64), func=AF.Copy), [rps], [self.rvca])
        m3 = self.modT[:].rearrange("p (b v) -> p b v", v=2)
        g3 = self.gs[:].rearrange("p (kc v) -> p kc v", v=4)
        k.dve(A("scalar_tensor_tensor", out=g3[:, :, 2], in0=m3[:, 32:40, 0], scalar=1.0, in1=self.n12[:, 8:16],
                op0=ALU.add, op1=ALU.mult), [self.rmodT2, self.rn12], [self.rgs])
        k.dump("acc", self.acc[:], self.racc, [128, 4096])
        self.S.flush()
        es.close()


def _rep(v, n=128):
    v = np.asarray(v, np.float32).reshape(1, -1)
    return np.ascontiguousarray(np.broadcast_to(v, (n, v.shape[1])))


def _fm(v, nblk):
    return np.ascontiguousarray(np.asarray(v, np.float32).reshape(nblk, 128).T)


def prep_inputs(inp):
    x = np.asarray(inp["x"], np.float32)
    ctx = np.asarray(inp["ctx"], np.float32)
    w_in = np.ascontiguousarray(np.asarray(inp["w_in"], np.float32)[0])
    d = np.arange(64)
    within = d % 32
    partner = np.where(within < 16, d + 16, d - 16)
    qk = np.concatenate([Q0 + h * 64 + partner for h in range(16)] + [K0 + h * 64 + partner for h in range(4)])
    w_qkp = np.ascontiguousarray(w_in[:, qk])
    consts = np.zeros((128, 3, 128), np.float32)
    ii = np.arange(128)
    consts[:, 0] = (ii[:, None] == ii[None, :])
    consts[:, 1] = (ii[:, None] <= ii[None, :])
    consts[:, 2] = (ii[:, None] >= ii[None, :])
    inv = (10000.0 ** (-np.arange(0, 32, 2, dtype=np.float32) / 32)).astype(np.float32)
    j = within % 16
    sgn = np.where(within < 16, -1.0, 1.0).astype(np.float32)
    shared = dict(
        w_mod=np.ascontiguousarray(np.asarray(inp["w_mod"], np.float32)[0]),
        bmodT=_fm(inp["b_mod"][0], 48),
        n1g=_fm(inp["norm1_g"][0], 8), n2g=_fm(inp["norm2_g"][0], 8), fg_row=_rep(inp["final_g"]),
        w_in=w_in, w_qkp=w_qkp,
        ssd_cw=np.ascontiguousarray(np.asarray(inp["ssd_conv_w"], np.float32)[0].reshape(3, 24, 128).transpose(2, 1, 0).reshape(128, 72)),
        ssd_cb=_fm(inp["ssd_conv_b"][0], 24),
        dtb=_rep(np.asarray(inp["ssd_dt_bias"])[0].reshape(-1)), alog=_rep(np.asarray(inp["ssd_a_log"])[0].reshape(-1)),
        dsk=_rep(inp["ssd_d"][0]), sng=_fm(inp["ssd_norm_g"][0], 16),
        w_o_ssd=np.ascontiguousarray(np.asarray(inp["w_o_ssd"], np.float32)[0]),
        w_o_att=np.ascontiguousarray(np.asarray(inp["w_o_att"], np.float32)[0]),
        sink=_rep(inp["att_sink"][0]),
        w_out=np.ascontiguousarray(np.asarray(inp["w_out"], np.float32)[0]),
        w_up=np.ascontiguousarray(np.asarray(inp["w_up"], np.float32)[0]),
        ffn_cw=np.ascontiguousarray(np.asarray(inp["ffn_conv_w"], np.float32)[0].reshape(3, 44, 128).transpose(2, 1, 0).reshape(128, 132)),
        ffn_cb=_fm(inp["ffn_conv_b"][0], 44),
        w_down=np.ascontiguousarray(np.asarray(inp["w_down"], np.float32)[0]),
        consts=consts.reshape(128, 384),
    )
    maps = []
    zero_row = np.zeros((1, D), np.float32)
    for core in range(8):
        b, q = core // 4, core % 4
        xb = x[b]
        t0 = q * TOK - 256
        x_own = np.zeros((NOWN * 128, D), np.float32)
        lo, hi = max(t0, 0), min(t0 + NOWN * 128, L)
        x_own[lo - t0:hi - t0] = xb[lo:hi]
        flags = np.zeros((NFLAG,), np.float32)
        for t in range(NOWN):
            flags[t] = 1.0 if 0 <= t0 + t * 128 < L else 0.0
        before = list(range(16 * q - 2, -1, -1))
        after = list(range(16 * q + 17, 64))
        slots = [(c, 0) for c in before] + [(c, 1) for c in after]
        assert len(slots) <= NSLOT
        x_oth = np.zeros((NSLOT * 128, D), np.float32)
        x_oth_h = np.zeros((2 * NSLOT, D), np.float32)
        for s, (c, side) in enumerate(slots):
            x_oth[s * 128:(s + 1) * 128] = xb[c * 128:(c + 1) * 128]
            flags[20 + 2 * s + side] = 1.0
            if c * 128 - 1 >= 0:
                x_oth_h[2 * s] = xb[c * 128 - 1]; flags[116 + 2 * s] = 1.0
            if c * 128 + 128 < L:
                x_oth_h[2 * s + 1] = xb[c * 128 + 128]; flags[116 + 2 * s + 1] = 1.0
        cv = np.zeros((128, 16), np.float32)
        cv[:, 0::2] = np.asarray(inp["c"], np.float32)[b].reshape(8, 128).T
        cv[:, 1::2] = np.asarray(inp["c_ctx"], np.float32).reshape(8, 128).T
        tpos = t0 + np.arange(NOWN * 128)
        pos = np.where((d < 32)[:, None], (tpos // 64)[None, :], (tpos % 64)[None, :]).astype(np.float32)
        ang = (pos * inv[j][:, None]).astype(np.float32)
        m = dict(shared)
        m.update(x_own=x_own, x_oth=x_oth, x_oth_h=x_oth_h, ctx=np.ascontiguousarray(ctx[b]), cvec=cv, flags=_rep(flags),
                 cosT=np.cos(ang).astype(np.float32), sinT=(np.sin(ang) * sgn[:, None]).astype(np.float32))
        maps.append(m)
    return maps


_NC_CACHE = {}


def kernel(**inputs):
    kb = K()
    nc = kb.build()
    maps = prep_inputs(inputs)
    cores = [int(c) for c in os.environ.get("KCORES", "0,1,2,3,4,5,6,7").split(",")]
    res = run_bass_kernel_spmd(nc, [maps[c] for c in cores], core_ids=list(range(len(cores))))
    out = np.zeros((2, L, D), np.float32)
    for i, core in enumerate(cores):
        b, q = core // 4, core % 4
        out[b, q * TOK:(q + 1) * TOK] = res.results[i]["out"]
    if DEBUG:
        kernel.dbg = {core: {n: res.results[i][v] for n, v in kb.dbg.items()} for i, core in enumerate(cores)}
    return out


def _p2(self):
    pass


def conv3(k, ps, rps, n, w3, bcol, rw, out_ap, rout, func, wregs, defer=False):
    (r_, rr_) = rw
    k.act(A("activation", out=r_[:, 0:n], in_=ps[:, 1:n + 1], func=AF.Identity, scale=w3[:, 1:2], bias=bcol), [rps] + wregs, [rr_])
    k.dve(A("scalar_tensor_tensor", out=r_[:, 0:n], in0=ps[:, 0:n], scalar=w3[:, 0:1], in1=r_[:, 0:n], op0=ALU.mult, op1=ALU.add),
          [rps, rr_] + wregs, [rr_])
    k.dve(A("scalar_tensor_tensor", out=r_[:, 0:n], in0=ps[:, 2:n + 2], scalar=w3[:, 2:3], in1=r_[:, 0:n], op0=ALU.mult, op1=ALU.add),
          [rps, rr_] + wregs, [rr_])
    fin = lambda: k.act(A("activation", out=out_ap, in_=r_[:, 0:n], func=func), [rr_], [rout])
    if defer:
        return fin
    fin()


def phase_own_h(self):
    k = self; I = self.I; es = self.arW
    self.hT, self.rhT = k.T("hT", [128, 8 * 2560], BF16, es=self.arHT)
    hT3 = self.hT[:].rearrange("p (kc c) -> p kc c", kc=8)
    self.hT3 = hT3
    xts = [k.T("xo%d" % i, [128, D], es=es) for i in range(8)]
    junk_ = k.T("junk", [128, D], BF16, es=es)
    tmps = [(junk_, k.T("ss%d" % i, [128, 4], es=es), k.T("xn%d" % i, [128, D], BF16, es=es)) for i in range(8)]

    def ld(gp):
        for t in range(gp * 4, gp * 4 + 4):
            k.dma(xts[t % 8][0][:], I["x_own"][t * 128:(t + 1) * 128, :], [], [xts[t % 8][1]])

    def st_a(gp):
        k.norm_a_multi([(xts[t % 8][0], xts[t % 8][1], 128, tmps[t % 8]) for t in range(gp * 4, gp * 4 + 4)])

    def st_b(t):
        k.norm_b(128, (lambda kc, t=t: hT3[:, kc, t * 128:(t + 1) * 128]), self.rhT, 0, 0, 0, tmps[t % 8])
        if t in (0, 1, 18, 19):
            for kc in range(8):
                k.dve(A("tensor_scalar", out=hT3[:, kc, t * 128:(t + 1) * 128], in0=hT3[:, kc, t * 128:(t + 1) * 128],
                        scalar1=self.flags[:, t:t + 1], scalar2=None, op0=ALU.mult), [self.rhT, self.rflags], [self.rhT])

    ld(0); ld(1)
    st_a(0)
    for gp in range(5):
        if gp + 1 < 5:
            st_a(gp + 1)
        if gp + 2 < 5:
            ld(gp + 2)
        for t in range(gp * 4, gp * 4 + 4):
            st_b(t)
    self.S.flush()
    es.close()


def phase_ssd(self):
    k = self; I = self.I; es = self.arW
    hT3 = self.hT3
    self.yz, _ = k.T("yz", [128, 18 * 2048], BF16, es=self.arYZ)
    self.ry = [Reg("y%d" % c) for c in range(18)]
    yz3 = self.yz[:].rearrange("p (c f) -> p c f", c=18)
    stage = k.T("stg", [128, 8 * 128], es=es)
    wbf = [k.T("wbf%d" % i, [128, 8 * 128], BF16, es=es) for i in range(2)]
    wdt, rwdt = k.T("wdt", [128, 8 * 64], BF16, es=es)
    wdt3 = wdt[:].rearrange("p (kc c) -> p kc c", kc=8)
    wdg, rwdg = k.T("wdg", [128, 8 * 16], BF16, es=es)
    wdg3 = wdg[:].rearrange("p (kc c) -> p kc c", kc=8)
    dbg_, rdbg = k.T("dbg_", [128, 16], es=es)
    nag_, rnag = k.T("nag_", [128, 16], es=es)
    xcg, rxcg = k.T("xcg", [128, 2304], BF16, es=es)
    raw = [k.T("raw%d" % i, [128, 512], es=es) for i in range(2)]
    xtok, _ = k.T("xtok", [128, 18 * 512], BF16, es=es)
    xtok3 = xtok[:].rearrange("p (c f) -> p c f", c=18)
    rxt = Reg("xtok")
    btok, rbt = k.T("btok", [128, 18 * 128], BF16, es=es)
    btok3 = btok[:].rearrange("p (c f) -> p c f", c=18)
    BT, rBT = k.T("BT", [128, 2304], BF16, es=es)
    CT, rCT = k.T("CT", [128, 2304], BF16, es=es)
    dte, rdte = k.T("dte", [128, 288], es=es)
    la, rla = k.T("la", [128, 288], es=es)
    dtr, rdtr = la, rla
    csx, rcsx = k.T("csx", [128, 288], es=es)
    wgt, rwgt = k.T("wgt", [128, 288], es=es)
    ecs, recs = k.T("ecs", [128, 288], es=es)
    dtot, rdtot = k.T("dtot", [128, 288], es=es)
    St, rSt = k.T("St", [128, 512], es=es)
    Sbf, rSbf = k.T("Sbf", [128, 512], BF16, es=es)
    xpd = [k.T("xpd%d" % i, [128, 512], BF16, es=es) for i in range(2)]
    CB, rCB = k.T("CB", [128, 128], es=es)
    Em = [k.T("Em%d" % i, [128, 512], BF16, es=es) for i in range(2)]
    ncs, rncs = k.T("ncs", [128, 288], es=es)
    Mt, rMt = k.T("Mt", [128, 8 * 128], BF16, es=es)
    tt, rtt = k.T("tt", [128, 512], es=es)
    uu, ruu = k.T("uu", [128, 512], BF16, es=es)
    idf = self.cst[:, 0:128]; triU = self.cst[:, 128:256]; triL = self.cst[:, 256:384]
    idb = self.cstb[:, 0:128]
    cw3 = self.scw[:].rearrange("p (b t) -> p b t", t=3)
    rot = [0]

    def rbank():
        i = rot[0]; rot[0] = (i + 1) % 2
        return self.PS[i], self.PR[i]

    k.load_w(stage, (wdt3, rwdt), I["w_in"], DT0, 64)
    tiles = [(128 + 510 * t, min(510, 2432 - (128 + 510 * t))) for t in range(5)]
    flg = self.flags[:, 1:19].unsqueeze(2).to_broadcast([128, 18, 16])

    def v3(ap):
        return ap.rearrange("p (c j) -> p c j", j=16)

    for g in range(4):
        blocks = [(XBC0 + (g * 4 + i) * 128, g * 4 + i, "x", i) for i in range(4)]
        blocks += [(XBC0 + 2048 + g * 128, 16 + g, "B", 0), (XBC0 + 2560 + g * 128, 20 + g, "C", 0)]
        for bi, (c0, cb, kind, i) in enumerate(blocks):
            wb, rwb = wbf[bi % 2]
            wb3 = wb[:].rearrange("p (kc c) -> p kc c", kc=8)
            k.load_w(stage, (wb3, rwb), I["w_in"], c0, 128)
            dst, rdst = {"x": (xcg, rxcg), "B": (BT, rBT), "C": (CT, rCT)}[kind]
            pendf = None
            for ti, (o0, n) in enumerate(tiles):
                ps, rps = rbank()
                for kc in range(8):
                    k.pe(A("matmul", ps[:, 0:n + 2], wb3[:, kc, :], hT3[:, kc, o0 - 1:o0 + n + 1], start=(kc == 0), stop=(kc == 7)),
                         [rwb, self.rhT], [rps])
                f_ = conv3(k, ps, rps, n, cw3[:, cb, :], self.scb[:, cb:cb + 1], raw[ti % 2], dst[:, o0 - 128:o0 - 128 + n], rdst, AF.Silu,
                           [self.rscw, self.rscb], defer=True)
                if pendf is not None:
                    pendf()
                pendf = f_
            pendf()
            if kind in ("x", "B"):
                for c8 in range(0, 18, 8):
                    nb = min(8, 18 - c8)
                    tb, rtb = self.PS[2 + (c8 // 8) % 2], self.PR[2 + (c8 // 8) % 2]
                    tbb = tb[:].bitcast(BF16)
                    for j in range(nb):
                        c = c8 + j
                        k.pe(A("transpose", tbb[:, j * 128:(j + 1) * 128], dst[:, c * 128:(c + 1) * 128], idb), [rdst, self.rcstb], [rtb])
                    if kind == "x":
                        k.act(A("activation", out=xtok3[:, c8:c8 + nb, i * 128:(i + 1) * 128],
                                in_=tbb[:, 0:nb * 128].rearrange("p (c f) -> p c f", f=128), func=AF.Copy), [rtb], [rxt])
                    else:
                        k.act(A("activation", out=btok3[:, c8:c8 + nb, :],
                                in_=tbb[:, 0:nb * 128].rearrange("p (c f) -> p c f", f=128), func=AF.Copy), [rtb], [rbt])
        if os.environ.get("KSSD") == "1":
            break
        for d in range(2):
            k.pool(A("tensor_copy", out=wdg3[:, :, d * 8:(d + 1) * 8], in_=wdt3[:, :, d * 32 + g * 8:d * 32 + g * 8 + 8]), [rwdt], [rwdg])
            k.dve(A("tensor_copy", out=dbg_[:, d * 8:(d + 1) * 8], in_=self.dtb[:, d * 32 + g * 8:d * 32 + g * 8 + 8]), [self.rdtb], [rdbg])
            k.dve(A("tensor_copy", out=nag_[:, d * 8:(d + 1) * 8], in_=self.negA[:, d * 32 + g * 8:d * 32 + g * 8 + 8]), [self.rnegA], [rnag])
        psd, rpsd = rbank()
        for c in range(18):
            for kc in range(8):
                k.pe(A("matmul", psd[:, c * 16:(c + 1) * 16], hT3[:, kc, (c + 1) * 128:(c + 2) * 128], wdg3[:, kc, :],
                       start=(kc == 0), stop=(kc == 7)), [rwdg, self.rhT], [rpsd])
        k.dve(A("tensor_tensor", out=v3(dtr[:]), in0=v3(psd[:, 0:288]), in1=dbg_[:].unsqueeze(1).to_broadcast([128, 18, 16]), op=ALU.add),
              [rpsd, rdbg], [rdtr])
        k.act(A("activation", out=dtr[:], in_=dtr[:], func=AF.Exp), [rdtr], [rdtr])
        k.act(A("activation", out=dte[:], in_=dtr[:], func=AF.Ln, bias=1.0), [rdtr], [rdte])
        for c in (0, 17):
            k.dve(A("tensor_scalar", out=dte[:, c * 16:(c + 1) * 16], in0=dte[:, c * 16:(c + 1) * 16], scalar1=self.flags[:, 1 + c:2 + c],
                    scalar2=None, op0=ALU.mult), [rdte, self.rflags], [rdte])
        if os.environ.get("KSSD") == "2a":
            break
        k.dve(A("tensor_tensor", out=v3(la[:]), in0=v3(dte[:]), in1=nag_[:].unsqueeze(1).to_broadcast([128, 18, 16]), op=ALU.mult),
              [rdte, rnag], [rla])
        psU, rpsU = rbank()
        k.pe(A("matmul", psU[:, 0:288], triU, la[:], start=True, stop=True), [rla, self.rcst], [rpsU])
        psL, rpsL = self.PS[2], self.PR[2]
        k.pe(A("matmul", psL[:, 0:288], triL, la[:], start=True, stop=True), [rla, self.rcst], [rpsL])
        k.act(A("activation", out=v3(csx[:])[:, :, 0:8], in_=v3(psU[:, 0:288])[:, :, 0:8], func=AF.Copy), [rpsU], [rcsx])
        k.act(A("activation", out=v3(csx[:])[:, :, 8:16], in_=v3(psL[:, 0:288])[:, :, 8:16], func=AF.Copy), [rpsL], [rcsx])
        if os.environ.get("KSSD") == "2b":
            break
        pst, rpst = rbank()
        k.pe(A("matmul", pst[:, 0:288], self.ones[:], la[:], start=True, stop=True), [rla, self.rones], [rpst])
        if os.environ.get("KSSD") == "2c":
            break
        k.act(A("activation", out=ecs[:], in_=csx[:], func=AF.Exp), [rcsx], [recs])
        if os.environ.get("KSSD") == "2d":
            break
        k.act(A("activation", out=dtot[:], in_=pst[:, 0:288], func=AF.Exp), [rpst], [rdtot])
        if os.environ.get("KSSD") == "2e":
            break
        k.dve(A("tensor_tensor", out=wgt[:], in0=pst[:, 0:288], in1=csx[:], op=ALU.subtract), [rpst, rcsx], [rwgt])
        k.act(A("activation", out=wgt[:], in_=wgt[:], func=AF.Exp), [rwgt], [rwgt])
        k.dve(A("tensor_tensor", out=wgt[:], in0=wgt[:], in1=dte[:], op=ALU.mult), [rwgt, rdte], [rwgt])
        if g == 0:
            k.dump("dte", dte[:], rdte, [128, 288]); k.dump("csx", csx[:], rcsx, [128, 288])
            k.dump("xtokg", xtok[:], rxt, [128, 18 * 512], BF16); k.dump("BTg", BT[:], rBT, [128, 2304], BF16)
        if os.environ.get("KSSD") == "2":
            break
        k.dve(A("tensor_scalar", out=ncs[:], in0=dte[:], scalar1=1e-30, scalar2=None, op0=ALU.max), [rdte], [rncs])
        k.act(A("activation", out=ncs[:], in_=ncs[:], func=AF.Ln), [rncs], [rncs])
        k.dve(A("tensor_tensor", out=ncs[:], in0=ncs[:], in1=csx[:], op=ALU.subtract), [rncs, rcsx], [rncs])
        h3 = lambda ap: ap.rearrange("p (h c) -> p h c", c=64)
        for d in (1, 0):
            a0 = d * 2048 + g * 512
            k.dve(A("tensor_copy", out=St[:], in_=self.acc[:, a0:a0 + 512]), [self.racc], [rSt])
            mask = triL if d == 1 else triU
            order = range(17, -1, -1) if d == 1 else range(18)
            order = list(order)

            def stA(c):
                cs_ = slice(c * 128, (c + 1) * 128)
                q8 = slice(c * 16 + d * 8, c * 16 + d * 8 + 8)
                xp_, rxp = xpd[c % 2]
                k.pool(A("tensor_tensor", out=h3(xp_[:]), in0=h3(xtok3[:, c, :]), in1=wgt[:, q8].unsqueeze(2).to_broadcast([128, 8, 64]), op=ALU.mult),
                       [rxt, rwgt], [rxp])
                for r4 in range(0, 8, 4):
                    pb, rpb = self.PS[2 + (r4 // 4)], self.PR[2 + (r4 // 4)]
                    for j in range(4):
                        col = c * 16 + d * 8 + r4 + j
                        k.pe(A("matmul", pb[:, j * 128:(j + 1) * 128], csx[:, col:col + 1].to_broadcast([128, 128]), idf, start=True, stop=False),
                             [rcsx, self.rcst], [rpb])
                        k.pe(A("matmul", pb[:, j * 128:(j + 1) * 128], idf, ncs[:, col:col + 1].to_broadcast([128, 128]), start=False, stop=True),
                             [rncs, self.rcst], [rpb])
                pss, rpss = self.PS[6 + (c % 2)], self.PR[6 + (c % 2)]
                k.pe(A("matmul", pss[:, 0:512], btok3[:, c, :], xp_[:], start=True, stop=True), [rbt, rxp], [rpss])
                pcb, rpcb = rbank()
                k.pe(A("matmul", pcb[:, 0:128], BT[:, cs_], CT[:, cs_], start=True, stop=True), [rBT, rCT], [rpcb])
                return pcb, rpcb

            def stB(c, pcb, rpcb):
                k.dve(A("tensor_tensor", out=CB[:], in0=pcb[:, 0:128], in1=mask, op=ALU.mult), [rpcb, self.rcst], [rCB])
                for ri, r4 in enumerate(range(0, 8, 4)):
                    pb, rpb = self.PS[2 + ri], self.PR[2 + ri]
                    em, rem = Em[ri]
                    k.act(A("activation", out=em[:], in_=pb[:, 0:512], func=AF.Exp), [rpb], [rem])
                    k.dve(A("scalar_tensor_tensor", out=Mt[:, r4 * 128:(r4 + 4) * 128].rearrange("p (h c) -> p h c", c=128),
                            in0=em[:].rearrange("p (h c) -> p h c", c=128), scalar=100.0, in1=CB[:].unsqueeze(1).to_broadcast([128, 4, 128]),
                            op0=ALU.min, op1=ALU.mult), [rem, rCB], [rMt])

            def stC(c):
                cs_ = slice(c * 128, (c + 1) * 128)
                q8 = slice(c * 16 + d * 8, c * 16 + d * 8 + 8)
                pss, rpss = self.PS[6 + (c % 2)], self.PR[6 + (c % 2)]
                py, rpy = self.PS[4], self.PR[4]
                for r in range(8):
                    k.pe(A("matmul", py[:, r * 64:(r + 1) * 64], Mt[:, r * 128:(r + 1) * 128], xtok3[:, c, r * 64:(r + 1) * 64],
                           start=True, stop=True), [rMt, rxt], [rpy])
                k.act(A("activation", out=Sbf[:], in_=St[:], func=AF.Copy), [rSt], [rSbf])
                po, rpo = self.PS[5], self.PR[5]
                k.pe(A("matmul", po[:, 0:512], CT[:, cs_], Sbf[:], start=True, stop=True), [rCT, rSbf], [rpo])
                t8 = dtot[:, q8].unsqueeze(2).to_broadcast([128, 8, 64])
                k.pool(A("tensor_tensor", out=h3(St[:]), in0=h3(St[:]), in1=t8, op=ALU.mult), [rSt, rdtot, rSbf], [rSt])
                k.dve(A("tensor_tensor", out=St[:], in0=St[:], in1=pss[:, 0:512], op=ALU.add), [rSt, rpss], [rSt])
                e8 = ecs[:, q8].unsqueeze(2).to_broadcast([128, 8, 64])
                k.dve(A("tensor_tensor", out=h3(tt[:]), in0=h3(po[:, 0:512]), in1=e8, op=ALU.mult), [rpo, recs], [rtt])
                ydst = yz3[:, c, g * 512:(g + 1) * 512]
                if d == 1:
                    k.dve(A("tensor_tensor", out=ydst, in0=tt[:], in1=py[:, 0:512], op=ALU.add), [rtt, rpy], [self.ry[c]])
                else:
                    d8 = self.dsk[:, g * 8:(g + 1) * 8].unsqueeze(2).to_broadcast([128, 8, 64])
                    k.pool(A("tensor_tensor", out=h3(uu[:]), in0=h3(xtok3[:, c, :]), in1=d8, op=ALU.mult), [rxt, self.rdsk], [ruu])
                    k.pool(A("tensor_tensor", out=uu[:], in0=uu[:], in1=ydst, op=ALU.add), [ruu, self.ry[c]], [ruu])
                    k.dve(A("tensor_tensor", out=tt[:], in0=tt[:], in1=py[:, 0:512], op=ALU.add), [rtt, rpy], [rtt])
                    k.dve(A("tensor_tensor", out=ydst, in0=tt[:], in1=uu[:], op=ALU.add), [rtt, ruu], [self.ry[c]])

            pc = stA(order[0])
            stB(order[0], *pc)
            for ci, c in enumerate(order):
                nxt = order[ci + 1] if ci + 1 < len(order) else None
                if nxt is not None:
                    pc = stA(nxt)
                stC(c)
                if nxt is not None:
                    stB(nxt, *pc)
    k.dump("yz", self.yz[:], self.ry, [128, 18 * 2048], BF16)
    self.S.flush()
    es.close()


K.phase_own_h = phase_own_h
K.phase_ssd = phase_ssd


def row_bcast(k, dst, rdst, colsT, rcols, nblk):
    for b0 in range(0, nblk, 4):
        ps, rps = k.bank()
        for b in range(b0, min(b0 + 4, nblk)):
            k.pe(A("transpose", ps[:, (b - b0) * 128:(b - b0 + 1) * 128], colsT(b).to_broadcast([128, 128]), k.cst[:, 0:128]),
                 [rcols, k.rcst], [rps])
        n = min(4, nblk - b0)
        k.act(A("activation", out=dst[:, b0 * 128:(b0 + n) * 128], in_=ps[:, 0:n * 128], func=AF.Copy), [rps], [rdst])


def phase_ssd_proj(self):
    k = self; I = self.I
    hT3 = self.hT3
    arW = self.arW; arA = self.arACC
    arA.close()
    self.mg, self.rmg = k.T("merged", [128, 8 * 2304], BF16, es=arW)
    mg3 = self.mg[:].rearrange("p (kc t) -> p kc t", kc=8)
    self.mg3 = mg3
    SUP = [(0, 4), (4, 4), (8, 4), (12, 4), (16, 2)]
    ryT = [Reg("yT%d" % i) for i in range(5)]
    tmpT, rtmpT = k.T("tmpT", [128, 16 * 512], BF16, es=arW)
    tmpT3 = tmpT[:].rearrange("p (j t) -> p j t", j=16)
    stage = k.T("stg", [128, 16 * 128], es=arW)
    wbf = [k.T("wbf%d" % i, [128, 8 * 128], BF16, es=arW) for i in range(2)]
    wpb, rwpb = k.T("wpb", [128, 16 * 128], BF16, es=arW)
    rstd = [k.T("rstd%d" % i, [128, 512], es=arA) for i in range(5)]
    szb = [k.T("szb%d" % i, [128, 512], BF16, es=arA) for i in range(2)]
    sqb = [k.T("sqb%d" % i, [128, 512], BF16, es=arA) for i in range(2)]
    sg, rsg = k.T("sg", [128, 512], es=arA)
    idb = self.cstb[:, 0:128]
    onesb = self.cstb[:, 128:256]
    ob, rob = k.T("onesb", [128, 128], BF16)
    k.dve(A("tensor_copy", out=ob[:], in_=self.ones[:]), [self.rones], [rob])
    yzf = self.yz

    def yT(st):
        n = SUP[st][1] * 128
        return yzf[:, st * 8192:st * 8192 + 16 * n].rearrange("p (j t) -> p j t", j=16), n

    yz3 = self.yz[:].rearrange("p (c f) -> p c f", c=18)
    for st, (c0, nc_) in enumerate(SUP):
        n = nc_ * 128
        for ci in range(nc_):
            for j8 in range(2):
                tb, rtb = k.bank()
                tbb = tb[:].bitcast(BF16)
                for j in range(8):
                    k.pe(A("transpose", tbb[:, j * 128:(j + 1) * 128], yz3[:, c0 + ci, (j8 * 8 + j) * 128:(j8 * 8 + j + 1) * 128], idb),
                         [self.ry[c0 + ci], self.rcstb], [rtb])
                k.act(A("activation", out=tmpT3[:, j8 * 8:j8 * 8 + 8, ci * 128:(ci + 1) * 128],
                        in_=tbb[:, 0:1024].rearrange("p (j t) -> p j t", j=8), func=AF.Copy), [rtb], [rtmpT])
        v, _ = yT(st)
        k.dve(A("tensor_copy", out=v[:, 0:8, :], in_=tmpT3[:, 0:8, 0:n]), [rtmpT] + [self.ry[c0 + ci] for ci in range(nc_)], [ryT[st]] + [self.ry[c0 + ci] for ci in range(nc_)])
        k.act(A("activation", out=v[:, 8:16, :], in_=tmpT3[:, 8:16, 0:n], func=AF.Copy), [rtmpT] + [self.ry[c0 + ci] for ci in range(nc_)], [ryT[st]] + [self.ry[c0 + ci] for ci in range(nc_)])
    accb = [(self.PS[3 + st], self.PR[3 + st]) for st in range(5)]
    rot = [0]

    def rbank():
        i = rot[0]; rot[0] = (i + 1) % 3
        return self.PS[i], self.PR[i]

    pend = None
    u = 0
    for j in range(16):
        wb, rwb = wbf[j % 2]
        wb3 = wb[:].rearrange("p (kc c) -> p kc c", kc=8)
        k.load_w((stage[0], stage[1]), (wb3, rwb), I["w_in"], j * 128, 128)
        for st, (c0, nc_) in enumerate(SUP):
            n = nc_ * 128
            t0 = (c0 + 1) * 128
            ps, rps = rbank()
            for kc in range(8):
                k.pe(A("matmul", ps[:, 0:n], wb3[:, kc, :], hT3[:, kc, t0:t0 + n], start=(kc == 0), stop=(kc == 7)), [rwb, self.rhT], [rps])
            if pend is not None:
                pend()
            sz_, rsz = szb[u % 2]; sq_, rsq = sqb[u % 2]
            u += 1
            k.act(A("activation", out=sz_[:, 0:n], in_=ps[:, 0:n], func=AF.Silu), [rps], [rsz])
            v, _ = yT(st)
            k.dve(A("tensor_tensor", out=v[:, j, :], in0=v[:, j, :], in1=sz_[:, 0:n], op=ALU.mult), [ryT[st], rsz], [ryT[st]])
            k.act(A("activation", out=sq_[:, 0:n], in_=v[:, j, :], func=AF.Square), [ryT[st]], [rsq])
            pend = (lambda st=st, n=n, sq_=sq_, rsq=rsq, j=j: k.pe(A("matmul", accb[st][0][:, 0:n], ob[:], sq_[:, 0:n], start=(j == 0), stop=(j == 15)),
                                                                   [rob, rsq], [accb[st][1]]))
    pend()
    for st, (c0, nc_) in enumerate(SUP):
        n = nc_ * 128
        r_, rr_ = rstd[st]
        k.dve(A("tensor_scalar", out=r_[:, 0:n], in0=accb[st][0][:, 0:n], scalar1=1.0 / 2048, scalar2=EPS, op0=ALU.mult, op1=ALU.add),
              [accb[st][1]], [rr_])
        k.act(A("activation", out=r_[:, 0:n], in_=r_[:, 0:n], func=AF.Sqrt), [rr_], [rr_])
        k.dve(A("reciprocal", out=r_[:, 0:n], in_=r_[:, 0:n]), [rr_], [rr_])
    ar2 = Arena(self.arW.lo + 8 * 2304 * 2, self.arW.lo + 8 * 2304 * 2 + 16 * 512 * 2)
    stgA = [stage, k.T("stgA1", [128, 16 * 128], es=ar2)]
    stgB = [k.T("stgB0", [128, 8 * 128], es=ar2)]
    wpbs = [(wpb, rwpb), k.T("wpb1", [128, 16 * 128], BF16, es=ar2)]
    first = [True]

    def load_cb(cbk):
        sA, rsA = stgA[cbk % 2]
        sA3 = sA[:].rearrange("p (j c) -> p j c", j=16)
        wp_, rwp_ = wpbs[cbk % 2]
        wp3_ = wp_[:].rearrange("p (j c) -> p j c", j=16)
        extra = [rtmpT] if cbk <= 1 else []
        k.dma(sA3, I["w_o_ssd"][:, cbk * 128:(cbk + 1) * 128].rearrange("(j p) c -> p j c", p=128), [], [rsA] + extra)
        k.dve(A("tensor_tensor", out=wp3_, in0=sA3, in1=self.sng[:].unsqueeze(2).to_broadcast([128, 16, 128]), op=ALU.mult),
              [rsA, self.rsng], [rwp_] + extra)
        wb, rwb = wbf[cbk % 2]
        wb3 = wb[:].rearrange("p (kc c) -> p kc c", kc=8)
        sB, rsB = stgB[0]
        sB3 = sB[:].rearrange("p (kc c) -> p kc c", kc=8)
        k.dma(sB3, I["w_in"][:, G0 + cbk * 128:G0 + (cbk + 1) * 128].rearrange("(kc p) c -> p kc c", p=128), [], [rsB] + ([rtmpT] if first[0] else []))
        k.pool(A("tensor_copy", out=wb3, in_=sB3), [rsB], [rwb])
        first[0] = False

    load_cb(0)
    for cbk in range(8):
        if cbk + 1 < 8:
            load_cb(cbk + 1)
        wp3 = wpbs[cbk % 2][0][:].rearrange("p (j c) -> p j c", j=16)
        rwpb = wpbs[cbk % 2][1]
        wb, rwb = wbf[cbk % 2]
        wb3 = wb[:].rearrange("p (kc c) -> p kc c", kc=8)
        for st, (c0, nc_) in enumerate(SUP):
            n = nc_ * 128
            t0 = (c0 + 1) * 128
            v, _ = yT(st)
            po, rpo = rbank()
            for j in range(16):
                k.pe(A("matmul", po[:, 0:n], wp3[:, j, :], v[:, j, :], start=(j == 0), stop=(j == 15)), [rwpb, ryT[st]], [rpo])
            pg, rpg = rbank()
            for kc in range(8):
                k.pe(A("matmul", pg[:, 0:n], wb3[:, kc, :], hT3[:, kc, t0:t0 + n], start=(kc == 0), stop=(kc == 7)), [rwb, self.rhT], [rpg])
            k.act(A("activation", out=sg[:, 0:n], in_=pg[:, 0:n], func=AF.Sigmoid), [rpg], [rsg])
            k.dve(A("tensor_tensor", out=sg[:, 0:n], in0=sg[:, 0:n], in1=rstd[st][0][:, 0:n], op=ALU.mult), [rsg, rstd[st][1]], [rsg])
            k.dve(A("tensor_tensor", out=mg3[:, cbk, c0 * 128:c0 * 128 + n], in0=po[:, 0:n], in1=sg[:, 0:n], op=ALU.mult), [rpo, rsg], [self.rmg])
    k.dump("mg_ssd", self.mg[:], self.rmg, [128, 8 * 2304], BF16)
    self.S.flush()
    arA.close()
    arW.cur = arW.lo + 8 * 2304 * 2


K.phase_ssd_proj = phase_ssd_proj


def phase_att(self):
    k = self; I = self.I
    hT3 = self.hT3
    arY = self.arYZ; arA = self.arACC; arW = self.arW
    arY.close(); arA.close()
    yatt, ryatt = k.T("yatt", [128, 18 * 1024], BF16, es=arY)
    yatt3 = yatt[:].rearrange("p (i f) -> p i f", i=18)
    cosT, rcos = k.T("cosT", [64, 2560], es=arY)
    sinT, rsin = k.T("sinT", [64, 2560], es=arY)
    kT, rkT = k.T("kT", [64, 2560], BF16, es=arY)
    Pbs = [k.T("Pb%d" % i, [128, 5 * 512], BF16, es=arY) for i in range(2)]
    Va, rVa = k.T("Va", [128, 20 * 4 * 65], BF16, es=arA)
    Va4 = Va[:].rearrange("p (t g c) -> p t g c", t=20, g=4)
    t_, rt_ = k.T("rt", [64, 512], es=arA)
    u_, ru_ = k.T("ru", [64, 512], es=arA)
    wmark = arW.cur
    qg, rqg = k.T("qg", [64, 18 * 512], BF16, es=arW)
    qg4 = qg[:].rearrange("p (i h c) -> p i h c", i=18, h=4)
    stage = k.T("stg", [128, 8 * 128], es=arW)
    wv, rwv = k.T("wv", [128, 8 * 256], BF16, es=arW)
    wv3 = wv[:].rearrange("p (kc c) -> p kc c", kc=8)
    wm = [k.T("wqm%d" % i, [128, 8 * 64], BF16, es=arW) for i in range(2)]
    wp = [k.T("wqp%d" % i, [128, 8 * 64], BF16, es=arW) for i in range(2)]
    den, rden = k.T("den", [128, 8], es=arW)
    vca4 = self.vca[:].rearrange("p (s g c) -> p s g c", s=2, g=4)
    triLb = self.cstb[:, 256:384]; triUb = self.cstb[:, 128:256]

    k.dma(cosT[:], I["cosT"], [], [rcos])
    k.dma(sinT[:], I["sinT"], [], [rsin])
    k.dve(A("memset", Va[:], 1.0), [], [rVa])
    for i in range(2):
        k.load_w(stage, (wv3[:, :, i * 128:(i + 1) * 128], rwv), I["w_in"], V0 + i * 128, 128)
    for t in range(NOWN):
        ps, rps = k.bank()
        for kc in range(8):
            k.pe(A("matmul", ps[:, 0:256], hT3[:, kc, t * 128:(t + 1) * 128], wv3[:, kc, :], start=(kc == 0), stop=(kc == 7)), [rwv, self.rhT], [rps])
        k.act(A("activation", out=Va4[:, t, :, 0:64], in_=ps[:, 0:256].rearrange("p (g c) -> p g c", c=64), func=AF.Copy), [rps], [rVa])

    def proj_rope(c_main, c_perm, wi, tok0, ntok, dst_fn, rdst):
        wm_, rwm = wm[wi % 2]; wp_, rwp = wp[wi % 2]
        wm3 = wm_[:].rearrange("p (kc c) -> p kc c", kc=8); wp3 = wp_[:].rearrange("p (kc c) -> p kc c", kc=8)
        k.load_w(stage, (wm3, rwm), I["w_in"], c_main, 64)
        k.load_w(stage, (wp3, rwp), I["w_qkp"], c_perm, 64)
        for e0 in range(0, ntok, 512):
            n = min(512, ntok - e0)
            pa, rpa = k.bank(); pb, rpb = k.bank()
            for kc in range(8):
                k.pe(A("matmul", pa[0:64, 0:n], wm3[:, kc, :], hT3[:, kc, tok0 + e0:tok0 + e0 + n], start=(kc == 0), stop=(kc == 7)), [rwm, self.rhT], [rpa])
            for kc in range(8):
                k.pe(A("matmul", pb[0:64, 0:n], wp3[:, kc, :], hT3[:, kc, tok0 + e0:tok0 + e0 + n], start=(kc == 0), stop=(kc == 7)), [rwp, self.rhT], [rpb])
            k.dve(A("tensor_tensor", out=t_[:, 0:n], in0=pa[0:64, 0:n], in1=cosT[:, tok0 + e0:tok0 + e0 + n], op=ALU.mult), [rpa, rcos], [rt_])
            k.dve(A("tensor_tensor", out=u_[:, 0:n], in0=pb[0:64, 0:n], in1=sinT[:, tok0 + e0:tok0 + e0 + n], op=ALU.mult), [rpb, rsin], [ru_])
            o_, view = dst_fn(e0, n)
            k.dve(A("tensor_tensor", out=o_, in0=view(t_[:, 0:n]), in1=view(u_[:, 0:n]), op=ALU.add), [rt_, ru_], [rdst])

    for g in range(4):
        proj_rope(K0 + g * 64, 1024 + g * 64, 0, 0, 2560, lambda e0, n: (kT[:, e0:e0 + n], (lambda a: a)), rkT)
        for h in range(4):
            hh = 4 * g + h
            proj_rope(Q0 + hh * 64, hh * 64, 1 + h, 128, 2304,
                      (lambda e0, n, h=h: (qg4[:, e0 // 128:(e0 + n) // 128, h, :], (lambda a: a.rearrange("p (i c) -> p i c", c=128)))), rqg)
        if g == 0:
            k.dump("kT0", kT[:], rkT, [64, 2560], BF16)
            k.dump("qg0", qg[:], rqg, [64, 18 * 512], BF16)
        def scores(i):
            Pb, rPb = Pbs[i % 2]
            kbs = [(kT[:, i * 128:(i + 1) * 128], rkT, Va4[:, i, g, :], rVa, triLb, i),
                   (kT[:, (i + 1) * 128:(i + 2) * 128], rkT, Va4[:, i + 1, g, :], rVa, None, None),
                   (kT[:, (i + 2) * 128:(i + 3) * 128], rkT, Va4[:, i + 2, g, :], rVa, triUb, i + 2),
                   (self.kcT[:, g * 256:g * 256 + 128], self.rkcT, vca4[:, 0, g, :], self.rvca, None, None),
                   (self.kcT[:, g * 256 + 128:g * 256 + 256], self.rkcT, vca4[:, 1, g, :], self.rvca, None, None)]
            for kb, (kap, rk_, vap, rv_, mask, fl) in enumerate(kbs):
                ps, rps = k.bank()
                k.pe(A("matmul", ps[:, 0:512], kap, qg[:, i * 512:(i + 1) * 512], start=True, stop=True), [rk_, rqg], [rps])
                pk = Pb[:, kb * 512:(kb + 1) * 512]
                k.act(A("activation", out=pk, in_=ps[:, 0:512], func=AF.Exp, scale=0.125), [rps], [rPb])
                if mask is not None:
                    p3 = pk.rearrange("p (h c) -> p h c", c=128)
                    k.dve(A("scalar_tensor_tensor", out=p3, in0=p3, scalar=self.flags[:, fl:fl + 1], in1=mask.unsqueeze(1).to_broadcast([128, 4, 128]),
                            op0=ALU.mult, op1=ALU.mult), [rPb, self.rflags, self.rcstb], [rPb])
            return kbs

        def pv(i, kbs):
            Pb, rPb = Pbs[i % 2]
            po, rpo = k.bank()
            for h in range(4):
                for kb, (kap, rk_, vap, rv_, mask, fl) in enumerate(kbs):
                    k.pe(A("matmul", po[:, h * 65:(h + 1) * 65], Pb[:, kb * 512 + h * 128:kb * 512 + (h + 1) * 128], vap, start=(kb == 0), stop=(kb == 4)),
                         [rPb, rv_], [rpo])
            po3 = po[:, 0:260].rearrange("p (h c) -> p h c", c=65)
            k.dve(A("tensor_tensor", out=den[:, 0:4], in0=po3[:, :, 64], in1=self.esink[:, 4 * g:4 * g + 4], op=ALU.add), [rpo, self.resink], [rden])
            k.dve(A("reciprocal", out=den[:, 4:8], in_=den[:, 0:4]), [rden], [rden])
            k.dve(A("tensor_tensor", out=yatt3[:, i, g * 256:(g + 1) * 256].rearrange("p (h c) -> p h c", c=64), in0=po3[:, :, 0:64],
                    in1=den[:, 4:8].unsqueeze(2).to_broadcast([128, 4, 64]), op=ALU.mult), [rpo, rden], [ryatt])

        prevk = None
        for i in range(18):
            cur = scores(i)
            if prevk is not None:
                pv(i - 1, prevk)
            prevk = cur
        pv(17, prevk)
    k.dump("yatt", yatt[:], ryatt, [128, 18 * 1024], BF16)
    self.S.flush()
    arW.cur = wmark
    arA.close()
    arY.cur = arY.lo + 18 * 1024 * 2
    woa, rwoa = k.T("woa", [128, 8 * 1024], BF16, es=arY)
    wga, rwga = k.T("wga", [128, 8 * 1024], BF16, es=arY)
    woa3 = woa[:].rearrange("p (kc c) -> p kc c", kc=8); wga3 = wga[:].rearrange("p (kc c) -> p kc c", kc=8)
    stage = k.T("stg", [128, 8 * 128], es=arW)
    yT_, ryT = k.T("yattT", [128, 8 * 512], BF16, es=arA)
    yT3 = yT_[:].rearrange("p (kc t) -> p kc t", kc=8)
    sgs = [k.T("sg%d" % i, [128, 512], es=arA) for i in range(2)]
    tts = [k.T("tt%d" % i, [128, 512], es=arA) for i in range(2)]
    idb = self.cstb[:, 0:128]
    mg3 = self.mg3
    for i in range(8):
        k.load_w(stage, (woa3[:, :, i * 128:(i + 1) * 128], rwoa), I["w_o_att"], i * 128, 128)
        k.load_w(stage, (wga3[:, :, i * 128:(i + 1) * 128], rwga), I["w_in"], G0 + 1024 + i * 128, 128)
    SUP = [(0, 4), (4, 4), (8, 4), (12, 4), (16, 2)]
    for st, (c0, nc_) in enumerate(SUP):
        n = nc_ * 128
        t0 = (c0 + 1) * 128
        for ci in range(nc_):
            tb, rtb = k.bank()
            tbb = tb[:].bitcast(BF16)
            for kc in range(8):
                k.pe(A("transpose", tbb[:, kc * 128:(kc + 1) * 128], yatt3[:, c0 + ci, kc * 128:(kc + 1) * 128], idb), [ryatt, self.rcstb], [rtb])
            k.act(A("activation", out=yT3[:, :, ci * 128:(ci + 1) * 128], in_=tbb[:, 0:1024].rearrange("p (kc t) -> p kc t", kc=8), func=AF.Copy), [rtb], [ryT])
        for cbk in range(8):
            sg, rsg = sgs[cbk % 2]; tt, rtt = tts[cbk % 2]
            po, rpo = k.bank(); pg, rpg = k.bank()
            for kc in range(8):
                k.pe(A("matmul", po[:, 0:n], woa3[:, kc, cbk * 128:(cbk + 1) * 128], yT3[:, kc, 0:n], start=(kc == 0), stop=(kc == 7)), [rwoa, ryT], [rpo])
            for kc in range(8):
                k.pe(A("matmul", pg[:, 0:n], wga3[:, kc, cbk * 128:(cbk + 1) * 128], hT3[:, kc, t0:t0 + n], start=(kc == 0), stop=(kc == 7)), [rwga, self.rhT], [rpg])
            k.act(A("activation", out=sg[:, 0:n], in_=pg[:, 0:n], func=AF.Sigmoid), [rpg], [rsg])
            k.dve(A("tensor_tensor", out=tt[:, 0:n], in0=po[:, 0:n], in1=sg[:, 0:n], op=ALU.mult), [rpo, rsg], [rtt])
            m_ = mg3[:, cbk, c0 * 128:c0 * 128 + n]
            k.dve(A("tensor_tensor", out=m_, in0=m_, in1=tt[:, 0:n], op=ALU.add), [rtt, self.rmg], [self.rmg])
    k.dump("mg", self.mg[:], self.rmg, [128, 8 * 2304], BF16)
    self.S.flush()
    arA.close(); arY.close()
    arW.cur = wmark


K.phase_att = phase_att


def phase_wout(self):
    k = self; I = self.I
    arY = self.arYZ; arA = self.arACC; arH = self.arHT
    arY.close(); arA.close(); arH.close()
    mg3 = self.mg3
    self.h2T, self.rh2T = k.T("h2T", [128, 8 * 2050], BF16, es=arH)
    h2T3 = self.h2T[:].rearrange("p (kc t) -> p kc t", kc=8)
    self.h2T3 = h2T3
    g1, rg1 = k.T("g1row", [128, D], es=arY)
    wo, rwo = k.T("wo", [128, 8 * D], BF16, es=arY)
    wo3 = wo[:].rearrange("p (kc c) -> p kc c", kc=8)
    stage = k.T("stg", [128, 8 * 128], es=arY)
    stg3 = stage[0][:].rearrange("p (kc c) -> p kc c", kc=8)
    xts = [k.T("xw%d" % i, [128, D], es=arY) for i in range(3)]
    x1t = [k.T("x1t%d" % i, [128, D], es=arY) for i in range(3)]
    tmps = [(k.T("junk%d" % i, [128, D], BF16, es=arY), k.T("ss%d" % i, [128, 4], es=arY), k.T("xn%d" % i, [128, D], BF16, es=arY)) for i in range(3)]
    m3 = self.modT[:].rearrange("p (b v) -> p b v", v=2)
    row_bcast(k, g1, rg1, (lambda b: m3[:, 16 + b, 0:1]), self.rmodT2, 8)
    for i in range(8):
        k.dma(stg3, I["w_out"][:, i * 128:(i + 1) * 128].rearrange("(kc p) c -> p kc c", p=128), [], [stage[1]])
        k.dve(A("tensor_tensor", out=wo3[:, :, i * 128:(i + 1) * 128], in0=stg3, in1=g1[:, i * 128:(i + 1) * 128].unsqueeze(1).to_broadcast([128, 8, 128]),
                op=ALU.mult), [stage[1], rg1], [rwo])
    self.rx1s = [Reg("x1s%d" % i) for i in range(16)]

    def st_a(i):
        xt_, rxt = xts[i % 3]; x1_, rx1 = x1t[i % 3]
        k.dma(xt_[:], I["x_own"][(i + 1) * 128:(i + 2) * 128, :], [], [rxt])
        for half in range(2):
            ps, rps = k.bank()
            for kc in range(8):
                k.pe(A("matmul", ps[:, 0:512], mg3[:, kc, i * 128:(i + 1) * 128], wo3[:, kc, half * 512:(half + 1) * 512], start=(kc == 0), stop=(kc == 7)),
                     [self.rmg, rwo], [rps])
            k.dve(A("tensor_tensor", out=x1_[:, half * 512:(half + 1) * 512], in0=ps[:, 0:512], in1=xt_[:, half * 512:(half + 1) * 512], op=ALU.add),
                  [rps, rxt], [rx1])
        if 1 <= i <= 16:
            k.dma(self.x1s[(i - 1) * 128:i * 128, :], x1_[:], [rx1], [self.rx1s[i - 1]], q="pool")
        k.norm_a(x1_, rx1, 128, tmps[i % 3])
        if i == 5:
            k.dump("x1_5", x1_[:], rx1, [128, D])

    def st_b(i):
        if 1 <= i <= 16:
            k.norm_b(128, (lambda kc, i=i: h2T3[:, kc, 1 + (i - 1) * 128:1 + i * 128]), self.rh2T, 2, 24, 0, tmps[i % 3])
        elif i == 0:
            k.norm_b(128, (lambda kc: h2T3[:, kc, 0:1]), self.rh2T, 2, 24, 0, tmps[i % 3], src_view=lambda a: a[:, 127:128])
        else:
            k.norm_b(128, (lambda kc: h2T3[:, kc, 2049:2050]), self.rh2T, 2, 24, 0, tmps[i % 3], src_view=lambda a: a[:, 0:1])

    st_a(0)
    for i in range(18):
        if i + 1 < 18:
            st_a(i + 1)
        st_b(i)
    for (col, fl) in ((0, 1), (2049, 18)):
        for kc in range(8):
            k.dve(A("tensor_scalar", out=h2T3[:, kc, col:col + 1], in0=h2T3[:, kc, col:col + 1], scalar1=self.flags[:, fl:fl + 1], scalar2=None, op0=ALU.mult),
                  [self.rh2T, self.rflags], [self.rh2T])
    k.dump("h2T", self.h2T[:], self.rh2T, [128, 8 * 2050], BF16)
    self.S.flush()
    arY.close()


def phase_ffn(self):
    k = self; I = self.I
    h2T3 = self.h2T3
    arA = self.arACC
    arB = Arena(self.arYZ.lo, self.arW.hi)
    arH = self.arHT
    arA.close()
    Gb, rGb = k.T("Gb", [128, 22 * 2048], BF16, es=arB)
    Gb3 = Gb[:].rearrange("p (j t) -> p j t", j=22)
    wd, rwd = k.T("wd", [128, 22 * D], BF16, es=arB)
    wd3 = wd[:].rearrange("p (j c) -> p j c", j=22)
    wab = [k.T("wab%d" % i, [128, 8 * 128], BF16, es=arB) for i in range(2)]
    stage = k.T("stg", [128, 8 * 128], es=arH)
    raw = [k.T("raw%d" % i, [128, 412], es=arA) for i in range(2)]
    ac, rac = k.T("ac", [128, 412], es=arA)
    bc, rbc = k.T("bc", [128, 412], es=arA)
    fcw, rfcw = k.T("fcw", [128, 132], es=arA)
    fcb, rfcb = k.T("fcb", [128, 44], es=arA)
    k.dma(fcw[:], I["ffn_cw"], [], [rfcw])
    k.dma(fcb[:], I["ffn_cb"], [], [rfcb])
    fw3 = fcw[:].rearrange("p (b t) -> p b t", t=3)
    g2, rg2 = k.T("g2row", [128, D], es=arA)
    stgd = k.T("stgd", [128, D], es=arA)
    m3 = self.modT[:].rearrange("p (b v) -> p b v", v=2)
    row_bcast(k, g2, rg2, (lambda b: m3[:, 40 + b, 0:1]), self.rmodT2, 8)
    tiles = [(1, 410), (411, 410), (821, 410), (1231, 410), (1641, 408)]
    for j in range(22):
        k.dma(stgd[0][:], I["w_down"][j * 128:(j + 1) * 128, :], [], [stgd[1]])
        k.pool(A("tensor_tensor", out=wd3[:, j, :], in0=stgd[0][:], in1=g2[:], op=ALU.mult), [stgd[1], rg2], [rwd])
        for ab in range(2):
            wb_, rwb = wab[ab]
            wb3 = wb_[:].rearrange("p (kc c) -> p kc c", kc=8)
            k.load_w(stage, (wb3, rwb), I["w_up"], ab * D_FF + j * 128, 128)
        wa3 = wab[0][0][:].rearrange("p (kc c) -> p kc c", kc=8); rwa = wab[0][1]
        wbb3 = wab[1][0][:].rearrange("p (kc c) -> p kc c", kc=8); rwb = wab[1][1]
        for (o0, n) in tiles:
            pa, rpa = k.bank(); pb, rpb = k.bank()
            for kc in range(8):
                k.pe(A("matmul", pa[:, 0:n + 2], wa3[:, kc, :], h2T3[:, kc, o0 - 1:o0 + n + 1], start=(kc == 0), stop=(kc == 7)), [rwa, self.rh2T], [rpa])
            for kc in range(8):
                k.pe(A("matmul", pb[:, 0:n + 2], wbb3[:, kc, :], h2T3[:, kc, o0 - 1:o0 + n + 1], start=(kc == 0), stop=(kc == 7)), [rwb, self.rh2T], [rpb])
            fa = conv3(k, pa, rpa, n, fw3[:, j, :], fcb[:, j:j + 1], raw[0], ac[:, 0:n], rac, AF.Silu, [rfcw, rfcb], defer=True)
            (rb_, rrb_) = raw[1]
            k.act(A("activation", out=rb_[:, 0:n], in_=pb[:, 1:n + 1], func=AF.Identity, scale=fw3[:, 22 + j, 1:2], bias=fcb[:, 22 + j:23 + j]),
                  [rpb, rfcw, rfcb], [rrb_])
            fa()
            k.dve(A("scalar_tensor_tensor", out=rb_[:, 0:n], in0=pb[:, 0:n], scalar=fw3[:, 22 + j, 0:1], in1=rb_[:, 0:n], op0=ALU.mult, op1=ALU.add),
                  [rpb, rrb_, rfcw], [rrb_])
            k.dve(A("scalar_tensor_tensor", out=rb_[:, 0:n], in0=pb[:, 2:n + 2], scalar=fw3[:, 22 + j, 2:3], in1=rb_[:, 0:n], op0=ALU.mult, op1=ALU.add),
                  [rpb, rrb_, rfcw], [rrb_])
            k.dve(A("tensor_tensor", out=Gb3[:, j, o0 - 1:o0 - 1 + n], in0=ac[:, 0:n], in1=rb_[:, 0:n], op=ALU.mult), [rac, rrb_], [rGb])
    k.dump("Gb", Gb[:], rGb, [128, 22 * 2048], BF16)
    self.S.flush()
    arA.close()
    fg, rfg = k.T("fgrow", [128, D], es=arA)
    x1r = [k.T("x1r%d" % i, [128, D], es=arA) for i in range(2)]
    k.dma(fg[:], I["fg_row"], [], [rfg])
    arB.cur = arB.lo + 22 * 2048 * 2 + 22 * D * 2
    arH.cur = arH.lo + 8 * 2050 * 2 + 64
    stg2 = k.T("stgd", [128, D], es=arB)
    xos = [k.T("xo0", [128, D], es=arB), stg2]
    junk, rjunk = k.T("junk", [128, D], BF16, es=arH)
    sss = [k.T("ssa", [128, 4], es=arH), k.T("ssb", [128, 4], es=arH)]
    for i in range(16):
        xr, rxr = x1r[i % 2]
        xo, rxo = xos[i % 2]
        ss, rss = sss[i % 2]
        k.dma(xr[:], self.x1s[i * 128:(i + 1) * 128, :], [self.rx1s[i]], [rxr])
        for half in range(2):
            ps, rps = k.bank()
            for j in range(22):
                k.pe(A("matmul", ps[:, 0:512], Gb3[:, j, i * 128:(i + 1) * 128], wd3[:, j, half * 512:(half + 1) * 512], start=(j == 0), stop=(j == 21)),
                     [rGb, rwd], [rps])
            k.dve(A("tensor_tensor", out=xo[:, half * 512:(half + 1) * 512], in0=ps[:, 0:512], in1=xr[:, half * 512:(half + 1) * 512], op=ALU.add),
                  [rps, rxr], [rxo])
        k.dve(A("memset", ss[:, 0:1], 0.0), [], [rss])
        k.act(A("activation", out=junk[:], in_=xo[:], func=AF.Square, accum_out=ss[:, 0:1]), [rxo], [rjunk, rss])
        k.dve(A("tensor_scalar", out=ss[:, 1:2], in0=ss[:, 0:1], scalar1=1.0 / D, scalar2=EPS, op0=ALU.mult, op1=ALU.add), [rss], [rss])
        k.act(A("activation", out=ss[:, 1:2], in_=ss[:, 1:2], func=AF.Sqrt), [rss], [rss])
        k.dve(A("reciprocal", out=ss[:, 2:3], in_=ss[:, 1:2]), [rss], [rss])
        k.dve(A("scalar_tensor_tensor", out=xr[:], in0=xo[:], scalar=ss[:, 2:3], in1=fg[:], op0=ALU.mult, op1=ALU.mult), [rxo, rss, rfg], [rxr])
        k.dma(self.out[i * 128:(i + 1) * 128, :], xr[:], [rxr], [], q="pool")
    self.S.flush()


K.phase_wout = phase_wout
K.phase_ffn = phase_ffn
```

```python
import os
from contextlib import ExitStack
import numpy as np
import concourse.bass as bass
import concourse.mybir as mybir
from concourse.bass_utils import run_bass_kernel_spmd

F32 = mybir.dt.float32
BF16 = mybir.dt.bfloat16
AF = mybir.ActivationFunctionType
ALU = mybir.AluOpType

D = 1024
L = 8192
NQ = 4
TOK = 2048
EPS = 1e-6
XBC0 = 2048
DT0 = 5120
Q0 = 5184
K0 = 6208
V0 = 6464
G0 = 6720
D_FF = 2816
NOWN = 20
NSLOT = 48
NSB = 16
NFLAG = 20 + 2 * NSLOT + 2 * NSLOT

DEBUG = os.environ.get("KDEBUG", "")


class Reg:
    __slots__ = ("name", "lw", "rd", "excl")

    def __init__(self, name, excl=False):
        self.name = name
        self.lw = None
        self.rd = []
        self.excl = excl


class Op:
    __slots__ = ("eng", "fn", "deps", "dma", "sem", "val", "needed", "prev")


class Sched:
    ENG = ("pe", "act", "dve", "pool", "sp")

    def __init__(self, nc, n_dma_sems=12):
        self.nc = nc
        self.esem = {e: nc.alloc_semaphore("s_" + e) for e in self.ENG}
        self.ecnt = {e: 0 for e in self.ENG}
        self.dq = ("sp", "pool", "act")
        self.dsem = {q: [nc.alloc_semaphore("d_%s%d" % (q, i)) for i in range(n_dma_sems)] for q in self.dq}
        self.dcnt = {q: [0] * n_dma_sems for q in self.dq}
        self.drr = {q: 0 for q in self.dq}
        self.dlast = {q: [None] * n_dma_sems for q in self.dq}
        self.ops = []
        self.known = {e: {} for e in self.ENG}
        self.nops = 0

    def op(self, eng, fn, r=(), w=(), dma=False):
        o = Op()
        o.eng = eng; o.fn = fn; o.dma = dma; o.needed = False; o.sem = None; o.val = 0; o.prev = None
        deps = []
        for t in r:
            if t.lw is not None:
                deps.append(t.lw)
            if t.excl:
                deps.extend(x for x in t.rd if x.eng != eng)
        for t in w:
            if t.lw is not None and (t.lw.eng != eng or t.lw.dma or dma):
                deps.append(t.lw)
            deps.extend(x for x in t.rd if (x.eng != eng or x.dma or dma))
        o.deps = []
        seen = set()
        for d in deps:
            if d is o or id(d) in seen:
                continue
            seen.add(id(d))
            if d.eng == "pe" and eng == "pe" and not d.dma and not dma:
                continue
            o.deps.append(d)
        for t in w:
            t.lw = o
            t.rd = []
        for t in r:
            if t.lw is not o:
                t.rd.append(o)
        self.ops.append(o)
        return o

    def flush(self):
        ops = self.ops
        for o in ops:
            for d in o.deps:
                d.needed = True
        last = {}
        for o in ops:
            if o.dma:
                o.needed = True
            else:
                last[o.eng] = o
        for o in last.values():
            o.needed = True
        for o in ops:
            if o.dma:
                q = o.eng
                i = self.drr[q]
                self.drr[q] = (i + 1) % len(self.dsem[q])
                o.prev = self.dlast[q][i]
                self.dcnt[q][i] += 16
                o.sem = ("d", q, i)
                o.val = self.dcnt[q][i]
                self.dlast[q][i] = o
            elif o.needed:
                self.ecnt[o.eng] += 1
                o.sem = ("e", o.eng)
                o.val = self.ecnt[o.eng]
        per = {e: [] for e in self.ENG}
        for o in ops:
            per[o.eng].append(o)
        fin = [o for o in ops if (o.dma or o is last.get(o.eng))]

        def semh(key):
            return self.esem[key[1]] if key[0] == "e" else self.dsem[key[1]][key[2]]

        snap = {}
        waits = {}
        for o in ops:
            kn = self.known[o.eng]
            wl = []
            dl = list(o.deps)
            if o.dma and o.prev is not None:
                dl.append(o.prev)
            for d in dl:
                if d.sem is None or kn.get(d.sem, 0) >= d.val:
                    continue
                wl.append((d.sem, d.val))
                kn[d.sem] = d.val
                sd = snap.get(id(d))
                if sd:
                    for k_, v_ in sd.items():
                        if kn.get(k_, 0) < v_:
                            kn[k_] = v_
            waits[id(o)] = wl
            if o.sem is not None:
                snap[id(o)] = dict(kn)

        def emit(e, eh):
            kn = self.known[e]

            def wait(key, val):
                if kn.get(key, 0) >= val:
                    return
                eh.wait_ge(semh(key), val)
                kn[key] = val

            for o in per[e]:
                wl = waits[id(o)]
                for (key, val) in wl[:-1]:
                    eh.wait_ge(semh(key), val)
                ins = o.fn(eh)
                if wl:
                    ins._wait_ge(semh(wl[-1][0]), wl[-1][1])
                if o.dma:
                    ins.then_inc(semh(o.sem), 16)
                elif o.needed:
                    ins.then_inc(semh(o.sem), 1)
            for d in fin:
                if d.eng == e and not d.dma:
                    continue
                wait(d.sem, d.val)

        with self.nc.Block() as block:
            @block.tensor
            def _(t):
                emit("pe", t)

            @block.scalar
            def _(t):
                emit("act", t)

            @block.vector
            def _(t):
                emit("dve", t)

            @block.gpsimd
            def _(t):
                emit("pool", t)

            @block.sync
            def _(t):
                emit("sp", t)
        self.nops += len(ops)
        self.ops = []


class Arena:
    def __init__(self, lo, hi):
        self.lo = lo; self.hi = hi; self.cur = lo

    def take(self, n, name=""):
        o = self.cur
        assert o + n <= self.hi, "arena overflow for %s: need %d have %d" % (name, n, self.hi - o)
        self.cur = o + n
        return o

    def close(self):
        self.cur = self.lo


def A(name, *args, **kw):
    return lambda e: getattr(e, name)(*args, **kw)


class K:
    def __init__(self):
        self.nc = bass.Bass("TRN2", target_bir_lowering=False)
        self.S = Sched(self.nc)
        self.dbg = {}
        self.pbank = 0
        self.es = ExitStack()
        self.uid = 0
        self.arC = Arena(17408, 26624)
        self.arACC = Arena(26624, 43008)
        self.arHT = Arena(43008, 83968)
        self.arYZ = Arena(83968, 157696)
        self.arW = Arena(157696, 229376)
        self.arBIG = Arena(43008, 229376 - 18 * 1024)

    def T(self, name, shape, dt=F32, es=None):
        ar = es or self.arC
        nbytes = int(np.prod(shape[1:])) * (2 if dt == BF16 else 4)
        nbytes = (nbytes + 63) // 64 * 64
        off = ar.take(nbytes, name)
        self.offs = getattr(self, "offs", {})
        self.offs[name] = off
        self.uid += 1
        t = self.nc.alloc_sbuf_tensor_at("sb%d_%s" % (self.uid, name), list(shape), dt, offset=off)
        return t, Reg(name)

    def din(self, name, shape, dt=F32):
        return self.nc.dram_tensor(name, list(shape), dt, kind="ExternalInput").ap()

    def pe(self, fn, r, w): return self.S.op("pe", fn, r, w)
    def act(self, fn, r, w): return self.S.op("act", fn, r, w)
    def dve(self, fn, r, w): return self.S.op("dve", fn, r, w)
    def pool(self, fn, r, w): return self.S.op("pool", fn, r, w)
    def dma(self, out, in_, r, w, q="sp"): return self.S.op(q, A("dma_start", out=out, in_=in_), r, w, dma=True)

    def bank(self):
        i = self.pbank
        self.pbank = (i + 1) % 8
        return self.PS[i], self.PR[i]

    def dump(self, name, ap, reg, shape, dt=F32):
        if name not in DEBUG.split(","):
            return
        o = self.nc.dram_tensor("dbg_" + name, list(shape), dt, kind="ExternalOutput").ap()
        self.dma(o, ap, list(reg) if isinstance(reg, (list, tuple)) else [reg], [], q="sp")
        self.dbg[name] = "dbg_" + name

    def build(self):
        nc = self.nc
        k = self
        I = {}
        I["x_own"] = k.din("x_own", [NOWN * 128, D])
        I["x_oth"] = k.din("x_oth", [NSLOT * 128, D])
        I["x_oth_h"] = k.din("x_oth_h", [2 * NSLOT, D])
        I["ctx"] = k.din("ctx", [256, D])
        I["cvec"] = k.din("cvec", [128, 16])
        I["flags"] = k.din("flags", [128, NFLAG])
        I["w_mod"] = k.din("w_mod", [D, 6 * D])
        I["bmodT"] = k.din("bmodT", [128, 48])
        I["n1g"] = k.din("n1g", [128, 8])
        I["n2g"] = k.din("n2g", [128, 8])
        I["fg_row"] = k.din("fg_row", [128, D])
        I["w_in"] = k.din("w_in", [D, 8768])
        I["w_qkp"] = k.din("w_qkp", [D, 1280])
        I["ssd_cw"] = k.din("ssd_cw", [128, 24 * 3])
        I["ssd_cb"] = k.din("ssd_cb", [128, 24])
        I["dtb"] = k.din("dtb", [128, 64])
        I["alog"] = k.din("alog", [128, 64])
        I["dsk"] = k.din("dsk", [128, 32])
        I["sng"] = k.din("sng", [128, 16])
        I["w_o_ssd"] = k.din("w_o_ssd", [2048, D])
        I["w_o_att"] = k.din("w_o_att", [D, D])
        I["sink"] = k.din("sink", [128, 16])
        I["w_out"] = k.din("w_out", [D, D])
        I["w_up"] = k.din("w_up", [D, 2 * D_FF])
        I["ffn_cw"] = k.din("ffn_cw", [128, 44 * 3])
        I["ffn_cb"] = k.din("ffn_cb", [128, 44])
        I["w_down"] = k.din("w_down", [D_FF, D])
        I["cosT"] = k.din("cosT", [64, NOWN * 128])
        I["sinT"] = k.din("sinT", [64, NOWN * 128])
        I["consts"] = k.din("consts", [128, 3 * 128])
        self.I = I
        self.out = nc.dram_tensor("out", [TOK, D], F32, kind="ExternalOutput").ap()
        self.x1s = nc.dram_tensor("x1s", [TOK, D], F32, kind="Internal").ap()

        self.PS = []
        self.PR = []
        for i in range(8):
            self.PS.append(self.es.enter_context(nc.psum_tensor("ps%d" % i, [128, 512], F32)))
            self.PR.append(Reg("ps%d" % i, excl=True))

        stop = os.environ.get("KSTOP", "")
        for ph in ("setup", "others", "own_h", "ssd", "ssd_proj", "att", "wout", "ffn"):
            getattr(self, "phase_" + ph)()
            if stop == ph:
                break
        return nc

    def phase_setup(self):
        k = self; nc = self.nc; I = self.I
        es = Arena(229376 - 18 * 1024, 229376)
        self.cst, self.rcst = k.T("cst", [128, 3 * 128])
        self.cstb, self.rcstb = k.T("cstb", [128, 3 * 128], BF16)
        self.ones, self.rones = k.T("ones", [128, 128])
        self.flags, self.rflags = k.T("flags", [128, NFLAG])
        self.modT, self.rmodT = k.T("modT", [128, 48 * 2])
        self.gs, self.rgs = k.T("gs", [128, 8 * 4])
        self.negA, self.rnegA = k.T("negA", [128, 64])
        self.esink, self.resink = k.T("esink", [128, 16])
        self.dtb, self.rdtb = k.T("dtb", [128, 64])
        self.dsk, self.rdsk = k.T("dsk", [128, 32])
        self.sng, self.rsng = k.T("sng", [128, 16])
        self.scw, self.rscw = k.T("scw", [128, 72])
        self.scb, self.rscb = k.T("scb", [128, 24])
        self.n12, self.rn12 = k.T("n12", [128, 16])
        cvec, rcvec = k.T("cvec", [128, 16], es=es)
        sc, rsc = k.T("sc", [128, 16], es=es)
        bmodT, rbmodT = k.T("bmodT", [128, 48], es=es)
        wm = [k.T("wm%d" % i, [128, 8 * 256], es=es) for i in range(2)]

        k.dma(self.cst[:], I["consts"], [], [self.rcst])
        k.dma(self.flags[:], I["flags"], [], [self.rflags])
        k.dma(cvec[:], I["cvec"], [], [rcvec])
        k.dma(bmodT[:], I["bmodT"], [], [rbmodT])
        k.dma(self.n12[:, 0:8], I["n1g"], [], [self.rn12])
        k.dma(self.n12[:, 8:16], I["n2g"], [], [self.rn12])
        k.dma(self.negA[:], I["alog"], [], [self.rnegA])
        k.dma(self.esink[:], I["sink"], [], [self.resink])
        k.dma(self.dtb[:], I["dtb"], [], [self.rdtb])
        k.dma(self.dsk[:], I["dsk"], [], [self.rdsk])
        k.dma(self.sng[:], I["sng"], [], [self.rsng])
        k.dma(self.scw[:], I["ssd_cw"], [], [self.rscw])
        k.dma(self.scb[:], I["ssd_cb"], [], [self.rscb])
        k.dve(A("tensor_copy", out=self.cstb[:], in_=self.cst[:]), [self.rcst], [self.rcstb])
        k.pool(A("memset", self.ones[:], 1.0), [], [self.rones])
        k.act(A("activation", out=sc[:], in_=cvec[:], func=AF.Silu), [rcvec], [rsc])
        k.act(A("activation", out=self.negA[:], in_=self.negA[:], func=AF.Exp), [self.rnegA], [self.rnegA])
        k.dve(A("tensor_scalar", out=self.negA[:], in0=self.negA[:], scalar1=-1.0, scalar2=None, op0=ALU.mult), [self.rnegA], [self.rnegA])
        k.act(A("activation", out=self.esink[:], in_=self.esink[:], func=AF.Exp), [self.resink], [self.resink])

        wmv = I["w_mod"].rearrange("(kc p) c -> p kc c", p=128)

        def mod_ct(ct):
            wt, rw = wm[ct % 2]
            wt3 = wt[:].rearrange("p (kc c) -> p kc c", kc=8)
            k.dma(wt3, wmv[:, :, ct * 256:(ct + 1) * 256], [], [rw])
            for j in range(2):
                blk = ct * 2 + j
                ps, rps = self.PS[7], self.PR[7]
                for kc in range(8):
                    k.pe(A("matmul", ps[:, 400 + 2 * j:402 + 2 * j], wt3[:, kc, j * 128:(j + 1) * 128], sc[:, 2 * kc:2 * kc + 2],
                           start=(kc == 0), stop=(kc == 7)), [rsc, rw], [rps])
                rm_ = self.rmodT if blk < 16 else self.rmodT2
                k.dve(A("tensor_scalar", out=self.modT[:, 2 * blk:2 * blk + 2], in0=ps[:, 400 + 2 * j:402 + 2 * j], scalar1=bmodT[:, blk:blk + 1],
                        scalar2=None, op0=ALU.add), [rps, rbmodT], [rm_])
        self.mod_ct = mod_ct
        self.rmodT2 = Reg("modT2")
        for ct in range(8):
            mod_ct(ct)
        m3 = self.modT[:].rearrange("p (b v) -> p b v", v=2)
        g3 = self.gs[:].rearrange("p (kc v) -> p kc v", v=4)
        for v in range(2):
            k.dve(A("scalar_tensor_tensor", out=g3[:, :, v], in0=m3[:, 8:16, v], scalar=1.0, in1=self.n12[:, 0:8],
                    op0=ALU.add, op1=ALU.mult), [self.rmodT, self.rn12], [self.rgs])
        k.dump("modT", self.modT[:], self.rmodT, [128, 96])
        k.dump("gs", self.gs[:], self.rgs, [128, 32])

    def norm_a(self, xt, rxt, rows, tmp):
        self.norm_a_multi([(xt, rxt, rows, tmp)])

    def norm_a_multi(self, items):
        k = self
        for (xt, rxt, rows, ((junk, rjunk), (ss, rss), (xn, rxn))) in items:
            k.dve(A("memset", ss[:, 0:1], 0.0), [], [rss])
        for (xt, rxt, rows, ((junk, rjunk), (ss, rss), (xn, rxn))) in items:
            k.act(A("activation", out=junk[0:rows, :], in_=xt[0:rows, :], func=AF.Square, accum_out=ss[0:rows, 0:1]), [rxt], [rjunk, rss])
        for (xt, rxt, rows, ((junk, rjunk), (ss, rss), (xn, rxn))) in items:
            k.dve(A("tensor_scalar", out=ss[0:rows, 1:2], in0=ss[0:rows, 0:1], scalar1=1.0 / D, scalar2=EPS, op0=ALU.mult, op1=ALU.add), [rss], [rss])
        for (xt, rxt, rows, ((junk, rjunk), (ss, rss), (xn, rxn))) in items:
            k.act(A("activation", out=ss[0:rows, 1:2], in_=ss[0:rows, 1:2], func=AF.Sqrt), [rss], [rss])
        for (xt, rxt, rows, ((junk, rjunk), (ss, rss), (xn, rxn))) in items:
            k.dve(A("reciprocal", out=ss[0:rows, 2:3], in_=ss[0:rows, 1:2]), [rss], [rss])
        for (xt, rxt, rows, ((junk, rjunk), (ss, rss), (xn, rxn))) in items:
            k.act(A("activation", out=xn[0:rows, :], in_=xt[0:rows, :], func=AF.Copy, scale=ss[0:rows, 2:3]), [rxt, rss], [rxn])

    def norm_b(self, rows, dst_fn, rdst, gcol, shblk, v, tmp, src_view=None, bankfn=None, dst3=None):
        k = self
        (junk, rjunk), (ss, rss), (xn, rxn) = tmp
        sv = src_view or (lambda a: a)
        ps, rps = (bankfn or k.bank)()
        psb = ps[:].bitcast(BF16)
        for kc in range(8):
            k.pe(A("transpose", psb[:, kc * 128:kc * 128 + rows], xn[0:rows, kc * 128:(kc + 1) * 128], self.cstb[0:rows, 0:rows]),
                 [rxn, self.rcstb], [rps])
        g3 = self.gs[:].rearrange("p (kc v) -> p kc v", v=4)
        m3 = self.modT[:].rearrange("p (b v) -> p b v", v=2)
        if dst3 is not None and rows == 128:
            p3 = psb[:, 0:1024].rearrange("p (kc t) -> p kc t", kc=8)
            k.dve(A("tensor_tensor", out=dst3, in0=p3, in1=g3[:, :, gcol:gcol + 1].to_broadcast([128, 8, 128]), op=ALU.mult),
                  [rps, self.rgs], [rdst])
            k.dve(A("tensor_tensor", out=dst3, in0=dst3, in1=m3[:, shblk:shblk + 8, v:v + 1].to_broadcast([128, 8, 128]), op=ALU.add),
                  [rdst, self.rmodT], [rdst])
            return
        for kc in range(8):
            k.act(A("activation", out=dst_fn(kc), in_=sv(psb[:, kc * 128:kc * 128 + rows]), func=AF.Identity,
                    scale=g3[:, kc, gcol:gcol + 1], bias=m3[:, shblk + kc, v:v + 1]), [rps, self.rgs, self.rmodT], [rdst])

    def norm_T(self, xt, rxt, rows, dst_fn, rdst, gcol, shblk, v, tmp, src_view=None):
        self.norm_a(xt, rxt, rows, tmp)
        self.norm_b(rows, dst_fn, rdst, gcol, shblk, v, tmp, src_view)

    def load_w(self, stage, dst3, src, c0, ncols, scale_ap=None, eng="pool"):
        k = self
        st, rst = stage
        st3 = st[:, 0:8 * ncols].rearrange("p (kc c) -> p kc c", kc=8)
        k.dma(st3, src[:, c0:c0 + ncols].rearrange("(kc p) c -> p kc c", p=128), [], [rst])
        dst, rdst = dst3
        k.S.op(eng, A("tensor_copy", out=dst, in_=st3), [rst], [rdst])

    def phase_others(self):
        k = self; nc = self.nc; I = self.I
        es = self.arBIG
        self.acc, self.racc = k.T("acc", [128, 2 * 2048], es=self.arACC)
        self.kcT, self.rkcT = k.T("kcT", [64, 4 * 256], BF16)
        self.vca, self.rvca = k.T("vca", [128, 2 * 4 * 65], BF16)
        logP, rlogP = k.T("logP", [128, 64], es=es)
        wres, rwres = k.T("wres", [128, 8 * 2624], BF16, es=es)
        wres3 = wres[:].rearrange("p (kc c) -> p kc c", kc=8)
        wkv, rwkv = k.T("wkv", [128, 8 * 512], BF16, es=es)
        wkv3 = wkv[:].rearrange("p (kc c) -> p kc c", kc=8)
        stg = [k.T("stg%d" % i, [128, 8 * 128], es=es) for i in range(2)]
        xts = [k.T("xto%d" % i, [128, D], es=es) for i in range(3)]
        xh, rxh = k.T("xh", [6, D], es=es)
        junk_ = k.T("junk", [128, D], BF16, es=es)
        tmps = [(junk_, k.T("ss%d" % i, [128, 4], es=es), k.T("xn%d" % i, [128, D], BF16, es=es)) for i in range(4)]
        tmp = tmps[0]
        hTo = [k.T("hTo%d" % i, [128, 8 * 390], BF16, es=es) for i in range(2)]
        raw = [k.T("raw%d" % i, [128, 388], es=es) for i in range(3)]
        xc = [k.T("xc%d" % i, [128, 388], BF16, es=es) for i in range(4)]
        xtok = [[k.T("xtok%d_%d" % (i, s), [128, 2048], BF16, es=es) for s in range(3)] for i in range(2)]
        btok = [[k.T("btok%d_%d" % (i, s), [128, 512], BF16, es=es) for s in range(3)] for i in range(2)]
        xp = [[k.T("xp%d_%d" % (d_, s), [128, 2048], BF16, es=es) for s in range(3)] for d_ in range(2)]
        dtr, rdtr = k.T("dtr", [128, 192], es=es)
        dte, rdte = k.T("dte", [128, 192], es=es)
        la, rla = k.T("la", [128, 192], es=es)
        arg, rarg = k.T("arg", [128, 192], es=es)
        wts = [k.T("wt%d" % i, [128, 192], es=es) for i in range(2)]
        wt_, rwt = wts[0]
        L1, rL1 = k.T("L1", [128, 192], es=es)
        hTc = self.nc.alloc_sbuf_tensor_at("sb_hTc_alias", [128, 8 * 258], BF16, offset=self.offs["hTo0"])
        rhTc = hTo[0][1]
        short = [0]

        def sbank():
            i = short[0]; short[0] = (i + 1) % 3
            return self.PS[i], self.PR[i]

        k.dve(A("memset", self.acc[:], 0.0), [], [self.racc])
        k.dve(A("memset", logP[:], 0.0), [], [rlogP])
        k.dve(A("memset", self.vca[:], 1.0), [], [self.rvca])
        for i in range(20):
            k.load_w(stg[i % 2], (wres3[:, :, i * 128:(i + 1) * 128], rwres), I["w_in"], XBC0 + i * 128, 128)
        k.load_w(stg[0], (wres3[:, :, 2560:2624], rwres), I["w_in"], DT0, 64)
        for i in range(4):
            k.load_w(stg[(i + 1) % 2], (wkv3[:, :, i * 128:(i + 1) * 128], rwkv), I["w_in"], K0 + i * 128, 128)
        cw3 = self.scw[:].rearrange("p (b t) -> p b t", t=3)
        idb = self.cstb[:, 0:128]
        triU = self.cst[:, 128:256]; triL = self.cst[:, 256:384]

        def conv_block(ps, n, cb, rw, xcw):
            return conv3(k, ps, self.PRmap[id(ps)], n, cw3[:, cb, :], self.scb[:, cb:cb + 1], rw, xcw[0][:, 0:n], xcw[1], AF.Silu,
                         [self.rscw, self.rscb], defer=True)
        self.conv_block = conv_block

        def dt_chain(nsl, hT3, col0, stride, flag0, split=False):
            psd, rpsd = self.PS[6], self.PR[6]
            for s in range(nsl):
                for kc in range(8):
                    k.pe(A("matmul", psd[:, s * 64:(s + 1) * 64], hT3[:, kc, col0 + s * stride:col0 + s * stride + 128], wres3[:, kc, 2560:2624],
                           start=(kc == 0), stop=(kc == 7)), [rwres, self.rhcur], [rpsd])
            n = nsl * 64
            k.dve(A("tensor_tensor", out=dtr[:, 0:n].rearrange("p (s c) -> p s c", c=64), in0=psd[:, 0:n].rearrange("p (s c) -> p s c", c=64),
                    in1=self.dtb[:].unsqueeze(1).to_broadcast([128, nsl, 64]), op=ALU.add), [rpsd, self.rdtb], [rdtr])
            k.act(A("activation", out=dtr[:, 0:n], in_=dtr[:, 0:n], func=AF.Exp), [rdtr], [rdtr])
            k.act(A("activation", out=dte[:, 0:n], in_=dtr[:, 0:n], func=AF.Ln, bias=1.0), [rdtr], [rdte])
            if flag0 is not None:
                for s in range(nsl):
                    for d in range(2):
                        f = flag0 + s * 2 + d
                        k.dve(A("tensor_scalar", out=dte[:, s * 64 + d * 32:s * 64 + d * 32 + 32], in0=dte[:, s * 64 + d * 32:s * 64 + d * 32 + 32],
                                scalar1=self.flags[:, f:f + 1], scalar2=None, op0=ALU.mult), [rdte, self.rflags], [rdte])
            k.dve(A("tensor_tensor", out=la[:, 0:n].rearrange("p (s c) -> p s c", c=64), in0=dte[:, 0:n].rearrange("p (s c) -> p s c", c=64),
                    in1=self.negA[:].unsqueeze(1).to_broadcast([128, nsl, 64]), op=ALU.mult), [rdte, self.rnegA], [rla])
            if split:
                return None, None
            return dt_cs(nsl)

        def dt_cs(nsl):
            psc, rpsc = self.PS[7], self.PR[7]
            for s in range(nsl):
                k.pe(A("matmul", psc[:, s * 64:s * 64 + 32], triU, la[:, s * 64:s * 64 + 32], start=True, stop=True), [rla, self.rcst], [rpsc])
                k.pe(A("matmul", psc[:, s * 64 + 32:s * 64 + 64], triL, la[:, s * 64 + 32:s * 64 + 64], start=True, stop=True), [rla, self.rcst], [rpsc])
                k.pe(A("matmul", psc[:, 192 + s * 64:192 + (s + 1) * 64], self.ones[:], la[:, s * 64:(s + 1) * 64], start=True, stop=True),
                     [rla, self.rones], [rpsc])
            return psc, rpsc
        self.dt_chain = dt_chain
        self.PRmap = {id(self.PS[i]): self.PR[i] for i in range(8)}

        def states(nsl, xt_set, bt_set, wtp=None):
            states_x(nsl, xt_set, bt_set, wtp)
            states_mm(nsl, xt_set, bt_set, wtp)

        def states_x(nsl, xt_set, bt_set, wtp=None):
            wt_, rwt = wtp or wts[0]
            for d in range(2):
                for s in range(nsl):
                    (k.pool if s == 2 else k.dve)(A("tensor_tensor", out=xp[d][s][0][:].rearrange("p (h c) -> p h c", c=64),
                             in0=xt_set[s][0][:].rearrange("p (h c) -> p h c", c=64),
                             in1=wt_[:, s * 64 + d * 32:s * 64 + d * 32 + 32].unsqueeze(2).to_broadcast([128, 32, 64]), op=ALU.mult),
                           [xt_set[s][1], rwt], [xp[d][s][1]])

        def states_mm(nsl, xt_set, bt_set, wtp=None):
            for d in range(2):
                for g in range(4):
                    ps, rps = sbank()
                    for s in range(nsl):
                        k.pe(A("matmul", ps[:, 0:512], bt_set[s][0][:, g * 128:(g + 1) * 128], xp[d][s][0][:, g * 512:(g + 1) * 512],
                               start=(s == 0), stop=(s == nsl - 1)), [bt_set[s][1], xp[d][s][1]], [rps])
                    a = self.acc[:, d * 2048 + g * 512:d * 2048 + (g + 1) * 512]
                    k.dve(A("tensor_tensor", out=a, in0=ps[:, 0:512], in1=a, op=ALU.add), [rps, self.racc], [self.racc])

        def states_old(nsl, xt_set, bt_set, wtp=None):
            wt_, rwt = wtp or wts[0]
            for d in range(2):
                for s in range(nsl):
                    (k.pool if s == 2 else k.dve)(A("tensor_tensor", out=xp[d][s][0][:].rearrange("p (h c) -> p h c", c=64),
                             in0=xt_set[s][0][:].rearrange("p (h c) -> p h c", c=64),
                             in1=wt_[:, s * 64 + d * 32:s * 64 + d * 32 + 32].unsqueeze(2).to_broadcast([128, 32, 64]), op=ALU.mult),
                           [xt_set[s][1], rwt], [xp[d][s][1]])
                for g in range(4):
                    ps, rps = sbank()
                    for s in range(nsl):
                        k.pe(A("matmul", ps[:, 0:512], bt_set[s][0][:, g * 128:(g + 1) * 128], xp[d][s][0][:, g * 512:(g + 1) * 512],
                               start=(s == 0), stop=(s == nsl - 1)), [bt_set[s][1], xp[d][s][1]], [rps])
                    a = self.acc[:, d * 2048 + g * 512:d * 2048 + (g + 1) * 512]
                    k.dve(A("tensor_tensor", out=a, in0=ps[:, 0:512], in1=a, op=ALU.add), [rps, self.racc], [self.racc])

        def proj_blocks(nsl, hT3, ncol, slot_off, xt_set, bt_set, hook=None):
            tb = [(self.PS[3 + s], self.PR[3 + s]) for s in range(nsl)]

            def trans(cb, xcw):
                for s in range(nsl):
                    tbb = tb[s][0][:].bitcast(BF16)
                    j = cb % 8
                    k.pe(A("transpose", tbb[:, j * 128:(j + 1) * 128], xcw[0][:, slot_off(s):slot_off(s) + 128], idb), [xcw[1], self.rcstb], [tb[s][1]])
                    if cb == 7:
                        k.dve(A("tensor_copy", out=xt_set[s][0][:, 0:1024], in_=tbb[:, 0:1024]), [tb[s][1]], [xt_set[s][1]])
                    elif cb == 15:
                        k.act(A("activation", out=xt_set[s][0][:, 1024:2048], in_=tbb[:, 0:1024], func=AF.Copy), [tb[s][1]], [xt_set[s][1]])
                    elif cb == 19:
                        k.dve(A("tensor_copy", out=bt_set[s][0][:, 0:512], in_=tbb[:, 0:512]), [tb[s][1]], [bt_set[s][1]])

            pend = []
            for cb in range(20):
                ps, rps = sbank()
                for kc in range(8):
                    k.pe(A("matmul", ps[:, 0:ncol], wres3[:, kc, cb * 128:(cb + 1) * 128], hT3[:, kc, 0:ncol], start=(kc == 0), stop=(kc == 7)),
                         [rwres, self.rhcur], [rps])
                xcw = xc[cb % 4]
                fin = conv_block(ps, ncol - 2, cb, raw[cb % 3], xcw)
                if pend:
                    pend[-1][2]()
                if len(pend) == 2:
                    trans(pend[0][0], pend[0][1])
                    pend.pop(0)
                pend.append((cb, xcw, fin))
                if hook is not None and cb in hook:
                    hook[cb]()
            pend[-1][2]()
            for p_ in pend:
                trans(p_[0], p_[1])

        xo = I["x_oth"]

        def prep_dma(sb):
            for s in range(3):
                xt_, rxt = xts[s]
                k.dma(xt_[:], xo[(sb * 3 + s) * 128:(sb * 3 + s + 1) * 128, :], [], [rxt])
            k.dma(xh[:], I["x_oth_h"][sb * 6:sb * 6 + 6, :], [], [rxh])

        def prep_a(sb, dma=True):
            if dma:
                prep_dma(sb)
            k.norm_a_multi([(xts[s][0], xts[s][1], 128, tmps[s]) for s in range(3)] + [(xh, rxh, 6, tmps[3])])

        def prep_b(sb):
            hT_, rhT = hTo[sb % 2]
            hT3 = hT_[:].rearrange("p (kc c) -> p kc c", kc=8)
            for s in range(3):
                k.norm_b(128, (lambda kc, s=s: hT3[:, kc, s * 130 + 1:s * 130 + 129]), rhT, 0, 0, 0, tmps[s], bankfn=sbank)
            k.norm_b(6, (lambda kc: hT3[:, kc, :].rearrange("p (s t) -> p s t", t=130)[:, :, 0::129]), rhT, 0, 0, 0, tmps[3],
                     src_view=lambda a: a.rearrange("p (s t) -> p s t", t=2), bankfn=sbank)
            hv = self.flags[:, 116 + sb * 6:116 + sb * 6 + 6].rearrange("p (s t) -> p s t", t=2)
            for kc in range(8):
                hh = hT3[:, kc, :].rearrange("p (s t) -> p s t", t=130)[:, :, 0::129]
                k.dve(A("tensor_tensor", out=hh, in0=hh, in1=hv, op=ALU.mult), [rhT, self.rflags], [rhT])

        def nb_tile(sb, s):
            hT_, rhT = hTo[sb % 2]
            hT3 = hT_[:].rearrange("p (kc c) -> p kc c", kc=8)
            if s < 3:
                k.norm_b(128, (lambda kc, s=s: hT3[:, kc, s * 130 + 1:s * 130 + 129]), rhT, 0, 0, 0, tmps[s], bankfn=sbank,
                         dst3=(hT3[:, :, s * 130 + 1:s * 130 + 129] if s != 1 else None))
            else:
                k.norm_b(6, (lambda kc: hT3[:, kc, :].rearrange("p (s t) -> p s t", t=130)[:, :, 0::129]), rhT, 0, 0, 0, tmps[3],
                         src_view=lambda a: a.rearrange("p (s t) -> p s t", t=2), bankfn=sbank)
                hv = self.flags[:, 116 + sb * 6:116 + sb * 6 + 6].rearrange("p (s t) -> p s t", t=2)
                for kc in range(8):
                    hh = hT3[:, kc, :].rearrange("p (s t) -> p s t", t=130)[:, :, 0::129]
                    k.dve(A("tensor_tensor", out=hh, in0=hh, in1=hv, op=ALU.mult), [rhT, self.rflags], [rhT])

        def dt1(sb):
            hT_, rhT = hTo[sb % 2]
            hT3 = hT_[:].rearrange("p (kc c) -> p kc c", kc=8)
            old_ = getattr(self, "rhcur", None)
            self.rhcur = rhT
            dt_chain(3, hT3, 1, 130, 20 + sb * 6, split=True)
            self.rhcur = old_

        def dt2(sb):
            wtp = wts[sb % 2]
            psc, rpsc = dt_cs(3)
            for s in range(3):
                k.dve(A("tensor_tensor", out=logP[:], in0=psc[:, 192 + s * 64:192 + (s + 1) * 64], in1=logP[:], op=ALU.add), [rpsc, rlogP], [rlogP])
                k.dve(A("tensor_tensor", out=arg[:, s * 64:(s + 1) * 64], in0=logP[:], in1=psc[:, s * 64:(s + 1) * 64], op=ALU.subtract),
                      [rpsc, rlogP], [rarg])
            k.act(A("activation", out=arg[:], in_=arg[:], func=AF.Exp), [rarg], [rarg])
            k.dve(A("tensor_tensor", out=wtp[0][:], in0=arg[:], in1=dte[:], op=ALU.mult), [rarg, rdte], [wtp[1]])

        prep_a(0)
        for s in range(4):
            nb_tile(0, s)
        dt1(0)
        prep_a(1)
        prev = None
        for sb in range(NSB):
            hT_, rhT = hTo[sb % 2]
            hT3 = hT_[:].rearrange("p (kc c) -> p kc c", kc=8)
            self.mod_ct(8 + sb)
            xt_set = xtok[sb % 2]; bt_set = btok[sb % 2]
            def h1(sb=sb):
                dt2(sb)
                if sb + 2 < NSB:
                    prep_dma(sb + 2)
            hooks = {1: h1}
            if prev is not None:
                hooks[2] = (lambda p=prev: states_x(3, *p))
                hooks[6] = (lambda p=prev: states_mm(3, *p))
            if sb + 1 < NSB:
                for s in range(4):
                    hooks[8 + 2 * s] = (lambda sb=sb, s=s: nb_tile(sb + 1, s))
                hooks[15] = (lambda sb=sb: dt1(sb + 1))
            if sb + 2 < NSB:
                hooks[17] = (lambda sb=sb: prep_a(sb + 2, dma=False))
            self.rhcur = rhT
            proj_blocks(3, hT3, 390, lambda s: s * 130, xt_set, bt_set, hook=hooks)
            prev = (xt_set, bt_set, wts[sb % 2])
            if sb == 0:
                k.dump("hTo0", hT_[:], rhT, [128, 8 * 390], BF16)
                k.dump("xtok0", xt_set[0][0][:], xt_set[0][1], [128, 2048], BF16)
        states(3, *prev)

        k.dve(A("memset", hTc[:], 0.0), [], [rhTc])
        hc3 = hTc[:].rearrange("p (kc c) -> p kc c", kc=8)
        self.rhcur = rhTc
        for s in range(2):
            xt_, rxt = xts[s]
            k.dma(xt_[:], I["ctx"][s * 128:(s + 1) * 128, :], [], [rxt])
            k.norm_T(xt_, rxt, 128, (lambda kc, s=s: hc3[:, kc, 1 + s * 128:1 + (s + 1) * 128]), rhTc, 1, 0, 1, tmps[s])
        xt_set = xtok[0]; bt_set = btok[0]
        proj_blocks(2, hc3, 258, lambda s: s * 128, xt_set, bt_set)
        psc, rpsc = dt_chain(2, hc3, 1, 128, None)
        tot0 = psc[:, 192:256]; tot1 = psc[:, 256:320]
        k.dve(A("tensor_tensor", out=L1[:, 0:64], in0=tot1, in1=logP[:], op=ALU.add), [rpsc, rlogP], [rL1])
        k.dve(A("tensor_tensor", out=L1[:, 64:128], in0=tot0, in1=logP[:], op=ALU.add), [rpsc, rlogP], [rL1])
        k.dve(A("tensor_tensor", out=L1[:, 128:192], in0=tot0, in1=L1[:, 0:64], op=ALU.add), [rpsc, rL1], [rL1])
        for (s, d, src) in ((1, 0, 0), (0, 0, 128), (0, 1, 64), (1, 1, 128)):
            k.dve(A("tensor_tensor", out=arg[:, s * 64 + d * 32:s * 64 + d * 32 + 32], in0=L1[:, src + d * 32:src + d * 32 + 32],
                    in1=psc[:, s * 64 + d * 32:s * 64 + d * 32 + 32], op=ALU.subtract), [rpsc, rL1], [rarg])
        k.act(A("activation", out=arg[:, 0:128], in_=arg[:, 0:128], func=AF.Exp), [rarg], [rarg])
        k.dve(A("tensor_tensor", out=wt_[:, 0:128], in0=arg[:, 0:128], in1=dte[:, 0:128], op=ALU.mult), [rarg, rdte], [rwt])
        states(2, xt_set, bt_set)
        for g in range(4):
            ps, rps = sbank()
            for kc in range(8):
                k.pe(A("matmul", ps[0:64, 0:256], wkv3[:, kc, g * 64:(g + 1) * 64], hc3[:, kc, 1:257], start=(kc == 0), stop=(kc == 7)),
                     [rwkv, rhTc], [rps])
            k.act(A("activation", out=self.kcT[:, g * 256:(g + 1) * 256], in_=ps[0:64, 0:256], func=AF.Copy), [rps], [self.rkcT])
        va = self.vca[:].rearrange("p (s g c) -> p s g c", s=2, g=4)
        for s in range(2):
            ps, rps = sbank()
            for kc in range(8):
                k.pe(A("matmul", ps[:, 0:256], hc3[:, kc, 1 + s * 128:1 + (s + 1) * 128], wkv3[:, kc, 256:512], start=(kc == 0), stop=(kc == 7)),
                     [rwkv, rhTc], [rps])
            k.act(A("activation", out=va[:, s, :, 0:64], in_=ps[:, 0:256].rearrange("p (g c) -> p g c", c=64), func=AF.Copy), [rps], [self.rvca])
        m3 = self.modT[:].rearrange("p (b v) -> p b v", v=2)
        g3 = self.gs[:].rearrange("p (kc v) -> p kc v", v=4)
        k.dve(A("scalar_tensor_tensor", out=g3[:, :, 2], in0=m3[:, 32:40, 0], scalar=1.0, in1=self.n12[:, 8:16],
                op0=ALU.add, op1=ALU.mult), [self.rmodT2, self.rn12], [self.rgs])
        k.dump("acc", self.acc[:], self.racc, [128, 4096])
        self.S.flush()
        es.close()


def _rep(v, n=128):
    v = np.asarray(v, np.float32).reshape(1, -1)
    return np.ascontiguousarray(np.broadcast_to(v, (n, v.shape[1])))


def _fm(v, nblk):
    return np.ascontiguousarray(np.asarray(v, np.float32).reshape(nblk, 128).T)


def prep_inputs(inp):
    x = np.asarray(inp["x"], np.float32)
    ctx = np.asarray(inp["ctx"], np.float32)
    w_in = np.ascontiguousarray(np.asarray(inp["w_in"], np.float32)[0])
    d = np.arange(64)
    within = d % 32
    partner = np.where(within < 16, d + 16, d - 16)
    qk = np.concatenate([Q0 + h * 64 + partner for h in range(16)] + [K0 + h * 64 + partner for h in range(4)])
    w_qkp = np.ascontiguousarray(w_in[:, qk])
    consts = np.zeros((128, 3, 128), np.float32)
    ii = np.arange(128)
    consts[:, 0] = (ii[:, None] == ii[None, :])
    consts[:, 1] = (ii[:, None] <= ii[None, :])
    consts[:, 2] = (ii[:, None] >= ii[None, :])
    inv = (10000.0 ** (-np.arange(0, 32, 2, dtype=np.float32) / 32)).astype(np.float32)
    j = within % 16
    sgn = np.where(within < 16, -1.0, 1.0).astype(np.float32)
    shared = dict(
        w_mod=np.ascontiguousarray(np.asarray(inp["w_mod"], np.float32)[0]),
        bmodT=_fm(inp["b_mod"][0], 48),
        n1g=_fm(inp["norm1_g"][0], 8), n2g=_fm(inp["norm2_g"][0], 8), fg_row=_rep(inp["final_g"]),
        w_in=w_in, w_qkp=w_qkp,
        ssd_cw=np.ascontiguousarray(np.asarray(inp["ssd_conv_w"], np.float32)[0].reshape(3, 24, 128).transpose(2, 1, 0).reshape(128, 72)),
        ssd_cb=_fm(inp["ssd_conv_b"][0], 24),
        dtb=_rep(np.asarray(inp["ssd_dt_bias"])[0].reshape(-1)), alog=_rep(np.asarray(inp["ssd_a_log"])[0].reshape(-1)),
        dsk=_rep(inp["ssd_d"][0]), sng=_fm(inp["ssd_norm_g"][0], 16),
        w_o_ssd=np.ascontiguousarray(np.asarray(inp["w_o_ssd"], np.float32)[0]),
        w_o_att=np.ascontiguousarray(np.asarray(inp["w_o_att"], np.float32)[0]),
        sink=_rep(inp["att_sink"][0]),
        w_out=np.ascontiguousarray(np.asarray(inp["w_out"], np.float32)[0]),
        w_up=np.ascontiguousarray(np.asarray(inp["w_up"], np.float32)[0]),
        ffn_cw=np.ascontiguousarray(np.asarray(inp["ffn_conv_w"], np.float32)[0].reshape(3, 44, 128).transpose(2, 1, 0).reshape(128, 132)),
        ffn_cb=_fm(inp["ffn_conv_b"][0], 44),
        w_down=np.ascontiguousarray(np.asarray(inp["w_down"], np.float32)[0]),
        consts=consts.reshape(128, 384),
    )
    maps = []
    zero_row = np.zeros((1, D), np.float32)
    for core in range(8):
        b, q = core // 4, core % 4
        xb = x[b]
        t0 = q * TOK - 256
        x_own = np.zeros((NOWN * 128, D), np.float32)
        lo, hi = max(t0, 0), min(t0 + NOWN * 128, L)
        x_own[lo - t0:hi - t0] = xb[lo:hi]
        flags = np.zeros((NFLAG,), np.float32)
        for t in range(NOWN):
            flags[t] = 1.0 if 0 <= t0 + t * 128 < L else 0.0
        before = list(range(16 * q - 2, -1, -1))
        after = list(range(16 * q + 17, 64))
        slots = [(c, 0) for c in before] + [(c, 1) for c in after]
        assert len(slots) <= NSLOT
        x_oth = np.zeros((NSLOT * 128, D), np.float32)
        x_oth_h = np.zeros((2 * NSLOT, D), np.float32)
        for s, (c, side) in enumerate(slots):
            x_oth[s * 128:(s + 1) * 128] = xb[c * 128:(c + 1) * 128]
            flags[20 + 2 * s + side] = 1.0
            if c * 128 - 1 >= 0:
                x_oth_h[2 * s] = xb[c * 128 - 1]; flags[116 + 2 * s] = 1.0
            if c * 128 + 128 < L:
                x_oth_h[2 * s + 1] = xb[c * 128 + 128]; flags[116 + 2 * s + 1] = 1.0
        cv = np.zeros((128, 16), np.float32)
        cv[:, 0::2] = np.asarray(inp["c"], np.float32)[b].reshape(8, 128).T
        cv[:, 1::2] = np.asarray(inp["c_ctx"], np.float32).reshape(8, 128).T
        tpos = t0 + np.arange(NOWN * 128)
        pos = np.where((d < 32)[:, None], (tpos // 64)[None, :], (tpos % 64)[None, :]).astype(np.float32)
        ang = (pos * inv[j][:, None]).astype(np.float32)
        m = dict(shared)
        m.update(x_own=x_own, x_oth=x_oth, x_oth_h=x_oth_h, ctx=np.ascontiguousarray(ctx[b]), cvec=cv, flags=_rep(flags),
                 cosT=np.cos(ang).astype(np.float32), sinT=(np.sin(ang) * sgn[:, None]).astype(np.float32))
        maps.append(m)
    return maps


_NC_CACHE = {}


def kernel(**inputs):
    kb = K()
    nc = kb.build()
    maps = prep_inputs(inputs)
    cores = [int(c) for c in os.environ.get("KCORES", "0,1,2,3,4,5,6,7").split(",")]
    res = run_bass_kernel_spmd(nc, [maps[c] for c in cores], core_ids=list(range(len(cores))))
    out = np.zeros((2, L, D), np.float32)
    for i, core in enumerate(cores):
        b, q = core // 4, core % 4
        out[b, q * TOK:(q + 1) * TOK] = res.results[i]["out"]
    if DEBUG:
        kernel.dbg = {core: {n: res.results[i][v] for n, v in kb.dbg.items()} for i, core in enumerate(cores)}
    return out


def _p2(self):
    pass


def conv3(k, ps, rps, n, w3, bcol, rw, out_ap, rout, func, wregs, defer=False):
    (r_, rr_) = rw
    k.act(A("activation", out=r_[:, 0:n], in_=ps[:, 1:n + 1], func=AF.Identity, scale=w3[:, 1:2], bias=bcol), [rps] + wregs, [rr_])
    k.dve(A("scalar_tensor_tensor", out=r_[:, 0:n], in0=ps[:, 0:n], scalar=w3[:, 0:1], in1=r_[:, 0:n], op0=ALU.mult, op1=ALU.add),
          [rps, rr_] + wregs, [rr_])
    k.dve(A("scalar_tensor_tensor", out=r_[:, 0:n], in0=ps[:, 2:n + 2], scalar=w3[:, 2:3], in1=r_[:, 0:n], op0=ALU.mult, op1=ALU.add),
          [rps, rr_] + wregs, [rr_])
    fin = lambda: k.act(A("activation", out=out_ap, in_=r_[:, 0:n], func=func), [rr_], [rout])
    if defer:
        return fin
    fin()


def phase_own_h(self):
    k = self; I = self.I; es = self.arW
    self.hT, self.rhT = k.T("hT", [128, 8 * 2560], BF16, es=self.arHT)
    hT3 = self.hT[:].rearrange("p (kc c) -> p kc c", kc=8)
    self.hT3 = hT3
    xts = [k.T("xo%d" % i, [128, D], es=es) for i in range(8)]
    junk_ = k.T("junk", [128, D], BF16, es=es)
    tmps = [(junk_, k.T("ss%d" % i, [128, 4], es=es), k.T("xn%d" % i, [128, D], BF16, es=es)) for i in range(8)]

    def ld(gp):
        for t in range(gp * 4, gp * 4 + 4):
            k.dma(xts[t % 8][0][:], I["x_own"][t * 128:(t + 1) * 128, :], [], [xts[t % 8][1]])

    def st_a(gp):
        k.norm_a_multi([(xts[t % 8][0], xts[t % 8][1], 128, tmps[t % 8]) for t in range(gp * 4, gp * 4 + 4)])

    def st_b(t):
        k.norm_b(128, (lambda kc, t=t: hT3[:, kc, t * 128:(t + 1) * 128]), self.rhT, 0, 0, 0, tmps[t % 8])
        if t in (0, 1, 18, 19):
            for kc in range(8):
                k.dve(A("tensor_scalar", out=hT3[:, kc, t * 128:(t + 1) * 128], in0=hT3[:, kc, t * 128:(t + 1) * 128],
                        scalar1=self.flags[:, t:t + 1], scalar2=None, op0=ALU.mult), [self.rhT, self.rflags], [self.rhT])

    ld(0); ld(1)
    st_a(0)
    for gp in range(5):
        if gp + 1 < 5:
            st_a(gp + 1)
        if gp + 2 < 5:
            ld(gp + 2)
        for t in range(gp * 4, gp * 4 + 4):
            st_b(t)
    self.S.flush()
    es.close()


def phase_ssd(self):
    k = self; I = self.I; es = self.arW
    hT3 = self.hT3
    self.yz, _ = k.T("yz", [128, 18 * 2048], BF16, es=self.arYZ)
    self.ry = [Reg("y%d" % c) for c in range(18)]
    yz3 = self.yz[:].rearrange("p (c f) -> p c f", c=18)
    stage = k.T("stg", [128, 8 * 128], es=es)
    wbf = [k.T("wbf%d" % i, [128, 8 * 128], BF16, es=es) for i in range(2)]
    wdt, rwdt = k.T("wdt", [128, 8 * 64], BF16, es=es)
    wdt3 = wdt[:].rearrange("p (kc c) -> p kc c", kc=8)
    wdg, rwdg = k.T("wdg", [128, 8 * 16], BF16, es=es)
    wdg3 = wdg[:].rearrange("p (kc c) -> p kc c", kc=8)
    dbg_, rdbg = k.T("dbg_", [128, 16], es=es)
    nag_, rnag = k.T("nag_", [128, 16], es=es)
    xcg, rxcg = k.T("xcg", [128, 2304], BF16, es=es)
    raw = [k.T("raw%d" % i, [128, 512], es=es) for i in range(2)]
    xtok, _ = k.T("xtok", [128, 18 * 512], BF16, es=es)
    xtok3 = xtok[:].rearrange("p (c f) -> p c f", c=18)
    rxt = Reg("xtok")
    btok, rbt = k.T("btok", [128, 18 * 128], BF16, es=es)
    btok3 = btok[:].rearrange("p (c f) -> p c f", c=18)
    BT, rBT = k.T("BT", [128, 2304], BF16, es=es)
    CT, rCT = k.T("CT", [128, 2304], BF16, es=es)
    dte, rdte = k.T("dte", [128, 288], es=es)
    la, rla = k.T("la", [128, 288], es=es)
    dtr, rdtr = la, rla
    csx, rcsx = k.T("csx", [128, 288], es=es)
    wgt, rwgt = k.T("wgt", [128, 288], es=es)
    ecs, recs = k.T("ecs", [128, 288], es=es)
    dtot, rdtot = k.T("dtot", [128, 288], es=es)
    St, rSt = k.T("St", [128, 512], es=es)
    Sbf, rSbf = k.T("Sbf", [128, 512], BF16, es=es)
    xpd = [k.T("xpd%d" % i, [128, 512], BF16, es=es) for i in range(2)]
    CB, rCB = k.T("CB", [128, 128], es=es)
    Em = [k.T("Em%d" % i, [128, 512], BF16, es=es) for i in range(2)]
    ncs, rncs = k.T("ncs", [128, 288], es=es)
    Mt, rMt = k.T("Mt", [128, 8 * 128], BF16, es=es)
    tt, rtt = k.T("tt", [128, 512], es=es)
    uu, ruu = k.T("uu", [128, 512], BF16, es=es)
    idf = self.cst[:, 0:128]; triU = self.cst[:, 128:256]; triL = self.cst[:, 256:384]
    idb = self.cstb[:, 0:128]
    cw3 = self.scw[:].rearrange("p (b t) -> p b t", t=3)
    rot = [0]

    def rbank():
        i = rot[0]; rot[0] = (i + 1) % 2
        return self.PS[i], self.PR[i]

    k.load_w(stage, (wdt3, rwdt), I["w_in"], DT0, 64)
    tiles = [(128 + 510 * t, min(510, 2432 - (128 + 510 * t))) for t in range(5)]
    flg = self.flags[:, 1:19].unsqueeze(2).to_broadcast([128, 18, 16])

    def v3(ap):
        return ap.rearrange("p (c j) -> p c j", j=16)

    for g in range(4):
        blocks = [(XBC0 + (g * 4 + i) * 128, g * 4 + i, "x", i) for i in range(4)]
        blocks += [(XBC0 + 2048 + g * 128, 16 + g, "B", 0), (XBC0 + 2560 + g * 128, 20 + g, "C", 0)]
        for bi, (c0, cb, kind, i) in enumerate(blocks):
            wb, rwb = wbf[bi % 2]
            wb3 = wb[:].rearrange("p (kc c) -> p kc c", kc=8)
            k.load_w(stage, (wb3, rwb), I["w_in"], c0, 128)
            dst, rdst = {"x": (xcg, rxcg), "B": (BT, rBT), "C": (CT, rCT)}[kind]
            pendf = None
            for ti, (o0, n) in enumerate(tiles):
                ps, rps = rbank()
                for kc in range(8):
                    k.pe(A("matmul", ps[:, 0:n + 2], wb3[:, kc, :], hT3[:, kc, o0 - 1:o0 + n + 1], start=(kc == 0), stop=(kc == 7)),
                         [rwb, self.rhT], [rps])
                f_ = conv3(k, ps, rps, n, cw3[:, cb, :], self.scb[:, cb:cb + 1], raw[ti % 2], dst[:, o0 - 128:o0 - 128 + n], rdst, AF.Silu,
                           [self.rscw, self.rscb], defer=True)
                if pendf is not None:
                    pendf()
                pendf = f_
            pendf()
            if kind in ("x", "B"):
                for c8 in range(0, 18, 8):
                    nb = min(8, 18 - c8)
                    tb, rtb = self.PS[2 + (c8 // 8) % 2], self.PR[2 + (c8 // 8) % 2]
                    tbb = tb[:].bitcast(BF16)
                    for j in range(nb):
                        c = c8 + j
                        k.pe(A("transpose", tbb[:, j * 128:(j + 1) * 128], dst[:, c * 128:(c + 1) * 128], idb), [rdst, self.rcstb], [rtb])
                    if kind == "x":
                        k.act(A("activation", out=xtok3[:, c8:c8 + nb, i * 128:(i + 1) * 128],
                                in_=tbb[:, 0:nb * 128].rearrange("p (c f) -> p c f", f=128), func=AF.Copy), [rtb], [rxt])
                    else:
                        k.act(A("activation", out=btok3[:, c8:c8 + nb, :],
                                in_=tbb[:, 0:nb * 128].rearrange("p (c f) -> p c f", f=128), func=AF.Copy), [rtb], [rbt])
        if os.environ.get("KSSD") == "1":
            break
        for d in range(2):
            k.pool(A("tensor_copy", out=wdg3[:, :, d * 8:(d + 1) * 8], in_=wdt3[:, :, d * 32 + g * 8:d * 32 + g * 8 + 8]), [rwdt], [rwdg])
            k.dve(A("tensor_copy", out=dbg_[:, d * 8:(d + 1) * 8], in_=self.dtb[:, d * 32 + g * 8:d * 32 + g * 8 + 8]), [self.rdtb], [rdbg])
            k.dve(A("tensor_copy", out=nag_[:, d * 8:(d + 1) * 8], in_=self.negA[:, d * 32 + g * 8:d * 32 + g * 8 + 8]), [self.rnegA], [rnag])
        psd, rpsd = rbank()
        for c in range(18):
            for kc in range(8):
                k.pe(A("matmul", psd[:, c * 16:(c + 1) * 16], hT3[:, kc, (c + 1) * 128:(c + 2) * 128], wdg3[:, kc, :],
                       start=(kc == 0), stop=(kc == 7)), [rwdg, self.rhT], [rpsd])
        k.dve(A("tensor_tensor", out=v3(dtr[:]), in0=v3(psd[:, 0:288]), in1=dbg_[:].unsqueeze(1).to_broadcast([128, 18, 16]), op=ALU.add),
              [rpsd, rdbg], [rdtr])
        k.act(A("activation", out=dtr[:], in_=dtr[:], func=AF.Exp), [rdtr], [rdtr])
        k.act(A("activation", out=dte[:], in_=dtr[:], func=AF.Ln, bias=1.0), [rdtr], [rdte])
        for c in (0, 17):
            k.dve(A("tensor_scalar", out=dte[:, c * 16:(c + 1) * 16], in0=dte[:, c * 16:(c + 1) * 16], scalar1=self.flags[:, 1 + c:2 + c],
                    scalar2=None, op0=ALU.mult), [rdte, self.rflags], [rdte])
        if os.environ.get("KSSD") == "2a":
            break
        k.dve(A("tensor_tensor", out=v3(la[:]), in0=v3(dte[:]), in1=nag_[:].unsqueeze(1).to_broadcast([128, 18, 16]), op=ALU.mult),
              [rdte, rnag], [rla])
        psU, rpsU = rbank()
        k.pe(A("matmul", psU[:, 0:288], triU, la[:], start=True, stop=True), [rla, self.rcst], [rpsU])
        psL, rpsL = self.PS[2], self.PR[2]
        k.pe(A("matmul", psL[:, 0:288], triL, la[:], start=True, stop=True), [rla, self.rcst], [rpsL])
        k.act(A("activation", out=v3(csx[:])[:, :, 0:8], in_=v3(psU[:, 0:288])[:, :, 0:8], func=AF.Copy), [rpsU], [rcsx])
        k.act(A("activation", out=v3(csx[:])[:, :, 8:16], in_=v3(psL[:, 0:288])[:, :, 8:16], func=AF.Copy), [rpsL], [rcsx])
        if os.environ.get("KSSD") == "2b":
            break
        pst, rpst = rbank()
        k.pe(A("matmul", pst[:, 0:288], self.ones[:], la[:], start=True, stop=True), [rla, self.rones], [rpst])
        if os.environ.get("KSSD") == "2c":
            break
        k.act(A("activation", out=ecs[:], in_=csx[:], func=AF.Exp), [rcsx], [recs])
        if os.environ.get("KSSD") == "2d":
            break
        k.act(A("activation", out=dtot[:], in_=pst[:, 0:288], func=AF.Exp), [rpst], [rdtot])
        if os.environ.get("KSSD") == "2e":
            break
        k.dve(A("tensor_tensor", out=wgt[:], in0=pst[:, 0:288], in1=csx[:], op=ALU.subtract), [rpst, rcsx], [rwgt])
        k.act(A("activation", out=wgt[:], in_=wgt[:], func=AF.Exp), [rwgt], [rwgt])
        k.dve(A("tensor_tensor", out=wgt[:], in0=wgt[:], in1=dte[:], op=ALU.mult), [rwgt, rdte], [rwgt])
        if g == 0:
            k.dump("dte", dte[:], rdte, [128, 288]); k.dump("csx", csx[:], rcsx, [128, 288])
            k.dump("xtokg", xtok[:], rxt, [128, 18 * 512], BF16); k.dump("BTg", BT[:], rBT, [128, 2304], BF16)
        if os.environ.get("KSSD") == "2":
            break
        k.dve(A("tensor_scalar", out=ncs[:], in0=dte[:], scalar1=1e-30, scalar2=None, op0=ALU.max), [rdte], [rncs])
        k.act(A("activation", out=ncs[:], in_=ncs[:], func=AF.Ln), [rncs], [rncs])
        k.dve(A("tensor_tensor", out=ncs[:], in0=ncs[:], in1=csx[:], op=ALU.subtract), [rncs, rcsx], [rncs])
        h3 = lambda ap: ap.rearrange("p (h c) -> p h c", c=64)
        for d in (1, 0):
            a0 = d * 2048 + g * 512
            k.dve(A("tensor_copy", out=St[:], in_=self.acc[:, a0:a0 + 512]), [self.racc], [rSt])
            mask = triL if d == 1 else triU
            order = range(17, -1, -1) if d == 1 else range(18)
            order = list(order)

            def stA(c):
                cs_ = slice(c * 128, (c + 1) * 128)
                q8 = slice(c * 16 + d * 8, c * 16 + d * 8 + 8)
                xp_, rxp = xpd[c % 2]
                k.pool(A("tensor_tensor", out=h3(xp_[:]), in0=h3(xtok3[:, c, :]), in1=wgt[:, q8].unsqueeze(2).to_broadcast([128, 8, 64]), op=ALU.mult),
                       [rxt, rwgt], [rxp])
                for r4 in range(0, 8, 4):
                    pb, rpb = self.PS[2 + (r4 // 4)], self.PR[2 + (r4 // 4)]
                    for j in range(4):
                        col = c * 16 + d * 8 + r4 + j
                        k.pe(A("matmul", pb[:, j * 128:(j + 1) * 128], csx[:, col:col + 1].to_broadcast([128, 128]), idf, start=True, stop=False),
                             [rcsx, self.rcst], [rpb])
                        k.pe(A("matmul", pb[:, j * 128:(j + 1) * 128], idf, ncs[:, col:col + 1].to_broadcast([128, 128]), start=False, stop=True),
                             [rncs, self.rcst], [rpb])
                pss, rpss = self.PS[6 + (c % 2)], self.PR[6 + (c % 2)]
                k.pe(A("matmul", pss[:, 0:512], btok3[:, c, :], xp_[:], start=True, stop=True), [rbt, rxp], [rpss])
                pcb, rpcb = rbank()
                k.pe(A("matmul", pcb[:, 0:128], BT[:, cs_], CT[:, cs_], start=True, stop=True), [rBT, rCT], [rpcb])
                return pcb, rpcb

            def stB(c, pcb, rpcb):
                k.dve(A("tensor_tensor", out=CB[:], in0=pcb[:, 0:128], in1=mask, op=ALU.mult), [rpcb, self.rcst], [rCB])
                for ri, r4 in enumerate(range(0, 8, 4)):
                    pb, rpb = self.PS[2 + ri], self.PR[2 + ri]
                    em, rem = Em[ri]
                    k.act(A("activation", out=em[:], in_=pb[:, 0:512], func=AF.Exp), [rpb], [rem])
                    k.dve(A("scalar_tensor_tensor", out=Mt[:, r4 * 128:(r4 + 4) * 128].rearrange("p (h c) -> p h c", c=128),
                            in0=em[:].rearrange("p (h c) -> p h c", c=128), scalar=100.0, in1=CB[:].unsqueeze(1).to_broadcast([128, 4, 128]),
                            op0=ALU.min, op1=ALU.mult), [rem, rCB], [rMt])

            def stC(c):
                cs_ = slice(c * 128, (c + 1) * 128)
                q8 = slice(c * 16 + d * 8, c * 16 + d * 8 + 8)
                pss, rpss = self.PS[6 + (c % 2)], self.PR[6 + (c % 2)]
                py, rpy = self.PS[4], self.PR[4]
                for r in range(8):
                    k.pe(A("matmul", py[:, r * 64:(r + 1) * 64], Mt[:, r * 128:(r + 1) * 128], xtok3[:, c, r * 64:(r + 1) * 64],
                           start=True, stop=True), [rMt, rxt], [rpy])
                k.act(A("activation", out=Sbf[:], in_=St[:], func=AF.Copy), [rSt], [rSbf])
                po, rpo = self.PS[5], self.PR[5]
                k.pe(A("matmul", po[:, 0:512], CT[:, cs_], Sbf[:], start=True, stop=True), [rCT, rSbf], [rpo])
                t8 = dtot[:, q8].unsqueeze(2).to_broadcast([128, 8, 64])
                k.pool(A("tensor_tensor", out=h3(St[:]), in0=h3(St[:]), in1=t8, op=ALU.mult), [rSt, rdtot, rSbf], [rSt])
                k.dve(A("tensor_tensor", out=St[:], in0=St[:], in1=pss[:, 0:512], op=ALU.add), [rSt, rpss], [rSt])
                e8 = ecs[:, q8].unsqueeze(2).to_broadcast([128, 8, 64])
                k.dve(A("tensor_tensor", out=h3(tt[:]), in0=h3(po[:, 0:512]), in1=e8, op=ALU.mult), [rpo, recs], [rtt])
                ydst = yz3[:, c, g * 512:(g + 1) * 512]
                if d == 1:
                    k.dve(A("tensor_tensor", out=ydst, in0=tt[:], in1=py[:, 0:512], op=ALU.add), [rtt, rpy], [self.ry[c]])
                else:
                    d8 = self.dsk[:, g * 8:(g + 1) * 8].unsqueeze(2).to_broadcast([128, 8, 64])
                    k.pool(A("tensor_tensor", out=h3(uu[:]), in0=h3(xtok3[:, c, :]), in1=d8, op=ALU.mult), [rxt, self.rdsk], [ruu])
                    k.pool(A("tensor_tensor", out=uu[:], in0=uu[:], in1=ydst, op=ALU.add), [ruu, self.ry[c]], [ruu])
                    k.dve(A("tensor_tensor", out=tt[:], in0=tt[:], in1=py[:, 0:512], op=ALU.add), [rtt, rpy], [rtt])
                    k.dve(A("tensor_tensor", out=ydst, in0=tt[:], in1=uu[:], op=ALU.add), [rtt, ruu], [self.ry[c]])

            pc = stA(order[0])
            stB(order[0], *pc)
            for ci, c in enumerate(order):
                nxt = order[ci + 1] if ci + 1 < len(order) else None
                if nxt is not None:
                    pc = stA(nxt)
                stC(c)
                if nxt is not None:
                    stB(nxt, *pc)
    k.dump("yz", self.yz[:], self.ry, [128, 18 * 2048], BF16)
    self.S.flush()
    es.close()


K.phase_own_h = phase_own_h
K.phase_ssd = phase_ssd


def row_bcast(k, dst, rdst, colsT, rcols, nblk):
    for b0 in range(0, nblk, 4):
        ps, rps = k.bank()
        for b in range(b0, min(b0 + 4, nblk)):
            k.pe(A("transpose", ps[:, (b - b0) * 128:(b - b0 + 1) * 128], colsT(b).to_broadcast([128, 128]), k.cst[:, 0:128]),
                 [rcols, k.rcst], [rps])
        n = min(4, nblk - b0)
        k.act(A("activation", out=dst[:, b0 * 128:(b0 + n) * 128], in_=ps[:, 0:n * 128], func=AF.Copy), [rps], [rdst])


def phase_ssd_proj(self):
    k = self; I = self.I
    hT3 = self.hT3
    arW = self.arW; arA = self.arACC
    arA.close()
    self.mg, self.rmg = k.T("merged", [128, 8 * 2304], BF16, es=arW)
    mg3 = self.mg[:].rearrange("p (kc t) -> p kc t", kc=8)
    self.mg3 = mg3
    SUP = [(0, 4), (4, 4), (8, 4), (12, 4), (16, 2)]
    ryT = [Reg("yT%d" % i) for i in range(5)]
    tmpT, rtmpT = k.T("tmpT", [128, 16 * 512], BF16, es=arW)
    tmpT3 = tmpT[:].rearrange("p (j t) -> p j t", j=16)
    stage = k.T("stg", [128, 16 * 128], es=arW)
    wbf = [k.T("wbf%d" % i, [128, 8 * 128], BF16, es=arW) for i in range(2)]
    wpb, rwpb = k.T("wpb", [128, 16 * 128], BF16, es=arW)
    rstd = [k.T("rstd%d" % i, [128, 512], es=arA) for i in range(5)]
    szb = [k.T("szb%d" % i, [128, 512], BF16, es=arA) for i in range(2)]
    sqb = [k.T("sqb%d" % i, [128, 512], BF16, es=arA) for i in range(2)]
    sg, rsg = k.T("sg", [128, 512], es=arA)
    idb = self.cstb[:, 0:128]
    onesb = self.cstb[:, 128:256]
    ob, rob = k.T("onesb", [128, 128], BF16)
    k.dve(A("tensor_copy", out=ob[:], in_=self.ones[:]), [self.rones], [rob])
    yzf = self.yz

    def yT(st):
        n = SUP[st][1] * 128
        return yzf[:, st * 8192:st * 8192 + 16 * n].rearrange("p (j t) -> p j t", j=16), n

    yz3 = self.yz[:].rearrange("p (c f) -> p c f", c=18)
    for st, (c0, nc_) in enumerate(SUP):
        n = nc_ * 128
        for ci in range(nc_):
            for j8 in range(2):
                tb, rtb = k.bank()
                tbb = tb[:].bitcast(BF16)
                for j in range(8):
                    k.pe(A("transpose", tbb[:, j * 128:(j + 1) * 128], yz3[:, c0 + ci, (j8 * 8 + j) * 128:(j8 * 8 + j + 1) * 128], idb),
                         [self.ry[c0 + ci], self.rcstb], [rtb])
                k.act(A("activation", out=tmpT3[:, j8 * 8:j8 * 8 + 8, ci * 128:(ci + 1) * 128],
                        in_=tbb[:, 0:1024].rearrange("p (j t) -> p j t", j=8), func=AF.Copy), [rtb], [rtmpT])
        v, _ = yT(st)
        k.dve(A("tensor_copy", out=v[:, 0:8, :], in_=tmpT3[:, 0:8, 0:n]), [rtmpT] + [self.ry[c0 + ci] for ci in range(nc_)], [ryT[st]] + [self.ry[c0 + ci] for ci in range(nc_)])
        k.act(A("activation", out=v[:, 8:16, :], in_=tmpT3[:, 8:16, 0:n], func=AF.Copy), [rtmpT] + [self.ry[c0 + ci] for ci in range(nc_)], [ryT[st]] + [self.ry[c0 + ci] for ci in range(nc_)])
    accb = [(self.PS[3 + st], self.PR[3 + st]) for st in range(5)]
    rot = [0]

    def rbank():
        i = rot[0]; rot[0] = (i + 1) % 3
        return self.PS[i], self.PR[i]

    pend = None
    u = 0
    for j in range(16):
        wb, rwb = wbf[j % 2]
        wb3 = wb[:].rearrange("p (kc c) -> p kc c", kc=8)
        k.load_w((stage[0], stage[1]), (wb3, rwb), I["w_in"], j * 128, 128)
        for st, (c0, nc_) in enumerate(SUP):
            n = nc_ * 128
            t0 = (c0 + 1) * 128
            ps, rps = rbank()
            for kc in range(8):
                k.pe(A("matmul", ps[:, 0:n], wb3[:, kc, :], hT3[:, kc, t0:t0 + n], start=(kc == 0), stop=(kc == 7)), [rwb, self.rhT], [rps])
            if pend is not None:
                pend()
            sz_, rsz = szb[u % 2]; sq_, rsq = sqb[u % 2]
            u += 1
            k.act(A("activation", out=sz_[:, 0:n], in_=ps[:, 0:n], func=AF.Silu), [rps], [rsz])
            v, _ = yT(st)
            k.dve(A("tensor_tensor", out=v[:, j, :], in0=v[:, j, :], in1=sz_[:, 0:n], op=ALU.mult), [ryT[st], rsz], [ryT[st]])
            k.act(A("activation", out=sq_[:, 0:n], in_=v[:, j, :], func=AF.Square), [ryT[st]], [rsq])
            pend = (lambda st=st, n=n, sq_=sq_, rsq=rsq, j=j: k.pe(A("matmul", accb[st][0][:, 0:n], ob[:], sq_[:, 0:n], start=(j == 0), stop=(j == 15)),
                                                                   [rob, rsq], [accb[st][1]]))
    pend()
    for st, (c0, nc_) in enumerate(SUP):
        n = nc_ * 128
        r_, rr_ = rstd[st]
        k.dve(A("tensor_scalar", out=r_[:, 0:n], in0=accb[st][0][:, 0:n], scalar1=1.0 / 2048, scalar2=EPS, op0=ALU.mult, op1=ALU.add),
              [accb[st][1]], [rr_])
        k.act(A("activation", out=r_[:, 0:n], in_=r_[:, 0:n], func=AF.Sqrt), [rr_], [rr_])
        k.dve(A("reciprocal", out=r_[:, 0:n], in_=r_[:, 0:n]), [rr_], [rr_])
    ar2 = Arena(self.arW.lo + 8 * 2304 * 2, self.arW.lo + 8 * 2304 * 2 + 16 * 512 * 2)
    stgA = [stage, k.T("stgA1", [128, 16 * 128], es=ar2)]
    stgB = [k.T("stgB0", [128, 8 * 128], es=ar2)]
    wpbs = [(wpb, rwpb), k.T("wpb1", [128, 16 * 128], BF16, es=ar2)]
    first = [True]

    def load_cb(cbk):
        sA, rsA = stgA[cbk % 2]
        sA3 = sA[:].rearrange("p (j c) -> p j c", j=16)
        wp_, rwp_ = wpbs[cbk % 2]
        wp3_ = wp_[:].rearrange("p (j c) -> p j c", j=16)
        extra = [rtmpT] if cbk <= 1 else []
        k.dma(sA3, I["w_o_ssd"][:, cbk * 128:(cbk + 1) * 128].rearrange("(j p) c -> p j c", p=128), [], [rsA] + extra)
        k.dve(A("tensor_tensor", out=wp3_, in0=sA3, in1=self.sng[:].unsqueeze(2).to_broadcast([128, 16, 128]), op=ALU.mult),
              [rsA, self.rsng], [rwp_] + extra)
        wb, rwb = wbf[cbk % 2]
        wb3 = wb[:].rearrange("p (kc c) -> p kc c", kc=8)
        sB, rsB = stgB[0]
        sB3 = sB[:].rearrange("p (kc c) -> p kc c", kc=8)
        k.dma(sB3, I["w_in"][:, G0 + cbk * 128:G0 + (cbk + 1) * 128].rearrange("(kc p) c -> p kc c", p=128), [], [rsB] + ([rtmpT] if first[0] else []))
        k.pool(A("tensor_copy", out=wb3, in_=sB3), [rsB], [rwb])
        first[0] = False

    load_cb(0)
    for cbk in range(8):
        if cbk + 1 < 8:
            load_cb(cbk + 1)
        wp3 = wpbs[cbk % 2][0][:].rearrange("p (j c) -> p j c", j=16)
        rwpb = wpbs[cbk % 2][1]
        wb, rwb = wbf[cbk % 2]
        wb3 = wb[:].rearrange("p (kc c) -> p kc c", kc=8)
        for st, (c0, nc_) in enumerate(SUP):
            n = nc_ * 128
            t0 = (c0 + 1) * 128
            v, _ = yT(st)
            po, rpo = rbank()
            for j in range(16):
                k.pe(A("matmul", po[:, 0:n], wp3[:, j, :], v[:, j, :], start=(j == 0), stop=(j == 15)), [rwpb, ryT[st]], [rpo])
            pg, rpg = rbank()
            for kc in range(8):
                k.pe(A("matmul", pg[:, 0:n], wb3[:, kc, :], hT3[:, kc, t0:t0 + n], start=(kc == 0), stop=(kc == 7)), [rwb, self.rhT], [rpg])
            k.act(A("activation", out=sg[:, 0:n], in_=pg[:, 0:n], func=AF.Sigmoid), [rpg], [rsg])
            k.dve(A("tensor_tensor", out=sg[:, 0:n], in0=sg[:, 0:n], in1=rstd[st][0][:, 0:n], op=ALU.mult), [rsg, rstd[st][1]], [rsg])
            k.dve(A("tensor_tensor", out=mg3[:, cbk, c0 * 128:c0 * 128 + n], in0=po[:, 0:n], in1=sg[:, 0:n], op=ALU.mult), [rpo, rsg], [self.rmg])
    k.dump("mg_ssd", self.mg[:], self.rmg, [128, 8 * 2304], BF16)
    self.S.flush()
    arA.close()
    arW.cur = arW.lo + 8 * 2304 * 2


K.phase_ssd_proj = phase_ssd_proj


def phase_att(self):
    k = self; I = self.I
    hT3 = self.hT3
    arY = self.arYZ; arA = self.arACC; arW = self.arW
    arY.close(); arA.close()
    yatt, ryatt = k.T("yatt", [128, 18 * 1024], BF16, es=arY)
    yatt3 = yatt[:].rearrange("p (i f) -> p i f", i=18)
    cosT, rcos = k.T("cosT", [64, 2560], es=arY)
    sinT, rsin = k.T("sinT", [64, 2560], es=arY)
    kT, rkT = k.T("kT", [64, 2560], BF16, es=arY)
    Pbs = [k.T("Pb%d" % i, [128, 5 * 512], BF16, es=arY) for i in range(2)]
    Va, rVa = k.T("Va", [128, 20 * 4 * 65], BF16, es=arA)
    Va4 = Va[:].rearrange("p (t g c) -> p t g c", t=20, g=4)
    t_, rt_ = k.T("rt", [64, 512], es=arA)
    u_, ru_ = k.T("ru", [64, 512], es=arA)
    wmark = arW.cur
    qg, rqg = k.T("qg", [64, 18 * 512], BF16, es=arW)
    qg4 = qg[:].rearrange("p (i h c) -> p i h c", i=18, h=4)
    stage = k.T("stg", [128, 8 * 128], es=arW)
    wv, rwv = k.T("wv", [128, 8 * 256], BF16, es=arW)
    wv3 = wv[:].rearrange("p (kc c) -> p kc c", kc=8)
    wm = [k.T("wqm%d" % i, [128, 8 * 64], BF16, es=arW) for i in range(2)]
    wp = [k.T("wqp%d" % i, [128, 8 * 64], BF16, es=arW) for i in range(2)]
    den, rden = k.T("den", [128, 8], es=arW)
    vca4 = self.vca[:].rearrange("p (s g c) -> p s g c", s=2, g=4)
    triLb = self.cstb[:, 256:384]; triUb = self.cstb[:, 128:256]

    k.dma(cosT[:], I["cosT"], [], [rcos])
    k.dma(sinT[:], I["sinT"], [], [rsin])
    k.dve(A("memset", Va[:], 1.0), [], [rVa])
    for i in range(2):
        k.load_w(stage, (wv3[:, :, i * 128:(i + 1) * 128], rwv), I["w_in"], V0 + i * 128, 128)
    for t in range(NOWN):
        ps, rps = k.bank()
        for kc in range(8):
            k.pe(A("matmul", ps[:, 0:256], hT3[:, kc, t * 128:(t + 1) * 128], wv3[:, kc, :], start=(kc == 0), stop=(kc == 7)), [rwv, self.rhT], [rps])
        k.act(A("activation", out=Va4[:, t, :, 0:64], in_=ps[:, 0:256].rearrange("p (g c) -> p g c", c=64), func=AF.Copy), [rps], [rVa])

    def proj_rope(c_main, c_perm, wi, tok0, ntok, dst_fn, rdst):
        wm_, rwm = wm[wi % 2]; wp_, rwp = wp[wi % 2]
        wm3 = wm_[:].rearrange("p (kc c) -> p kc c", kc=8); wp3 = wp_[:].rearrange("p (kc c) -> p kc c", kc=8)
        k.load_w(stage, (wm3, rwm), I["w_in"], c_main, 64)
        k.load_w(stage, (wp3, rwp), I["w_qkp"], c_perm, 64)
        for e0 in range(0, ntok, 512):
            n = min(512, ntok - e0)
            pa, rpa = k.bank(); pb, rpb = k.bank()
            for kc in range(8):
                k.pe(A("matmul", pa[0:64, 0:n], wm3[:, kc, :], hT3[:, kc, tok0 + e0:tok0 + e0 + n], start=(kc == 0), stop=(kc == 7)), [rwm, self.rhT], [rpa])
            for kc in range(8):
                k.pe(A("matmul", pb[0:64, 0:n], wp3[:, kc, :], hT3[:, kc, tok0 + e0:tok0 + e0 + n], start=(kc == 0), stop=(kc == 7)), [rwp, self.rhT], [rpb])
            k.dve(A("tensor_tensor", out=t_[:, 0:n], in0=pa[0:64, 0:n], in1=cosT[:, tok0 + e0:tok0 + e0 + n], op=ALU.mult), [rpa, rcos], [rt_])
            k.dve(A("tensor_tensor", out=u_[:, 0:n], in0=pb[0:64, 0:n], in1=sinT[:, tok0 + e0:tok0 + e0 + n], op=ALU.mult), [rpb, rsin], [ru_])
            o_, view = dst_fn(e0, n)
            k.dve(A("tensor_tensor", out=o_, in0=view(t_[:, 0:n]), in1=view(u_[:, 0:n]), op=ALU.add), [rt_, ru_], [rdst])

    for g in range(4):
        proj_rope(K0 + g * 64, 1024 + g * 64, 0, 0, 2560, lambda e0, n: (kT[:, e0:e0 + n], (lambda a: a)), rkT)
        for h in range(4):
            hh = 4 * g + h
            proj_rope(Q0 + hh * 64, hh * 64, 1 + h, 128, 2304,
                      (lambda e0, n, h=h: (qg4[:, e0 // 128:(e0 + n) // 128, h, :], (lambda a: a.rearrange("p (i c) -> p i c", c=128)))), rqg)
        if g == 0:
            k.dump("kT0", kT[:], rkT, [64, 2560], BF16)
            k.dump("qg0", qg[:], rqg, [64, 18 * 512], BF16)
        def scores(i):
            Pb, rPb = Pbs[i % 2]
            kbs = [(kT[:, i * 128:(i + 1) * 128], rkT, Va4[:, i, g, :], rVa, triLb, i),
                   (kT[:, (i + 1) * 128:(i + 2) * 128], rkT, Va4[:, i + 1, g, :], rVa, None, None),
                   (kT[:, (i + 2) * 128:(i + 3) * 128], rkT, Va4[:, i + 2, g, :], rVa, triUb, i + 2),
                   (self.kcT[:, g * 256:g * 256 + 128], self.rkcT, vca4[:, 0, g, :], self.rvca, None, None),
                   (self.kcT[:, g * 256 + 128:g * 256 + 256], self.rkcT, vca4[:, 1, g, :], self.rvca, None, None)]
            for kb, (kap, rk_, vap, rv_, mask, fl) in enumerate(kbs):
                ps, rps = k.bank()
                k.pe(A("matmul", ps[:, 0:512], kap, qg[:, i * 512:(i + 1) * 512], start=True, stop=True), [rk_, rqg], [rps])
                pk = Pb[:, kb * 512:(kb + 1) * 512]
                k.act(A("activation", out=pk, in_=ps[:, 0:512], func=AF.Exp, scale=0.125), [rps], [rPb])
                if mask is not None:
                    p3 = pk.rearrange("p (h c) -> p h c", c=128)
                    k.dve(A("scalar_tensor_tensor", out=p3, in0=p3, scalar=self.flags[:, fl:fl + 1], in1=mask.unsqueeze(1).to_broadcast([128, 4, 128]),
                            op0=ALU.mult, op1=ALU.mult), [rPb, self.rflags, self.rcstb], [rPb])
            return kbs

        def pv(i, kbs):
            Pb, rPb = Pbs[i % 2]
            po, rpo = k.bank()
            for h in range(4):
                for kb, (kap, rk_, vap, rv_, mask, fl) in enumerate(kbs):
                    k.pe(A("matmul", po[:, h * 65:(h + 1) * 65], Pb[:, kb * 512 + h * 128:kb * 512 + (h + 1) * 128], vap, start=(kb == 0), stop=(kb == 4)),
                         [rPb, rv_], [rpo])
            po3 = po[:, 0:260].rearrange("p (h c) -> p h c", c=65)
            k.dve(A("tensor_tensor", out=den[:, 0:4], in0=po3[:, :, 64], in1=self.esink[:, 4 * g:4 * g + 4], op=ALU.add), [rpo, self.resink], [rden])
            k.dve(A("reciprocal", out=den[:, 4:8], in_=den[:, 0:4]), [rden], [rden])
            k.dve(A("tensor_tensor", out=yatt3[:, i, g * 256:(g + 1) * 256].rearrange("p (h c) -> p h c", c=64), in0=po3[:, :, 0:64],
                    in1=den[:, 4:8].unsqueeze(2).to_broadcast([128, 4, 64]), op=ALU.mult), [rpo, rden], [ryatt])

        prevk = None
        for i in range(18):
            cur = scores(i)
            if prevk is not None:
                pv(i - 1, prevk)
            prevk = cur
        pv(17, prevk)
    k.dump("yatt", yatt[:], ryatt, [128, 18 * 1024], BF16)
    self.S.flush()
    arW.cur = wmark
    arA.close()
    arY.cur = arY.lo + 18 * 1024 * 2
    woa, rwoa = k.T("woa", [128, 8 * 1024], BF16, es=arY)
    wga, rwga = k.T("wga", [128, 8 * 1024], BF16, es=arY)
    woa3 = woa[:].rearrange("p (kc c) -> p kc c", kc=8); wga3 = wga[:].rearrange("p (kc c) -> p kc c", kc=8)
    stage = k.T("stg", [128, 8 * 128], es=arW)
    yT_, ryT = k.T("yattT", [128, 8 * 512], BF16, es=arA)
    yT3 = yT_[:].rearrange("p (kc t) -> p kc t", kc=8)
    sgs = [k.T("sg%d" % i, [128, 512], es=arA) for i in range(2)]
    tts = [k.T("tt%d" % i, [128, 512], es=arA) for i in range(2)]
    idb = self.cstb[:, 0:128]
    mg3 = self.mg3
    for i in range(8):
        k.load_w(stage, (woa3[:, :, i * 128:(i + 1) * 128], rwoa), I["w_o_att"], i * 128, 128)
        k.load_w(stage, (wga3[:, :, i * 128:(i + 1) * 128], rwga), I["w_in"], G0 + 1024 + i * 128, 128)
    SUP = [(0, 4), (4, 4), (8, 4), (12, 4), (16, 2)]
    for st, (c0, nc_) in enumerate(SUP):
        n = nc_ * 128
        t0 = (c0 + 1) * 128
        for ci in range(nc_):
            tb, rtb = k.bank()
            tbb = tb[:].bitcast(BF16)
            for kc in range(8):
                k.pe(A("transpose", tbb[:, kc * 128:(kc + 1) * 128], yatt3[:, c0 + ci, kc * 128:(kc + 1) * 128], idb), [ryatt, self.rcstb], [rtb])
            k.act(A("activation", out=yT3[:, :, ci * 128:(ci + 1) * 128], in_=tbb[:, 0:1024].rearrange("p (kc t) -> p kc t", kc=8), func=AF.Copy), [rtb], [ryT])
        for cbk in range(8):
            sg, rsg = sgs[cbk % 2]; tt, rtt = tts[cbk % 2]
            po, rpo = k.bank(); pg, rpg = k.bank()
            for kc in range(8):
                k.pe(A("matmul", po[:, 0:n], woa3[:, kc, cbk * 128:(cbk + 1) * 128], yT3[:, kc, 0:n], start=(kc == 0), stop=(kc == 7)), [rwoa, ryT], [rpo])
            for kc in range(8):
                k.pe(A("matmul", pg[:, 0:n], wga3[:, kc, cbk * 128:(cbk + 1) * 128], hT3[:, kc, t0:t0 + n], start=(kc == 0), stop=(kc == 7)), [rwga, self.rhT], [rpg])
            k.act(A("activation", out=sg[:, 0:n], in_=pg[:, 0:n], func=AF.Sigmoid), [rpg], [rsg])
            k.dve(A("tensor_tensor", out=tt[:, 0:n], in0=po[:, 0:n], in1=sg[:, 0:n], op=ALU.mult), [rpo, rsg], [rtt])
            m_ = mg3[:, cbk, c0 * 128:c0 * 128 + n]
            k.dve(A("tensor_tensor", out=m_, in0=m_, in1=tt[:, 0:n], op=ALU.add), [rtt, self.rmg], [self.rmg])
    k.dump("mg", self.mg[:], self.rmg, [128, 8 * 2304], BF16)
    self.S.flush()
    arA.close(); arY.close()
    arW.cur = wmark


K.phase_att = phase_att


def phase_wout(self):
    k = self; I = self.I
    arY = self.arYZ; arA = self.arACC; arH = self.arHT
    arY.close(); arA.close(); arH.close()
    mg3 = self.mg3
    self.h2T, self.rh2T = k.T("h2T", [128, 8 * 2050], BF16, es=arH)
    h2T3 = self.h2T[:].rearrange("p (kc t) -> p kc t", kc=8)
    self.h2T3 = h2T3
    g1, rg1 = k.T("g1row", [128, D], es=arY)
    wo, rwo = k.T("wo", [128, 8 * D], BF16, es=arY)
    wo3 = wo[:].rearrange("p (kc c) -> p kc c", kc=8)
    stage = k.T("stg", [128, 8 * 128], es=arY)
    stg3 = stage[0][:].rearrange("p (kc c) -> p kc c", kc=8)
    xts = [k.T("xw%d" % i, [128, D], es=arY) for i in range(3)]
    x1t = [k.T("x1t%d" % i, [128, D], es=arY) for i in range(3)]
    tmps = [(k.T("junk%d" % i, [128, D], BF16, es=arY), k.T("ss%d" % i, [128, 4], es=arY), k.T("xn%d" % i, [128, D], BF16, es=arY)) for i in range(3)]
    m3 = self.modT[:].rearrange("p (b v) -> p b v", v=2)
    row_bcast(k, g1, rg1, (lambda b: m3[:, 16 + b, 0:1]), self.rmodT2, 8)
    for i in range(8):
        k.dma(stg3, I["w_out"][:, i * 128:(i + 1) * 128].rearrange("(kc p) c -> p kc c", p=128), [], [stage[1]])
        k.dve(A("tensor_tensor", out=wo3[:, :, i * 128:(i + 1) * 128], in0=stg3, in1=g1[:, i * 128:(i + 1) * 128].unsqueeze(1).to_broadcast([128, 8, 128]),
                op=ALU.mult), [stage[1], rg1], [rwo])
    self.rx1s = [Reg("x1s%d" % i) for i in range(16)]

    def st_a(i):
        xt_, rxt = xts[i % 3]; x1_, rx1 = x1t[i % 3]
        k.dma(xt_[:], I["x_own"][(i + 1) * 128:(i + 2) * 128, :], [], [rxt])
        for half in range(2):
            ps, rps = k.bank()
            for kc in range(8):
                k.pe(A("matmul", ps[:, 0:512], mg3[:, kc, i * 128:(i + 1) * 128], wo3[:, kc, half * 512:(half + 1) * 512], start=(kc == 0), stop=(kc == 7)),
                     [self.rmg, rwo], [rps])
            k.dve(A("tensor_tensor", out=x1_[:, half * 512:(half + 1) * 512], in0=ps[:, 0:512], in1=xt_[:, half * 512:(half + 1) * 512], op=ALU.add),
                  [rps, rxt], [rx1])
        if 1 <= i <= 16:
            k.dma(self.x1s[(i - 1) * 128:i * 128, :], x1_[:], [rx1], [self.rx1s[i - 1]], q="pool")
        k.norm_a(x1_, rx1, 128, tmps[i % 3])
        if i == 5:
            k.dump("x1_5", x1_[:], rx1, [128, D])

    def st_b(i):
        if 1 <= i <= 16:
            k.norm_b(128, (lambda kc, i=i: h2T3[:, kc, 1 + (i - 1) * 128:1 + i * 128]), self.rh2T, 2, 24, 0, tmps[i % 3])
        elif i == 0:
            k.norm_b(128, (lambda kc: h2T3[:, kc, 0:1]), self.rh2T, 2, 24, 0, tmps[i % 3], src_view=lambda a: a[:, 127:128])
        else:
            k.norm_b(128, (lambda kc: h2T3[:, kc, 2049:2050]), self.rh2T, 2, 24, 0, tmps[i % 3], src_view=lambda a: a[:, 0:1])

    st_a(0)
    for i in range(18):
        if i + 1 < 18:
            st_a(i + 1)
        st_b(i)
    for (col, fl) in ((0, 1), (2049, 18)):
        for kc in range(8):
            k.dve(A("tensor_scalar", out=h2T3[:, kc, col:col + 1], in0=h2T3[:, kc, col:col + 1], scalar1=self.flags[:, fl:fl + 1], scalar2=None, op0=ALU.mult),
                  [self.rh2T, self.rflags], [self.rh2T])
    k.dump("h2T", self.h2T[:], self.rh2T, [128, 8 * 2050], BF16)
    self.S.flush()
    arY.close()


def phase_ffn(self):
    k = self; I = self.I
    h2T3 = self.h2T3
    arA = self.arACC
    arB = Arena(self.arYZ.lo, self.arW.hi)
    arH = self.arHT
    arA.close()
    Gb, rGb = k.T("Gb", [128, 22 * 2048], BF16, es=arB)
    Gb3 = Gb[:].rearrange("p (j t) -> p j t", j=22)
    wd, rwd = k.T("wd", [128, 22 * D], BF16, es=arB)
    wd3 = wd[:].rearrange("p (j c) -> p j c", j=22)
    wab = [k.T("wab%d" % i, [128, 8 * 128], BF16, es=arB) for i in range(2)]
    stage = k.T("stg", [128, 8 * 128], es=arH)
    raw = [k.T("raw%d" % i, [128, 412], es=arA) for i in range(2)]
    ac, rac = k.T("ac", [128, 412], es=arA)
    bc, rbc = k.T("bc", [128, 412], es=arA)
    fcw, rfcw = k.T("fcw", [128, 132], es=arA)
    fcb, rfcb = k.T("fcb", [128, 44], es=arA)
    k.dma(fcw[:], I["ffn_cw"], [], [rfcw])
    k.dma(fcb[:], I["ffn_cb"], [], [rfcb])
    fw3 = fcw[:].rearrange("p (b t) -> p b t", t=3)
    g2, rg2 = k.T("g2row", [128, D], es=arA)
    stgd = k.T("stgd", [128, D], es=arA)
    m3 = self.modT[:].rearrange("p (b v) -> p b v", v=2)
    row_bcast(k, g2, rg2, (lambda b: m3[:, 40 + b, 0:1]), self.rmodT2, 8)
    tiles = [(1, 410), (411, 410), (821, 410), (1231, 410), (1641, 408)]
    for j in range(22):
        k.dma(stgd[0][:], I["w_down"][j * 128:(j + 1) * 128, :], [], [stgd[1]])
        k.pool(A("tensor_tensor", out=wd3[:, j, :], in0=stgd[0][:], in1=g2[:], op=ALU.mult), [stgd[1], rg2], [rwd])
        for ab in range(2):
            wb_, rwb = wab[ab]
            wb3 = wb_[:].rearrange("p (kc c) -> p kc c", kc=8)
            k.load_w(stage, (wb3, rwb), I["w_up"], ab * D_FF + j * 128, 128)
        wa3 = wab[0][0][:].rearrange("p (kc c) -> p kc c", kc=8); rwa = wab[0][1]
        wbb3 = wab[1][0][:].rearrange("p (kc c) -> p kc c", kc=8); rwb = wab[1][1]
        for (o0, n) in tiles:
            pa, rpa = k.bank(); pb, rpb = k.bank()
            for kc in range(8):
                k.pe(A("matmul", pa[:, 0:n + 2], wa3[:, kc, :], h2T3[:, kc, o0 - 1:o0 + n + 1], start=(kc == 0), stop=(kc == 7)), [rwa, self.rh2T], [rpa])
            for kc in range(8):
                k.pe(A("matmul", pb[:, 0:n + 2], wbb3[:, kc, :], h2T3[:, kc, o0 - 1:o0 + n + 1], start=(kc == 0), stop=(kc == 7)), [rwb, self.rh2T], [rpb])
            fa = conv3(k, pa, rpa, n, fw3[:, j, :], fcb[:, j:j + 1], raw[0], ac[:, 0:n], rac, AF.Silu, [rfcw, rfcb], defer=True)
            (rb_, rrb_) = raw[1]
            k.act(A("activation", out=rb_[:, 0:n], in_=pb[:, 1:n + 1], func=AF.Identity, scale=fw3[:, 22 + j, 1:2], bias=fcb[:, 22 + j:23 + j]),
                  [rpb, rfcw, rfcb], [rrb_])
            fa()
            k.dve(A("scalar_tensor_tensor", out=rb_[:, 0:n], in0=pb[:, 0:n], scalar=fw3[:, 22 + j, 0:1], in1=rb_[:, 0:n], op0=ALU.mult, op1=ALU.add),
                  [rpb, rrb_, rfcw], [rrb_])
            k.dve(A("scalar_tensor_tensor", out=rb_[:, 0:n], in0=pb[:, 2:n + 2], scalar=fw3[:, 22 + j, 2:3], in1=rb_[:, 0:n], op0=ALU.mult, op1=ALU.add),
                  [rpb, rrb_, rfcw], [rrb_])
            k.dve(A("tensor_tensor", out=Gb3[:, j, o0 - 1:o0 - 1 + n], in0=ac[:, 0:n], in1=rb_[:, 0:n], op=ALU.mult), [rac, rrb_], [rGb])
    k.dump("Gb", Gb[:], rGb, [128, 22 * 2048], BF16)
    self.S.flush()
    arA.close()
    fg, rfg = k.T("fgrow", [128, D], es=arA)
    x1r = [k.T("x1r%d" % i, [128, D], es=arA) for i in range(2)]
    k.dma(fg[:], I["fg_row"], [], [rfg])
    arB.cur = arB.lo + 22 * 2048 * 2 + 22 * D * 2
    arH.cur = arH.lo + 8 * 2050 * 2 + 64
    stg2 = k.T("stgd", [128, D], es=arB)
    xos = [k.T("xo0", [128, D], es=arB), stg2]
    junk, rjunk = k.T("junk", [128, D], BF16, es=arH)
    sss = [k.T("ssa", [128, 4], es=arH), k.T("ssb", [128, 4], es=arH)]
    for i in range(16):
        xr, rxr = x1r[i % 2]
        xo, rxo = xos[i % 2]
        ss, rss = sss[i % 2]
        k.dma(xr[:], self.x1s[i * 128:(i + 1) * 128, :], [self.rx1s[i]], [rxr])
        for half in range(2):
            ps, rps = k.bank()
            for j in range(22):
                k.pe(A("matmul", ps[:, 0:512], Gb3[:, j, i * 128:(i + 1) * 128], wd3[:, j, half * 512:(half + 1) * 512], start=(j == 0), stop=(j == 21)),
                     [rGb, rwd], [rps])
            k.dve(A("tensor_tensor", out=xo[:, half * 512:(half + 1) * 512], in0=ps[:, 0:512], in1=xr[:, half * 512:(half + 1) * 512], op=ALU.add),
                  [rps, rxr], [rxo])
        k.dve(A("memset", ss[:, 0:1], 0.0), [], [rss])
        k.act(A("activation", out=junk[:], in_=xo[:], func=AF.Square, accum_out=ss[:, 0:1]), [rxo], [rjunk, rss])
        k.dve(A("tensor_scalar", out=ss[:, 1:2], in0=ss[:, 0:1], scalar1=1.0 / D, scalar2=EPS, op0=ALU.mult, op1=ALU.add), [rss], [rss])
        k.act(A("activation", out=ss[:, 1:2], in_=ss[:, 1:2], func=AF.Sqrt), [rss], [rss])
        k.dve(A("reciprocal", out=ss[:, 2:3], in_=ss[:, 1:2]), [rss], [rss])
        k.dve(A("scalar_tensor_tensor", out=xr[:], in0=xo[:], scalar=ss[:, 2:3], in1=fg[:], op0=ALU.mult, op1=ALU.mult), [rxo, rss, rfg], [rxr])
        k.dma(self.out[i * 128:(i + 1) * 128, :], xr[:], [rxr], [], q="pool")
    self.S.flush()


K.phase_wout = phase_wout
K.phase_ffn = phase_ffn
```
